# Optimizing a Trainium2 kernel written in Bass

```python
import jax, jax.numpy as jnp
from jax import lax
import numpy as np

D_MODEL = 2048
BATCH = 2
SEQ = 16384
DEPTH = 1

GRID_W = 64
CTX_LEN = 256
EPS = 1e-6

A_WIDTH = D_MODEL // 2
A_HEAD = 64
A_HEADS = A_WIDTH // A_HEAD
A_DECAY_RANK = 64
A_ICLR_RANK = 64
A_LN_EPS = 64e-5

B_WIDTH = D_MODEL // 2
B_HEADS = 4
B_KEY = B_WIDTH // 2
B_DK = B_KEY // B_HEADS
B_DV = B_WIDTH // B_HEADS
B_GATE_RANK = 16
B_GATE_NORM = 16.0
B_CHUNK = 64

IN_SIZES = (3 * A_WIDTH, A_WIDTH, 2 * A_DECAY_RANK, 2 * A_ICLR_RANK,
            B_KEY, B_KEY, B_WIDTH, B_WIDTH, 2 * B_GATE_RANK, D_MODEL, D_MODEL)
P_IN = 3 * A_WIDTH + A_WIDTH + 2 * A_DECAY_RANK + 2 * A_ICLR_RANK + 2 * B_KEY + 2 * B_WIDTH + 2 * B_GATE_RANK + 2 * D_MODEL

kernel_name = "hybrid_rwkv7_gla_gated_dit_layer"


def _rmsnorm(x, g):
    xf = x.astype(jnp.float32)
    y = xf * lax.rsqrt(jnp.mean(xf * xf, axis=-1, keepdims=True) + EPS)
    return (y * g).astype(x.dtype)


def _modulation(cond, w_mod, b_mod):
    m = jax.nn.silu(cond) @ w_mod + b_mod
    return jnp.split(m, 3, axis=-1)


def _split_points():
    return [int(s) for s in np.cumsum(IN_SIZES)[:-1]]


def _grid_conv(u, w):
    bsz, t, ch = u.shape
    rows = t // GRID_W
    img = u.reshape(bsz, rows, GRID_W, ch)
    out = lax.conv_general_dilated(img, w[:, :, None, :], (1, 1), "SAME",
                                   dimension_numbers=("NHWC", "HWIO", "NHWC"),
                                   feature_group_count=ch)
    return out.reshape(bsz, t, ch)


def _seq_conv(u, w_row):
    up = jnp.pad(u, ((0, 0), (1, 1), (0, 0)))
    return up[:, :-2] * w_row[0] + up[:, 1:-1] * w_row[1] + up[:, 2:] * w_row[2]


def _features(h, lp, grid):
    f32 = jnp.float32
    bsz, t, _ = h.shape
    p = h @ lp["w_in"]
    (rkv, z_a, lora_w, lora_a, q_b, k_b, v_b, z_b, lora_g, g_a, g_b) = jnp.split(p, _split_points(), axis=-1)
    rkv = _grid_conv(rkv, lp["conv_w"]) if grid else _seq_conv(rkv, lp["conv_w"][1])
    r, k, v = jnp.split(rkv.astype(f32), 3, axis=-1)
    lw = jnp.tanh(lora_w.astype(f32).reshape(bsz, t, 2, A_DECAY_RANK))
    w_log = lp["a_w0"] + jnp.einsum("btdr,drc->btdc", lw, lp["a_w2"])
    decay = jnp.exp(-jnp.exp(-jax.nn.softplus(-w_log) - 0.5))
    la = lora_a.astype(f32).reshape(bsz, t, 2, A_ICLR_RANK)
    iclr = jax.nn.sigmoid(lp["a_a0"] + jnp.einsum("btdr,drc->btdc", la, lp["a_a2"]))
    kk = (k * lp["a_k_k"]).reshape(bsz, t, A_HEADS, A_HEAD)
    kk = kk * lax.rsqrt(jnp.sum(kk * kk, axis=-1, keepdims=True) + 1e-12)
    k_dir = k[:, :, None, :] * (1.0 + (iclr - 1.0) * lp["a_k_a"])
    heads_a = lambda z: z.reshape(z.shape[:-1] + (A_HEADS, A_HEAD))
    gk = jnp.einsum("btdr,drc->btdc", lora_g.astype(f32).reshape(bsz, t, 2, B_GATE_RANK), lp["b_gk_w2"])
    gk = jax.nn.log_sigmoid(gk + lp["b_gk_b"]) / B_GATE_NORM
    return {
        "r": heads_a(r), "v": heads_a(v), "kk": kk, "k": heads_a(k_dir),
        "decay": heads_a(decay), "iclr": heads_a(iclr), "z_a": z_a,
        "q": q_b.astype(f32).reshape(bsz, t, B_HEADS, B_DK) * (B_DK ** -0.5),
        "kg": k_b.astype(f32).reshape(bsz, t, B_HEADS, B_DK),
        "vg": v_b.astype(f32).reshape(bsz, t, B_HEADS, B_DV),
        "gk": gk.reshape(bsz, t, 2, B_HEADS, B_DK), "z_b": z_b, "g_a": g_a, "g_b": g_b,
    }


def _rwkv_scan(r, decay, k, v, a_vec, b_vec, s0, reverse, emit):
    def step(s, inp):
        w_t, k_t, v_t, a_t, b_t = inp[:5]
        sa = jnp.einsum("bhvk,bhk->bhv", s, a_t)
        s = s * w_t[:, :, None, :] + sa[..., None] * b_t[:, :, None, :] + v_t[..., None] * k_t[:, :, None, :]
        y = jnp.einsum("bhvk,bhk->bhv", s, inp[5]) if emit else None
        return s, y
    seqs = (decay, k, v, a_vec, b_vec) + ((r,) if emit else ())
    xs = tuple(jnp.moveaxis(z, 1, 0) for z in seqs)
    s, ys = lax.scan(step, s0, xs, reverse=reverse)
    return s, (jnp.moveaxis(ys, 0, 1) if emit else None)


def _gla_chunked(q, k, v, g, s0, emit):
    bsz, t, nh, dk = q.shape
    dv = v.shape[-1]
    n = t // B_CHUNK
    chunks = lambda z: z.reshape(bsz, n, B_CHUNK, nh, z.shape[-1])
    qc, kc, vc, gc = chunks(q), chunks(k), chunks(v), chunks(g)
    bcum = jnp.cumsum(gc, axis=2)
    btot = bcum[:, :, -1:]
    k_state = kc * jnp.exp(btot - bcum)
    chunk_decay = jnp.exp(btot[:, :, 0])

    def step(s, inp):
        dec, kst, vv = inp
        s_next = s * dec[..., None] + jnp.einsum("bchk,bchv->bhkv", kst, vv)
        return s_next, s
    s_fin, s_starts = lax.scan(step, s0, (jnp.moveaxis(chunk_decay, 1, 0), jnp.moveaxis(k_state, 1, 0),
                                         jnp.moveaxis(vc, 1, 0)))
    if not emit:
        return None, s_fin
    s_starts = jnp.moveaxis(s_starts, 0, 1)
    q_dec = qc * jnp.exp(bcum)
    k_inv = kc * jnp.exp(-bcum)
    inter = jnp.einsum("bnchk,bnhkv->bnchv", q_dec, s_starts)
    att = jnp.einsum("bnchk,bnshk->bnhcs", q_dec, k_inv)
    att = jnp.where(jnp.tril(jnp.ones((B_CHUNK, B_CHUNK), dtype=bool)), att, 0.0)
    intra = jnp.einsum("bnhcs,bnshv->bnchv", att, vc)
    return (inter + intra).reshape(bsz, t, nh, dv), s_fin


def _merge(f, wkv, o_b, lp):
    bsz, t = wkv.shape[:2]
    mu = jnp.mean(wkv, axis=-1, keepdims=True)
    var = jnp.mean(jnp.square(wkv - mu), axis=-1, keepdims=True)
    y_a = ((wkv - mu) * lax.rsqrt(var + A_LN_EPS)).reshape(bsz, t, A_WIDTH) * lp["a_lnx_g"] + lp["a_lnx_b"]
    bonus = jnp.einsum("bthn,btdhn,hn->bth", f["r"], f["k"], lp["a_r_k"])[..., None] * f["v"]
    y_a = (y_a + bonus.reshape(bsz, t, A_WIDTH)) * jax.nn.silu(f["z_a"])
    ob = o_b * lax.rsqrt(jnp.mean(o_b * o_b, axis=-1, keepdims=True) + EPS) * lp["b_norm_g"]
    y_b = ob.reshape(bsz, t, B_WIDTH) * jax.nn.silu(f["z_b"])
    dt = f["z_a"].dtype
    mixed = (jax.nn.sigmoid(f["g_a"]) * (y_a.astype(dt) @ lp["w_a"])
             + jax.nn.sigmoid(f["g_b"]) * (y_b.astype(dt) @ lp["w_b"]))
    return mixed @ lp["w_out"]


def _mix(f, lp, init, emit):
    sa0, sb0 = init
    a_vec = -f["kk"]
    wkv = 0.0
    s_a = []
    for d, rev in enumerate((False, True)):
        s, y = _rwkv_scan(f["r"], f["decay"][:, :, d], f["k"][:, :, d], f["v"], a_vec,
                          f["kk"] * f["iclr"][:, :, d], sa0[d], rev, emit)
        s_a.append(s)
        if emit:
            wkv = wkv + y
    o_b = 0.0
    s_b = []
    for d in range(2):
        seqs = (f["q"], f["kg"], f["vg"], f["gk"][:, :, d])
        if d == 1:
            seqs = tuple(jnp.flip(z, axis=1) for z in seqs)
        o, s = _gla_chunked(seqs[0], seqs[1], seqs[2], seqs[3], sb0[d], emit)
        s_b.append(s)
        if emit:
            o_b = o_b + (jnp.flip(o, axis=1) if d == 1 else o)
    states = ((s_a[0], s_a[1]), (s_b[0], s_b[1]))
    if not emit:
        return None, states
    return _merge(f, wkv, o_b, lp), states


def setup_inputs(seed: int = 0) -> dict:
    key = jax.random.key(seed)
    ks = jax.random.split(key, 26)
    nrm = lambda k, shape, s: jax.random.normal(k, shape, jnp.float32) * s
    L = DEPTH
    conv_id = jnp.zeros((L, 3, 3, 3 * A_WIDTH), jnp.float32).at[:, 1, 1].set(1.0)
    return {
        "x": nrm(ks[0], (BATCH, SEQ, D_MODEL), 1.0),
        "c": nrm(ks[1], (BATCH, D_MODEL), 1.0),
        "ctx": nrm(ks[2], (BATCH, CTX_LEN, D_MODEL), 1.0),
        "c_ctx": nrm(ks[3], (D_MODEL,), 1.0),
        "w_mod": nrm(ks[4], (L, D_MODEL, 3 * D_MODEL), 0.5 * D_MODEL ** -0.5),
        "b_mod": nrm(ks[5], (L, 3 * D_MODEL), 0.01),
        "norm_g": 1.0 + nrm(ks[6], (L, D_MODEL), 0.02),
        "w_in": nrm(ks[7], (L, D_MODEL, P_IN), D_MODEL ** -0.5),
        "conv_w": conv_id + nrm(ks[8], (L, 3, 3, 3 * A_WIDTH), 0.15),
        "a_w0": jax.random.uniform(ks[9], (L, 2, A_WIDTH), jnp.float32, -4.0, 2.0),
        "a_w2": nrm(ks[10], (L, 2, A_DECAY_RANK, A_WIDTH), 0.5 * A_DECAY_RANK ** -0.5),
        "a_a0": nrm(ks[11], (L, 2, A_WIDTH), 0.1),
        "a_a2": nrm(ks[12], (L, 2, A_ICLR_RANK, A_WIDTH), 0.5 * A_ICLR_RANK ** -0.5),
        "a_k_k": 0.85 + nrm(ks[13], (L, A_WIDTH), 0.05),
        "a_k_a": 1.0 + nrm(ks[14], (L, A_WIDTH), 0.05),
        "a_r_k": nrm(ks[15], (L, A_HEADS, A_HEAD), 0.1),
        "a_lnx_g": 1.0 + nrm(ks[16], (L, A_WIDTH), 0.02),
        "a_lnx_b": nrm(ks[17], (L, A_WIDTH), 0.01),
        "b_gk_w2": nrm(ks[18], (L, 2, B_GATE_RANK, B_KEY), B_GATE_RANK ** -0.5),
        "b_gk_b": nrm(ks[19], (L, 2, B_KEY), 0.5),
        "b_norm_g": 1.0 + nrm(ks[20], (L, B_DV), 0.02),
        "w_a": nrm(ks[21], (L, A_WIDTH, D_MODEL), A_WIDTH ** -0.5),
        "w_b": nrm(ks[22], (L, B_WIDTH, D_MODEL), B_WIDTH ** -0.5),
        "w_out": nrm(ks[23], (L, D_MODEL, D_MODEL), D_MODEL ** -0.5),
        "final_g": 1.0 + nrm(ks[24], (D_MODEL,), 0.02),
    }


def reference(x, c, ctx, c_ctx, w_mod, b_mod, norm_g, w_in, conv_w, a_w0, a_w2, a_a0, a_a2, a_k_k, a_k_a,
              a_r_k, a_lnx_g, a_lnx_b, b_gk_w2, b_gk_b, b_norm_g, w_a, w_b, w_out, final_g):
    bsz = x.shape[0]
    zero_a = jnp.zeros((bsz, A_HEADS, A_HEAD, A_HEAD), jnp.float32)
    zero_b = jnp.zeros((bsz, B_HEADS, B_DK, B_DV), jnp.float32)
    zero_init = ((zero_a, zero_a), (zero_b, zero_b))
    for l in range(DEPTH):
        lp = {"w_in": w_in[l], "conv_w": conv_w[l], "a_w0": a_w0[l], "a_w2": a_w2[l], "a_a0": a_a0[l],
              "a_a2": a_a2[l], "a_k_k": a_k_k[l], "a_k_a": a_k_a[l], "a_r_k": a_r_k[l],
              "a_lnx_g": a_lnx_g[l], "a_lnx_b": a_lnx_b[l], "b_gk_w2": b_gk_w2[l], "b_gk_b": b_gk_b[l],
              "b_norm_g": b_norm_g[l], "w_a": w_a[l], "w_b": w_b[l], "w_out": w_out[l]}
        last = l + 1 == DEPTH
        shift_c, scale_c, gate_c = _modulation(c_ctx, w_mod[l], b_mod[l])
        h_ctx = _rmsnorm(ctx, norm_g[l]) * (1.0 + scale_c) + shift_c
        ctx_out, ctx_states = _mix(_features(h_ctx, lp, grid=False), lp, zero_init, emit=not last)
        shift, scale, gate = _modulation(c, w_mod[l], b_mod[l])
        h = _rmsnorm(x, norm_g[l]) * (1.0 + scale[:, None]) + shift[:, None]
        y, _ = _mix(_features(h, lp, grid=True), lp, ctx_states, emit=True)
        x = x + gate[:, None] * y.astype(x.dtype)
        if not last:
            ctx = ctx + gate_c * ctx_out.astype(ctx.dtype)
    return _rmsnorm(x, final_g)
```

```python
import numpy as np
from contextlib import ExitStack
import concourse.bass as bass
import concourse.mybir as mybir
from concourse.bass_utils import run_bass_kernel_spmd

F32 = mybir.dt.float32
BF16 = mybir.dt.bfloat16
AF = mybir.ActivationFunctionType
ALU = mybir.AluOpType
AX = mybir.AxisListType

D = 2048
PIN = 11552
SEG = 4096
NCH = 64
KDEC = 0.6065306597126334
C_Z_A, C_LW, C_LA, C_QB, C_KB, C_VB, C_ZB, C_LG, C_GA, C_GB = 3072, 4096, 4224, 4352, 4864, 5376, 6400, 7424, 7456, 9504
NCOLV = 290
CV_W0, CV_A0, CV_KK, CV_KA, CV_RK, CV_GB, CV_HALO, CV_FM = 216, 232, 248, 256, 264, 272, 280, 282
NCOMP = 16 * 128 + 8 * 257
DBG = {}


class Buf:
    __slots__ = ("name", "w", "r", "sem", "cnt", "dram")

    def __init__(self, name, dram=False):
        self.name = name
        self.w = {}
        self.r = {}
        self.sem = None
        self.cnt = 0
        self.dram = dram


def _mx(need, d):
    for s, v in d.items():
        if v > need.get(s, 0):
            need[s] = v


class Eng:
    def __init__(self, fw, name, sem):
        self.fw = fw
        self.name = name
        self.sem = sem
        self.count = 0
        self.seen = {}
        self.prog = []

    def replay(self, e):
        for it in self.prog:
            if it[0] == "w":
                e.wait_ge(it[1], it[2])
            else:
                it[1](e).then_inc(it[2], it[3])
        self.prog = []

    def _wait(self, need):
        for sem, val in need.items():
            if self.seen.get(sem, 0) >= val:
                continue
            self.prog.append(("w", sem, val))
            self.seen[sem] = val

    def op(self, fn, reads=(), writes=()):
        need = {}
        for b in reads:
            _mx(need, b.w)
        for b in writes:
            _mx(need, b.w)
            _mx(need, b.r)
        if self.name == "pe":
            need.pop(self.sem, None)
        self._wait(need)
        self.count += 1
        self.prog.append(("o", fn, self.sem, 1))
        for b in reads:
            b.r[self.sem] = self.count
        for b in writes:
            b.w = {self.sem: self.count}
            b.r = {}

    def dma(self, out, in_, reads=(), writes=(), sem_buf=None):
        need = {}
        for b in reads:
            _mx(need, b.w)
        for b in writes:
            if b.dram:
                continue
            _mx(need, b.w)
            _mx(need, b.r)
        self._wait(need)
        sb = sem_buf
        if sb is None:
            for b in list(writes) + list(reads):
                if not b.dram:
                    sb = b
                    break
        if sb.sem is None:
            sb.sem, sb.cnt = self.fw.get_dsem(sb.name)
        self.prog.append(("o", (lambda e, o=out, i=in_: e.dma_start(out=o, in_=i)), sb.sem, 16))
        sb.cnt += 16
        for b in reads:
            b.r[sb.sem] = sb.cnt
        for b in writes:
            if b.dram:
                b.w[sb.sem] = sb.cnt
            else:
                b.w = {sb.sem: sb.cnt}
                b.r = {}


class FW:
    def __init__(self, nc, stack):
        self.nc = nc
        self.stack = stack
        self.dsems = []
        self.pe = Eng(self, "pe", self._sem("pe"))
        self.act = Eng(self, "act", self._sem("act"))
        self.dve = Eng(self, "dve", self._sem("dve"))
        self.pool = Eng(self, "pool", self._sem("pool"))
        self.sp = Eng(self, "sp", self._sem("sp"))
        self.engs = [self.pe, self.act, self.dve, self.pool, self.sp]
        self.dbufs = []
        self.sem_pool = []
        self.rr = 0
        self.erot = 0

    def _sem(self, name):
        return self.stack.enter_context(self.nc.semaphore(name))

    def new_sem(self, name):
        return self._sem(name)

    def get_dsem(self, name):
        if self.sem_pool:
            return self.sem_pool.pop()
        return self._sem("d_" + name), 0

    def sbuf(self, st, name, shape, dt=F32):
        self.nid = getattr(self, "nid", 0) + 1
        t = st.enter_context(self.nc.sbuf_tensor(f"sb{self.nid}_{name}", list(shape), dt))
        b = Buf(name)
        self.dbufs.append(b)
        return t, b

    def barrier(self):
        need = {}
        for e in self.engs:
            need[e.sem] = e.count
        for b in self.dbufs:
            if b.sem is not None:
                need[b.sem] = b.cnt
        for e in self.engs:
            n2 = dict(need)
            n2.pop(e.sem, None) if e.name == "pe" else None
            e._wait(n2)
        for b in self.dbufs:
            if b.sem is not None:
                self.sem_pool.append((b.sem, b.cnt))
                b.sem = None

    def emit(self):
        with self.nc.Block() as block:
            @block.tensor
            def _(e):
                self.pe.replay(e)

            @block.scalar
            def _(e):
                self.act.replay(e)

            @block.vector
            def _(e):
                self.dve.replay(e)

            @block.gpsimd
            def _(e):
                self.pool.replay(e)

            @block.sync
            def _(e):
                self.sp.replay(e)


def build_program(phases=9, dbg=()):
    nc = bass.Bass("TRN2", target_bir_lowering=False)

    def din(name, shape, dt=F32):
        return nc.dram_tensor(name, list(shape), dt, kind="ExternalInput")

    def dint(name, shape, dt=F32):
        return nc.dram_tensor(name, list(shape), dt, kind=("ExternalOutput" if name in dbg else "Internal"))

    xs = din("xs", [4224, D]).ap()
    ctxb = din("ctxb", [256, D]).ap()
    cT_d = din("cT", [128, 16, 2]).ap()
    w_mod = din("w_mod", [D, 3 * D]).ap()
    b_mod = din("b_mod", [3 * D])
    norm_g = din("norm_g", [D])
    w_in = din("w_in", [D, PIN]).ap()
    colv_d = din("colv", [128, NCOLV]).ap()
    aw2_d = din("aw2p", [128, 1024]).ap()
    aa2_d = din("aa2p", [128, 1024]).ap()
    gkw_d = din("gkw2p", [32, 2, 512]).ap()
    lnxg = din("lnxg", [1024])
    lnxb = din("lnxb", [1024])
    bng = din("bng", [256])
    w_a = din("w_a", [1024, D]).ap()
    w_b = din("w_b", [1024, D]).ap()
    w_out = din("w_out", [D, D]).ap()
    final_g = din("final_g", [D])
    identf_d = din("identf", [128, 128]).ap()
    identst_d = din("identst", [128, 64]).ap()
    maskG_d = din("maskG", [128, 2, 128]).ap()
    maskL_d = din("maskL", [128, 2, 64]).ap()
    maskA_d = din("maskA", [64, 2, 64]).ap()
    bones_d = din("bones", [128, 128]).ap()
    ind2_d = din("ind2", [128, 2]).ap()
    rmask_d = din("rmask", [128, 512]).ap()
    out_d = nc.dram_tensor("out", [SEG, D], F32, kind="ExternalOutput").ap()

    hTd = dint("hTd", [128, 16, 4480], BF16).ap()
    spA = dint("spA", [2, NCH, 128, 8 * 192]).ap()
    y0d = dint("y0d", [NCH, 128, 8, 64]).ap()
    bvd = dint("bvd", [NCH, 128, 8, 64]).ap()
    spG = dint("spG", [2, NCH, 128, 4 * 321]).ap()
    o0d = dint("o0d", [SEG, 1024]).ap()
    zgd = dint("zgd", [SEG, 4096], BF16).ap()
    zaTd = dint("zaTd", [16, 128, SEG], BF16).ap()
    mTd = dint("mTd", [128, 16, SEG], BF16).ap()
    gate_d = dint("gate_d", [128, D]).ap()
    ypd = dint("ypd", [NCH, 128, 8, 64]).ap()
    opd = dint("opd", [SEG, 1024]).ap()
    CSPL = [(0, 2048), (2048, 3080), (3080, 4104)]
    comp_in = [dint(f"comp_in{i}", [128, b - a]) for i, (a, b) in enumerate(CSPL)]
    comp_out = [dint(f"comp_out{i}", [512, b - a]) for i, (a, b) in enumerate(CSPL)]
    bD = Buf("dram", dram=True)

    def bc(t, n, inner=None):
        if inner is None:
            return bass.AP(t, 0, [[0, 128], [1, n]])
        return bass.AP(t, 0, [[0, 128], [0, inner], [1, n]])

    with ExitStack() as top:
        fw = FW(nc, top)
        pe, act, dve, pool, sp = fw.pe, fw.act, fw.dve, fw.pool, fw.sp
        PS = []
        for i in range(8):
            t = top.enter_context(nc.psum_tensor(f"ps{i}", [128, 512], F32))
            PS.append((t, Buf(f"ps{i}")))

        def bank():
            fw.rr = (fw.rr + 1) % 8
            return PS[fw.rr]

        def mm(out, lhsT, rhs, rd, bps, start=True, stop=True):
            pe.op(lambda e: e.matmul(out, lhsT=lhsT, rhs=rhs, start=start, stop=stop), reads=rd, writes=[bps])

        def acopy(out, in_, rd, wr, func=AF.Copy, **kw):
            act.op(lambda e: e.activation(out=out, in_=in_, func=func, **kw), reads=rd, writes=wr)

        def vcopy(out, in_, rd, wr):
            dve.op(lambda e: e.tensor_copy(out=out, in_=in_), reads=rd, writes=wr)

        def ecopy(out, in_, rd, wr):
            fw.erot += 1
            if fw.erot % 2:
                acopy(out, in_, rd, wr)
            else:
                vcopy(out, in_, rd, wr)

        def tt(eng, out, in0, in1, op, rd, wr):
            eng.op(lambda e: e.tensor_tensor(out=out, in0=in0, in1=in1, op=op), reads=rd, writes=wr)

        def ts(eng, out, in0, s1, s2, op0, op1, rd, wr):
            if s2 is None:
                eng.op(lambda e: e.tensor_scalar(out=out, in0=in0, scalar1=s1, scalar2=None, op0=op0), reads=rd, writes=wr)
            else:
                eng.op(lambda e: e.tensor_scalar(out=out, in0=in0, scalar1=s1, scalar2=s2, op0=op0, op1=op1), reads=rd, writes=wr)

        def treduce(out, in_, rd, wr):
            dve.op(lambda e: e.tensor_reduce(out=out, in_=in_, axis=AX.X, op=ALU.add), reads=rd, writes=wr)

        def recip(out, in_, rd, wr):
            dve.op(lambda e: e.reciprocal(out=out, in_=in_), reads=rd, writes=wr)

        def stt(out, in0, sc, in1, op0, op1, rd, wr):
            dve.op(lambda e: e.scalar_tensor_tensor(out=out, in0=in0, scalar=sc, in1=in1, op0=op0, op1=op1), reads=rd, writes=wr)

        identf, b_identf = fw.sbuf(top, "identf", [128, 128])
        identb, b_identb = fw.sbuf(top, "identb", [128, 128], BF16)
        identst, b_identst = fw.sbuf(top, "identst", [128, 64])
        colv, b_colv = fw.sbuf(top, "colv", [128, NCOLV])
        sp.dma(identf[:], identf_d, writes=[b_identf])
        sp.dma(identst[:], identst_d, writes=[b_identst])
        sp.dma(colv[:], colv_d, writes=[b_colv])
        vcopy(identb[:], identf[:], [b_identf], [b_identb])
        w_in_v = w_in.rearrange("(k p) n -> p k n", p=128)
        Hctx = [fw.sbuf(top, f"Hctx{p}", [128, 2, 64]) for p in range(8)]
        Hgctx, b_Hgctx = fw.sbuf(top, "Hgctx", [128, 8, 256])

        with ExitStack() as s0:
            cTt, b_cT = fw.sbuf(s0, "cTt", [128, 16, 2])
            sT, b_sT = fw.sbuf(s0, "sT", [128, 16, 2])
            srep, b_srep = fw.sbuf(s0, "srep", [128, 2, 16, 128])
            bmods = [fw.sbuf(s0, f"bmod{j}", [128, 256]) for j in range(2)]
            ng_t, b_ng = fw.sbuf(s0, "ng_t", [128, D])
            mt = [fw.sbuf(s0, f"m{j}", [128, 3 * D]) for j in range(2)]
            modA = [fw.sbuf(s0, f"modA{j}", [128, D]) for j in range(2)]
            wm = [fw.sbuf(s0, f"wm{j}", [128, 16, 256]) for j in range(2)]
            sp.dma(cTt[:], cT_d, writes=[b_cT])
            sp.dma(ng_t[:], bc(norm_g, D), writes=[b_ng])
            acopy(sT[:], cTt[:], [b_cT], [b_sT], func=AF.Silu)
            for j in range(2):
                vcopy(srep[:, j], sT[:, :, j:j + 1].to_broadcast([128, 16, 128]), [b_sT], [b_srep])
            wmv = w_mod.rearrange("(k p) n -> p k n", p=128)
            for nb in range(24):
                wt, bw = wm[nb % 2]
                sp.dma(wt[:], wmv[:, :, nb * 256:(nb + 1) * 256], writes=[bw])
                bmod_t, b_bmod = bmods[nb % 2]
                sp.dma(bmod_t[:], bass.AP(b_mod, nb * 256, [[0, 128], [1, 256]]), writes=[b_bmod])
                for j in range(2):
                    ps, bps = bank()
                    for k in range(16):
                        mm(ps[:, 0:256], srep[:, j, k, :], wt[:, k, :], [b_srep, bw], bps, start=(k == 0), stop=(k == 15))
                    tt(dve, mt[j][0][:, nb * 256:(nb + 1) * 256], ps[:, 0:256], bmod_t[:, :], ALU.add,
                       [bps, b_bmod], [bps, mt[j][1]])
            for j in range(2):
                stt(modA[j][0][:], mt[j][0][:, D:2 * D], 1.0, ng_t[:], ALU.add, ALU.mult, [mt[j][1], b_ng], [modA[j][1]])
            act.dma(gate_d, mt[0][0][:, 2 * D:3 * D], reads=[mt[0][1]], writes=[bD], sem_buf=mt[0][1])
            xt = [fw.sbuf(s0, f"xt{j}", [128, D]) for j in range(2)]
            hf, b_hf = fw.sbuf(s0, "hf", [128, D])
            hb, b_hb = fw.sbuf(s0, "hb", [128, D], BF16)
            ss, b_ss = fw.sbuf(s0, "ss", [128, 4])
            hTt = [fw.sbuf(s0, f"hTt{j}", [128, 16, 128], BF16) for j in range(2)]
            for ti in range(35):
                j = 0 if ti < 33 else 1
                src = xs[ti * 128:(ti + 1) * 128, :] if ti < 33 else ctxb[(ti - 33) * 128:(ti - 32) * 128, :]
                x_t, bx = xt[ti % 2]
                sp.dma(x_t[:], src, writes=[bx])
                acopy(hf[:], x_t[:], [bx], [b_hf, b_ss], func=AF.Square, accum_out=ss[:, 0:1])
                ts(dve, ss[:, 1:2], ss[:, 0:1], 1.0 / D, 1e-6, ALU.mult, ALU.add, [b_ss], [b_ss])
                acopy(ss[:, 2:3], ss[:, 1:2], [b_ss], [b_ss], func=AF.Sqrt)
                dve.op(lambda e: e.reciprocal(out=ss[:, 3:4], in_=ss[:, 2:3]), reads=[b_ss], writes=[b_ss])
                stt(hf[:], x_t[:], ss[:, 3:4], modA[j][0][:], ALU.mult, ALU.mult, [bx, b_ss, modA[j][1]], [b_hf])
                tt(pool, hb[:], hf[:], mt[j][0][:, 0:D], ALU.add, [b_hf, mt[j][1]], [b_hb])
                ht, bht = hTt[ti % 2]
                for k4 in range(4):
                    ps, bps = bank()
                    for kk in range(4):
                        k = k4 * 4 + kk
                        mm(ps[:, kk * 128:(kk + 1) * 128], hb[:, k * 128:(k + 1) * 128], identb[:], [b_hb, b_identb], bps)
                    ecopy(ht[:, k4 * 4:(k4 + 1) * 4, :], ps[:, :].rearrange("p (a b) -> p a b", b=128), [bps], [bps, bht])
                act.dma(hTd[:, :, ti * 128:(ti + 1) * 128], ht[:], reads=[bht], writes=[bD], sem_buf=bht)
            fw.barrier()
            fw.emit()
        if phases <= 0:
            return nc, fw

        with ExitStack() as s1:
            cur_st = [s1]

            def T(name, shape, dt=F32):
                return fw.sbuf(cur_st[0], name, shape, dt)
            maskG, b_maskG = T("maskG", [128, 2, 128])
            maskL, b_maskL = T("maskL", [128, 2, 64])
            maskA, b_maskA = T("maskA", [64, 2, 64])
            bonesf, b_bonesf = T("bonesf", [128, 128])
            bones, b_bones = T("bones", [128, 128], BF16)
            ind2f, b_ind2f = T("ind2f", [128, 2])
            ind2, b_ind2 = T("ind2", [128, 2], BF16)
            rmask, b_rmask = T("rmask", [128, 256])
            aw2, b_aw2 = T("aw2", [128, 1024])
            aa2, b_aa2 = T("aa2", [128, 1024])
            gkw, b_gkw = T("gkw", [32, 2, 512])
            dcol, b_dcol = T("dcol", [128, 16])
            for t_, b_, d_ in ((maskG, b_maskG, maskG_d), (maskL, b_maskL, maskL_d), (maskA, b_maskA, maskA_d),
                               (bonesf, b_bonesf, bones_d), (ind2f, b_ind2f, ind2_d), (rmask, b_rmask, rmask_d[:, 0:256]),
                               (aw2, b_aw2, aw2_d), (aa2, b_aa2, aa2_d), (gkw, b_gkw, gkw_d)):
                sp.dma(t_[:], d_, writes=[b_])
            vcopy(bones[:], bonesf[:], [b_bonesf], [b_bones])
            vcopy(ind2[:], ind2f[:], [b_ind2f], [b_ind2])
            ts(dve, dcol[:, 0:8], colv[:, CV_KA:CV_KA + 8], -1.0, 1.0, ALU.mult, ALU.add, [b_colv], [b_dcol])
            ts(dve, dcol[:, 8:16], colv[:, CV_GB:CV_GB + 8], -1.0, None, ALU.mult, None, [b_colv], [b_dcol])

            hT, b_hT = T("hT", [128, 16, 1152], BF16)
            lw, b_lw = T("lw", [128, 1024])
            la, b_la = T("la", [128, 1024])
            lg, b_lg = T("lg", [32, 1024])
            wf = [T(f"wf{j}", [128, 16, 128]) for j in range(2)]
            wb = [T(f"wb{j}", [128, 16, 128], BF16) for j in range(4)]
            wcnt = [0]

            def load_w(col0, ncols, slot):
                wt, bw = wf[wcnt[0] % 2]
                wcnt[0] += 1
                sp.dma(wt[:, :, 0:ncols], w_in_v[:, :, col0:col0 + ncols], writes=[bw])
                dst, bd = wb[slot]
                acopy(dst[:, 0:8, 0:ncols], wt[:, 0:8, 0:ncols], [bw], [bd])
                pool.op(lambda e: e.tensor_copy(out=dst[:, 8:16, 0:ncols], in_=wt[:, 8:16, 0:ncols]), reads=[bw], writes=[bd])
                return dst, bd

            def proj_fm(wt, bw, ncols, col0, ntok, dst, bdst, func=AF.Copy):
                for tb in range((ntok + 511) // 512):
                    n = min(512, ntok - tb * 512)
                    ps, bps = bank()
                    for k in range(16):
                        mm(ps[0:ncols, 0:n], wt[:, k, 0:ncols], hT[:, k, col0 + tb * 512:col0 + tb * 512 + n], [bw, b_hT], bps,
                           start=(k == 0), stop=(k == 15))
                    if func == AF.Copy:
                        ecopy(dst[0:ncols, tb * 512:tb * 512 + n], ps[0:ncols, 0:n], [bps], [bps, bdst])
                    else:
                        acopy(dst[0:ncols, tb * 512:tb * 512 + n], ps[0:ncols, 0:n], [bps], [bps, bdst], func=func)

            cin = [T(f"cin{j}", [128, 384]) for j in range(3)]
            cout = [T(f"cout{j}", [128, 256]) for j in range(3)]
            A = {}
            for nm in ("kk", "t1", "sg", "icl", "cs", "g", "gx", "eng", "ec", "bb", "kd", "rk"):
                A[nm] = T("a_" + nm, [128, 256])
            A["kka"] = A["kk"]
            A["t2"] = A["t1"]
            A["egx"] = A["gx"]
            sqb, b_sqb = T("sqb", [128, 256], BF16)
            vbS = [T(f"vb{i}", [128, 256], BF16) for i in range(2)]
            rkbS = [T(f"rkb{i}", [128, 256], BF16) for i in range(2)]
            egdS = [[T(f"eg{i}_{d}", [128, 256]) for d in range(2)] for i in range(2)]
            rtdS = [[T(f"rt{i}_{d}", [128, 256]) for d in range(2)] for i in range(2)]
            ardS = [[T(f"ar{i}_{d}", [128, 2, 256], BF16) for d in range(2)] for i in range(2)]
            btdS = [[T(f"bt{i}_{d}", [128, 256], BF16) for d in range(2)] for i in range(2)]
            ktdS = [[T(f"kt{i}_{d}", [128, 256], BF16) for d in range(2)] for i in range(2)]
            bhdS = [[T(f"bh{i}_{d}", [128, 256], BF16) for d in range(2)] for i in range(2)]
            khdS = [[T(f"kh{i}_{d}", [128, 256], BF16) for d in range(2)] for i in range(2)]
            vb, b_vb = vbS[0]
            rkb, b_rkb = rkbS[0]
            egd, rtd, ard, btd, ktd, bhd, khd = egdS[0], rtdS[0], ardS[0], btdS[0], ktdS[0], bhdS[0], khdS[0]
            QS = 128.0 ** -0.5

            def reset_gla():
                pool.op(lambda e: e.memset(GN[:], 0.0), writes=[b_GN] + b_GNs)
                pool.op(lambda e: e.memset(GD[:], 1.0), writes=[b_GD])

            def gla_batch(job, S):
                g, col0, tok0, Tn, cg0, spill = job["args"]
                egd, btd, ktd, khd = egdS[S], btdS[S], ktdS[S], khdS[S]
                stG, b_stG = stGS[S]
                if job.get("seg") is not None:
                    hcol, ntok_own, halo_ = job["seg"]
                    own0 = load_seg(hcol, ntok_own, halo_)
                    wt, bw = load_w(C_LG, 32, 3)
                    proj_fm(wt, bw, 32, own0, ntok_own, lg, b_lg)
                    yield
                if job.get("newhead"):
                    load_w(C_QB + g * 128, 128, 0)
                    load_w(C_KB + g * 128, 128, 1)
                    load_w(C_VB + g * 256, 128, 2)
                    load_w(C_VB + g * 256 + 128, 128, 3)
                nch = Tn // 64
                sl = slice(0, Tn)
                qf, bqf = A["kk"]
                kf, bkf = A["t1"]
                vbg, bvbg = ardS[S][0]
                proj_fm(wb[0][0], wb[0][1], 128, col0, Tn, qf, bqf)
                yield
                proj_fm(wb[1][0], wb[1][1], 128, col0, Tn, kf, bkf)
                yield
                for j in range(2):
                    proj_fm(wb[2 + j][0], wb[2 + j][1], 128, col0, Tn, vbg[:, j, :], bvbg)
                    yield
                c3 = lambda ap: ap.rearrange("p (c t) -> p c t", t=64)
                for d in range(2):
                    e1, be1 = A["cs"]
                    spl, bspl = A["g"]
                    csg, bcsg = A["gx"]
                    Gs, bGs = A["eng"]
                    ek, bek = A["bb"]
                    tk, btk = A["kd"]
                    eq, beq = egd[d]
                    ps, bps = bank()
                    mm(ps[:, 0:Tn], gkw[0:32, d, g * 128:(g + 1) * 128], lg[0:32, tok0:tok0 + Tn], [b_gkw, b_lg], bps)
                    acopy(e1[:, sl], ps[:, 0:Tn], [bps, b_dcol], [bps, be1], func=AF.Exp, scale=-1.0,
                          bias=dcol[:, 8 + d * 4 + g:8 + d * 4 + g + 1])
                    acopy(spl[:, sl], e1[:, sl], [be1], [bspl], func=AF.Ln, bias=1.0)
                    yield
                    dve.op(lambda e: e.tensor_tensor_scan(out=csg[:, sl], data0=rmask[:, sl], data1=spl[:, sl], initial=0.0,
                                                          op0=ALU.mult, op1=ALU.add), reads=[b_rmask, bspl], writes=[bcsg])
                    if d == 0:
                        Gsrc, bGsrc = csg, bcsg
                    else:
                        tt(dve, c3(Gs[:, sl]), c3(csg[:, sl])[:, :, 63:64].to_broadcast([128, nch, 64]), c3(csg[:, sl]), ALU.subtract,
                           [bcsg], [bGs])
                        tt(pool, Gs[:, sl], Gs[:, sl], spl[:, sl], ALU.add, [bGs, bspl], [bGs])
                        Gsrc, bGsrc = Gs, bGs
                    acopy(eq[:, sl], Gsrc[:, sl], [bGsrc], [beq], func=AF.Exp, scale=-1.0 / 16)
                    acopy(ek[:, sl], Gsrc[:, sl], [bGsrc], [bek], func=AF.Exp, scale=1.0 / 16)
                    yield
                    eq3 = c3(eq[:, sl])
                    tcol = 63 if d == 0 else 0
                    stt(stG[:, 0:nch, d, 0:64], c3(qf[:, sl]), QS, eq3, ALU.mult, ALU.mult, [bqf, beq], [b_stG])
                    stt(btd[d][0][:, sl], qf[:, sl], QS, eq[:, sl], ALU.mult, ALU.mult, [bqf, beq], [btd[d][1]])
                    tt(pool, ktd[d][0][:, sl], kf[:, sl], ek[:, sl], ALU.mult, [bkf, bek], [ktd[d][1]])
                    tt(dve, c3(tk[:, sl]), c3(ek[:, sl]), eq3[:, :, tcol:tcol + 1].to_broadcast([128, nch, 64]), ALU.mult, [bek, beq], [btk])
                    tt(pool, khd[d][0][:, sl], kf[:, sl], tk[:, sl], ALU.mult, [bkf, btk], [khd[d][1]])
                    vcopy(stG[:, 0:nch, d, 320:321], eq3[:, :, tcol:tcol + 1], [beq], [b_stG])
                    yield
                yield "PREP_DONE"
                if job.get("reset"):
                    reset_gla()
                adv = DBG.get('adv', 6)

                def gcommon(ci):
                    c = slice(ci * 64, (ci + 1) * 64)
                    vtg, bvtg = VTg[ci]
                    psV, bpsV = bank()
                    for j in range(2):
                        mm(psV[0:64, j * 128:(j + 1) * 128], vbg[:, j, c], identb[:], [bvbg, b_identb], bpsV)
                    acopy(vtg[:, :], psV[0:64, 0:256], [bpsV], [bpsV, bvtg])

                def gchain(ci, d):
                    c = slice(ci * 64, (ci + 1) * 64)
                    vtg, bvtg = VTg[ci]
                    att, batt = AttT[ci * 2 + d]
                    kht, bkht = KhT[ci * 2 + d]
                    ps, bps = bank()
                    mm(ps[0:64, 0:64], ktd[d][0][:, c], btd[d][0][:, c], [ktd[d][1], btd[d][1]], bps)
                    mm(ps[0:64, 64:192], khd[d][0][:, c], identb[:], [khd[d][1], b_identb], bps)
                    tt(dve, att[:, :], ps[0:64, 0:64], maskA[:, d, :], ALU.mult, [bps, b_maskA], [bps, batt])
                    acopy(kht[:, :], ps[0:64, 64:192], [bps], [bps, bkht])
                    yield
                    psN, bpsN = bank()
                    mm(psN[:, 0:256], kht[:, :], vtg[:, :], [bkht, bvtg], bpsN)
                    acopy(stG[:, ci, d, 64:320], psN[:, 0:256], [bpsN], [bpsN, b_stG])
                    dc = stG[:, ci, d, 320:321]
                    gi = g * 2 + d
                    if d == 0:
                        stt(GN[:, gi, :], GN[:, gi, :], dc, psN[:, 0:256], ALU.mult, ALU.add, [b_GNs[gi], b_stG, bpsN], [bpsN, b_GNs[gi]])
                    else:
                        stt(GN[:, gi, :], psN[:, 0:256], GD[:, gi:gi + 1], GN[:, gi, :], ALU.mult, ALU.add, [b_GNs[gi], b_GD, bpsN],
                            [bpsN, b_GNs[gi]])
                    ts(dve, GD[:, gi:gi + 1], GD[:, gi:gi + 1], dc, None, ALU.mult, None, [b_GD, b_stG], [b_GD])
                    yield

                for ci in range(nch):
                    gcommon(ci)
                gens = [gchain(ci, d) for ci in range(nch) for d in range(2)]
                while gens:
                    alive = []
                    for g_ in gens:
                        try:
                            next(g_)
                            alive.append(g_)
                        except StopIteration:
                            pass
                    gens = alive
                    adv_extra(adv)
                if spill:
                    for ci in range(nch):
                        par = (cg0 + ci) % 2
                        pp = slice(par * 64, (par + 1) * 64)
                        vtg, bvtg = VTg[ci]
                        psO, bpsO = bank()
                        mm(psO[pp, 0:256], AttT[ci * 2][0][:, :], vtg[:, :], [AttT[ci * 2][1], bvtg], bpsO, start=True, stop=False)
                        mm(psO[pp, 0:256], AttT[ci * 2 + 1][0][:, :], vtg[:, :], [AttT[ci * 2 + 1][1], bvtg], bpsO, start=False, stop=True)
                        acopy(o0st[pp, ci // 2, :], psO[pp, 0:256], [bpsO], [bpsO, b_o0st])
                if spill:
                    for d in range(2):
                        act.dma(spG[d, cg0:cg0 + nch, :, g * 321:(g + 1) * 321].rearrange("c p f -> p c f"), stG[:, 0:nch, d, :],
                               reads=[b_stG], writes=[bD], sem_buf=b_stG)
                    tk0 = cg0 * 64
                    act.dma(o0d[tk0:tk0 + Tn, g * 256:(g + 1) * 256].rearrange("(a p) f -> p a f", p=128), o0st[:, 0:nch // 2, :],
                           reads=[b_o0st], writes=[bD], sem_buf=b_o0st)
                adv_extra(adv)
                if job.get("ctx_last"):
                    vcopy(Hgctx[:], GN[:], [b_GN] + b_GNs, [b_Hgctx])

            def reset_acc():
                for p in range(8):
                    for d, ACC in ((0, ACCf), (1, ACCb)):
                        accsel[p][d] = 0
                        t_, b_ = ACC[p][0]
                        pool.op(lambda e, t_=t_: e.memset(t_[:, 64:128], 0.0), writes=[b_])
                        pool.op(lambda e, t_=t_: e.tensor_copy(out=t_[:, 0:64], in_=identst[:]), reads=[b_identst], writes=[b_])

            PIPE = {"extra": None, "extra_done": True}

            def run_prep(g_):
                while next(g_) != "PREP_DONE":
                    pass

            def adv_extra(n):
                g_ = PIPE["extra"]
                if g_ is None or PIPE["extra_done"]:
                    return
                for _ in range(n):
                    if next(g_) == "PREP_DONE":
                        PIPE["extra_done"] = True
                        return

            def rwkv_batch(job, S):
                p, col0, tok0, Tn, rows, Wd, halo, cg0, spill, halo_mask = job["args"]
                vb, b_vb = vbS[S]
                rkb, b_rkb = rkbS[S]
                egd, rtd, ard, btd, ktd, bhd, khd = egdS[S], rtdS[S], ardS[S], btdS[S], ktdS[S], bhdS[S], khdS[S]
                if job.get("seg") is not None:
                    hcol, ntok_own, halo_ = job["seg"]
                    own0 = load_seg(hcol, ntok_own, halo_)
                    wt, bw = load_w(C_LW, 128, 3)
                    proj_fm(wt, bw, 128, own0, ntok_own, lw, b_lw, func=AF.Tanh)
                    yield
                    wt, bw = load_w(C_LA, 128, 3)
                    proj_fm(wt, bw, 128, own0, ntok_own, la, b_la)
                    yield
                if job.get("newpair"):
                    for j in range(3):
                        load_w(j * 1024 + p * 128, 128, j)
                nch = Tn // 64
                Tin = Tn + (2 * Wd if halo else 0)
                for j in range(3):
                    proj_fm(wb[j][0], wb[j][1], 128, col0, Tin, cin[j][0], cin[j][1])
                    yield
                    if halo_mask is not None:
                        hm_lo, hm_hi = halo_mask
                        if hm_lo:
                            ts(pool, cin[j][0][:, 0:Wd], cin[j][0][:, 0:Wd], colv[:, CV_HALO:CV_HALO + 1], None, ALU.mult, None,
                               [cin[j][1], b_colv], [cin[j][1]])
                        if hm_hi:
                            ts(pool, cin[j][0][:, Tin - Wd:Tin], cin[j][0][:, Tin - Wd:Tin], colv[:, CV_HALO + 1:CV_HALO + 2], None,
                               ALU.mult, None, [cin[j][1], b_colv], [cin[j][1]])
                    ti = j * 8 + p
                    i3 = cin[j][0][:, 0:Tin].rearrange("p (r w) -> p r w", w=Wd)
                    o3 = cout[j][0][:, 0:Tn].rearrange("p (r w) -> p r w", w=Wd)
                    r0 = 1 if halo else 0
                    ts(dve, o3, i3[:, r0:r0 + rows, :], colv[:, ti * 9 + 4:ti * 9 + 5], None, ALU.mult, None,
                       [cin[j][1], b_colv], [cout[j][1]])
                    for dy in ((-1, 0, 1) if halo else (0,)):
                        for dx in (-1, 0, 1):
                            if dy == 0 and dx == 0:
                                continue
                            tap = (dy + 1) * 3 + (dx + 1)
                            xo = slice(1, Wd) if dx == -1 else (slice(0, Wd - 1) if dx == 1 else slice(0, Wd))
                            xi = slice(0, Wd - 1) if dx == -1 else (slice(1, Wd) if dx == 1 else slice(0, Wd))
                            stt(o3[:, :, xo], i3[:, r0 + dy:r0 + dy + rows, xi], colv[:, ti * 9 + tap:ti * 9 + tap + 1], o3[:, :, xo],
                                ALU.mult, ALU.add, [cin[j][1], b_colv, cout[j][1]], [cout[j][1]])
                        yield
                r_, br = cout[0]
                k_, bk = cout[1]
                v_, bv_ = cout[2]
                sl = slice(0, Tn)
                acopy(vb[:, sl], v_[:, sl], [bv_], [b_vb])
                kka, bkka = A["kka"]
                kk, bkk = A["kk"]
                t1, bt1 = A["t1"]
                t2, bt2 = A["t2"]
                acopy(kka[:, sl], k_[:, sl], [bk, b_colv], [bkka], scale=colv[:, CV_KK + p:CV_KK + p + 1])
                acopy(sqb[:, sl], kka[:, sl], [bkka], [b_sqb], func=AF.Square)
                ps, bps = bank()
                mm(ps[:, 0:Tn], bones[:], sqb[:, sl], [b_bones, b_sqb], bps)
                acopy(t1[:, sl], ps[:, 0:Tn], [bps], [bps, bt1], func=AF.Sqrt, bias=1e-12)
                dve.op(lambda e: e.reciprocal(out=t2[:, sl], in_=t1[:, sl]), reads=[bt1], writes=[bt2])
                tt(pool, kk[:, sl], kka[:, sl], t2[:, sl], ALU.mult, [bkka, bt2], [bkk])
                yield
                rk, brk = A["rk"]
                for d in range(2):
                    sg, bsg = A["sg"]
                    icl, bicl = A["icl"]
                    cs, bcs = A["cs"]
                    g, bg = A["g"]
                    gx, bgx = A["gx"]
                    eng, beng = A["eng"]
                    egx, begx = A["egx"]
                    ec, bec = A["ec"]
                    bb, bbb = A["bb"]
                    kd, bkd = A["kd"]
                    eg, beg = egd[d]
                    ps, bps = bank()
                    mm(ps[:, 0:Tn], aw2[d * 64:(d + 1) * 64, p * 128:(p + 1) * 128], lw[d * 64:(d + 1) * 64, tok0:tok0 + Tn],
                       [b_aw2, b_lw], bps)
                    acopy(sg[:, sl], ps[:, 0:Tn], [bps, b_colv], [bps, bsg], func=AF.Sigmoid,
                          bias=colv[:, CV_W0 + d * 8 + p:CV_W0 + d * 8 + p + 1])
                    ps, bps = bank()
                    mm(ps[:, 0:Tn], aa2[d * 64:(d + 1) * 64, p * 128:(p + 1) * 128], la[d * 64:(d + 1) * 64, tok0:tok0 + Tn],
                       [b_aa2, b_la], bps)
                    acopy(icl[:, sl], ps[:, 0:Tn], [bps, b_colv], [bps, bicl], func=AF.Sigmoid,
                          bias=colv[:, CV_A0 + d * 8 + p:CV_A0 + d * 8 + p + 1])
                    yield
                    dve.op(lambda e: e.tensor_tensor_scan(out=cs[:, sl], data0=rmask[:, sl], data1=sg[:, sl], initial=0.0,
                                                          op0=ALU.mult, op1=ALU.add), reads=[b_rmask, bsg], writes=[bcs])
                    cs3 = cs[:, sl].rearrange("p (c t) -> p c t", t=64)
                    if d == 0:
                        gsrc, bgs = cs, bcs
                        tt(pool, gx[:, sl], cs[:, sl], sg[:, sl], ALU.subtract, [bcs, bsg], [bgx])
                    else:
                        tt(dve, gx[:, sl].rearrange("p (c t) -> p c t", t=64), cs3[:, :, 63:64].to_broadcast([128, nch, 64]), cs3,
                           ALU.subtract, [bcs], [bgx])
                        tt(pool, g[:, sl], gx[:, sl], sg[:, sl], ALU.add, [bgx, bsg], [bg])
                        gsrc, bgs = g, bg
                    acopy(eg[:, sl], gsrc[:, sl], [bgs], [beg], func=AF.Exp, scale=-KDEC)
                    acopy(eng[:, sl], gsrc[:, sl], [bgs], [beng], func=AF.Exp, scale=KDEC)
                    acopy(egx[:, sl], gx[:, sl], [bgx], [begx], func=AF.Exp, scale=-KDEC)
                    yield
                    eg3 = eg[:, sl].rearrange("p (c t) -> p c t", t=64)
                    tcol = 63 if d == 0 else 0
                    tt(dve, ec[:, sl].rearrange("p (c t) -> p c t", t=64), eng[:, sl].rearrange("p (c t) -> p c t", t=64),
                       eg3[:, :, tcol:tcol + 1].to_broadcast([128, nch, 64]), ALU.mult, [beng, beg], [bec])
                    ar, bar = ard[d]
                    stt(ar[:, 0, sl], kk[:, sl], -1.0, egx[:, sl], ALU.mult, ALU.mult, [bkk, begx], [bar])
                    tt(pool, bb[:, sl], kk[:, sl], icl[:, sl], ALU.mult, [bkk, bicl], [bbb])
                    tt(pool, btd[d][0][:, sl], bb[:, sl], eng[:, sl], ALU.mult, [bbb, beng], [btd[d][1]])
                    tt(dve, bhd[d][0][:, sl], bb[:, sl], ec[:, sl], ALU.mult, [bbb, bec], [bhd[d][1]])
                    yield
                    ts(dve, t1[:, sl], icl[:, sl], colv[:, CV_KA + p:CV_KA + p + 1], dcol[:, p:p + 1], ALU.mult, ALU.add,
                       [bicl, b_colv, b_dcol], [bt1])
                    tt(pool, kd[:, sl], t1[:, sl], k_[:, sl], ALU.mult, [bt1, bk], [bkd])
                    tt(dve, ktd[d][0][:, sl], kd[:, sl], eng[:, sl], ALU.mult, [bkd, beng], [ktd[d][1]])
                    tt(pool, khd[d][0][:, sl], kd[:, sl], ec[:, sl], ALU.mult, [bkd, bec], [khd[d][1]])
                    yield
                    tt(dve, rtd[d][0][:, sl], r_[:, sl], eg[:, sl], ALU.mult, [br, beg], [rtd[d][1]])
                    acopy(ar[:, 1, sl], rtd[d][0][:, sl], [rtd[d][1]], [bar])
                    if d == 0:
                        acopy(rk[:, sl], kd[:, sl], [bkd], [brk])
                    else:
                        tt(pool, rk[:, sl], rk[:, sl], kd[:, sl], ALU.add, [brk, bkd], [brk])
                stt(rkb[:, sl], rk[:, sl], colv[:, CV_RK + p:CV_RK + p + 1], r_[:, sl], ALU.mult, ALU.mult, [brk, b_colv, br], [b_rkb])

                yield "PREP_DONE"
                if job.get("reset"):
                    reset_acc()
                nci = min(nch, DBG.get('maxc', 9))

                def common(ci):
                    c = slice(ci * 64, (ci + 1) * 64)
                    vts, bvts = VTs[ci]
                    bo, bbo = bon[ci]
                    psV, bpsV = bank()
                    for h in range(2):
                        hs = slice(h * 64, (h + 1) * 64)
                        mm(psV[hs, 0:64], vb[hs, c], identb[hs, hs], [b_vb, b_identb], bpsV)
                    for h in range(2):
                        hs = slice(h * 64, (h + 1) * 64)
                        mm(psV[hs, 64:66], rkb[hs, c], ind2[hs, :], [b_rkb, b_ind2], bpsV)
                    acopy(vts[:], psV[:, 0:64], [bpsV], [bpsV, bvts])
                    vcopy(bo[:, :], psV[:, 64:66], [bpsV], [bpsV, bbo])
                    if spill and DBG.get('sp_bv', True):
                        for h in range(2):
                            hs = slice(h * 64, (h + 1) * 64)
                            ts(dve, bvst[hs, ci, :], psV[hs, 0:64], bo[hs, h:h + 1], None, ALU.mult, None, [bpsV, bbo],
                               [bpsV, b_bvst])

                RS = {}

                def chain(ci, d):
                    c = slice(ci * 64, (ci + 1) * 64)
                    sidx = ci * 2 + d
                    vts, bvts = VTs[ci]
                    ar, bar = ard[d]
                    tm3 = TM3all[:, sidx]
                    gxt, bgxt = GX[sidx]
                    l0, bl0 = L0[sidx]
                    bsg_ = b_stgs[sidx]
                    H2 = [slice(0, 64), slice(64, 128)]
                    ps, bps = bank()
                    for j, (X, bX) in enumerate(((ar, bar), bhd[d], khd[d])):
                        for hs in H2:
                            src = X[hs, 0, c] if j == 0 else X[hs, c]
                            mm(ps[hs, j * 64:(j + 1) * 64], src, identb[hs, hs], [bX, b_identb], bps)
                    ps2, bps2 = bank()
                    for hs in H2:
                        mm(ps2[hs, 0:128], btd[d][0][hs, c], ar[hs, :, c], [btd[d][1], bar], bps2)
                        mm(ps2[hs, 128:256], ktd[d][0][hs, c], ar[hs, :, c], [ktd[d][1], bar], bps2)
                        mm(ps2[hs, 256:320], ar[hs, 0, c], btd[d][0][hs, c], [btd[d][1], bar], bps2)
                    acopy(tm3[:, :, 0:64], ps[:, 0:192].rearrange("p (a b) -> p a b", b=64), [bps], [bps, b_TM3])
                    tt(dve, gxt[:], ps2[:, 0:256].rearrange("p (a b) -> p a b", b=128),
                       maskG[:, d:d + 1, :].to_broadcast([128, 2, 128]), ALU.mult, [bps2, b_maskG], [bps2, bgxt])
                    tt(dve, l0[:], ps2[:, 256:320], maskL[:, d, :], ALU.mult, [bps2, b_maskL], [bps2, bl0])
                    tt(dve, TTall[0][0][:, sidx, :], gxt[:, 0, 0:64], identst[:], ALU.add, [bgxt, b_identst], [TTall[0][1]])
                    yield
                    Xc, bXc = gxt[:, 0, 0:64], bgxt
                    Lc, bLc = l0[:], bl0
                    for j in range(6):
                        Tc, bTc = TTall[j % 2][0][:, sidx, :], TTall[j % 2][1]
                        if j < 5:
                            psq, bpsq = RS["bq"][sidx // 4]
                            o0_ = (sidx % 4) * 128
                            for hs in H2:
                                if j < 4:
                                    mm(psq[hs, o0_:o0_ + 64], Lc[hs], Xc[hs], [bLc, bXc], bpsq)
                                mm(psq[hs, o0_ + 64:o0_ + 128], Xc[hs], Lc[hs], [bLc, bXc], bpsq)
                        if j >= 1:
                            Tp, bTp = TTall[(j - 1) % 2][0][:, sidx, :], TTall[(j - 1) % 2][1]
                            pst, bpst = RS["bt"]
                            for hs in H2:
                                mm(pst[hs, sidx * 64:(sidx + 1) * 64], Lc[hs], Tp[hs], [bLc, bTp], bpst)
                        if j == 0:
                            psx, bpsx = RS["bx"]
                            for hs in H2:
                                mm(psx[hs, sidx * 64:(sidx + 1) * 64], gxt[hs, 1, 0:64], vts[hs], [bgxt, bvts], bpsx)
                        yield
                        if j < 5:
                            xl, bxl = XLall[j % 2]
                            Xc, bXc = xl[:, sidx, 0, :], bxl
                            Lc, bLc = xl[:, sidx, 1, :], bxl
                    Tc, bTc = TTall[5 % 2][0][:, sidx, :], TTall[5 % 2][1]
                    psa, bpsa = RS["ba"][sidx // 4]
                    o0_ = (sidx % 4) * 128
                    for hs in H2:
                        mm(psa[hs, o0_:o0_ + 128], Tc[hs], tm3[hs, 0, :], [bTc, b_TM3], bpsa)
                    yield
                    au, bau = AUall[:, sidx, :], b_AU
                    ps, bps = bank()
                    for hs in H2:
                        mm(ps[hs, 0:64], au[hs, 0:64], tm3[hs, 1, 0:64], [bau, b_TM3], bps)
                        mm(ps[hs, 64:128], au[hs, 0:64], gxt[hs, 0, 64:128], [bau, bgxt], bps)
                        if d == 1:
                            mm(ps[hs, 128:192], tm3[hs, 1, 0:64], au[hs, 0:64], [bau, b_TM3], bps)
                    ps2, bps2 = bank()
                    for hs in H2:
                        mm(ps2[hs, 0:64], tm3[hs, 1, 0:64], au[hs, 64:128], [b_TM3, bau], bps2, start=True, stop=False)
                        mm(ps2[hs, 0:64], tm3[hs, 2, 0:64], vts[hs], [b_TM3, bvts], bps2, start=False, stop=True)
                    eg3 = egd[d][0][:, sl].rearrange("p (c t) -> p c t", t=64)
                    tcol = 63 if d == 0 else 0
                    gam = eg3[:, ci, tcol:tcol + 1]
                    stt(stg[:, ci, d, 0:64], identst[:], gam, ps[:, 0:64], ALU.mult, ALU.add, [b_identst, egd[d][1], bps],
                        [bps, bsg_])
                    tt(dve, stg[:, ci, d, 64:128], ps[:, 64:128], rtd[d][0][:, c], ALU.add, [bps, rtd[d][1]], [bps, bsg_])
                    if d == 1:
                        stt(Mn[ci][0][:], identst[:], gam, ps[:, 128:192], ALU.mult, ALU.add, [b_identst, egd[d][1], bps],
                            [bps, Mn[ci][1]])
                    acopy(stg[:, ci, d, 128:192], ps2[:, 0:64], [bps2], [bps2, bsg_])
                    yield

                for ci in range(nci):
                    common(ci)
                gens = [chain(ci, d) for ci in range(nci) for d in range(2)]
                ns_ = len(gens)
                nh_ = (ns_ + 3) // 4
                adv = DBG.get('adv', 6)
                for g_ in gens:
                    next(g_)
                adv_extra(adv)
                for j in range(6):
                    if j < 5:
                        RS["bq"] = [bank() for _ in range(nh_)]
                    if j >= 1:
                        RS["bt"] = bank()
                    if j == 0:
                        RS["bx"] = bank()
                    for g_ in gens:
                        next(g_)
                    if j < 5:
                        xl, bxl = XLall[j % 2]
                        for hf in range(nh_):
                            n4 = min(4, ns_ - hf * 4)
                            psq, bpsq = RS["bq"][hf]
                            if j < 4:
                                acopy(xl[:, hf * 4:hf * 4 + n4, :, :].rearrange("p s a b -> p s (a b)"),
                                      psq[:, 0:n4 * 128].rearrange("p (s f) -> p s f", f=128), [bpsq], [bpsq, bxl])
                            else:
                                acopy(xl[:, hf * 4:hf * 4 + n4, 1, :], psq[:, 0:n4 * 128].rearrange("p (s f) -> p s f", f=128)[:, :, 64:128],
                                      [bpsq], [bpsq, bxl])
                    if j >= 1:
                        pst, bpst = RS["bt"]
                        tt(dve, TTall[j % 2][0][:, 0:ns_, :], pst[:, 0:ns_ * 64].rearrange("p (s f) -> p s f", f=64),
                           TTall[(j - 1) % 2][0][:, 0:ns_, :], ALU.add, [bpst, TTall[(j - 1) % 2][1]], [bpst, TTall[j % 2][1]])
                    if j == 0:
                        psx, bpsx = RS["bx"]
                        acopy(TM3all[:, 0:ns_, 0, 64:128], psx[:, 0:ns_ * 64].rearrange("p (s f) -> p s f", f=64), [bpsx], [bpsx, b_TM3])
                    adv_extra(adv)
                RS["ba"] = [bank() for _ in range(nh_)]
                for g_ in gens:
                    next(g_)
                for hf in range(nh_):
                    n4 = min(4, ns_ - hf * 4)
                    psa, bpsa = RS["ba"][hf]
                    acopy(AUall[:, hf * 4:hf * 4 + n4, :], psa[:, 0:n4 * 128].rearrange("p (s f) -> p s f", f=128), [bpsa], [bpsa, b_AU])
                adv_extra(adv)
                for g_ in gens:
                    next(g_)
                for g_ in gens:
                    for _ in g_:
                        pass
                adv_extra(adv)
                for ci in range(nci):
                    for d in range(2):
                        bsg_ = b_stgs[ci * 2 + d]
                        ACC = ACCf if d == 0 else ACCb
                        ao, bao = ACC[p][accsel[p][d]]
                        an, ban = ACC[p][1 - accsel[p][d]]
                        accsel[p][d] = 1 - accsel[p][d]
                        ps, bps = bank()
                        if d == 0:
                            for h in range(2):
                                hs = slice(h * 64, (h + 1) * 64)
                                mm(ps[hs, 0:128], stg[hs, ci, 0, 0:64], ao[hs, :], [bsg_, bao], bps)
                            acopy(an[:, 0:64], ps[:, 0:64], [bps], [bps, ban])
                            tt(dve, an[:, 64:128], ps[:, 64:128], stg[:, ci, 0, 128:192], ALU.add, [bps, bsg_], [bps, ban])
                        else:
                            mn_, bmn_ = Mn[ci]
                            for h in range(2):
                                hs = slice(h * 64, (h + 1) * 64)
                                mm(ps[hs, 0:64], mn_[hs], ao[hs, 0:64], [bmn_, bao], bps)
                                mm(ps[hs, 64:128], ao[hs, 0:64], stg[hs, ci, 1, 128:192], [bsg_, bao], bps)
                            acopy(an[:, 0:64], ps[:, 0:64], [bps], [bps, ban])
                            tt(dve, an[:, 64:128], ps[:, 64:128], ao[:, 64:128], ALU.add, [bps, bao], [bps, ban])
                    if spill and DBG.get('sp_y0', True):
                        vts, bvts = VTs[ci]
                        g0, bg0 = GX[ci * 2]
                        g1, bg1 = GX[ci * 2 + 1]
                        a0, ba0 = AUall[:, ci * 2, :], b_AU
                        a1, ba1 = AUall[:, ci * 2 + 1, :], b_AU
                        ps, bps = bank()
                        for h in range(2):
                            hs = slice(h * 64, (h + 1) * 64)
                            o_ = ps[hs, 0:64]
                            rds = [bg0, bg1, ba0, ba1, bvts]
                            mm(o_, g0[hs, 0, 64:128], a0[hs, 64:128], rds, bps, start=True, stop=False)
                            mm(o_, g0[hs, 1, 64:128], vts[hs], rds, bps, start=False, stop=False)
                            mm(o_, g1[hs, 0, 64:128], a1[hs, 64:128], rds, bps, start=False, stop=False)
                            mm(o_, g1[hs, 1, 64:128], vts[hs], rds, bps, start=False, stop=True)
                        acopy(y0st[:, ci, :], ps[:, 0:64], [bps], [bps, b_y0st])
                if spill and DBG.get('sp_dma', True):
                    for d in range(2):
                        act.dma(spA[d, cg0:cg0 + nch, :, p * 192:(p + 1) * 192].rearrange("c p f -> p c f"), stg[:, 0:nch, d, :],
                               reads=[b_stg] + b_stgs, writes=[bD], sem_buf=b_stg)
                    tk0 = cg0 * 64
                    act.dma(y0d[cg0:cg0 + nch, :, p, :].rearrange("c q v -> q c v"), y0st[:, 0:nch, :],
                           reads=[b_y0st], writes=[bD], sem_buf=b_y0st)
                    act.dma(bvd[cg0:cg0 + nch, :, p, :].rearrange("c q v -> q c v"), bvst[:, 0:nch, :],
                           reads=[b_bvst], writes=[bD], sem_buf=b_bvst)
                if job.get("ctx_last"):
                    for d, ACC in ((0, ACCf), (1, ACCb)):
                        a_, ba_ = ACC[p][accsel[p][d]]
                        vcopy(Hctx[p][0][:, d, :], a_[:, 64:128], [ba_], [Hctx[p][1]])

            segs = [("ctx", 4224, 256, 1, 256, False, 1, 256)] + [(f"q{i}", i * 1024, 1024, 4, 64, True, 4, 256) for i in range(4)]

            def load_seg(hcol, ntok_own, halo):
                ncols_h = ntok_own + (128 if halo else 0)
                sp.dma(hT[:, :, 0:ncols_h], hTd[:, :, hcol:hcol + ncols_h], reads=[bD], writes=[b_hT])
                return 64 if halo else 0

            with ExitStack() as s1a:
                cur_st[0] = s1a
                VTs = [T(f"VTs{i}", [128, 64], BF16) for i in range(4)]
                bon = [T(f"bon{i}", [128, 2]) for i in range(4)]
                TM3all, b_TM3 = T("TM3all", [128, 8, 3, 128], BF16)
                GX = [T(f"GX{i}", [128, 2, 128], BF16) for i in range(8)]
                AUall, b_AU = T("AUall", [128, 8, 128], BF16)
                L0 = [T(f"L0{i}", [128, 64], BF16) for i in range(8)]
                XLall = [T(f"XLall{j}", [128, 8, 2, 64], BF16) for j in range(2)]
                TTall = [T(f"TTall{j}", [128, 8, 64], BF16) for j in range(2)]
                Mn = [T(f"Mn{i}", [128, 64]) for i in range(4)]
                b_stgs = [Buf(f"stg{i}") for i in range(8)]
                stg, b_stg = T("stg", [128, 4, 2, 192])
                y0st, b_y0st = T("y0st", [128, 4, 64])
                bvst, b_bvst = T("bvst", [128, 4, 64])
                ACCf = [[T(f"ACCf{p}_{j}", [128, 128]) for j in range(2)] for p in range(8)]
                ACCb = [[T(f"ACCb{p}_{j}", [128, 128]) for j in range(2)] for p in range(8)]
                accsel = [[0, 0] for _ in range(8)]
                jobs = []
                for sname, hcol, ntok_own, rows, Wd, halo, nb, Tn in segs[:DBG.get('nsegs', 5)]:
                    for p in range(DBG.get('maxp', 8)):
                        for b in range(min(nb, DBG.get('maxb', 9))):
                            if sname == "ctx":
                                job = dict(args=(p, 0, 0, 256, 1, 256, False, 0, False, None), ctx_last=True)
                            else:
                                qi = int(sname[1])
                                hm = (qi == 0 and b == 0, qi == 3 and b == 3)
                                job = dict(args=(p, b * 256, b * 256, 256, 4, 64, True, qi * 16 + b * 4, DBG.get("spill", True), hm))
                            if p == 0 and b == 0:
                                job["seg"] = (hcol, ntok_own, halo)
                                if sname in ("ctx", "q0"):
                                    job["reset"] = True
                            if b == 0:
                                job["newpair"] = True
                            jobs.append(job)
                gens_j = [rwkv_batch(job, i % 2) for i, job in enumerate(jobs)]

                def run_prep(g_):
                    while next(g_) != "PREP_DONE":
                        pass

                run_prep(gens_j[0])
                for i in range(len(jobs)):
                    if i + 1 < len(jobs):
                        PIPE["extra"] = gens_j[i + 1]
                        PIPE["extra_done"] = False
                    else:
                        PIPE["extra"] = None
                        PIPE["extra_done"] = True
                    if not DBG.get("pipe", True) and PIPE["extra"] is not None:
                        pass
                    for _ in gens_j[i]:
                        pass
                    if PIPE["extra"] is not None and not PIPE["extra_done"]:
                        run_prep(PIPE["extra"])
                        PIPE["extra_done"] = True
                ci_ap = comp_in[0].ap()
                for p in range(8):
                    for d, ACC in ((0, ACCf), (1, ACCb)):
                        a_, ba_ = ACC[p][accsel[p][d]]
                        c0 = (p * 2 + d) * 128
                        if d == 0:
                            ps, bps = bank()
                            for h in range(2):
                                hs = slice(h * 64, (h + 1) * 64)
                                mm(ps[hs, 0:64], a_[hs, 0:64], identf[hs, hs], [ba_, b_identf], bps)
                            mn_, bmn_ = Mn[p % 4]
                            vcopy(mn_[:], ps[:, 0:64], [bps], [bps, bmn_])
                            act.dma(ci_ap[:, c0:c0 + 64], mn_[:], reads=[bmn_], writes=[bD], sem_buf=bmn_)
                            act.dma(ci_ap[:, c0 + 64:c0 + 128], a_[:, 64:128], reads=[ba_], writes=[bD], sem_buf=ba_)
                        else:
                            act.dma(ci_ap[:, c0:c0 + 128], a_[:, :], reads=[ba_], writes=[bD], sem_buf=ba_)
                fw.barrier()
                fw.emit()
            with ExitStack() as s1b:
                cur_st[0] = s1b
                stGS = [T(f"stG{i}", [128, 4, 2, 321]) for i in range(2)]
                o0st, b_o0st = T("o0st", [128, 2, 256])
                VTg = [T(f"VTg{i}", [64, 256], BF16) for i in range(4)]
                AttT = [T(f"AttT{i}", [64, 64], BF16) for i in range(8)]
                KhT = [T(f"KhT{i}", [64, 128], BF16) for i in range(8)]
                b_GNs = [Buf(f"GN{i}") for i in range(8)]
                GN, b_GN = T("GN", [128, 8, 256])
                GD, b_GD = T("GD", [128, 8])
                gjobs = []
                for sname, hcol, ntok_own, rows, Wd, halo, nb, Tn in segs[:DBG.get('nsegs', 5)]:
                    for g in range(DBG.get('maxg', 4)):
                        for b in range(nb):
                            if sname == "ctx":
                                job = dict(args=(g, 0, 0, 256, 0, False))
                                if g == 3:
                                    job["ctx_last"] = True
                            else:
                                qi = int(sname[1])
                                job = dict(args=(g, 64 + b * 256, b * 256, 256, qi * 16 + b * 4, True))
                            if g == 0 and b == 0:
                                job["seg"] = (hcol, ntok_own, halo)
                                if sname in ("ctx", "q0"):
                                    job["reset"] = True
                            if b == 0:
                                job["newhead"] = True
                            gjobs.append(job)
                gens_g = [gla_batch(job, i % 2) for i, job in enumerate(gjobs)]
                run_prep(gens_g[0])
                for i in range(len(gjobs)):
                    if i + 1 < len(gjobs):
                        PIPE["extra"] = gens_g[i + 1]
                        PIPE["extra_done"] = False
                    else:
                        PIPE["extra"] = None
                        PIPE["extra_done"] = True
                    for _ in gens_g[i]:
                        pass
                    if PIPE["extra"] is not None and not PIPE["extra_done"]:
                        run_prep(PIPE["extra"])
                        PIPE["extra_done"] = True
                act.dma(comp_in[1].ap()[:, 0:8], GD[:, :], reads=[b_GD], writes=[bD], sem_buf=b_GD)
                act.dma(comp_in[1].ap()[:, 8:1032], GN[:, 0:4, :].rearrange("p g f -> p (g f)"), reads=[b_GN] + b_GNs, writes=[bD], sem_buf=b_GN)
                act.dma(comp_in[2].ap()[:, :], GN[:, 4:8, :].rearrange("p g f -> p (g f)"), reads=[b_GN] + b_GNs, writes=[bD], sem_buf=b_GN)
                fw.barrier()
                fw.emit()
            pass
        with ExitStack() as s1c:
            def T(name, shape, dt=F32):
                return fw.sbuf(s1c, name, shape, dt)
            hT2, b_hT2 = T("hT2", [128, 16, 2048], BF16)
            w5f = [T(f"w5f{j}", [128, 16, 512]) for j in range(2)]
            w5b = [T(f"w5b{j}", [128, 16, 512], BF16) for j in range(2)]
            zst = [T(f"zst{j}", [128, 2048], BF16) for j in range(2)]
            gst = [T(f"gst{j}", [128, 512], BF16) for j in range(3)]
            zblocks = [(C_Z_A, 0), (C_Z_A + 512, 4), (C_ZB, 8), (C_ZB + 512, 12)]
            nw = 0
            ng_ = 0
            for half in range(2):
                sp.dma(hT2[:], hTd[:, :, 64 + half * 2048:64 + (half + 1) * 2048], reads=[bD], writes=[b_hT2])
                for colw, blk0 in zblocks:
                    wt, bw = w5f[nw % 2]
                    wbf, bwb = w5b[nw % 2]
                    nw += 1
                    sp.dma(wt[:], w_in_v[:, :, colw:colw + 512], writes=[bw])
                    for k4 in range(4):
                        eng_ = pool if k4 % 2 == 0 else act
                        if eng_ is pool:
                            pool.op(lambda e, wbf=wbf, wt=wt, k4=k4: e.tensor_copy(out=wbf[:, k4 * 4:(k4 + 1) * 4, :], in_=wt[:, k4 * 4:(k4 + 1) * 4, :]),
                                    reads=[bw], writes=[bwb])
                        else:
                            acopy(wbf[:, k4 * 4:(k4 + 1) * 4, :], wt[:, k4 * 4:(k4 + 1) * 4, :], [bw], [bwb])
                    for sb_ in range(4):
                        zt, bzt = zst[(blk0 + sb_) % 2]
                        for tb in range(4):
                            ps, bps = bank()
                            for k in range(16):
                                mm(ps[:, :], wbf[:, k, sb_ * 128:(sb_ + 1) * 128], hT2[:, k, tb * 512:(tb + 1) * 512], [bwb, b_hT2], bps,
                                   start=(k == 0), stop=(k == 15))
                            acopy(zt[:, tb * 512:(tb + 1) * 512], ps[:, :], [bps], [bps, bzt], func=AF.Silu)
                        act.dma(zaTd[blk0 + sb_, :, half * 2048:(half + 1) * 2048], zt[:, :], reads=[bzt], writes=[bD], sem_buf=bzt)
                for cb in range(8):
                    wt, bw = w5f[nw % 2]
                    wbf, bwb = w5b[nw % 2]
                    nw += 1
                    sp.dma(wt[:], w_in_v[:, :, C_GA + cb * 512:C_GA + (cb + 1) * 512], writes=[bw])
                    for k4 in range(4):
                        if k4 % 2 == 0:
                            pool.op(lambda e, wbf=wbf, wt=wt, k4=k4: e.tensor_copy(out=wbf[:, k4 * 4:(k4 + 1) * 4, :], in_=wt[:, k4 * 4:(k4 + 1) * 4, :]),
                                    reads=[bw], writes=[bwb])
                        else:
                            vcopy(wbf[:, k4 * 4:(k4 + 1) * 4, :], wt[:, k4 * 4:(k4 + 1) * 4, :], [bw], [bwb])
                    for tl in range(16):
                        ps, bps = bank()
                        for k in range(16):
                            mm(ps[:, :], hT2[:, k, tl * 128:(tl + 1) * 128], wbf[:, k, :], [b_hT2, bwb], bps, start=(k == 0), stop=(k == 15))
                        gt, bgt = gst[ng_ % 3]
                        ng_ += 1
                        acopy(gt[:], ps[:, :], [bps], [bps, bgt], func=AF.Sigmoid)
                        r0 = half * 2048 + tl * 128
                        act.dma(zgd[r0:r0 + 128, cb * 512:(cb + 1) * 512], gt[:], reads=[bgt], writes=[bD], sem_buf=bgt)
            fw.barrier()
            fw.emit()
        if phases <= 1:
            return nc, fw

        Hf0, b_Hf0 = fw.sbuf(top, "Hf0", [128, 8, 64])
        Hb0, b_Hb0 = fw.sbuf(top, "Hb0", [128, 8, 64])
        Hgf0, b_Hgf0 = fw.sbuf(top, "Hgf0", [128, 4, 256])
        Hgb0, b_Hgb0 = fw.sbuf(top, "Hgb0", [128, 4, 256])
        identstb, b_identstb = fw.sbuf(top, "identstb", [128, 64], BF16)
        vcopy(identstb[:], identst[:], [b_identst], [b_identstb])

        with ExitStack() as sE:
            gath, b_gath = fw.sbuf(sE, "gath", [128, 4, NCOMP])
            Hx = [fw.sbuf(sE, f"Hx{j}", [128, 8, 64]) for j in range(2)]
            t1x, b_t1x = fw.sbuf(sE, "t1x", [128, 8, 64])
            Gx = [fw.sbuf(sE, f"Gx{j}", [128, 4, 256]) for j in range(2)]
            t2x, b_t2x = fw.sbuf(sE, "t2x", [128, 4, 256])
            if DBG.get("nocc", False):
                for i in range(3):
                    for r in range(4):
                        sp.dma(comp_out[i].ap()[r * 128:(r + 1) * 128, :], comp_in[i].ap(), reads=[bD], writes=[bD], sem_buf=b_gath)
                bCO = bD
            else:
                cc_sem = fw.new_sem("cc")
                pool._wait(dict(bD.w))
                for i in range(3):
                    pool.prog.append(("o", (lambda e, i=i: e.collective_compute(
                        "AllGather", ALU.bypass, replica_groups=[[0, 1, 2, 3], [4, 5, 6, 7]],
                        ins=[comp_in[i].ap()], outs=[comp_out[i].ap()])), cc_sem, 1))
                bCO = Buf("comp_out", dram=True)
                bCO.w = {cc_sem: 3}
            for i, (ca, cb_) in enumerate(CSPL):
                sp.dma(gath[:, :, ca:cb_], comp_out[i].ap().rearrange("(r p) n -> p r n", p=128), reads=[bCO], writes=[b_gath])
            for d in range(2):
                cur, bcur = Hx[0]
                for p in range(8):
                    vcopy(cur[:, p, :], Hctx[p][0][:, d, :], [Hctx[p][1]], [bcur])
                order = (0, 1, 2) if d == 0 else (3, 2, 1)
                for i, sgi in enumerate(order):
                    nxt, bnxt = (Hx[(i + 1) % 2]) if i < 2 else ((Hf0, b_Hf0) if d == 0 else (Hb0, b_Hb0))
                    ps, bps = bank()
                    for p in range(8):
                        c0 = (p * 2 + d) * 128
                        for h in range(2):
                            hs = slice(h * 64, (h + 1) * 64)
                            mm(ps[hs, p * 64:(p + 1) * 64], gath[hs, sgi, c0:c0 + 64], cur[hs, p, :], [b_gath, bcur], bps)
                    nview = gath[:, sgi, 0:2048].rearrange("q (p d f) -> q p d f", d=2, f=128)[:, :, d, 64:128]
                    tt(dve, t1x[:], ps[:, :].rearrange("q (p f) -> q p f", f=64), nview, ALU.add, [bps, b_gath], [bps, b_t1x])
                    tt(dve, t1x[:], t1x[:], cur[:], ALU.subtract, [b_t1x, bcur], [b_t1x])
                    mcol = CV_FM + d * 4 + sgi
                    stt(nxt[:], t1x[:], colv[:, mcol:mcol + 1], cur[:], ALU.mult, ALU.add, [b_t1x, b_colv, bcur], [bnxt])
                    cur, bcur = nxt, bnxt
                cur, bcur = Gx[0]
                for g in range(4):
                    vcopy(cur[:, g, :], Hgctx[:, g * 2 + d, :], [b_Hgctx], [bcur])
                for i, sgi in enumerate(order):
                    nxt, bnxt = (Gx[(i + 1) % 2]) if i < 2 else ((Hgf0, b_Hgf0) if d == 0 else (Hgb0, b_Hgb0))
                    for g in range(4):
                        gi = g * 2 + d
                        stt(t2x[:, g, :], cur[:, g, :], gath[:, sgi, 2048 + gi:2048 + gi + 1],
                            gath[:, sgi, 2056 + gi * 256:2056 + (gi + 1) * 256], ALU.mult, ALU.add, [bcur, b_gath], [b_t2x])
                    tt(dve, t2x[:], t2x[:], cur[:], ALU.subtract, [b_t2x, bcur], [b_t2x])
                    mcol = CV_FM + d * 4 + sgi
                    stt(nxt[:], t2x[:], colv[:, mcol:mcol + 1], cur[:], ALU.mult, ALU.add, [b_t2x, b_colv, bcur], [bnxt])
                    cur, bcur = nxt, bnxt
            fw.barrier()
            fw.emit()
        if phases <= 2:
            return nc, fw

        def passB(stk, d, c, Hc, bHc, Hn, bHn, Gc, bGc, Gn_, bGn, stA_t, b_stA, stG_t, b_stG2):
            sp.dma(stA_t[:], spA[d, c], reads=[bD], writes=[b_stA])
            sp.dma(stG_t[:], spG[d, c], reads=[bD], writes=[b_stG2])
            par = c % 2
            pp = slice(par * 64, (par + 1) * 64)
            psH, bpsH = bank()
            psY, bpsY = bank()
            for p in range(8):
                for h in range(2):
                    hs = slice(h * 64, (h + 1) * 64)
                    mm(psH[hs, p * 64:(p + 1) * 64], stA_t[hs, p * 192:p * 192 + 64], Hc[hs, p, :], [b_stA, bHc], bpsH)
            nview = stA_t[:, :].rearrange("q (p f) -> q p f", f=192)[:, :, 128:192]
            tt(dve, Hn[:], psH[:, :].rearrange("q (p f) -> q p f", f=64), nview, ALU.add, [bpsH, b_stA], [bpsH, bHn])
            for g in range(4):
                stt(Gn_[:, g, :], Gc[:, g, :], stG_t[:, g * 321 + 320:g * 321 + 321], stG_t[:, g * 321 + 64:g * 321 + 320],
                    ALU.mult, ALU.add, [bGc, b_stG2], [bGn])
            for p in range(8):
                for h in range(2):
                    hs = slice(h * 64, (h + 1) * 64)
                    mm(psY[hs, p * 64:(p + 1) * 64], stA_t[hs, p * 192 + 64:p * 192 + 128], Hc[hs, p, :], [b_stA, bHc], bpsY)
            psO = [bank(), bank()]
            for g in range(4):
                po, bpo = psO[g // 2]
                mm(po[pp, (g % 2) * 256:(g % 2 + 1) * 256], stG_t[:, g * 321:g * 321 + 64], Gc[:, g, :], [b_stG2, bGc], bpo)
            return (psY, bpsY), psO, pp

        sW = ExitStack()
        wa_bf, b_wa = fw.sbuf(sW, "wa_bf", [128, 8, D], BF16)
        wb_bf, b_wbb = fw.sbuf(sW, "wb_bf", [128, 8, D], BF16)
        with ExitStack() as s2:
            wstW = [fw.sbuf(s2, f"wstW{j}", [128, D]) for j in range(2)]
            nwl = 0
            for (wsrc, wdst, bdst) in ((w_a, wa_bf, b_wa), (w_b, wb_bf, b_wbb)):
                for p in range(8):
                    wst_, b_wst_ = wstW[nwl % 2]
                    nwl += 1
                    sp.dma(wst_[:], wsrc[p * 128:(p + 1) * 128, :], writes=[b_wst_])
                    pool.op(lambda e, wdst=wdst, p=p, wst_=wst_: e.tensor_copy(out=wdst[:, p, :], in_=wst_[:]), reads=[b_wst_], writes=[bdst])
            def T2(name, shape, dt=F32):
                return fw.sbuf(s2, name, shape, dt)
            stA_b = [T2(f"stAb{j}", [128, 1536]) for j in range(2)]
            stG_b = [T2(f"stGb{j}", [128, 1284]) for j in range(2)]
            Hb = [T2(f"Hbk{j}", [128, 8, 64]) for j in range(2)]
            Gb = [T2(f"Gbk{j}", [128, 4, 256]) for j in range(2)]
            y0_t = [T2(f"y0t{j}", [128, 512]) for j in range(2)]
            yp_o = [T2(f"ypo{j}", [128, 512]) for j in range(2)]
            o0_t = [T2(f"o0t{j}", [128, 1024]) for j in range(2)]
            op_o = [T2(f"opo{j}", [128, 1024]) for j in range(2)]
            vcopy(Hb[0][0][:], Hb0[:], [b_Hb0], [Hb[0][1]])
            vcopy(Gb[0][0][:], Hgb0[:], [b_Hgb0], [Gb[0][1]])
            for i, c in enumerate(range(NCH - 1, -1, -1)):
                j = c // 2
                Hc, bHc = Hb[i % 2]
                Hn, bHn = Hb[(i + 1) % 2]
                Gc, bGc = Gb[i % 2]
                Gn_, bGn = Gb[(i + 1) % 2]
                (psY, bpsY), psO, pp = passB(s2, 1, c, Hc, bHc, Hn, bHn, Gc, bGc, Gn_, bGn, stA_b[i % 2][0], stA_b[i % 2][1],
                                             stG_b[i % 2][0], stG_b[i % 2][1])
                yt, byt = y0_t[i % 2]
                sp.dma(yt[:], y0d[c].rearrange("q p v -> q (p v)"), reads=[bD], writes=[byt])
                yo, byo = yp_o[i % 2]
                tt(dve, yo[:], psY[:, :], yt[:], ALU.add, [bpsY, byt], [bpsY, byo])
                act.dma(ypd[c].rearrange("q p v -> q (p v)"), yo[:], reads=[byo], writes=[bD], sem_buf=byo)
                ot, bot = o0_t[j % 2]
                oo, boo = op_o[j % 2]
                if c % 2 == 1:
                    sp.dma(ot[:], o0d[j * 128:(j + 1) * 128, :], reads=[bD], writes=[bot])
                for hb_ in range(2):
                    po, bpo = psO[hb_]
                    tt(dve, oo[pp, hb_ * 512:(hb_ + 1) * 512], po[pp, :], ot[pp, hb_ * 512:(hb_ + 1) * 512], ALU.add, [bpo, bot],
                       [bpo, boo])
                if c % 2 == 0:
                    act.dma(opd[j * 128:(j + 1) * 128, :], oo[:], reads=[boo], writes=[bD], sem_buf=boo)
            fw.barrier()
            fw.emit()
        if phases <= 3:
            return nc, fw

        with ExitStack() as s3:
            def T3(name, shape, dt=F32):
                return fw.sbuf(s3, name, shape, dt)
            lnxg_st, b_lnxg = T3("lnxg_st", [128, 8, 64])
            lnxb_st, b_lnxb = T3("lnxb_st", [128, 8, 64])
            bng_b, b_bng = T3("bng_b", [128, 4, 256])
            for h in range(2):
                hs = slice(h * 64, (h + 1) * 64)
                sp.dma(lnxg_st[hs, :, :], bass.AP(lnxg, h * 64, [[0, 64], [128, 8], [1, 64]]), writes=[b_lnxg])
                sp.dma(lnxb_st[hs, :, :], bass.AP(lnxb, h * 64, [[0, 64], [128, 8], [1, 64]]), writes=[b_lnxb])
            sp.dma(bng_b[:], bass.AP(bng, 0, [[0, 128], [0, 4], [1, 256]]), writes=[b_bng])
            stA_f = [T3(f"stAf{j}", [128, 1536]) for j in range(2)]
            stG_f = [T3(f"stGf{j}", [128, 1284]) for j in range(2)]
            Hf = [T3(f"Hfk{j}", [128, 8, 64]) for j in range(2)]
            Gf = [T3(f"Gfk{j}", [128, 4, 256]) for j in range(2)]
            yp_t = [T3(f"ypt{j}", [128, 512]) for j in range(2)]
            bv_t = [T3(f"bvt{j}", [128, 512]) for j in range(2)]
            w1S = [T3(f"w1_{j}", [128, 512]) for j in range(2)]
            w2S = [T3(f"w2_{j}", [128, 512]) for j in range(2)]
            stS = [T3(f"st_{j}", [128, 64]) for j in range(2)]
            ya_bdS = [T3(f"ya_bd{j}", [128, 8, 128], BF16) for j in range(2)]
            yaTS = [T3(f"yaT{j}", [128, 8, 128], BF16) for j in range(2)]
            ybTS = [T3(f"ybT{j}", [128, 8, 128], BF16) for j in range(1)] * 2
            zaT_t = [T3(f"zaTt{j}", [128, 16, 128], BF16) for j in range(2)]
            op_tS = [T3(f"op_t{j}", [128, 1024]) for j in range(1)] * 2
            obS = [T3(f"ob{j}", [128, 1024]) for j in range(2)]
            tmpb, b_tmpb = T3("tmpb", [128, 1024])
            yb_bfS = [T3(f"yb_bf{j}", [128, 1024], BF16) for j in range(1)] * 2
            zg_t, b_zgt = T3("zg_t", [128, 4096], BF16)
            m1, b_m1 = T3("m1", [128, 512])
            m2, b_m2 = T3("m2", [128, 512])
            mixedS = [T3(f"mixed{j}", [128, D], BF16) for j in range(1)] * 2
            mT_t = [T3(f"mTt{j}", [128, 16, 128], BF16) for j in range(1)] * 2
            for j_ in range(2):
                pool.op(lambda e, j_=j_: e.memset(ya_bdS[j_][0][:], 0.0), writes=[ya_bdS[j_][1]])
            vcopy(Hf[0][0][:], Hf0[:], [b_Hf0], [Hf[0][1]])
            vcopy(Gf[0][0][:], Hgf0[:], [b_Hgf0], [Gf[0][1]])
            b3 = lambda ap, n, m: ap.to_broadcast([128, n, m])
            for c in range(NCH):
                j = c // 2
                par = c % 2
                Hc, bHc = Hf[c % 2]
                Hn, bHn = Hf[(c + 1) % 2]
                Gc, bGc = Gf[c % 2]
                Gn_, bGn = Gf[(c + 1) % 2]
                zt_, bzt_ = zaT_t[j % 2]
                w1, b_w1 = w1S[c % 2]
                w2, b_w2 = w2S[c % 2]
                st, b_st = stS[c % 2]
                ya_bd, b_yabd = ya_bdS[c % 2]
                yaT, b_yaT = yaTS[j % 2]
                ybT, b_ybT = ybTS[j % 2]
                op_t, b_opt = op_tS[j % 2]
                ob, b_ob = obS[j % 2]
                yb_bf, b_ybbf = yb_bfS[j % 2]
                mixed, b_mixed = mixedS[j % 2]
                if par == 0:
                    sp.dma(zt_[:], zaTd[:, :, j * 128:(j + 1) * 128].rearrange("b p t -> p b t"), reads=[bD], writes=[bzt_])
                    sp.dma(op_t[:], opd[j * 128:(j + 1) * 128, :], reads=[bD], writes=[b_opt])
                    sp.dma(zg_t[:], zgd[j * 128:(j + 1) * 128, :], reads=[bD], writes=[b_zgt])
                (psY, bpsY), psO, pp = passB(s3, 0, c, Hc, bHc, Hn, bHn, Gc, bGc, Gn_, bGn, stA_f[c % 2][0], stA_f[c % 2][1],
                                             stG_f[c % 2][0], stG_f[c % 2][1])
                ypt, bypt = yp_t[c % 2]
                bvt, bbvt = bv_t[c % 2]
                sp.dma(ypt[:], ypd[c].rearrange("q p v -> q (p v)"), reads=[bD], writes=[bypt])
                sp.dma(bvt[:], bvd[c].rearrange("q p v -> q (p v)"), reads=[bD], writes=[bbvt])
                tt(dve, w1[:], psY[:, :], ypt[:], ALU.add, [bpsY, bypt], [bpsY, b_w1])
                for hb_ in range(2):
                    po, bpo = psO[hb_]
                    tt(dve, ob[pp, hb_ * 512:(hb_ + 1) * 512], po[pp, :], op_t[pp, hb_ * 512:(hb_ + 1) * 512], ALU.add, [bpo, b_opt],
                       [bpo, b_ob])
                w13 = w1[:].rearrange("q (p v) -> q p v", v=64)
                treduce(st[:, 0:8], w13, [b_w1], [b_st])
                acopy(w2[:], w1[:], [b_w1], [b_w2], func=AF.Square)
                treduce(st[:, 8:16], w2[:].rearrange("q (p v) -> q p v", v=64), [b_w2], [b_st])
                ts(dve, st[:, 16:24], st[:, 0:8], 1.0 / 64, None, ALU.mult, None, [b_st], [b_st])
                tt(dve, st[:, 24:32], st[:, 16:24], st[:, 16:24], ALU.mult, [b_st], [b_st])
                stt(st[:, 32:40], st[:, 8:16], 1.0 / 64, st[:, 24:32], ALU.mult, ALU.subtract, [b_st], [b_st])
                acopy(st[:, 40:48], st[:, 32:40], [b_st], [b_st], func=AF.Sqrt, bias=64e-5)
                recip(st[:, 48:56], st[:, 40:48], [b_st], [b_st])
                tt(dve, w13, w13, b3(st[:, 16:24].rearrange("q (p o) -> q p o", o=1), 8, 64), ALU.subtract, [b_w1, b_st], [b_w1])
                tt(dve, w13, w13, b3(st[:, 48:56].rearrange("q (p o) -> q p o", o=1), 8, 64), ALU.mult, [b_w1, b_st], [b_w1])
                tt(pool, w13, w13, lnxg_st[:], ALU.mult, [b_w1, b_lnxg], [b_w1])
                tt(pool, w13, w13, lnxb_st[:], ALU.add, [b_w1, b_lnxb], [b_w1])
                for h in range(2):
                    hs = slice(h * 64, (h + 1) * 64)
                    tt(dve if h == 0 else pool, ya_bd[hs, :, h * 64:(h + 1) * 64], w1[hs, :].rearrange("q (p v) -> q p v", v=64),
                       bvt[hs, :].rearrange("q (p v) -> q p v", v=64), ALU.add, [b_w1, bbvt], [b_yabd])
                psT, bpsT = bank()
                for p in range(8):
                    mm(psT[:, p * 64:(p + 1) * 64], ya_bd[:, p, :], identstb[:], [b_yabd, b_identstb], bpsT)
                tt(dve, yaT[:, :, par * 64:(par + 1) * 64], psT[:, :].rearrange("q (p t) -> q p t", t=64),
                   zt_[:, 0:8, par * 64:(par + 1) * 64], ALU.mult, [bpsT, bzt_], [bpsT, b_yaT])
                if par == 0:
                    continue
                acopy(tmpb[:], ob[:], [b_ob], [b_tmpb], func=AF.Square)
                treduce(st[:, 56:60], tmpb[:].rearrange("q (g v) -> q g v", v=256), [b_tmpb], [b_st])
                ts(dve, st[:, 60:64], st[:, 56:60], 1.0 / 256, None, ALU.mult, None, [b_st], [b_st])
                acopy(st[:, 56:60], st[:, 60:64], [b_st], [b_st], func=AF.Sqrt, bias=1e-6)
                recip(st[:, 60:64], st[:, 56:60], [b_st], [b_st])
                ob3 = ob[:].rearrange("q (g v) -> q g v", v=256)
                tt(dve, ob3, ob3, b3(st[:, 60:64].rearrange("q (g o) -> q g o", o=1), 4, 256), ALU.mult, [b_ob, b_st], [b_ob])
                tt(pool, yb_bf[:].rearrange("q (g v) -> q g v", v=256), ob3, bng_b[:], ALU.mult, [b_ob, b_bng], [b_ybbf])
                for k4 in range(2):
                    psT, bpsT = bank()
                    for kk in range(4):
                        k = k4 * 4 + kk
                        mm(psT[:, kk * 128:(kk + 1) * 128], yb_bf[:, k * 128:(k + 1) * 128], identb[:], [b_ybbf, b_identb], bpsT)
                    tt(dve, ybT[:, k4 * 4:(k4 + 1) * 4, :], psT[:, :].rearrange("q (a t) -> q a t", t=128),
                       zt_[:, 8 + k4 * 4:8 + (k4 + 1) * 4, :], ALU.mult, [bpsT, bzt_], [bpsT, b_ybT])
                for nbk in range(4):
                    ns = slice(nbk * 512, (nbk + 1) * 512)
                    psA, bpsA = bank()
                    for p in range(8):
                        mm(psA[:, :], yaT[:, p, :], wa_bf[:, p, ns], [b_yaT, b_wa], bpsA, start=(p == 0), stop=(p == 7))
                    psB, bpsB = bank()
                    for p in range(8):
                        mm(psB[:, :], ybT[:, p, :], wb_bf[:, p, ns], [b_ybT, b_wbb], bpsB, start=(p == 0), stop=(p == 7))
                    tt(dve, m1[:], psA[:, :], zg_t[:, ns], ALU.mult, [bpsA, b_zgt], [bpsA, b_m1])
                    tt(dve, m2[:], psB[:, :], zg_t[:, 2048 + nbk * 512:2048 + (nbk + 1) * 512], ALU.mult, [bpsB, b_zgt], [bpsB, b_m2])
                    tt(pool, mixed[:, ns], m1[:], m2[:], ALU.add, [b_m1, b_m2], [b_mixed])
                mt_, bmt_ = mT_t[j % 2]
                for k4 in range(4):
                    psT, bpsT = bank()
                    for kk in range(4):
                        k = k4 * 4 + kk
                        mm(psT[:, kk * 128:(kk + 1) * 128], mixed[:, k * 128:(k + 1) * 128], identb[:], [b_mixed, b_identb], bpsT)
                    ecopy(mt_[:, k4 * 4:(k4 + 1) * 4, :], psT[:, :].rearrange("q (a t) -> q a t", t=128), [bpsT], [bpsT, bmt_])
                act.dma(mTd[:, :, j * 128:(j + 1) * 128], mt_[:], reads=[bmt_], writes=[bD], sem_buf=bmt_)
            fw.barrier()
            fw.emit()
        sW.close()
        if phases <= 4:
            return nc, fw

        with ExitStack() as s4:
            def T4(name, shape, dt=F32):
                return fw.sbuf(s4, name, shape, dt)
            wo_bf, b_wo = T4("wo_bf", [128, 16, D], BF16)
            wst4 = [T4(f"wst4_{j}", [128, D]) for j in range(3)]
            b_wos = [Buf(f"wo{k}") for k in range(16)]
            for k in range(16):
                wst, b_wst = wst4[k % 3]
                sp.dma(wst[:], w_out[k * 128:(k + 1) * 128, :], writes=[b_wst])
                if k % 3 == 0:
                    pool.op(lambda e, k=k, wst=wst: e.tensor_copy(out=wo_bf[:, k, :], in_=wst[:]), reads=[b_wst], writes=[b_wos[k]])
                elif k % 3 == 1:
                    vcopy(wo_bf[:, k, :], wst[:], [b_wst], [b_wos[k]])
                else:
                    acopy(wo_bf[:, k, :], wst[:], [b_wst], [b_wos[k]])
            gate_t, b_gt = T4("gate_t", [128, D])
            fg_t, b_fg = T4("fg_t", [128, D])
            sp.dma(gate_t[:], gate_d, reads=[bD], writes=[b_gt])
            sp.dma(fg_t[:], bc(final_g, D), writes=[b_fg])
            mTi = [T4(f"mTi{j}", [128, 16, 128], BF16) for j in range(2)]
            xin = [T4(f"xin{j}", [128, D]) for j in range(2)]
            o_t = [T4(f"o_t{j}", [128, D]) for j in range(2)]
            res_t = [T4(f"res{j}", [128, D]) for j in range(2)]
            sq4, b_sq4 = T4("sq4", [128, D])
            s4t, b_s4t = T4("s4t", [128, 4])
            for j in range(32):
                mi, bmi = mTi[j % 2]
                xi, bxi = xin[j % 2]
                ot, bot = o_t[j % 2]
                rt_, brt = res_t[j % 2]
                sp.dma(mi[:], mTd[:, :, j * 128:(j + 1) * 128], reads=[bD], writes=[bmi])
                sp.dma(xi[:], xs[64 + j * 128:64 + (j + 1) * 128, :], writes=[bxi])
                for nbk in range(4):
                    ns = slice(nbk * 512, (nbk + 1) * 512)
                    psW, bpsW = bank()
                    for k in range(16):
                        mm(psW[:, :], mi[:, k, :], wo_bf[:, k, ns], [bmi, b_wos[k]], bpsW, start=(k == 0), stop=(k == 15))
                    tt(dve, ot[:, ns], psW[:, :], gate_t[:, ns], ALU.mult, [bpsW, b_gt], [bpsW, bot])
                    tt(pool, ot[:, ns], ot[:, ns], xi[:, ns], ALU.add, [bot, bxi], [bot])
                acopy(sq4[:], ot[:], [bot], [b_sq4, b_s4t], func=AF.Square, accum_out=s4t[:, 0:1])
                ts(dve, s4t[:, 1:2], s4t[:, 0:1], 1.0 / D, 1e-6, ALU.mult, ALU.add, [b_s4t], [b_s4t])
                acopy(s4t[:, 2:3], s4t[:, 1:2], [b_s4t], [b_s4t], func=AF.Sqrt)
                dve.op(lambda e: e.reciprocal(out=s4t[:, 3:4], in_=s4t[:, 2:3]), reads=[b_s4t], writes=[b_s4t])
                stt(rt_[:], ot[:], s4t[:, 3:4], fg_t[:], ALU.mult, ALU.mult, [bot, b_s4t, b_fg], [brt])
                act.dma(out_d[j * 128:(j + 1) * 128, :], rt_[:], reads=[brt], writes=[bD], sem_buf=brt)
            fw.barrier()
            fw.emit()
    return nc, fw


_CACHE = {}


def _consts():
    idx = np.arange(64)
    identf = np.eye(128, dtype=np.float32)
    identst = np.concatenate([np.eye(64), np.eye(64)], 0).astype(np.float32)
    st_f = (idx[None, :] > idx[:, None]).astype(np.float32)
    in_f = (idx[None, :] >= idx[:, None]).astype(np.float32)
    st_b = (idx[None, :] < idx[:, None]).astype(np.float32)
    in_b = (idx[None, :] <= idx[:, None]).astype(np.float32)
    mG = np.zeros((128, 2, 128), np.float32)
    for h in range(2):
        mG[h * 64:(h + 1) * 64, 0, 0:64] = st_f
        mG[h * 64:(h + 1) * 64, 0, 64:128] = in_f
        mG[h * 64:(h + 1) * 64, 1, 0:64] = st_b
        mG[h * 64:(h + 1) * 64, 1, 64:128] = in_b
    mL = np.zeros((128, 2, 64), np.float32)
    for h in range(2):
        mL[h * 64:(h + 1) * 64, 0, :] = st_f.T
        mL[h * 64:(h + 1) * 64, 1, :] = st_b.T
    mA = np.stack([in_f, in_b], 1).astype(np.float32)
    bones = np.zeros((128, 128), np.float32)
    bones[0:64, 0:64] = 1
    bones[64:128, 64:128] = 1
    ind2 = np.zeros((128, 2), np.float32)
    ind2[0:64, 0] = 1
    ind2[64:128, 1] = 1
    rmask = np.ones((128, 512), np.float32)
    rmask[:, ::64] = 0
    return dict(identf=identf, identst=identst, maskG=mG, maskL=mL, maskA=mA, bones=bones, ind2=ind2, rmask=rmask)


def _prep_inputs(inp):
    f = lambda a: np.ascontiguousarray(np.asarray(a, dtype=np.float32))
    x = f(inp["x"]); c = f(inp["c"]); ctx = f(inp["ctx"]); c_ctx = f(inp["c_ctx"])
    conv_w = f(inp["conv_w"])[0].reshape(9, 3072)
    shared = dict(
        w_mod=f(inp["w_mod"])[0], b_mod=f(inp["b_mod"])[0], norm_g=f(inp["norm_g"])[0], w_in=f(inp["w_in"])[0],
        aw2p=f(inp["a_w2"])[0].reshape(128, 1024), aa2p=f(inp["a_a2"])[0].reshape(128, 1024),
        lnxg=f(inp["a_lnx_g"])[0], lnxb=f(inp["a_lnx_b"])[0], bng=f(inp["b_norm_g"])[0],
        w_a=f(inp["w_a"])[0], w_b=f(inp["w_b"])[0], w_out=f(inp["w_out"])[0], final_g=f(inp["final_g"]),
    )
    gk = f(inp["b_gk_w2"])[0]
    gkw2p = np.zeros((32, 2, 512), np.float32)
    gkw2p[0:16, 0] = gk[0]
    gkw2p[16:32, 1] = gk[1]
    shared["gkw2p"] = gkw2p
    shared.update(_consts())
    colv0 = np.zeros((128, NCOLV), np.float32)
    colv0[:, 0:216] = conv_w.reshape(9, 24, 128).transpose(2, 1, 0).reshape(128, 216)
    for d in range(2):
        colv0[:, CV_W0 + d * 8:CV_W0 + d * 8 + 8] = f(inp["a_w0"])[0, d].reshape(8, 128).T
        colv0[:, CV_A0 + d * 8:CV_A0 + d * 8 + 8] = f(inp["a_a0"])[0, d].reshape(8, 128).T
        colv0[:, CV_GB + d * 4:CV_GB + d * 4 + 4] = f(inp["b_gk_b"])[0, d].reshape(4, 128).T
    colv0[:, CV_KK:CV_KK + 8] = f(inp["a_k_k"])[0].reshape(8, 128).T
    colv0[:, CV_KA:CV_KA + 8] = f(inp["a_k_a"])[0].reshape(8, 128).T
    colv0[:, CV_RK:CV_RK + 8] = f(inp["a_r_k"])[0].reshape(8, 128).T
    maps = []
    for core in range(8):
        b, q = core // 4, core % 4
        xs = np.zeros((4224, D), np.float32)
        lo, hi = q * SEG - 64, (q + 1) * SEG + 64
        slo, shi = max(lo, 0), min(hi, 4 * SEG)
        xs[slo - lo:shi - lo] = x[b, slo:shi]
        cv = colv0.copy()
        cv[:, CV_HALO] = 0.0 if q == 0 else 1.0
        cv[:, CV_HALO + 1] = 0.0 if q == 3 else 1.0
        for s in range(4):
            cv[:, CV_FM + s] = 1.0 if s < q else 0.0
            cv[:, CV_FM + 4 + s] = 1.0 if s > q else 0.0
        cT = np.stack([c[b], c_ctx], 1).reshape(16, 128, 2).transpose(1, 0, 2)
        m = dict(shared)
        m.update(xs=xs, ctxb=np.ascontiguousarray(ctx[b]), cT=np.ascontiguousarray(cT), colv=cv)
        maps.append(m)
    return maps


def kernel(**inputs):
    maps = _prep_inputs(inputs)
    if "nc" not in _CACHE:
        _CACHE["nc"] = build_program()[0]
    res = run_bass_kernel_spmd(_CACHE["nc"], maps, core_ids=list(range(8)))
    out = np.zeros((2, 4 * SEG, D), np.float32)
    for core in range(8):
        b, q = core // 4, core % 4
        out[b, q * SEG:(q + 1) * SEG] = res.results[core]["out"]
    return out
```

```python
import numpy as np
from contextlib import ExitStack
import concourse.bass as bass
import concourse.mybir as mybir
from concourse.bass_utils import run_bass_kernel_spmd

F32 = mybir.dt.float32
BF16 = mybir.dt.bfloat16
AF = mybir.ActivationFunctionType
ALU = mybir.AluOpType
AX = mybir.AxisListType

D = 2048
PIN = 11552
SEG = 4096
NCH = 64
KDEC = 0.6065306597126334
C_Z_A, C_LW, C_LA, C_QB, C_KB, C_VB, C_ZB, C_LG, C_GA, C_GB = 3072, 4096, 4224, 4352, 4864, 5376, 6400, 7424, 7456, 9504
NCOLV = 290
CV_W0, CV_A0, CV_KK, CV_KA, CV_RK, CV_GB, CV_HALO, CV_FM = 216, 232, 248, 256, 264, 272, 280, 282
NCOMP = 16 * 128 + 8 * 257
DBG = {}


class Buf:
    __slots__ = ("name", "w", "r", "sem", "cnt", "dram")

    def __init__(self, name, dram=False):
        self.name = name
        self.w = {}
        self.r = {}
        self.sem = None
        self.cnt = 0
        self.dram = dram


def _mx(need, d):
    for s, v in d.items():
        if v > need.get(s, 0):
            need[s] = v


class Eng:
    def __init__(self, fw, name, sem):
        self.fw = fw
        self.name = name
        self.sem = sem
        self.count = 0
        self.seen = {}
        self.prog = []

    def replay(self, e):
        for it in self.prog:
            if it[0] == "w":
                e.wait_ge(it[1], it[2])
            else:
                it[1](e).then_inc(it[2], it[3])
        self.prog = []

    def _wait(self, need):
        for sem, val in need.items():
            if self.seen.get(sem, 0) >= val:
                continue
            self.prog.append(("w", sem, val))
            self.seen[sem] = val

    def op(self, fn, reads=(), writes=()):
        need = {}
        for b in reads:
            _mx(need, b.w)
        for b in writes:
            _mx(need, b.w)
            _mx(need, b.r)
        if self.name == "pe":
            need.pop(self.sem, None)
        self._wait(need)
        self.count += 1
        self.prog.append(("o", fn, self.sem, 1))
        for b in reads:
            b.r[self.sem] = self.count
        for b in writes:
            b.w = {self.sem: self.count}
            b.r = {}

    def dma(self, out, in_, reads=(), writes=(), sem_buf=None):
        need = {}
        for b in reads:
            _mx(need, b.w)
        for b in writes:
            if b.dram:
                continue
            _mx(need, b.w)
            _mx(need, b.r)
        self._wait(need)
        sb = sem_buf
        if sb is None:
            for b in list(writes) + list(reads):
                if not b.dram:
                    sb = b
                    break
        if sb.sem is None:
            sb.sem, sb.cnt = self.fw.get_dsem(sb.name)
        self.prog.append(("o", (lambda e, o=out, i=in_: e.dma_start(out=o, in_=i)), sb.sem, 16))
        sb.cnt += 16
        for b in reads:
            b.r[sb.sem] = sb.cnt
        for b in writes:
            if b.dram:
                b.w[sb.sem] = sb.cnt
            else:
                b.w = {sb.sem: sb.cnt}
                b.r = {}


class FW:
    def __init__(self, nc, stack):
        self.nc = nc
        self.stack = stack
        self.dsems = []
        self.pe = Eng(self, "pe", self._sem("pe"))
        self.act = Eng(self, "act", self._sem("act"))
        self.dve = Eng(self, "dve", self._sem("dve"))
        self.pool = Eng(self, "pool", self._sem("pool"))
        self.sp = Eng(self, "sp", self._sem("sp"))
        self.engs = [self.pe, self.act, self.dve, self.pool, self.sp]
        self.dbufs = []
        self.sem_pool = []
        self.rr = 0
        self.erot = 0

    def _sem(self, name):
        return self.stack.enter_context(self.nc.semaphore(name))

    def new_sem(self, name):
        return self._sem(name)

    def get_dsem(self, name):
        if self.sem_pool:
            return self.sem_pool.pop()
        return self._sem("d_" + name), 0

    def sbuf(self, st, name, shape, dt=F32):
        self.nid = getattr(self, "nid", 0) + 1
        t = st.enter_context(self.nc.sbuf_tensor(f"sb{self.nid}_{name}", list(shape), dt))
        b = Buf(name)
        self.dbufs.append(b)
        return t, b

    def barrier(self):
        need = {}
        for e in self.engs:
            need[e.sem] = e.count
        for b in self.dbufs:
            if b.sem is not None:
                need[b.sem] = b.cnt
        for e in self.engs:
            n2 = dict(need)
            n2.pop(e.sem, None) if e.name == "pe" else None
            e._wait(n2)
        for b in self.dbufs:
            if b.sem is not None:
                self.sem_pool.append((b.sem, b.cnt))
                b.sem = None

    def emit(self):
        with self.nc.Block() as block:
            @block.tensor
            def _(e):
                self.pe.replay(e)

            @block.scalar
            def _(e):
                self.act.replay(e)

            @block.vector
            def _(e):
                self.dve.replay(e)

            @block.gpsimd
            def _(e):
                self.pool.replay(e)

            @block.sync
            def _(e):
                self.sp.replay(e)


def build_program(phases=9, dbg=()):
    nc = bass.Bass("TRN2", target_bir_lowering=False)

    def din(name, shape, dt=F32):
        return nc.dram_tensor(name, list(shape), dt, kind="ExternalInput")

    def dint(name, shape, dt=F32):
        return nc.dram_tensor(name, list(shape), dt, kind=("ExternalOutput" if name in dbg else "Internal"))

    xs = din("xs", [4224, D]).ap()
    ctxb = din("ctxb", [256, D]).ap()
    cT_d = din("cT", [128, 16, 2]).ap()
    w_mod = din("w_mod", [D, 3 * D]).ap()
    b_mod = din("b_mod", [3 * D])
    norm_g = din("norm_g", [D])
    w_in = din("w_in", [D, PIN]).ap()
    colv_d = din("colv", [128, NCOLV]).ap()
    aw2_d = din("aw2p", [128, 1024]).ap()
    aa2_d = din("aa2p", [128, 1024]).ap()
    gkw_d = din("gkw2p", [32, 2, 512]).ap()
    lnxg = din("lnxg", [1024])
    lnxb = din("lnxb", [1024])
    bng = din("bng", [256])
    w_a = din("w_a", [1024, D]).ap()
    w_b = din("w_b", [1024, D]).ap()
    w_out = din("w_out", [D, D]).ap()
    final_g = din("final_g", [D])
    identf_d = din("identf", [128, 128]).ap()
    identst_d = din("identst", [128, 64]).ap()
    maskG_d = din("maskG", [128, 2, 128]).ap()
    maskL_d = din("maskL", [128, 2, 64]).ap()
    maskA_d = din("maskA", [64, 2, 64]).ap()
    bones_d = din("bones", [128, 128]).ap()
    ind2_d = din("ind2", [128, 2]).ap()
    rmask_d = din("rmask", [128, 512]).ap()
    out_d = nc.dram_tensor("out", [SEG, D], F32, kind="ExternalOutput").ap()

    hTd = dint("hTd", [128, 16, 4480], BF16).ap()
    spA = dint("spA", [2, NCH, 128, 8 * 192]).ap()
    y0d = dint("y0d", [NCH, 128, 8, 64]).ap()
    bvd = dint("bvd", [NCH, 128, 8, 64]).ap()
    spG = dint("spG", [2, NCH, 128, 4 * 321]).ap()
    o0d = dint("o0d", [SEG, 1024]).ap()
    zgd = dint("zgd", [SEG, 4096], BF16).ap()
    zaTd = dint("zaTd", [16, 128, SEG], BF16).ap()
    mTd = dint("mTd", [128, 16, SEG], BF16).ap()
    gate_d = dint("gate_d", [128, D]).ap()
    ypd = dint("ypd", [NCH, 128, 8, 64]).ap()
    opd = dint("opd", [SEG, 1024]).ap()
    CSPL = [(0, 2048), (2048, 3080), (3080, 4104)]
    comp_in = [dint(f"comp_in{i}", [128, b - a]) for i, (a, b) in enumerate(CSPL)]
    comp_out = [dint(f"comp_out{i}", [512, b - a]) for i, (a, b) in enumerate(CSPL)]
    bD = Buf("dram", dram=True)

    def bc(t, n, inner=None):
        if inner is None:
            return bass.AP(t, 0, [[0, 128], [1, n]])
        return bass.AP(t, 0, [[0, 128], [0, inner], [1, n]])

    with ExitStack() as top:
        fw = FW(nc, top)
        pe, act, dve, pool, sp = fw.pe, fw.act, fw.dve, fw.pool, fw.sp
        PS = []
        for i in range(8):
            t = top.enter_context(nc.psum_tensor(f"ps{i}", [128, 512], F32))
            PS.append((t, Buf(f"ps{i}")))

        def bank():
            fw.rr = (fw.rr + 1) % 8
            return PS[fw.rr]

        def mm(out, lhsT, rhs, rd, bps, start=True, stop=True):
            pe.op(lambda e: e.matmul(out, lhsT=lhsT, rhs=rhs, start=start, stop=stop), reads=rd, writes=[bps])

        def acopy(out, in_, rd, wr, func=AF.Copy, **kw):
            act.op(lambda e: e.activation(out=out, in_=in_, func=func, **kw), reads=rd, writes=wr)

        def vcopy(out, in_, rd, wr):
            dve.op(lambda e: e.tensor_copy(out=out, in_=in_), reads=rd, writes=wr)

        def ecopy(out, in_, rd, wr):
            fw.erot += 1
            if fw.erot % 2:
                acopy(out, in_, rd, wr)
            else:
                vcopy(out, in_, rd, wr)

        def tt(eng, out, in0, in1, op, rd, wr):
            eng.op(lambda e: e.tensor_tensor(out=out, in0=in0, in1=in1, op=op), reads=rd, writes=wr)

        def ts(eng, out, in0, s1, s2, op0, op1, rd, wr):
            if s2 is None:
                eng.op(lambda e: e.tensor_scalar(out=out, in0=in0, scalar1=s1, scalar2=None, op0=op0), reads=rd, writes=wr)
            else:
                eng.op(lambda e: e.tensor_scalar(out=out, in0=in0, scalar1=s1, scalar2=s2, op0=op0, op1=op1), reads=rd, writes=wr)

        def treduce(out, in_, rd, wr):
            dve.op(lambda e: e.tensor_reduce(out=out, in_=in_, axis=AX.X, op=ALU.add), reads=rd, writes=wr)

        def recip(out, in_, rd, wr):
            dve.op(lambda e: e.reciprocal(out=out, in_=in_), reads=rd, writes=wr)

        def stt(out, in0, sc, in1, op0, op1, rd, wr):
            dve.op(lambda e: e.scalar_tensor_tensor(out=out, in0=in0, scalar=sc, in1=in1, op0=op0, op1=op1), reads=rd, writes=wr)

        identf, b_identf = fw.sbuf(top, "identf", [128, 128])
        identb, b_identb = fw.sbuf(top, "identb", [128, 128], BF16)
        identst, b_identst = fw.sbuf(top, "identst", [128, 64])
        colv, b_colv = fw.sbuf(top, "colv", [128, NCOLV])
        sp.dma(identf[:], identf_d, writes=[b_identf])
        sp.dma(identst[:], identst_d, writes=[b_identst])
        sp.dma(colv[:], colv_d, writes=[b_colv])
        vcopy(identb[:], identf[:], [b_identf], [b_identb])
        w_in_v = w_in.rearrange("(k p) n -> p k n", p=128)
        Hctx = [fw.sbuf(top, f"Hctx{p}", [128, 2, 64]) for p in range(8)]
        Hgctx, b_Hgctx = fw.sbuf(top, "Hgctx", [128, 8, 256])

        with ExitStack() as s0:
            cTt, b_cT = fw.sbuf(s0, "cTt", [128, 16, 2])
            sT, b_sT = fw.sbuf(s0, "sT", [128, 16, 2])
            srep, b_srep = fw.sbuf(s0, "srep", [128, 2, 16, 128])
            bmods = [fw.sbuf(s0, f"bmod{j}", [128, 256]) for j in range(2)]
            ng_t, b_ng = fw.sbuf(s0, "ng_t", [128, D])
            mt = [fw.sbuf(s0, f"m{j}", [128, 3 * D]) for j in range(2)]
            modA = [fw.sbuf(s0, f"modA{j}", [128, D]) for j in range(2)]
            wm = [fw.sbuf(s0, f"wm{j}", [128, 16, 256]) for j in range(2)]
            sp.dma(cTt[:], cT_d, writes=[b_cT])
            sp.dma(ng_t[:], bc(norm_g, D), writes=[b_ng])
            acopy(sT[:], cTt[:], [b_cT], [b_sT], func=AF.Silu)
            for j in range(2):
                vcopy(srep[:, j], sT[:, :, j:j + 1].to_broadcast([128, 16, 128]), [b_sT], [b_srep])
            wmv = w_mod.rearrange("(k p) n -> p k n", p=128)
            for nb in range(24):
                wt, bw = wm[nb % 2]
                sp.dma(wt[:], wmv[:, :, nb * 256:(nb + 1) * 256], writes=[bw])
                bmod_t, b_bmod = bmods[nb % 2]
                sp.dma(bmod_t[:], bass.AP(b_mod, nb * 256, [[0, 128], [1, 256]]), writes=[b_bmod])
                for j in range(2):
                    ps, bps = bank()
                    for k in range(16):
                        mm(ps[:, 0:256], srep[:, j, k, :], wt[:, k, :], [b_srep, bw], bps, start=(k == 0), stop=(k == 15))
                    tt(dve, mt[j][0][:, nb * 256:(nb + 1) * 256], ps[:, 0:256], bmod_t[:, :], ALU.add,
                       [bps, b_bmod], [bps, mt[j][1]])
            for j in range(2):
                stt(modA[j][0][:], mt[j][0][:, D:2 * D], 1.0, ng_t[:], ALU.add, ALU.mult, [mt[j][1], b_ng], [modA[j][1]])
            act.dma(gate_d, mt[0][0][:, 2 * D:3 * D], reads=[mt[0][1]], writes=[bD], sem_buf=mt[0][1])
            xt = [fw.sbuf(s0, f"xt{j}", [128, D]) for j in range(2)]
            hf, b_hf = fw.sbuf(s0, "hf", [128, D])
            hb, b_hb = fw.sbuf(s0, "hb", [128, D], BF16)
            ss, b_ss = fw.sbuf(s0, "ss", [128, 4])
            hTt = [fw.sbuf(s0, f"hTt{j}", [128, 16, 128], BF16) for j in range(2)]
            for ti in range(35):
                j = 0 if ti < 33 else 1
                src = xs[ti * 128:(ti + 1) * 128, :] if ti < 33 else ctxb[(ti - 33) * 128:(ti - 32) * 128, :]
                x_t, bx = xt[ti % 2]
                sp.dma(x_t[:], src, writes=[bx])
                acopy(hf[:], x_t[:], [bx], [b_hf, b_ss], func=AF.Square, accum_out=ss[:, 0:1])
                ts(dve, ss[:, 1:2], ss[:, 0:1], 1.0 / D, 1e-6, ALU.mult, ALU.add, [b_ss], [b_ss])
                acopy(ss[:, 2:3], ss[:, 1:2], [b_ss], [b_ss], func=AF.Sqrt)
                dve.op(lambda e: e.reciprocal(out=ss[:, 3:4], in_=ss[:, 2:3]), reads=[b_ss], writes=[b_ss])
                stt(hf[:], x_t[:], ss[:, 3:4], modA[j][0][:], ALU.mult, ALU.mult, [bx, b_ss, modA[j][1]], [b_hf])
                tt(pool, hb[:], hf[:], mt[j][0][:, 0:D], ALU.add, [b_hf, mt[j][1]], [b_hb])
                ht, bht = hTt[ti % 2]
                for k4 in range(4):
                    ps, bps = bank()
                    for kk in range(4):
                        k = k4 * 4 + kk
                        mm(ps[:, kk * 128:(kk + 1) * 128], hb[:, k * 128:(k + 1) * 128], identb[:], [b_hb, b_identb], bps)
                    ecopy(ht[:, k4 * 4:(k4 + 1) * 4, :], ps[:, :].rearrange("p (a b) -> p a b", b=128), [bps], [bps, bht])
                act.dma(hTd[:, :, ti * 128:(ti + 1) * 128], ht[:], reads=[bht], writes=[bD], sem_buf=bht)
            fw.barrier()
            fw.emit()
        if phases <= 0:
            return nc, fw

        with ExitStack() as s1:
            cur_st = [s1]

            def T(name, shape, dt=F32):
                return fw.sbuf(cur_st[0], name, shape, dt)
            maskG, b_maskG = T("maskG", [128, 2, 128])
            maskL, b_maskL = T("maskL", [128, 2, 64])
            maskA, b_maskA = T("maskA", [64, 2, 64])
            bonesf, b_bonesf = T("bonesf", [128, 128])
            bones, b_bones = T("bones", [128, 128], BF16)
            ind2f, b_ind2f = T("ind2f", [128, 2])
            ind2, b_ind2 = T("ind2", [128, 2], BF16)
            rmask, b_rmask = T("rmask", [128, 256])
            aw2, b_aw2 = T("aw2", [128, 1024])
            aa2, b_aa2 = T("aa2", [128, 1024])
            gkw, b_gkw = T("gkw", [32, 2, 512])
            dcol, b_dcol = T("dcol", [128, 16])
            for t_, b_, d_ in ((maskG, b_maskG, maskG_d), (maskL, b_maskL, maskL_d), (maskA, b_maskA, maskA_d),
                               (bonesf, b_bonesf, bones_d), (ind2f, b_ind2f, ind2_d), (rmask, b_rmask, rmask_d[:, 0:256]),
                               (aw2, b_aw2, aw2_d), (aa2, b_aa2, aa2_d), (gkw, b_gkw, gkw_d)):
                sp.dma(t_[:], d_, writes=[b_])
            vcopy(bones[:], bonesf[:], [b_bonesf], [b_bones])
            vcopy(ind2[:], ind2f[:], [b_ind2f], [b_ind2])
            ts(dve, dcol[:, 0:8], colv[:, CV_KA:CV_KA + 8], -1.0, 1.0, ALU.mult, ALU.add, [b_colv], [b_dcol])
            ts(dve, dcol[:, 8:16], colv[:, CV_GB:CV_GB + 8], -1.0, None, ALU.mult, None, [b_colv], [b_dcol])

            hT, b_hT = T("hT", [128, 16, 1152], BF16)
            lw, b_lw = T("lw", [128, 1024])
            la, b_la = T("la", [128, 1024])
            lg, b_lg = T("lg", [32, 1024])
            wf = [T(f"wf{j}", [128, 16, 128]) for j in range(2)]
            wb = [T(f"wb{j}", [128, 16, 128], BF16) for j in range(4)]
            wcnt = [0]

            def load_w(col0, ncols, slot):
                wt, bw = wf[wcnt[0] % 2]
                wcnt[0] += 1
                sp.dma(wt[:, :, 0:ncols], w_in_v[:, :, col0:col0 + ncols], writes=[bw])
                dst, bd = wb[slot]
                acopy(dst[:, 0:8, 0:ncols], wt[:, 0:8, 0:ncols], [bw], [bd])
                pool.op(lambda e: e.tensor_copy(out=dst[:, 8:16, 0:ncols], in_=wt[:, 8:16, 0:ncols]), reads=[bw], writes=[bd])
                return dst, bd

            def proj_fm(wt, bw, ncols, col0, ntok, dst, bdst, func=AF.Copy):
                for tb in range((ntok + 511) // 512):
                    n = min(512, ntok - tb * 512)
                    ps, bps = bank()
                    for k in range(16):
                        mm(ps[0:ncols, 0:n], wt[:, k, 0:ncols], hT[:, k, col0 + tb * 512:col0 + tb * 512 + n], [bw, b_hT], bps,
                           start=(k == 0), stop=(k == 15))
                    if func == AF.Copy:
                        ecopy(dst[0:ncols, tb * 512:tb * 512 + n], ps[0:ncols, 0:n], [bps], [bps, bdst])
                    else:
                        acopy(dst[0:ncols, tb * 512:tb * 512 + n], ps[0:ncols, 0:n], [bps], [bps, bdst], func=func)

            cin = [T(f"cin{j}", [128, 384]) for j in range(3)]
            cout = [T(f"cout{j}", [128, 256]) for j in range(3)]
            A = {}
            for nm in ("kk", "t1", "sg", "icl", "cs", "g", "gx", "eng", "ec", "bb", "kd", "rk"):
                A[nm] = T("a_" + nm, [128, 256])
            A["kka"] = A["kk"]
            A["t2"] = A["t1"]
            A["egx"] = A["gx"]
            sqb, b_sqb = T("sqb", [128, 256], BF16)
            vbS = [T(f"vb{i}", [128, 256], BF16) for i in range(2)]
            rkbS = [T(f"rkb{i}", [128, 256], BF16) for i in range(2)]
            egdS = [[T(f"eg{i}_{d}", [128, 256]) for d in range(2)] for i in range(2)]
            rtdS = [[T(f"rt{i}_{d}", [128, 256]) for d in range(2)] for i in range(2)]
            ardS = [[T(f"ar{i}_{d}", [128, 2, 256], BF16) for d in range(2)] for i in range(2)]
            btdS = [[T(f"bt{i}_{d}", [128, 256], BF16) for d in range(2)] for i in range(2)]
            ktdS = [[T(f"kt{i}_{d}", [128, 256], BF16) for d in range(2)] for i in range(2)]
            bhdS = [[T(f"bh{i}_{d}", [128, 256], BF16) for d in range(2)] for i in range(2)]
            khdS = [[T(f"kh{i}_{d}", [128, 256], BF16) for d in range(2)] for i in range(2)]
            vb, b_vb = vbS[0]
            rkb, b_rkb = rkbS[0]
            egd, rtd, ard, btd, ktd, bhd, khd = egdS[0], rtdS[0], ardS[0], btdS[0], ktdS[0], bhdS[0], khdS[0]
            QS = 128.0 ** -0.5

            def reset_gla():
                pool.op(lambda e: e.memset(GN[:], 0.0), writes=[b_GN] + b_GNs)
                pool.op(lambda e: e.memset(GD[:], 1.0), writes=[b_GD])

            def gla_batch(job, S):
                g, col0, tok0, Tn, cg0, spill = job["args"]
                egd, btd, ktd, khd = egdS[S], btdS[S], ktdS[S], khdS[S]
                stG, b_stG = stGS[S]
                if job.get("seg") is not None:
                    hcol, ntok_own, halo_ = job["seg"]
                    own0 = load_seg(hcol, ntok_own, halo_)
                    wt, bw = load_w(C_LG, 32, 3)
                    proj_fm(wt, bw, 32, own0, ntok_own, lg, b_lg)
                    yield
                if job.get("newhead"):
                    load_w(C_QB + g * 128, 128, 0)
                    load_w(C_KB + g * 128, 128, 1)
                    load_w(C_VB + g * 256, 128, 2)
                    load_w(C_VB + g * 256 + 128, 128, 3)
                nch = Tn // 64
                sl = slice(0, Tn)
                qf, bqf = A["kk"]
                kf, bkf = A["t1"]
                vbg, bvbg = ardS[S][0]
                proj_fm(wb[0][0], wb[0][1], 128, col0, Tn, qf, bqf)
                yield
                proj_fm(wb[1][0], wb[1][1], 128, col0, Tn, kf, bkf)
                yield
                for j in range(2):
                    proj_fm(wb[2 + j][0], wb[2 + j][1], 128, col0, Tn, vbg[:, j, :], bvbg)
                    yield
                c3 = lambda ap: ap.rearrange("p (c t) -> p c t", t=64)
                for d in range(2):
                    e1, be1 = A["cs"]
                    spl, bspl = A["g"]
                    csg, bcsg = A["gx"]
                    Gs, bGs = A["eng"]
                    ek, bek = A["bb"]
                    tk, btk = A["kd"]
                    eq, beq = egd[d]
                    ps, bps = bank()
                    mm(ps[:, 0:Tn], gkw[0:32, d, g * 128:(g + 1) * 128], lg[0:32, tok0:tok0 + Tn], [b_gkw, b_lg], bps)
                    acopy(e1[:, sl], ps[:, 0:Tn], [bps, b_dcol], [bps, be1], func=AF.Exp, scale=-1.0,
                          bias=dcol[:, 8 + d * 4 + g:8 + d * 4 + g + 1])
                    acopy(spl[:, sl], e1[:, sl], [be1], [bspl], func=AF.Ln, bias=1.0)
                    yield
                    dve.op(lambda e: e.tensor_tensor_scan(out=csg[:, sl], data0=rmask[:, sl], data1=spl[:, sl], initial=0.0,
                                                          op0=ALU.mult, op1=ALU.add), reads=[b_rmask, bspl], writes=[bcsg])
                    if d == 0:
                        Gsrc, bGsrc = csg, bcsg
                    else:
                        tt(dve, c3(Gs[:, sl]), c3(csg[:, sl])[:, :, 63:64].to_broadcast([128, nch, 64]), c3(csg[:, sl]), ALU.subtract,
                           [bcsg], [bGs])
                        tt(pool, Gs[:, sl], Gs[:, sl], spl[:, sl], ALU.add, [bGs, bspl], [bGs])
                        Gsrc, bGsrc = Gs, bGs
                    acopy(eq[:, sl], Gsrc[:, sl], [bGsrc], [beq], func=AF.Exp, scale=-1.0 / 16)
                    acopy(ek[:, sl], Gsrc[:, sl], [bGsrc], [bek], func=AF.Exp, scale=1.0 / 16)
                    yield
                    eq3 = c3(eq[:, sl])
                    tcol = 63 if d == 0 else 0
                    stt(stG[:, 0:nch, d, 0:64], c3(qf[:, sl]), QS, eq3, ALU.mult, ALU.mult, [bqf, beq], [b_stG])
                    stt(btd[d][0][:, sl], qf[:, sl], QS, eq[:, sl], ALU.mult, ALU.mult, [bqf, beq], [btd[d][1]])
                    tt(pool, ktd[d][0][:, sl], kf[:, sl], ek[:, sl], ALU.mult, [bkf, bek], [ktd[d][1]])
                    tt(dve, c3(tk[:, sl]), c3(ek[:, sl]), eq3[:, :, tcol:tcol + 1].to_broadcast([128, nch, 64]), ALU.mult, [bek, beq], [btk])
                    tt(pool, khd[d][0][:, sl], kf[:, sl], tk[:, sl], ALU.mult, [bkf, btk], [khd[d][1]])
                    vcopy(stG[:, 0:nch, d, 320:321], eq3[:, :, tcol:tcol + 1], [beq], [b_stG])
                    yield
                yield "PREP_DONE"
                if job.get("reset"):
                    reset_gla()
                adv = DBG.get('adv', 3)

                def gcommon(ci):
                    c = slice(ci * 64, (ci + 1) * 64)
                    vtg, bvtg = VTg[ci]
                    psV, bpsV = bank()
                    for j in range(2):
                        mm(psV[0:64, j * 128:(j + 1) * 128], vbg[:, j, c], identb[:], [bvbg, b_identb], bpsV)
                    acopy(vtg[:, :], psV[0:64, 0:256], [bpsV], [bpsV, bvtg])

                def gchain(ci, d):
                    c = slice(ci * 64, (ci + 1) * 64)
                    vtg, bvtg = VTg[ci]
                    att, batt = AttT[ci * 2 + d]
                    kht, bkht = KhT[ci * 2 + d]
                    ps, bps = bank()
                    mm(ps[0:64, 0:64], ktd[d][0][:, c], btd[d][0][:, c], [ktd[d][1], btd[d][1]], bps)
                    mm(ps[0:64, 64:192], khd[d][0][:, c], identb[:], [khd[d][1], b_identb], bps)
                    tt(dve, att[:, :], ps[0:64, 0:64], maskA[:, d, :], ALU.mult, [bps, b_maskA], [bps, batt])
                    acopy(kht[:, :], ps[0:64, 64:192], [bps], [bps, bkht])
                    yield
                    psN, bpsN = bank()
                    mm(psN[:, 0:256], kht[:, :], vtg[:, :], [bkht, bvtg], bpsN)
                    acopy(stG[:, ci, d, 64:320], psN[:, 0:256], [bpsN], [bpsN, b_stG])
                    dc = stG[:, ci, d, 320:321]
                    gi = g * 2 + d
                    if d == 0:
                        stt(GN[:, gi, :], GN[:, gi, :], dc, psN[:, 0:256], ALU.mult, ALU.add, [b_GNs[gi], b_stG, bpsN], [bpsN, b_GNs[gi]])
                    else:
                        stt(GN[:, gi, :], psN[:, 0:256], GD[:, gi:gi + 1], GN[:, gi, :], ALU.mult, ALU.add, [b_GNs[gi], b_GD, bpsN],
                            [bpsN, b_GNs[gi]])
                    ts(dve, GD[:, gi:gi + 1], GD[:, gi:gi + 1], dc, None, ALU.mult, None, [b_GD, b_stG], [b_GD])
                    yield

                for ci in range(nch):
                    gcommon(ci)
                gens = [gchain(ci, d) for ci in range(nch) for d in range(2)]
                while gens:
                    alive = []
                    for g_ in gens:
                        try:
                            next(g_)
                            alive.append(g_)
                        except StopIteration:
                            pass
                    gens = alive
                    adv_extra(adv)
                if spill:
                    for ci in range(nch):
                        par = (cg0 + ci) % 2
                        pp = slice(par * 64, (par + 1) * 64)
                        vtg, bvtg = VTg[ci]
                        psO, bpsO = bank()
                        mm(psO[pp, 0:256], AttT[ci * 2][0][:, :], vtg[:, :], [AttT[ci * 2][1], bvtg], bpsO, start=True, stop=False)
                        mm(psO[pp, 0:256], AttT[ci * 2 + 1][0][:, :], vtg[:, :], [AttT[ci * 2 + 1][1], bvtg], bpsO, start=False, stop=True)
                        acopy(o0st[pp, ci // 2, :], psO[pp, 0:256], [bpsO], [bpsO, b_o0st])
                if spill:
                    for d in range(2):
                        act.dma(spG[d, cg0:cg0 + nch, :, g * 321:(g + 1) * 321].rearrange("c p f -> p c f"), stG[:, 0:nch, d, :],
                               reads=[b_stG], writes=[bD], sem_buf=b_stG)
                    tk0 = cg0 * 64
                    act.dma(o0d[tk0:tk0 + Tn, g * 256:(g + 1) * 256].rearrange("(a p) f -> p a f", p=128), o0st[:, 0:nch // 2, :],
                           reads=[b_o0st], writes=[bD], sem_buf=b_o0st)
                adv_extra(adv)
                if job.get("ctx_last"):
                    vcopy(Hgctx[:], GN[:], [b_GN] + b_GNs, [b_Hgctx])

            def reset_acc():
                for p in range(8):
                    for d, ACC in ((0, ACCf), (1, ACCb)):
                        accsel[p][d] = 0
                        t_, b_ = ACC[p][0]
                        pool.op(lambda e, t_=t_: e.memset(t_[:, 64:128], 0.0), writes=[b_])
                        pool.op(lambda e, t_=t_: e.tensor_copy(out=t_[:, 0:64], in_=identst[:]), reads=[b_identst], writes=[b_])

            PIPE = {"extra": None, "extra_done": True}

            def run_prep(g_):
                while next(g_) != "PREP_DONE":
                    pass

            def adv_extra(n):
                g_ = PIPE["extra"]
                if g_ is None or PIPE["extra_done"]:
                    return
                for _ in range(n):
                    if next(g_) == "PREP_DONE":
                        PIPE["extra_done"] = True
                        return

            def rwkv_batch(job, S):
                p, col0, tok0, Tn, rows, Wd, halo, cg0, spill, halo_mask = job["args"]
                vb, b_vb = vbS[S]
                rkb, b_rkb = rkbS[S]
                egd, rtd, ard, btd, ktd, bhd, khd = egdS[S], rtdS[S], ardS[S], btdS[S], ktdS[S], bhdS[S], khdS[S]
                if job.get("seg") is not None:
                    hcol, ntok_own, halo_ = job["seg"]
                    own0 = load_seg(hcol, ntok_own, halo_)
                    wt, bw = load_w(C_LW, 128, 3)
                    proj_fm(wt, bw, 128, own0, ntok_own, lw, b_lw, func=AF.Tanh)
                    yield
                    wt, bw = load_w(C_LA, 128, 3)
                    proj_fm(wt, bw, 128, own0, ntok_own, la, b_la)
                    yield
                if job.get("newpair"):
                    for j in range(3):
                        load_w(j * 1024 + p * 128, 128, j)
                nch = Tn // 64
                Tin = Tn + (2 * Wd if halo else 0)
                for j in range(3):
                    proj_fm(wb[j][0], wb[j][1], 128, col0, Tin, cin[j][0], cin[j][1])
                    yield
                    if halo_mask is not None:
                        hm_lo, hm_hi = halo_mask
                        if hm_lo:
                            ts(pool, cin[j][0][:, 0:Wd], cin[j][0][:, 0:Wd], colv[:, CV_HALO:CV_HALO + 1], None, ALU.mult, None,
                               [cin[j][1], b_colv], [cin[j][1]])
                        if hm_hi:
                            ts(pool, cin[j][0][:, Tin - Wd:Tin], cin[j][0][:, Tin - Wd:Tin], colv[:, CV_HALO + 1:CV_HALO + 2], None,
                               ALU.mult, None, [cin[j][1], b_colv], [cin[j][1]])
                    ti = j * 8 + p
                    i3 = cin[j][0][:, 0:Tin].rearrange("p (r w) -> p r w", w=Wd)
                    o3 = cout[j][0][:, 0:Tn].rearrange("p (r w) -> p r w", w=Wd)
                    r0 = 1 if halo else 0
                    ts(dve, o3, i3[:, r0:r0 + rows, :], colv[:, ti * 9 + 4:ti * 9 + 5], None, ALU.mult, None,
                       [cin[j][1], b_colv], [cout[j][1]])
                    for dy in ((-1, 0, 1) if halo else (0,)):
                        for dx in (-1, 0, 1):
                            if dy == 0 and dx == 0:
                                continue
                            tap = (dy + 1) * 3 + (dx + 1)
                            xo = slice(1, Wd) if dx == -1 else (slice(0, Wd - 1) if dx == 1 else slice(0, Wd))
                            xi = slice(0, Wd - 1) if dx == -1 else (slice(1, Wd) if dx == 1 else slice(0, Wd))
                            stt(o3[:, :, xo], i3[:, r0 + dy:r0 + dy + rows, xi], colv[:, ti * 9 + tap:ti * 9 + tap + 1], o3[:, :, xo],
                                ALU.mult, ALU.add, [cin[j][1], b_colv, cout[j][1]], [cout[j][1]])
                        yield
                r_, br = cout[0]
                k_, bk = cout[1]
                v_, bv_ = cout[2]
                sl = slice(0, Tn)
                acopy(vb[:, sl], v_[:, sl], [bv_], [b_vb])
                kka, bkka = A["kka"]
                kk, bkk = A["kk"]
                t1, bt1 = A["t1"]
                t2, bt2 = A["t2"]
                acopy(kka[:, sl], k_[:, sl], [bk, b_colv], [bkka], scale=colv[:, CV_KK + p:CV_KK + p + 1])
                acopy(sqb[:, sl], kka[:, sl], [bkka], [b_sqb], func=AF.Square)
                ps, bps = bank()
                mm(ps[:, 0:Tn], bones[:], sqb[:, sl], [b_bones, b_sqb], bps)
                acopy(t1[:, sl], ps[:, 0:Tn], [bps], [bps, bt1], func=AF.Sqrt, bias=1e-12)
                dve.op(lambda e: e.reciprocal(out=t2[:, sl], in_=t1[:, sl]), reads=[bt1], writes=[bt2])
                tt(pool, kk[:, sl], kka[:, sl], t2[:, sl], ALU.mult, [bkka, bt2], [bkk])
                yield
                rk, brk = A["rk"]
                for d in range(2):
                    sg, bsg = A["sg"]
                    icl, bicl = A["icl"]
                    cs, bcs = A["cs"]
                    g, bg = A["g"]
                    gx, bgx = A["gx"]
                    eng, beng = A["eng"]
                    egx, begx = A["egx"]
                    ec, bec = A["ec"]
                    bb, bbb = A["bb"]
                    kd, bkd = A["kd"]
                    eg, beg = egd[d]
                    ps, bps = bank()
                    mm(ps[:, 0:Tn], aw2[d * 64:(d + 1) * 64, p * 128:(p + 1) * 128], lw[d * 64:(d + 1) * 64, tok0:tok0 + Tn],
                       [b_aw2, b_lw], bps)
                    acopy(sg[:, sl], ps[:, 0:Tn], [bps, b_colv], [bps, bsg], func=AF.Sigmoid,
                          bias=colv[:, CV_W0 + d * 8 + p:CV_W0 + d * 8 + p + 1])
                    ps, bps = bank()
                    mm(ps[:, 0:Tn], aa2[d * 64:(d + 1) * 64, p * 128:(p + 1) * 128], la[d * 64:(d + 1) * 64, tok0:tok0 + Tn],
                       [b_aa2, b_la], bps)
                    acopy(icl[:, sl], ps[:, 0:Tn], [bps, b_colv], [bps, bicl], func=AF.Sigmoid,
                          bias=colv[:, CV_A0 + d * 8 + p:CV_A0 + d * 8 + p + 1])
                    yield
                    dve.op(lambda e: e.tensor_tensor_scan(out=cs[:, sl], data0=rmask[:, sl], data1=sg[:, sl], initial=0.0,
                                                          op0=ALU.mult, op1=ALU.add), reads=[b_rmask, bsg], writes=[bcs])
                    cs3 = cs[:, sl].rearrange("p (c t) -> p c t", t=64)
                    if d == 0:
                        gsrc, bgs = cs, bcs
                        tt(pool, gx[:, sl], cs[:, sl], sg[:, sl], ALU.subtract, [bcs, bsg], [bgx])
                    else:
                        tt(dve, gx[:, sl].rearrange("p (c t) -> p c t", t=64), cs3[:, :, 63:64].to_broadcast([128, nch, 64]), cs3,
                           ALU.subtract, [bcs], [bgx])
                        tt(pool, g[:, sl], gx[:, sl], sg[:, sl], ALU.add, [bgx, bsg], [bg])
                        gsrc, bgs = g, bg
                    acopy(eg[:, sl], gsrc[:, sl], [bgs], [beg], func=AF.Exp, scale=-KDEC)
                    acopy(eng[:, sl], gsrc[:, sl], [bgs], [beng], func=AF.Exp, scale=KDEC)
                    acopy(egx[:, sl], gx[:, sl], [bgx], [begx], func=AF.Exp, scale=-KDEC)
                    yield
                    eg3 = eg[:, sl].rearrange("p (c t) -> p c t", t=64)
                    tcol = 63 if d == 0 else 0
                    tt(dve, ec[:, sl].rearrange("p (c t) -> p c t", t=64), eng[:, sl].rearrange("p (c t) -> p c t", t=64),
                       eg3[:, :, tcol:tcol + 1].to_broadcast([128, nch, 64]), ALU.mult, [beng, beg], [bec])
                    ar, bar = ard[d]
                    stt(ar[:, 0, sl], kk[:, sl], -1.0, egx[:, sl], ALU.mult, ALU.mult, [bkk, begx], [bar])
                    tt(pool, bb[:, sl], kk[:, sl], icl[:, sl], ALU.mult, [bkk, bicl], [bbb])
                    tt(pool, btd[d][0][:, sl], bb[:, sl], eng[:, sl], ALU.mult, [bbb, beng], [btd[d][1]])
                    tt(dve, bhd[d][0][:, sl], bb[:, sl], ec[:, sl], ALU.mult, [bbb, bec], [bhd[d][1]])
                    yield
                    ts(dve, t1[:, sl], icl[:, sl], colv[:, CV_KA + p:CV_KA + p + 1], dcol[:, p:p + 1], ALU.mult, ALU.add,
                       [bicl, b_colv, b_dcol], [bt1])
                    tt(pool, kd[:, sl], t1[:, sl], k_[:, sl], ALU.mult, [bt1, bk], [bkd])
                    tt(dve, ktd[d][0][:, sl], kd[:, sl], eng[:, sl], ALU.mult, [bkd, beng], [ktd[d][1]])
                    tt(pool, khd[d][0][:, sl], kd[:, sl], ec[:, sl], ALU.mult, [bkd, bec], [khd[d][1]])
                    yield
                    tt(dve, rtd[d][0][:, sl], r_[:, sl], eg[:, sl], ALU.mult, [br, beg], [rtd[d][1]])
                    acopy(ar[:, 1, sl], rtd[d][0][:, sl], [rtd[d][1]], [bar])
                    if d == 0:
                        acopy(rk[:, sl], kd[:, sl], [bkd], [brk])
                    else:
                        tt(pool, rk[:, sl], rk[:, sl], kd[:, sl], ALU.add, [brk, bkd], [brk])
                stt(rkb[:, sl], rk[:, sl], colv[:, CV_RK + p:CV_RK + p + 1], r_[:, sl], ALU.mult, ALU.mult, [brk, b_colv, br], [b_rkb])

                yield "PREP_DONE"
                if job.get("reset"):
                    reset_acc()
                nci = min(nch, DBG.get('maxc', 9))

                def common(ci):
                    c = slice(ci * 64, (ci + 1) * 64)
                    vts, bvts = VTs[ci]
                    bo, bbo = bon[ci]
                    psV, bpsV = bank()
                    for h in range(2):
                        hs = slice(h * 64, (h + 1) * 64)
                        mm(psV[hs, 0:64], vb[hs, c], identb[hs, hs], [b_vb, b_identb], bpsV)
                    for h in range(2):
                        hs = slice(h * 64, (h + 1) * 64)
                        mm(psV[hs, 64:66], rkb[hs, c], ind2[hs, :], [b_rkb, b_ind2], bpsV)
                    acopy(vts[:], psV[:, 0:64], [bpsV], [bpsV, bvts])
                    vcopy(bo[:, :], psV[:, 64:66], [bpsV], [bpsV, bbo])
                    if spill and DBG.get('sp_bv', True):
                        for h in range(2):
                            hs = slice(h * 64, (h + 1) * 64)
                            ts(dve, bvst[hs, ci, :], psV[hs, 0:64], bo[hs, h:h + 1], None, ALU.mult, None, [bpsV, bbo],
                               [bpsV, b_bvst])

                RS = {}

                def chain(ci, d):
                    c = slice(ci * 64, (ci + 1) * 64)
                    sidx = ci * 2 + d
                    vts, bvts = VTs[ci]
                    ar, bar = ard[d]
                    tm3 = TM3all[:, sidx]
                    gxt, bgxt = GX[sidx]
                    l0, bl0 = L0[sidx]
                    bsg_ = b_stgs[sidx]
                    H2 = [slice(0, 64), slice(64, 128)]
                    ps, bps = bank()
                    for j, (X, bX) in enumerate(((ar, bar), bhd[d], khd[d])):
                        for hs in H2:
                            src = X[hs, 0, c] if j == 0 else X[hs, c]
                            mm(ps[hs, j * 64:(j + 1) * 64], src, identb[hs, hs], [bX, b_identb], bps)
                    ps2, bps2 = bank()
                    for hs in H2:
                        mm(ps2[hs, 0:128], btd[d][0][hs, c], ar[hs, :, c], [btd[d][1], bar], bps2)
                        mm(ps2[hs, 128:256], ktd[d][0][hs, c], ar[hs, :, c], [ktd[d][1], bar], bps2)
                        mm(ps2[hs, 256:320], ar[hs, 0, c], btd[d][0][hs, c], [btd[d][1], bar], bps2)
                    acopy(tm3[:, :, 0:64], ps[:, 0:192].rearrange("p (a b) -> p a b", b=64), [bps], [bps, b_TM3])
                    tt(dve, gxt[:], ps2[:, 0:256].rearrange("p (a b) -> p a b", b=128),
                       maskG[:, d:d + 1, :].to_broadcast([128, 2, 128]), ALU.mult, [bps2, b_maskG], [bps2, bgxt])
                    tt(dve, l0[:], ps2[:, 256:320], maskL[:, d, :], ALU.mult, [bps2, b_maskL], [bps2, bl0])
                    tt(dve, TTall[0][0][:, sidx, :], gxt[:, 0, 0:64], identst[:], ALU.add, [bgxt, b_identst], [TTall[0][1]])
                    yield
                    Xc, bXc = gxt[:, 0, 0:64], bgxt
                    Lc, bLc = l0[:], bl0
                    for j in range(6):
                        Tc, bTc = TTall[j % 2][0][:, sidx, :], TTall[j % 2][1]
                        if j < 5:
                            psq, bpsq = RS["bq"][sidx // 4]
                            o0_ = (sidx % 4) * 128
                            for hs in H2:
                                if j < 4:
                                    mm(psq[hs, o0_:o0_ + 64], Lc[hs], Xc[hs], [bLc, bXc], bpsq)
                                mm(psq[hs, o0_ + 64:o0_ + 128], Xc[hs], Lc[hs], [bLc, bXc], bpsq)
                        if j >= 1:
                            Tp, bTp = TTall[(j - 1) % 2][0][:, sidx, :], TTall[(j - 1) % 2][1]
                            pst, bpst = RS["bt"]
                            for hs in H2:
                                mm(pst[hs, sidx * 64:(sidx + 1) * 64], Lc[hs], Tp[hs], [bLc, bTp], bpst)
                        if j == 0:
                            psx, bpsx = RS["bx"]
                            for hs in H2:
                                mm(psx[hs, sidx * 64:(sidx + 1) * 64], gxt[hs, 1, 0:64], vts[hs], [bgxt, bvts], bpsx)
                        yield
                        if j < 5:
                            xl, bxl = XLall[j % 2]
                            Xc, bXc = xl[:, sidx, 0, :], bxl
                            Lc, bLc = xl[:, sidx, 1, :], bxl
                    Tc, bTc = TTall[5 % 2][0][:, sidx, :], TTall[5 % 2][1]
                    psa, bpsa = RS["ba"][sidx // 4]
                    o0_ = (sidx % 4) * 128
                    for hs in H2:
                        mm(psa[hs, o0_:o0_ + 128], Tc[hs], tm3[hs, 0, :], [bTc, b_TM3], bpsa)
                    yield
                    au, bau = AUall[:, sidx, :], b_AU
                    ps, bps = bank()
                    for hs in H2:
                        mm(ps[hs, 0:64], au[hs, 0:64], tm3[hs, 1, 0:64], [bau, b_TM3], bps)
                        mm(ps[hs, 64:128], au[hs, 0:64], gxt[hs, 0, 64:128], [bau, bgxt], bps)
                        if d == 1:
                            mm(ps[hs, 128:192], tm3[hs, 1, 0:64], au[hs, 0:64], [bau, b_TM3], bps)
                    ps2, bps2 = bank()
                    for hs in H2:
                        mm(ps2[hs, 0:64], tm3[hs, 1, 0:64], au[hs, 64:128], [b_TM3, bau], bps2, start=True, stop=False)
                        mm(ps2[hs, 0:64], tm3[hs, 2, 0:64], vts[hs], [b_TM3, bvts], bps2, start=False, stop=True)
                    eg3 = egd[d][0][:, sl].rearrange("p (c t) -> p c t", t=64)
                    tcol = 63 if d == 0 else 0
                    gam = eg3[:, ci, tcol:tcol + 1]
                    stt(stg[:, ci, d, 0:64], identst[:], gam, ps[:, 0:64], ALU.mult, ALU.add, [b_identst, egd[d][1], bps],
                        [bps, bsg_])
                    tt(dve, stg[:, ci, d, 64:128], ps[:, 64:128], rtd[d][0][:, c], ALU.add, [bps, rtd[d][1]], [bps, bsg_])
                    if d == 1:
                        stt(Mn[ci][0][:], identst[:], gam, ps[:, 128:192], ALU.mult, ALU.add, [b_identst, egd[d][1], bps],
                            [bps, Mn[ci][1]])
                    acopy(stg[:, ci, d, 128:192], ps2[:, 0:64], [bps2], [bps2, bsg_])
                    yield

                for ci in range(nci):
                    common(ci)
                gens = [chain(ci, d) for ci in range(nci) for d in range(2)]
                ns_ = len(gens)
                nh_ = (ns_ + 3) // 4
                adv = DBG.get('adv', 3)
                for gi_, g_ in enumerate(gens):
                    next(g_)
                    if gi_ % 3 == 2:
                        adv_extra(1)
                adv_extra(1)
                for j in range(6):
                    if j < 5:
                        RS["bq"] = [bank() for _ in range(nh_)]
                    if j >= 1:
                        RS["bt"] = bank()
                    if j == 0:
                        RS["bx"] = bank()
                    for gi_, g_ in enumerate(gens):
                        next(g_)
                        if gi_ % 3 == 2:
                            adv_extra(1)
                    if j < 5:
                        xl, bxl = XLall[j % 2]
                        for hf in range(nh_):
                            n4 = min(4, ns_ - hf * 4)
                            psq, bpsq = RS["bq"][hf]
                            if j < 4:
                                acopy(xl[:, hf * 4:hf * 4 + n4, :, :].rearrange("p s a b -> p s (a b)"),
                                      psq[:, 0:n4 * 128].rearrange("p (s f) -> p s f", f=128), [bpsq], [bpsq, bxl])
                            else:
                                acopy(xl[:, hf * 4:hf * 4 + n4, 1, :], psq[:, 0:n4 * 128].rearrange("p (s f) -> p s f", f=128)[:, :, 64:128],
                                      [bpsq], [bpsq, bxl])
                    if j >= 1:
                        pst, bpst = RS["bt"]
                        tt(dve, TTall[j % 2][0][:, 0:ns_, :], pst[:, 0:ns_ * 64].rearrange("p (s f) -> p s f", f=64),
                           TTall[(j - 1) % 2][0][:, 0:ns_, :], ALU.add, [bpst, TTall[(j - 1) % 2][1]], [bpst, TTall[j % 2][1]])
                    if j == 0:
                        psx, bpsx = RS["bx"]
                        acopy(TM3all[:, 0:ns_, 0, 64:128], psx[:, 0:ns_ * 64].rearrange("p (s f) -> p s f", f=64), [bpsx], [bpsx, b_TM3])
                    adv_extra(1)
                RS["ba"] = [bank() for _ in range(nh_)]
                for gi_, g_ in enumerate(gens):
                    next(g_)
                    if gi_ % 3 == 2:
                        adv_extra(1)
                for hf in range(nh_):
                    n4 = min(4, ns_ - hf * 4)
                    psa, bpsa = RS["ba"][hf]
                    acopy(AUall[:, hf * 4:hf * 4 + n4, :], psa[:, 0:n4 * 128].rearrange("p (s f) -> p s f", f=128), [bpsa], [bpsa, b_AU])
                adv_extra(1)
                for gi_, g_ in enumerate(gens):
                    next(g_)
                    if gi_ % 3 == 2:
                        adv_extra(1)
                for g_ in gens:
                    for _ in g_:
                        pass
                adv_extra(1)
                for ci in range(nci):
                    for d in range(2):
                        bsg_ = b_stgs[ci * 2 + d]
                        ACC = ACCf if d == 0 else ACCb
                        ao, bao = ACC[p][accsel[p][d]]
                        an, ban = ACC[p][1 - accsel[p][d]]
                        accsel[p][d] = 1 - accsel[p][d]
                        ps, bps = bank()
                        if d == 0:
                            for h in range(2):
                                hs = slice(h * 64, (h + 1) * 64)
                                mm(ps[hs, 0:128], stg[hs, ci, 0, 0:64], ao[hs, :], [bsg_, bao], bps)
                            acopy(an[:, 0:64], ps[:, 0:64], [bps], [bps, ban])
                            tt(dve, an[:, 64:128], ps[:, 64:128], stg[:, ci, 0, 128:192], ALU.add, [bps, bsg_], [bps, ban])
                        else:
                            mn_, bmn_ = Mn[ci]
                            for h in range(2):
                                hs = slice(h * 64, (h + 1) * 64)
                                mm(ps[hs, 0:64], mn_[hs], ao[hs, 0:64], [bmn_, bao], bps)
                                mm(ps[hs, 64:128], ao[hs, 0:64], stg[hs, ci, 1, 128:192], [bsg_, bao], bps)
                            acopy(an[:, 0:64], ps[:, 0:64], [bps], [bps, ban])
                            tt(dve, an[:, 64:128], ps[:, 64:128], ao[:, 64:128], ALU.add, [bps, bao], [bps, ban])
                    if spill and DBG.get('sp_y0', True):
                        vts, bvts = VTs[ci]
                        g0, bg0 = GX[ci * 2]
                        g1, bg1 = GX[ci * 2 + 1]
                        a0, ba0 = AUall[:, ci * 2, :], b_AU
                        a1, ba1 = AUall[:, ci * 2 + 1, :], b_AU
                        ps, bps = bank()
                        for h in range(2):
                            hs = slice(h * 64, (h + 1) * 64)
                            o_ = ps[hs, 0:64]
                            rds = [bg0, bg1, ba0, ba1, bvts]
                            mm(o_, g0[hs, 0, 64:128], a0[hs, 64:128], rds, bps, start=True, stop=False)
                            mm(o_, g0[hs, 1, 64:128], vts[hs], rds, bps, start=False, stop=False)
                            mm(o_, g1[hs, 0, 64:128], a1[hs, 64:128], rds, bps, start=False, stop=False)
                            mm(o_, g1[hs, 1, 64:128], vts[hs], rds, bps, start=False, stop=True)
                        acopy(y0st[:, ci, :], ps[:, 0:64], [bps], [bps, b_y0st])
                if spill and DBG.get('sp_dma', True):
                    for d in range(2):
                        act.dma(spA[d, cg0:cg0 + nch, :, p * 192:(p + 1) * 192].rearrange("c p f -> p c f"), stg[:, 0:nch, d, :],
                               reads=[b_stg] + b_stgs, writes=[bD], sem_buf=b_stg)
                    tk0 = cg0 * 64
                    act.dma(y0d[cg0:cg0 + nch, :, p, :].rearrange("c q v -> q c v"), y0st[:, 0:nch, :],
                           reads=[b_y0st], writes=[bD], sem_buf=b_y0st)
                    act.dma(bvd[cg0:cg0 + nch, :, p, :].rearrange("c q v -> q c v"), bvst[:, 0:nch, :],
                           reads=[b_bvst], writes=[bD], sem_buf=b_bvst)
                if job.get("ctx_last"):
                    for d, ACC in ((0, ACCf), (1, ACCb)):
                        a_, ba_ = ACC[p][accsel[p][d]]
                        vcopy(Hctx[p][0][:, d, :], a_[:, 64:128], [ba_], [Hctx[p][1]])

            segs = [("ctx", 4224, 256, 1, 256, False, 1, 256)] + [(f"q{i}", i * 1024, 1024, 4, 64, True, 4, 256) for i in range(4)]

            def load_seg(hcol, ntok_own, halo):
                ncols_h = ntok_own + (128 if halo else 0)
                sp.dma(hT[:, :, 0:ncols_h], hTd[:, :, hcol:hcol + ncols_h], reads=[bD], writes=[b_hT])
                return 64 if halo else 0

            with ExitStack() as s1a:
                cur_st[0] = s1a
                VTs = [T(f"VTs{i}", [128, 64], BF16) for i in range(4)]
                bon = [T(f"bon{i}", [128, 2]) for i in range(4)]
                TM3all, b_TM3 = T("TM3all", [128, 8, 3, 128], BF16)
                GX = [T(f"GX{i}", [128, 2, 128], BF16) for i in range(8)]
                AUall, b_AU = T("AUall", [128, 8, 128], BF16)
                L0 = [T(f"L0{i}", [128, 64], BF16) for i in range(8)]
                XLall = [T(f"XLall{j}", [128, 8, 2, 64], BF16) for j in range(2)]
                TTall = [T(f"TTall{j}", [128, 8, 64], BF16) for j in range(2)]
                Mn = [T(f"Mn{i}", [128, 64]) for i in range(4)]
                b_stgs = [Buf(f"stg{i}") for i in range(8)]
                stg, b_stg = T("stg", [128, 4, 2, 192])
                y0st, b_y0st = T("y0st", [128, 4, 64])
                bvst, b_bvst = T("bvst", [128, 4, 64])
                ACCf = [[T(f"ACCf{p}_{j}", [128, 128]) for j in range(2)] for p in range(8)]
                ACCb = [[T(f"ACCb{p}_{j}", [128, 128]) for j in range(2)] for p in range(8)]
                accsel = [[0, 0] for _ in range(8)]
                jobs = []
                for sname, hcol, ntok_own, rows, Wd, halo, nb, Tn in segs[:DBG.get('nsegs', 5)]:
                    for p in range(DBG.get('maxp', 8)):
                        for b in range(min(nb, DBG.get('maxb', 9))):
                            if sname == "ctx":
                                job = dict(args=(p, 0, 0, 256, 1, 256, False, 0, False, None), ctx_last=True)
                            else:
                                qi = int(sname[1])
                                hm = (qi == 0 and b == 0, qi == 3 and b == 3)
                                job = dict(args=(p, b * 256, b * 256, 256, 4, 64, True, qi * 16 + b * 4, DBG.get("spill", True), hm))
                            if p == 0 and b == 0:
                                job["seg"] = (hcol, ntok_own, halo)
                                if sname in ("ctx", "q0"):
                                    job["reset"] = True
                            if b == 0:
                                job["newpair"] = True
                            jobs.append(job)
                gens_j = [rwkv_batch(job, i % 2) for i, job in enumerate(jobs)]

                def run_prep(g_):
                    while next(g_) != "PREP_DONE":
                        pass

                run_prep(gens_j[0])
                for i in range(len(jobs)):
                    if i + 1 < len(jobs):
                        PIPE["extra"] = gens_j[i + 1]
                        PIPE["extra_done"] = False
                    else:
                        PIPE["extra"] = None
                        PIPE["extra_done"] = True
                    if not DBG.get("pipe", True) and PIPE["extra"] is not None:
                        pass
                    for _ in gens_j[i]:
                        pass
                    if PIPE["extra"] is not None and not PIPE["extra_done"]:
                        run_prep(PIPE["extra"])
                        PIPE["extra_done"] = True
                ci_ap = comp_in[0].ap()
                for p in range(8):
                    for d, ACC in ((0, ACCf), (1, ACCb)):
                        a_, ba_ = ACC[p][accsel[p][d]]
                        c0 = (p * 2 + d) * 128
                        if d == 0:
                            ps, bps = bank()
                            for h in range(2):
                                hs = slice(h * 64, (h + 1) * 64)
                                mm(ps[hs, 0:64], a_[hs, 0:64], identf[hs, hs], [ba_, b_identf], bps)
                            mn_, bmn_ = Mn[p % 4]
                            vcopy(mn_[:], ps[:, 0:64], [bps], [bps, bmn_])
                            act.dma(ci_ap[:, c0:c0 + 64], mn_[:], reads=[bmn_], writes=[bD], sem_buf=bmn_)
                            act.dma(ci_ap[:, c0 + 64:c0 + 128], a_[:, 64:128], reads=[ba_], writes=[bD], sem_buf=ba_)
                        else:
                            act.dma(ci_ap[:, c0:c0 + 128], a_[:, :], reads=[ba_], writes=[bD], sem_buf=ba_)
                fw.barrier()
                fw.emit()
            with ExitStack() as s1b:
                cur_st[0] = s1b
                stGS = [T(f"stG{i}", [128, 4, 2, 321]) for i in range(2)]
                o0st, b_o0st = T("o0st", [128, 2, 256])
                VTg = [T(f"VTg{i}", [64, 256], BF16) for i in range(4)]
                AttT = [T(f"AttT{i}", [64, 64], BF16) for i in range(8)]
                KhT = [T(f"KhT{i}", [64, 128], BF16) for i in range(8)]
                b_GNs = [Buf(f"GN{i}") for i in range(8)]
                GN, b_GN = T("GN", [128, 8, 256])
                GD, b_GD = T("GD", [128, 8])
                gjobs = []
                for sname, hcol, ntok_own, rows, Wd, halo, nb, Tn in segs[:DBG.get('nsegs', 5)]:
                    for g in range(DBG.get('maxg', 4)):
                        for b in range(nb):
                            if sname == "ctx":
                                job = dict(args=(g, 0, 0, 256, 0, False))
                                if g == 3:
                                    job["ctx_last"] = True
                            else:
                                qi = int(sname[1])
                                job = dict(args=(g, 64 + b * 256, b * 256, 256, qi * 16 + b * 4, True))
                            if g == 0 and b == 0:
                                job["seg"] = (hcol, ntok_own, halo)
                                if sname in ("ctx", "q0"):
                                    job["reset"] = True
                            if b == 0:
                                job["newhead"] = True
                            gjobs.append(job)
                gens_g = [gla_batch(job, i % 2) for i, job in enumerate(gjobs)]
                run_prep(gens_g[0])
                for i in range(len(gjobs)):
                    if i + 1 < len(gjobs):
                        PIPE["extra"] = gens_g[i + 1]
                        PIPE["extra_done"] = False
                    else:
                        PIPE["extra"] = None
                        PIPE["extra_done"] = True
                    for _ in gens_g[i]:
                        pass
                    if PIPE["extra"] is not None and not PIPE["extra_done"]:
                        run_prep(PIPE["extra"])
                        PIPE["extra_done"] = True
                act.dma(comp_in[1].ap()[:, 0:8], GD[:, :], reads=[b_GD], writes=[bD], sem_buf=b_GD)
                act.dma(comp_in[1].ap()[:, 8:1032], GN[:, 0:4, :].rearrange("p g f -> p (g f)"), reads=[b_GN] + b_GNs, writes=[bD], sem_buf=b_GN)
                act.dma(comp_in[2].ap()[:, :], GN[:, 4:8, :].rearrange("p g f -> p (g f)"), reads=[b_GN] + b_GNs, writes=[bD], sem_buf=b_GN)
                fw.barrier()
                fw.emit()
            pass
        with ExitStack() as s1c:
            def T(name, shape, dt=F32):
                return fw.sbuf(s1c, name, shape, dt)
            hT2, b_hT2 = T("hT2", [128, 16, 2048], BF16)
            w5f = [T(f"w5f{j}", [128, 16, 512]) for j in range(2)]
            w5b = [T(f"w5b{j}", [128, 16, 512], BF16) for j in range(2)]
            zst = [T(f"zst{j}", [128, 2048], BF16) for j in range(2)]
            gst = [T(f"gst{j}", [128, 512], BF16) for j in range(3)]
            zblocks = [(C_Z_A, 0), (C_Z_A + 512, 4), (C_ZB, 8), (C_ZB + 512, 12)]
            nw = 0
            ng_ = 0
            for half in range(2):
                sp.dma(hT2[:], hTd[:, :, 64 + half * 2048:64 + (half + 1) * 2048], reads=[bD], writes=[b_hT2])
                for colw, blk0 in zblocks:
                    wt, bw = w5f[nw % 2]
                    wbf, bwb = w5b[nw % 2]
                    nw += 1
                    sp.dma(wt[:], w_in_v[:, :, colw:colw + 512], writes=[bw])
                    for k4 in range(4):
                        eng_ = pool if k4 % 2 == 0 else act
                        if eng_ is pool:
                            pool.op(lambda e, wbf=wbf, wt=wt, k4=k4: e.tensor_copy(out=wbf[:, k4 * 4:(k4 + 1) * 4, :], in_=wt[:, k4 * 4:(k4 + 1) * 4, :]),
                                    reads=[bw], writes=[bwb])
                        else:
                            acopy(wbf[:, k4 * 4:(k4 + 1) * 4, :], wt[:, k4 * 4:(k4 + 1) * 4, :], [bw], [bwb])
                    for sb_ in range(4):
                        zt, bzt = zst[(blk0 + sb_) % 2]
                        for tb in range(4):
                            ps, bps = bank()
                            for k in range(16):
                                mm(ps[:, :], wbf[:, k, sb_ * 128:(sb_ + 1) * 128], hT2[:, k, tb * 512:(tb + 1) * 512], [bwb, b_hT2], bps,
                                   start=(k == 0), stop=(k == 15))
                            acopy(zt[:, tb * 512:(tb + 1) * 512], ps[:, :], [bps], [bps, bzt], func=AF.Silu)
                        act.dma(zaTd[blk0 + sb_, :, half * 2048:(half + 1) * 2048], zt[:, :], reads=[bzt], writes=[bD], sem_buf=bzt)
                for cb in range(8):
                    wt, bw = w5f[nw % 2]
                    wbf, bwb = w5b[nw % 2]
                    nw += 1
                    sp.dma(wt[:], w_in_v[:, :, C_GA + cb * 512:C_GA + (cb + 1) * 512], writes=[bw])
                    for k4 in range(4):
                        if k4 % 2 == 0:
                            pool.op(lambda e, wbf=wbf, wt=wt, k4=k4: e.tensor_copy(out=wbf[:, k4 * 4:(k4 + 1) * 4, :], in_=wt[:, k4 * 4:(k4 + 1) * 4, :]),
                                    reads=[bw], writes=[bwb])
                        else:
                            vcopy(wbf[:, k4 * 4:(k4 + 1) * 4, :], wt[:, k4 * 4:(k4 + 1) * 4, :], [bw], [bwb])
                    for tl in range(16):
                        ps, bps = bank()
                        for k in range(16):
                            mm(ps[:, :], hT2[:, k, tl * 128:(tl + 1) * 128], wbf[:, k, :], [b_hT2, bwb], bps, start=(k == 0), stop=(k == 15))
                        gt, bgt = gst[ng_ % 3]
                        ng_ += 1
                        acopy(gt[:], ps[:, :], [bps], [bps, bgt], func=AF.Sigmoid)
                        r0 = half * 2048 + tl * 128
                        act.dma(zgd[r0:r0 + 128, cb * 512:(cb + 1) * 512], gt[:], reads=[bgt], writes=[bD], sem_buf=bgt)
            fw.barrier()
            fw.emit()
        if phases <= 1:
            return nc, fw

        Hf0, b_Hf0 = fw.sbuf(top, "Hf0", [128, 8, 64])
        Hb0, b_Hb0 = fw.sbuf(top, "Hb0", [128, 8, 64])
        Hgf0, b_Hgf0 = fw.sbuf(top, "Hgf0", [128, 4, 256])
        Hgb0, b_Hgb0 = fw.sbuf(top, "Hgb0", [128, 4, 256])
        identstb, b_identstb = fw.sbuf(top, "identstb", [128, 64], BF16)
        vcopy(identstb[:], identst[:], [b_identst], [b_identstb])

        with ExitStack() as sE:
            gath, b_gath = fw.sbuf(sE, "gath", [128, 4, NCOMP])
            Hx = [fw.sbuf(sE, f"Hx{j}", [128, 8, 64]) for j in range(2)]
            t1x, b_t1x = fw.sbuf(sE, "t1x", [128, 8, 64])
            Gx = [fw.sbuf(sE, f"Gx{j}", [128, 4, 256]) for j in range(2)]
            t2x, b_t2x = fw.sbuf(sE, "t2x", [128, 4, 256])
            if DBG.get("nocc", False):
                for i in range(3):
                    for r in range(4):
                        sp.dma(comp_out[i].ap()[r * 128:(r + 1) * 128, :], comp_in[i].ap(), reads=[bD], writes=[bD], sem_buf=b_gath)
                bCO = bD
            else:
                cc_sem = fw.new_sem("cc")
                pool._wait(dict(bD.w))
                for i in range(3):
                    pool.prog.append(("o", (lambda e, i=i: e.collective_compute(
                        "AllGather", ALU.bypass, replica_groups=[[0, 1, 2, 3], [4, 5, 6, 7]],
                        ins=[comp_in[i].ap()], outs=[comp_out[i].ap()])), cc_sem, 1))
                bCO = Buf("comp_out", dram=True)
                bCO.w = {cc_sem: 3}
            for i, (ca, cb_) in enumerate(CSPL):
                sp.dma(gath[:, :, ca:cb_], comp_out[i].ap().rearrange("(r p) n -> p r n", p=128), reads=[bCO], writes=[b_gath])
            for d in range(2):
                cur, bcur = Hx[0]
                for p in range(8):
                    vcopy(cur[:, p, :], Hctx[p][0][:, d, :], [Hctx[p][1]], [bcur])
                order = (0, 1, 2) if d == 0 else (3, 2, 1)
                for i, sgi in enumerate(order):
                    nxt, bnxt = (Hx[(i + 1) % 2]) if i < 2 else ((Hf0, b_Hf0) if d == 0 else (Hb0, b_Hb0))
                    ps, bps = bank()
                    for p in range(8):
                        c0 = (p * 2 + d) * 128
                        for h in range(2):
                            hs = slice(h * 64, (h + 1) * 64)
                            mm(ps[hs, p * 64:(p + 1) * 64], gath[hs, sgi, c0:c0 + 64], cur[hs, p, :], [b_gath, bcur], bps)
                    nview = gath[:, sgi, 0:2048].rearrange("q (p d f) -> q p d f", d=2, f=128)[:, :, d, 64:128]
                    tt(dve, t1x[:], ps[:, :].rearrange("q (p f) -> q p f", f=64), nview, ALU.add, [bps, b_gath], [bps, b_t1x])
                    tt(dve, t1x[:], t1x[:], cur[:], ALU.subtract, [b_t1x, bcur], [b_t1x])
                    mcol = CV_FM + d * 4 + sgi
                    stt(nxt[:], t1x[:], colv[:, mcol:mcol + 1], cur[:], ALU.mult, ALU.add, [b_t1x, b_colv, bcur], [bnxt])
                    cur, bcur = nxt, bnxt
                cur, bcur = Gx[0]
                for g in range(4):
                    vcopy(cur[:, g, :], Hgctx[:, g * 2 + d, :], [b_Hgctx], [bcur])
                for i, sgi in enumerate(order):
                    nxt, bnxt = (Gx[(i + 1) % 2]) if i < 2 else ((Hgf0, b_Hgf0) if d == 0 else (Hgb0, b_Hgb0))
                    for g in range(4):
                        gi = g * 2 + d
                        stt(t2x[:, g, :], cur[:, g, :], gath[:, sgi, 2048 + gi:2048 + gi + 1],
                            gath[:, sgi, 2056 + gi * 256:2056 + (gi + 1) * 256], ALU.mult, ALU.add, [bcur, b_gath], [b_t2x])
                    tt(dve, t2x[:], t2x[:], cur[:], ALU.subtract, [b_t2x, bcur], [b_t2x])
                    mcol = CV_FM + d * 4 + sgi
                    stt(nxt[:], t2x[:], colv[:, mcol:mcol + 1], cur[:], ALU.mult, ALU.add, [b_t2x, b_colv, bcur], [bnxt])
                    cur, bcur = nxt, bnxt
            fw.barrier()
            fw.emit()
        if phases <= 2:
            return nc, fw

        def passB(stk, d, c, Hc, bHc, Hn, bHn, Gc, bGc, Gn_, bGn, stA_t, b_stA, stG_t, b_stG2):
            sp.dma(stA_t[:], spA[d, c], reads=[bD], writes=[b_stA])
            sp.dma(stG_t[:], spG[d, c], reads=[bD], writes=[b_stG2])
            par = c % 2
            pp = slice(par * 64, (par + 1) * 64)
            psH, bpsH = bank()
            psY, bpsY = bank()
            for p in range(8):
                for h in range(2):
                    hs = slice(h * 64, (h + 1) * 64)
                    mm(psH[hs, p * 64:(p + 1) * 64], stA_t[hs, p * 192:p * 192 + 64], Hc[hs, p, :], [b_stA, bHc], bpsH)
            nview = stA_t[:, :].rearrange("q (p f) -> q p f", f=192)[:, :, 128:192]
            tt(dve, Hn[:], psH[:, :].rearrange("q (p f) -> q p f", f=64), nview, ALU.add, [bpsH, b_stA], [bpsH, bHn])
            for g in range(4):
                stt(Gn_[:, g, :], Gc[:, g, :], stG_t[:, g * 321 + 320:g * 321 + 321], stG_t[:, g * 321 + 64:g * 321 + 320],
                    ALU.mult, ALU.add, [bGc, b_stG2], [bGn])
            for p in range(8):
                for h in range(2):
                    hs = slice(h * 64, (h + 1) * 64)
                    mm(psY[hs, p * 64:(p + 1) * 64], stA_t[hs, p * 192 + 64:p * 192 + 128], Hc[hs, p, :], [b_stA, bHc], bpsY)
            psO = [bank(), bank()]
            for g in range(4):
                po, bpo = psO[g // 2]
                mm(po[pp, (g % 2) * 256:(g % 2 + 1) * 256], stG_t[:, g * 321:g * 321 + 64], Gc[:, g, :], [b_stG2, bGc], bpo)
            return (psY, bpsY), psO, pp

        sW = ExitStack()
        wa_bf, b_wa = fw.sbuf(sW, "wa_bf", [128, 8, D], BF16)
        wb_bf, b_wbb = fw.sbuf(sW, "wb_bf", [128, 8, D], BF16)
        with ExitStack() as s2:
            wstW = [fw.sbuf(s2, f"wstW{j}", [128, D]) for j in range(2)]
            nwl = 0
            for (wsrc, wdst, bdst) in ((w_a, wa_bf, b_wa), (w_b, wb_bf, b_wbb)):
                for p in range(8):
                    wst_, b_wst_ = wstW[nwl % 2]
                    nwl += 1
                    sp.dma(wst_[:], wsrc[p * 128:(p + 1) * 128, :], writes=[b_wst_])
                    pool.op(lambda e, wdst=wdst, p=p, wst_=wst_: e.tensor_copy(out=wdst[:, p, :], in_=wst_[:]), reads=[b_wst_], writes=[bdst])
            def T2(name, shape, dt=F32):
                return fw.sbuf(s2, name, shape, dt)
            stA_b = [T2(f"stAb{j}", [128, 1536]) for j in range(2)]
            stG_b = [T2(f"stGb{j}", [128, 1284]) for j in range(2)]
            Hb = [T2(f"Hbk{j}", [128, 8, 64]) for j in range(2)]
            Gb = [T2(f"Gbk{j}", [128, 4, 256]) for j in range(2)]
            y0_t = [T2(f"y0t{j}", [128, 512]) for j in range(2)]
            yp_o = [T2(f"ypo{j}", [128, 512]) for j in range(2)]
            o0_t = [T2(f"o0t{j}", [128, 1024]) for j in range(2)]
            op_o = [T2(f"opo{j}", [128, 1024]) for j in range(2)]
            vcopy(Hb[0][0][:], Hb0[:], [b_Hb0], [Hb[0][1]])
            vcopy(Gb[0][0][:], Hgb0[:], [b_Hgb0], [Gb[0][1]])
            for i, c in enumerate(range(NCH - 1, -1, -1)):
                j = c // 2
                Hc, bHc = Hb[i % 2]
                Hn, bHn = Hb[(i + 1) % 2]
                Gc, bGc = Gb[i % 2]
                Gn_, bGn = Gb[(i + 1) % 2]
                (psY, bpsY), psO, pp = passB(s2, 1, c, Hc, bHc, Hn, bHn, Gc, bGc, Gn_, bGn, stA_b[i % 2][0], stA_b[i % 2][1],
                                             stG_b[i % 2][0], stG_b[i % 2][1])
                yt, byt = y0_t[i % 2]
                sp.dma(yt[:], y0d[c].rearrange("q p v -> q (p v)"), reads=[bD], writes=[byt])
                yo, byo = yp_o[i % 2]
                tt(dve, yo[:], psY[:, :], yt[:], ALU.add, [bpsY, byt], [bpsY, byo])
                act.dma(ypd[c].rearrange("q p v -> q (p v)"), yo[:], reads=[byo], writes=[bD], sem_buf=byo)
                ot, bot = o0_t[j % 2]
                oo, boo = op_o[j % 2]
                if c % 2 == 1:
                    sp.dma(ot[:], o0d[j * 128:(j + 1) * 128, :], reads=[bD], writes=[bot])
                for hb_ in range(2):
                    po, bpo = psO[hb_]
                    tt(dve, oo[pp, hb_ * 512:(hb_ + 1) * 512], po[pp, :], ot[pp, hb_ * 512:(hb_ + 1) * 512], ALU.add, [bpo, bot],
                       [bpo, boo])
                if c % 2 == 0:
                    act.dma(opd[j * 128:(j + 1) * 128, :], oo[:], reads=[boo], writes=[bD], sem_buf=boo)
            fw.barrier()
            fw.emit()
        if phases <= 3:
            return nc, fw

        with ExitStack() as s3:
            def T3(name, shape, dt=F32):
                return fw.sbuf(s3, name, shape, dt)
            lnxg_st, b_lnxg = T3("lnxg_st", [128, 8, 64])
            lnxb_st, b_lnxb = T3("lnxb_st", [128, 8, 64])
            bng_b, b_bng = T3("bng_b", [128, 4, 256])
            for h in range(2):
                hs = slice(h * 64, (h + 1) * 64)
                sp.dma(lnxg_st[hs, :, :], bass.AP(lnxg, h * 64, [[0, 64], [128, 8], [1, 64]]), writes=[b_lnxg])
                sp.dma(lnxb_st[hs, :, :], bass.AP(lnxb, h * 64, [[0, 64], [128, 8], [1, 64]]), writes=[b_lnxb])
            sp.dma(bng_b[:], bass.AP(bng, 0, [[0, 128], [0, 4], [1, 256]]), writes=[b_bng])
            stA_f = [T3(f"stAf{j}", [128, 1536]) for j in range(2)]
            stG_f = [T3(f"stGf{j}", [128, 1284]) for j in range(2)]
            Hf = [T3(f"Hfk{j}", [128, 8, 64]) for j in range(2)]
            Gf = [T3(f"Gfk{j}", [128, 4, 256]) for j in range(2)]
            yp_t = [T3(f"ypt{j}", [128, 512]) for j in range(2)]
            bv_t = [T3(f"bvt{j}", [128, 512]) for j in range(2)]
            w1S = [T3(f"w1_{j}", [128, 512]) for j in range(2)]
            w2S = [T3(f"w2_{j}", [128, 512]) for j in range(2)]
            stS = [T3(f"st_{j}", [128, 64]) for j in range(2)]
            ya_bdS = [T3(f"ya_bd{j}", [128, 8, 128], BF16) for j in range(2)]
            yaTS = [T3(f"yaT{j}", [128, 8, 128], BF16) for j in range(2)]
            ybTS = [T3(f"ybT{j}", [128, 8, 128], BF16) for j in range(1)] * 2
            zaT_t = [T3(f"zaTt{j}", [128, 16, 128], BF16) for j in range(2)]
            op_tS = [T3(f"op_t{j}", [128, 1024]) for j in range(1)] * 2
            obS = [T3(f"ob{j}", [128, 1024]) for j in range(2)]
            tmpb, b_tmpb = T3("tmpb", [128, 1024])
            yb_bfS = [T3(f"yb_bf{j}", [128, 1024], BF16) for j in range(1)] * 2
            zg_t, b_zgt = T3("zg_t", [128, 4096], BF16)
            m1, b_m1 = T3("m1", [128, 512])
            m2, b_m2 = T3("m2", [128, 512])
            mixedS = [T3(f"mixed{j}", [128, D], BF16) for j in range(1)] * 2
            mT_t = [T3(f"mTt{j}", [128, 16, 128], BF16) for j in range(1)] * 2
            for j_ in range(2):
                pool.op(lambda e, j_=j_: e.memset(ya_bdS[j_][0][:], 0.0), writes=[ya_bdS[j_][1]])
            vcopy(Hf[0][0][:], Hf0[:], [b_Hf0], [Hf[0][1]])
            vcopy(Gf[0][0][:], Hgf0[:], [b_Hgf0], [Gf[0][1]])
            b3 = lambda ap, n, m: ap.to_broadcast([128, n, m])
            for c in range(NCH):
                j = c // 2
                par = c % 2
                Hc, bHc = Hf[c % 2]
                Hn, bHn = Hf[(c + 1) % 2]
                Gc, bGc = Gf[c % 2]
                Gn_, bGn = Gf[(c + 1) % 2]
                zt_, bzt_ = zaT_t[j % 2]
                w1, b_w1 = w1S[c % 2]
                w2, b_w2 = w2S[c % 2]
                st, b_st = stS[c % 2]
                ya_bd, b_yabd = ya_bdS[c % 2]
                yaT, b_yaT = yaTS[j % 2]
                ybT, b_ybT = ybTS[j % 2]
                op_t, b_opt = op_tS[j % 2]
                ob, b_ob = obS[j % 2]
                yb_bf, b_ybbf = yb_bfS[j % 2]
                mixed, b_mixed = mixedS[j % 2]
                if par == 0:
                    sp.dma(zt_[:], zaTd[:, :, j * 128:(j + 1) * 128].rearrange("b p t -> p b t"), reads=[bD], writes=[bzt_])
                    sp.dma(op_t[:], opd[j * 128:(j + 1) * 128, :], reads=[bD], writes=[b_opt])
                    sp.dma(zg_t[:], zgd[j * 128:(j + 1) * 128, :], reads=[bD], writes=[b_zgt])
                (psY, bpsY), psO, pp = passB(s3, 0, c, Hc, bHc, Hn, bHn, Gc, bGc, Gn_, bGn, stA_f[c % 2][0], stA_f[c % 2][1],
                                             stG_f[c % 2][0], stG_f[c % 2][1])
                ypt, bypt = yp_t[c % 2]
                bvt, bbvt = bv_t[c % 2]
                sp.dma(ypt[:], ypd[c].rearrange("q p v -> q (p v)"), reads=[bD], writes=[bypt])
                sp.dma(bvt[:], bvd[c].rearrange("q p v -> q (p v)"), reads=[bD], writes=[bbvt])
                tt(dve, w1[:], psY[:, :], ypt[:], ALU.add, [bpsY, bypt], [bpsY, b_w1])
                for hb_ in range(2):
                    po, bpo = psO[hb_]
                    tt(dve, ob[pp, hb_ * 512:(hb_ + 1) * 512], po[pp, :], op_t[pp, hb_ * 512:(hb_ + 1) * 512], ALU.add, [bpo, b_opt],
                       [bpo, b_ob])
                w13 = w1[:].rearrange("q (p v) -> q p v", v=64)
                treduce(st[:, 0:8], w13, [b_w1], [b_st])
                acopy(w2[:], w1[:], [b_w1], [b_w2], func=AF.Square)
                treduce(st[:, 8:16], w2[:].rearrange("q (p v) -> q p v", v=64), [b_w2], [b_st])
                ts(dve, st[:, 16:24], st[:, 0:8], 1.0 / 64, None, ALU.mult, None, [b_st], [b_st])
                tt(dve, st[:, 24:32], st[:, 16:24], st[:, 16:24], ALU.mult, [b_st], [b_st])
                stt(st[:, 32:40], st[:, 8:16], 1.0 / 64, st[:, 24:32], ALU.mult, ALU.subtract, [b_st], [b_st])
                acopy(st[:, 40:48], st[:, 32:40], [b_st], [b_st], func=AF.Sqrt, bias=64e-5)
                recip(st[:, 48:56], st[:, 40:48], [b_st], [b_st])
                tt(dve, w13, w13, b3(st[:, 16:24].rearrange("q (p o) -> q p o", o=1), 8, 64), ALU.subtract, [b_w1, b_st], [b_w1])
                tt(dve, w13, w13, b3(st[:, 48:56].rearrange("q (p o) -> q p o", o=1), 8, 64), ALU.mult, [b_w1, b_st], [b_w1])
                tt(pool, w13, w13, lnxg_st[:], ALU.mult, [b_w1, b_lnxg], [b_w1])
                tt(pool, w13, w13, lnxb_st[:], ALU.add, [b_w1, b_lnxb], [b_w1])
                for h in range(2):
                    hs = slice(h * 64, (h + 1) * 64)
                    tt(dve if h == 0 else pool, ya_bd[hs, :, h * 64:(h + 1) * 64], w1[hs, :].rearrange("q (p v) -> q p v", v=64),
                       bvt[hs, :].rearrange("q (p v) -> q p v", v=64), ALU.add, [b_w1, bbvt], [b_yabd])
                psT, bpsT = bank()
                for p in range(8):
                    mm(psT[:, p * 64:(p + 1) * 64], ya_bd[:, p, :], identstb[:], [b_yabd, b_identstb], bpsT)
                tt(dve, yaT[:, :, par * 64:(par + 1) * 64], psT[:, :].rearrange("q (p t) -> q p t", t=64),
                   zt_[:, 0:8, par * 64:(par + 1) * 64], ALU.mult, [bpsT, bzt_], [bpsT, b_yaT])
                if par == 0:
                    continue
                acopy(tmpb[:], ob[:], [b_ob], [b_tmpb], func=AF.Square)
                treduce(st[:, 56:60], tmpb[:].rearrange("q (g v) -> q g v", v=256), [b_tmpb], [b_st])
                ts(dve, st[:, 60:64], st[:, 56:60], 1.0 / 256, None, ALU.mult, None, [b_st], [b_st])
                acopy(st[:, 56:60], st[:, 60:64], [b_st], [b_st], func=AF.Sqrt, bias=1e-6)
                recip(st[:, 60:64], st[:, 56:60], [b_st], [b_st])
                ob3 = ob[:].rearrange("q (g v) -> q g v", v=256)
                tt(dve, ob3, ob3, b3(st[:, 60:64].rearrange("q (g o) -> q g o", o=1), 4, 256), ALU.mult, [b_ob, b_st], [b_ob])
                tt(pool, yb_bf[:].rearrange("q (g v) -> q g v", v=256), ob3, bng_b[:], ALU.mult, [b_ob, b_bng], [b_ybbf])
                for k4 in range(2):
                    psT, bpsT = bank()
                    for kk in range(4):
                        k = k4 * 4 + kk
                        mm(psT[:, kk * 128:(kk + 1) * 128], yb_bf[:, k * 128:(k + 1) * 128], identb[:], [b_ybbf, b_identb], bpsT)
                    tt(dve, ybT[:, k4 * 4:(k4 + 1) * 4, :], psT[:, :].rearrange("q (a t) -> q a t", t=128),
                       zt_[:, 8 + k4 * 4:8 + (k4 + 1) * 4, :], ALU.mult, [bpsT, bzt_], [bpsT, b_ybT])
                for nbk in range(4):
                    ns = slice(nbk * 512, (nbk + 1) * 512)
                    psA, bpsA = bank()
                    for p in range(8):
                        mm(psA[:, :], yaT[:, p, :], wa_bf[:, p, ns], [b_yaT, b_wa], bpsA, start=(p == 0), stop=(p == 7))
                    psB, bpsB = bank()
                    for p in range(8):
                        mm(psB[:, :], ybT[:, p, :], wb_bf[:, p, ns], [b_ybT, b_wbb], bpsB, start=(p == 0), stop=(p == 7))
                    tt(dve, m1[:], psA[:, :], zg_t[:, ns], ALU.mult, [bpsA, b_zgt], [bpsA, b_m1])
                    tt(dve, m2[:], psB[:, :], zg_t[:, 2048 + nbk * 512:2048 + (nbk + 1) * 512], ALU.mult, [bpsB, b_zgt], [bpsB, b_m2])
                    tt(pool, mixed[:, ns], m1[:], m2[:], ALU.add, [b_m1, b_m2], [b_mixed])
                mt_, bmt_ = mT_t[j % 2]
                for k4 in range(4):
                    psT, bpsT = bank()
                    for kk in range(4):
                        k = k4 * 4 + kk
                        mm(psT[:, kk * 128:(kk + 1) * 128], mixed[:, k * 128:(k + 1) * 128], identb[:], [b_mixed, b_identb], bpsT)
                    ecopy(mt_[:, k4 * 4:(k4 + 1) * 4, :], psT[:, :].rearrange("q (a t) -> q a t", t=128), [bpsT], [bpsT, bmt_])
                act.dma(mTd[:, :, j * 128:(j + 1) * 128], mt_[:], reads=[bmt_], writes=[bD], sem_buf=bmt_)
            fw.barrier()
            fw.emit()
        sW.close()
        if phases <= 4:
            return nc, fw

        with ExitStack() as s4:
            def T4(name, shape, dt=F32):
                return fw.sbuf(s4, name, shape, dt)
            wo_bf, b_wo = T4("wo_bf", [128, 16, D], BF16)
            wst4 = [T4(f"wst4_{j}", [128, D]) for j in range(3)]
            b_wos = [Buf(f"wo{k}") for k in range(16)]
            for k in range(16):
                wst, b_wst = wst4[k % 3]
                sp.dma(wst[:], w_out[k * 128:(k + 1) * 128, :], writes=[b_wst])
                if k % 3 == 0:
                    pool.op(lambda e, k=k, wst=wst: e.tensor_copy(out=wo_bf[:, k, :], in_=wst[:]), reads=[b_wst], writes=[b_wos[k]])
                elif k % 3 == 1:
                    vcopy(wo_bf[:, k, :], wst[:], [b_wst], [b_wos[k]])
                else:
                    acopy(wo_bf[:, k, :], wst[:], [b_wst], [b_wos[k]])
            gate_t, b_gt = T4("gate_t", [128, D])
            fg_t, b_fg = T4("fg_t", [128, D])
            sp.dma(gate_t[:], gate_d, reads=[bD], writes=[b_gt])
            sp.dma(fg_t[:], bc(final_g, D), writes=[b_fg])
            mTi = [T4(f"mTi{j}", [128, 16, 128], BF16) for j in range(2)]
            xin = [T4(f"xin{j}", [128, D]) for j in range(2)]
            o_t = [T4(f"o_t{j}", [128, D]) for j in range(2)]
            res_t = [T4(f"res{j}", [128, D]) for j in range(2)]
            sq4, b_sq4 = T4("sq4", [128, D])
            s4t, b_s4t = T4("s4t", [128, 4])
            for j in range(32):
                mi, bmi = mTi[j % 2]
                xi, bxi = xin[j % 2]
                ot, bot = o_t[j % 2]
                rt_, brt = res_t[j % 2]
                sp.dma(mi[:], mTd[:, :, j * 128:(j + 1) * 128], reads=[bD], writes=[bmi])
                sp.dma(xi[:], xs[64 + j * 128:64 + (j + 1) * 128, :], writes=[bxi])
                for nbk in range(4):
                    ns = slice(nbk * 512, (nbk + 1) * 512)
                    psW, bpsW = bank()
                    for k in range(16):
                        mm(psW[:, :], mi[:, k, :], wo_bf[:, k, ns], [bmi, b_wos[k]], bpsW, start=(k == 0), stop=(k == 15))
                    tt(dve, ot[:, ns], psW[:, :], gate_t[:, ns], ALU.mult, [bpsW, b_gt], [bpsW, bot])
                    tt(pool, ot[:, ns], ot[:, ns], xi[:, ns], ALU.add, [bot, bxi], [bot])
                acopy(sq4[:], ot[:], [bot], [b_sq4, b_s4t], func=AF.Square, accum_out=s4t[:, 0:1])
                ts(dve, s4t[:, 1:2], s4t[:, 0:1], 1.0 / D, 1e-6, ALU.mult, ALU.add, [b_s4t], [b_s4t])
                acopy(s4t[:, 2:3], s4t[:, 1:2], [b_s4t], [b_s4t], func=AF.Sqrt)
                dve.op(lambda e: e.reciprocal(out=s4t[:, 3:4], in_=s4t[:, 2:3]), reads=[b_s4t], writes=[b_s4t])
                stt(rt_[:], ot[:], s4t[:, 3:4], fg_t[:], ALU.mult, ALU.mult, [bot, b_s4t, b_fg], [brt])
                act.dma(out_d[j * 128:(j + 1) * 128, :], rt_[:], reads=[brt], writes=[bD], sem_buf=brt)
            fw.barrier()
            fw.emit()
    return nc, fw


_CACHE = {}


def _consts():
    idx = np.arange(64)
    identf = np.eye(128, dtype=np.float32)
    identst = np.concatenate([np.eye(64), np.eye(64)], 0).astype(np.float32)
    st_f = (idx[None, :] > idx[:, None]).astype(np.float32)
    in_f = (idx[None, :] >= idx[:, None]).astype(np.float32)
    st_b = (idx[None, :] < idx[:, None]).astype(np.float32)
    in_b = (idx[None, :] <= idx[:, None]).astype(np.float32)
    mG = np.zeros((128, 2, 128), np.float32)
    for h in range(2):
        mG[h * 64:(h + 1) * 64, 0, 0:64] = st_f
        mG[h * 64:(h + 1) * 64, 0, 64:128] = in_f
        mG[h * 64:(h + 1) * 64, 1, 0:64] = st_b
        mG[h * 64:(h + 1) * 64, 1, 64:128] = in_b
    mL = np.zeros((128, 2, 64), np.float32)
    for h in range(2):
        mL[h * 64:(h + 1) * 64, 0, :] = st_f.T
        mL[h * 64:(h + 1) * 64, 1, :] = st_b.T
    mA = np.stack([in_f, in_b], 1).astype(np.float32)
    bones = np.zeros((128, 128), np.float32)
    bones[0:64, 0:64] = 1
    bones[64:128, 64:128] = 1
    ind2 = np.zeros((128, 2), np.float32)
    ind2[0:64, 0] = 1
    ind2[64:128, 1] = 1
    rmask = np.ones((128, 512), np.float32)
    rmask[:, ::64] = 0
    return dict(identf=identf, identst=identst, maskG=mG, maskL=mL, maskA=mA, bones=bones, ind2=ind2, rmask=rmask)


def _prep_inputs(inp):
    f = lambda a: np.ascontiguousarray(np.asarray(a, dtype=np.float32))
    x = f(inp["x"]); c = f(inp["c"]); ctx = f(inp["ctx"]); c_ctx = f(inp["c_ctx"])
    conv_w = f(inp["conv_w"])[0].reshape(9, 3072)
    shared = dict(
        w_mod=f(inp["w_mod"])[0], b_mod=f(inp["b_mod"])[0], norm_g=f(inp["norm_g"])[0], w_in=f(inp["w_in"])[0],
        aw2p=f(inp["a_w2"])[0].reshape(128, 1024), aa2p=f(inp["a_a2"])[0].reshape(128, 1024),
        lnxg=f(inp["a_lnx_g"])[0], lnxb=f(inp["a_lnx_b"])[0], bng=f(inp["b_norm_g"])[0],
        w_a=f(inp["w_a"])[0], w_b=f(inp["w_b"])[0], w_out=f(inp["w_out"])[0], final_g=f(inp["final_g"]),
    )
    gk = f(inp["b_gk_w2"])[0]
    gkw2p = np.zeros((32, 2, 512), np.float32)
    gkw2p[0:16, 0] = gk[0]
    gkw2p[16:32, 1] = gk[1]
    shared["gkw2p"] = gkw2p
    shared.update(_consts())
    colv0 = np.zeros((128, NCOLV), np.float32)
    colv0[:, 0:216] = conv_w.reshape(9, 24, 128).transpose(2, 1, 0).reshape(128, 216)
    for d in range(2):
        colv0[:, CV_W0 + d * 8:CV_W0 + d * 8 + 8] = f(inp["a_w0"])[0, d].reshape(8, 128).T
        colv0[:, CV_A0 + d * 8:CV_A0 + d * 8 + 8] = f(inp["a_a0"])[0, d].reshape(8, 128).T
        colv0[:, CV_GB + d * 4:CV_GB + d * 4 + 4] = f(inp["b_gk_b"])[0, d].reshape(4, 128).T
    colv0[:, CV_KK:CV_KK + 8] = f(inp["a_k_k"])[0].reshape(8, 128).T
    colv0[:, CV_KA:CV_KA + 8] = f(inp["a_k_a"])[0].reshape(8, 128).T
    colv0[:, CV_RK:CV_RK + 8] = f(inp["a_r_k"])[0].reshape(8, 128).T
    maps = []
    for core in range(8):
        b, q = core // 4, core % 4
        xs = np.zeros((4224, D), np.float32)
        lo, hi = q * SEG - 64, (q + 1) * SEG + 64
        slo, shi = max(lo, 0), min(hi, 4 * SEG)
        xs[slo - lo:shi - lo] = x[b, slo:shi]
        cv = colv0.copy()
        cv[:, CV_HALO] = 0.0 if q == 0 else 1.0
        cv[:, CV_HALO + 1] = 0.0 if q == 3 else 1.0
        for s in range(4):
            cv[:, CV_FM + s] = 1.0 if s < q else 0.0
            cv[:, CV_FM + 4 + s] = 1.0 if s > q else 0.0
        cT = np.stack([c[b], c_ctx], 1).reshape(16, 128, 2).transpose(1, 0, 2)
        m = dict(shared)
        m.update(xs=xs, ctxb=np.ascontiguousarray(ctx[b]), cT=np.ascontiguousarray(cT), colv=cv)
        maps.append(m)
    return maps


def kernel(**inputs):
    maps = _prep_inputs(inputs)
    if "nc" not in _CACHE:
        _CACHE["nc"] = build_program()[0]
    res = run_bass_kernel_spmd(_CACHE["nc"], maps, core_ids=list(range(8)))
    out = np.zeros((2, 4 * SEG, D), np.float32)
    for core in range(8):
        b, q = core // 4, core % 4
        out[b, q * SEG:(q + 1) * SEG] = res.results[core]["out"]
    return out
```

```python
import numpy as np
from contextlib import ExitStack
import concourse.bass as bass
import concourse.mybir as mybir
from concourse.bass_utils import run_bass_kernel_spmd

F32 = mybir.dt.float32
BF16 = mybir.dt.bfloat16
AF = mybir.ActivationFunctionType
ALU = mybir.AluOpType
AX = mybir.AxisListType

D = 2048
PIN = 11552
SEG = 4096
NCH = 64
KDEC = 0.6065306597126334
C_Z_A, C_LW, C_LA, C_QB, C_KB, C_VB, C_ZB, C_LG, C_GA, C_GB = 3072, 4096, 4224, 4352, 4864, 5376, 6400, 7424, 7456, 9504
NCOLV = 290
CV_W0, CV_A0, CV_KK, CV_KA, CV_RK, CV_GB, CV_HALO, CV_FM = 216, 232, 248, 256, 264, 272, 280, 282
NCOMP = 16 * 128 + 8 * 257
DBG = {}


class Buf:
    __slots__ = ("name", "w", "r", "sem", "cnt", "dram")

    def __init__(self, name, dram=False):
        self.name = name
        self.w = {}
        self.r = {}
        self.sem = None
        self.cnt = 0
        self.dram = dram


def _mx(need, d):
    for s, v in d.items():
        if v > need.get(s, 0):
            need[s] = v


class Eng:
    def __init__(self, fw, name, sem):
        self.fw = fw
        self.name = name
        self.sem = sem
        self.count = 0
        self.seen = {}
        self.prog = []

    def replay(self, e):
        for it in self.prog:
            if it[0] == "w":
                e.wait_ge(it[1], it[2])
            else:
                it[1](e).then_inc(it[2], it[3])
        self.prog = []

    def _wait(self, need):
        for sem, val in need.items():
            if self.seen.get(sem, 0) >= val:
                continue
            self.prog.append(("w", sem, val))
            self.seen[sem] = val

    def op(self, fn, reads=(), writes=()):
        need = {}
        for b in reads:
            _mx(need, b.w)
        for b in writes:
            _mx(need, b.w)
            _mx(need, b.r)
        if self.name == "pe":
            need.pop(self.sem, None)
        self._wait(need)
        self.count += 1
        self.prog.append(("o", fn, self.sem, 1))
        for b in reads:
            b.r[self.sem] = self.count
        for b in writes:
            b.w = {self.sem: self.count}
            b.r = {}

    def dma(self, out, in_, reads=(), writes=(), sem_buf=None):
        need = {}
        for b in reads:
            _mx(need, b.w)
        for b in writes:
            if b.dram:
                continue
            _mx(need, b.w)
            _mx(need, b.r)
        self._wait(need)
        sb = sem_buf
        if sb is None:
            for b in list(writes) + list(reads):
                if not b.dram:
                    sb = b
                    break
        if sb.sem is None:
            sb.sem, sb.cnt = self.fw.get_dsem(sb.name)
        self.prog.append(("o", (lambda e, o=out, i=in_: e.dma_start(out=o, in_=i)), sb.sem, 16))
        sb.cnt += 16
        for b in reads:
            b.r[sb.sem] = sb.cnt
        for b in writes:
            if b.dram:
                b.w[sb.sem] = sb.cnt
            else:
                b.w = {sb.sem: sb.cnt}
                b.r = {}


class FW:
    def __init__(self, nc, stack):
        self.nc = nc
        self.stack = stack
        self.dsems = []
        self.pe = Eng(self, "pe", self._sem("pe"))
        self.act = Eng(self, "act", self._sem("act"))
        self.dve = Eng(self, "dve", self._sem("dve"))
        self.pool = Eng(self, "pool", self._sem("pool"))
        self.sp = Eng(self, "sp", self._sem("sp"))
        self.engs = [self.pe, self.act, self.dve, self.pool, self.sp]
        self.dbufs = []
        self.sem_pool = []
        self.rr = 0
        self.erot = 0

    def _sem(self, name):
        return self.stack.enter_context(self.nc.semaphore(name))

    def new_sem(self, name):
        return self._sem(name)

    def get_dsem(self, name):
        if self.sem_pool:
            return self.sem_pool.pop()
        return self._sem("d_" + name), 0

    def sbuf(self, st, name, shape, dt=F32):
        self.nid = getattr(self, "nid", 0) + 1
        t = st.enter_context(self.nc.sbuf_tensor(f"sb{self.nid}_{name}", list(shape), dt))
        b = Buf(name)
        self.dbufs.append(b)
        return t, b

    def barrier(self):
        need = {}
        for e in self.engs:
            need[e.sem] = e.count
        for b in self.dbufs:
            if b.sem is not None:
                need[b.sem] = b.cnt
        for e in self.engs:
            n2 = dict(need)
            n2.pop(e.sem, None) if e.name == "pe" else None
            e._wait(n2)
        for b in self.dbufs:
            if b.sem is not None:
                self.sem_pool.append((b.sem, b.cnt))
                b.sem = None

    def emit(self):
        with self.nc.Block() as block:
            @block.tensor
            def _(e):
                self.pe.replay(e)

            @block.scalar
            def _(e):
                self.act.replay(e)

            @block.vector
            def _(e):
                self.dve.replay(e)

            @block.gpsimd
            def _(e):
                self.pool.replay(e)

            @block.sync
            def _(e):
                self.sp.replay(e)


def build_program(phases=9, dbg=()):
    nc = bass.Bass("TRN2", target_bir_lowering=False)

    def din(name, shape, dt=F32):
        return nc.dram_tensor(name, list(shape), dt, kind="ExternalInput")

    def dint(name, shape, dt=F32):
        return nc.dram_tensor(name, list(shape), dt, kind=("ExternalOutput" if name in dbg else "Internal"))

    xs = din("xs", [4224, D]).ap()
    ctxb = din("ctxb", [256, D]).ap()
    cT_d = din("cT", [128, 16, 2]).ap()
    w_mod = din("w_mod", [D, 3 * D]).ap()
    b_mod = din("b_mod", [3 * D])
    norm_g = din("norm_g", [D])
    w_in = din("w_in", [D, PIN]).ap()
    colv_d = din("colv", [128, NCOLV]).ap()
    aw2_d = din("aw2p", [128, 1024]).ap()
    aa2_d = din("aa2p", [128, 1024]).ap()
    gkw_d = din("gkw2p", [32, 2, 512]).ap()
    lnxg = din("lnxg", [1024])
    lnxb = din("lnxb", [1024])
    bng = din("bng", [256])
    w_a = din("w_a", [1024, D]).ap()
    w_b = din("w_b", [1024, D]).ap()
    w_out = din("w_out", [D, D]).ap()
    final_g = din("final_g", [D])
    identf_d = din("identf", [128, 128]).ap()
    identst_d = din("identst", [128, 64]).ap()
    maskG_d = din("maskG", [128, 2, 128]).ap()
    maskL_d = din("maskL", [128, 2, 64]).ap()
    maskA_d = din("maskA", [64, 2, 64]).ap()
    bones_d = din("bones", [128, 128]).ap()
    ind2_d = din("ind2", [128, 2]).ap()
    rmask_d = din("rmask", [128, 512]).ap()
    out_d = nc.dram_tensor("out", [SEG, D], F32, kind="ExternalOutput").ap()

    hTd = dint("hTd", [128, 16, 4480], BF16).ap()
    spA = dint("spA", [2, NCH, 128, 8 * 192]).ap()
    y0d = dint("y0d", [NCH, 128, 8, 64]).ap()
    bvd = dint("bvd", [NCH, 128, 8, 64]).ap()
    spG = dint("spG", [2, NCH, 128, 4 * 321]).ap()
    o0d = dint("o0d", [SEG, 1024]).ap()
    zgd = dint("zgd", [SEG, 4096], BF16).ap()
    zaTd = dint("zaTd", [16, 128, SEG], BF16).ap()
    mTd = dint("mTd", [128, 16, SEG], BF16).ap()
    gate_d = dint("gate_d", [128, D]).ap()
    ypd = dint("ypd", [NCH, 128, 8, 64]).ap()
    opd = dint("opd", [SEG, 1024]).ap()
    CSPL = [(0, 2048), (2048, 3080), (3080, 4104)]
    comp_in = [dint(f"comp_in{i}", [128, b - a]) for i, (a, b) in enumerate(CSPL)]
    comp_out = [dint(f"comp_out{i}", [512, b - a]) for i, (a, b) in enumerate(CSPL)]
    bD = Buf("dram", dram=True)

    def bc(t, n, inner=None):
        if inner is None:
            return bass.AP(t, 0, [[0, 128], [1, n]])
        return bass.AP(t, 0, [[0, 128], [0, inner], [1, n]])

    with ExitStack() as top:
        fw = FW(nc, top)
        pe, act, dve, pool, sp = fw.pe, fw.act, fw.dve, fw.pool, fw.sp
        PS = []
        for i in range(8):
            t = top.enter_context(nc.psum_tensor(f"ps{i}", [128, 512], F32))
            PS.append((t, Buf(f"ps{i}")))

        def bank():
            fw.rr = (fw.rr + 1) % 8
            return PS[fw.rr]

        def mm(out, lhsT, rhs, rd, bps, start=True, stop=True):
            pe.op(lambda e: e.matmul(out, lhsT=lhsT, rhs=rhs, start=start, stop=stop), reads=rd, writes=[bps])

        def acopy(out, in_, rd, wr, func=AF.Copy, **kw):
            act.op(lambda e: e.activation(out=out, in_=in_, func=func, **kw), reads=rd, writes=wr)

        def vcopy(out, in_, rd, wr):
            dve.op(lambda e: e.tensor_copy(out=out, in_=in_), reads=rd, writes=wr)

        def ecopy(out, in_, rd, wr):
            fw.erot += 1
            if fw.erot % 2:
                acopy(out, in_, rd, wr)
            else:
                vcopy(out, in_, rd, wr)

        def tt(eng, out, in0, in1, op, rd, wr):
            eng.op(lambda e: e.tensor_tensor(out=out, in0=in0, in1=in1, op=op), reads=rd, writes=wr)

        def ts(eng, out, in0, s1, s2, op0, op1, rd, wr):
            if s2 is None:
                eng.op(lambda e: e.tensor_scalar(out=out, in0=in0, scalar1=s1, scalar2=None, op0=op0), reads=rd, writes=wr)
            else:
                eng.op(lambda e: e.tensor_scalar(out=out, in0=in0, scalar1=s1, scalar2=s2, op0=op0, op1=op1), reads=rd, writes=wr)

        def treduce(out, in_, rd, wr):
            dve.op(lambda e: e.tensor_reduce(out=out, in_=in_, axis=AX.X, op=ALU.add), reads=rd, writes=wr)

        def recip(out, in_, rd, wr):
            dve.op(lambda e: e.reciprocal(out=out, in_=in_), reads=rd, writes=wr)

        def stt(out, in0, sc, in1, op0, op1, rd, wr):
            dve.op(lambda e: e.scalar_tensor_tensor(out=out, in0=in0, scalar=sc, in1=in1, op0=op0, op1=op1), reads=rd, writes=wr)

        identf, b_identf = fw.sbuf(top, "identf", [128, 128])
        identb, b_identb = fw.sbuf(top, "identb", [128, 128], BF16)
        identst, b_identst = fw.sbuf(top, "identst", [128, 64])
        colv, b_colv = fw.sbuf(top, "colv", [128, NCOLV])
        sp.dma(identf[:], identf_d, writes=[b_identf])
        sp.dma(identst[:], identst_d, writes=[b_identst])
        sp.dma(colv[:], colv_d, writes=[b_colv])
        vcopy(identb[:], identf[:], [b_identf], [b_identb])
        w_in_v = w_in.rearrange("(k p) n -> p k n", p=128)
        Hctx = [fw.sbuf(top, f"Hctx{p}", [128, 2, 64]) for p in range(8)]
        Hgctx, b_Hgctx = fw.sbuf(top, "Hgctx", [128, 8, 256])

        with ExitStack() as s0:
            cTt, b_cT = fw.sbuf(s0, "cTt", [128, 16, 2])
            sT, b_sT = fw.sbuf(s0, "sT", [128, 16, 2])
            srep, b_srep = fw.sbuf(s0, "srep", [128, 2, 16, 128])
            bmods = [fw.sbuf(s0, f"bmod{j}", [128, 256]) for j in range(2)]
            ng_t, b_ng = fw.sbuf(s0, "ng_t", [128, D])
            mt = [fw.sbuf(s0, f"m{j}", [128, 3 * D]) for j in range(2)]
            modA = [fw.sbuf(s0, f"modA{j}", [128, D]) for j in range(2)]
            wm = [fw.sbuf(s0, f"wm{j}", [128, 16, 256]) for j in range(2)]
            sp.dma(cTt[:], cT_d, writes=[b_cT])
            sp.dma(ng_t[:], bc(norm_g, D), writes=[b_ng])
            acopy(sT[:], cTt[:], [b_cT], [b_sT], func=AF.Silu)
            for j in range(2):
                vcopy(srep[:, j], sT[:, :, j:j + 1].to_broadcast([128, 16, 128]), [b_sT], [b_srep])
            wmv = w_mod.rearrange("(k p) n -> p k n", p=128)
            for nb in range(24):
                wt, bw = wm[nb % 2]
                sp.dma(wt[:], wmv[:, :, nb * 256:(nb + 1) * 256], writes=[bw])
                bmod_t, b_bmod = bmods[nb % 2]
                sp.dma(bmod_t[:], bass.AP(b_mod, nb * 256, [[0, 128], [1, 256]]), writes=[b_bmod])
                for j in range(2):
                    ps, bps = bank()
                    for k in range(16):
                        mm(ps[:, 0:256], srep[:, j, k, :], wt[:, k, :], [b_srep, bw], bps, start=(k == 0), stop=(k == 15))
                    tt(dve, mt[j][0][:, nb * 256:(nb + 1) * 256], ps[:, 0:256], bmod_t[:, :], ALU.add,
                       [bps, b_bmod], [bps, mt[j][1]])
            for j in range(2):
                stt(modA[j][0][:], mt[j][0][:, D:2 * D], 1.0, ng_t[:], ALU.add, ALU.mult, [mt[j][1], b_ng], [modA[j][1]])
            act.dma(gate_d, mt[0][0][:, 2 * D:3 * D], reads=[mt[0][1]], writes=[bD], sem_buf=mt[0][1])
            xt = [fw.sbuf(s0, f"xt{j}", [128, D]) for j in range(2)]
            hf, b_hf = fw.sbuf(s0, "hf", [128, D])
            hb, b_hb = fw.sbuf(s0, "hb", [128, D], BF16)
            ss, b_ss = fw.sbuf(s0, "ss", [128, 4])
            hTt = [fw.sbuf(s0, f"hTt{j}", [128, 16, 128], BF16) for j in range(2)]
            for ti in range(35):
                j = 0 if ti < 33 else 1
                src = xs[ti * 128:(ti + 1) * 128, :] if ti < 33 else ctxb[(ti - 33) * 128:(ti - 32) * 128, :]
                x_t, bx = xt[ti % 2]
                sp.dma(x_t[:], src, writes=[bx])
                acopy(hf[:], x_t[:], [bx], [b_hf, b_ss], func=AF.Square, accum_out=ss[:, 0:1])
                ts(dve, ss[:, 1:2], ss[:, 0:1], 1.0 / D, 1e-6, ALU.mult, ALU.add, [b_ss], [b_ss])
                acopy(ss[:, 2:3], ss[:, 1:2], [b_ss], [b_ss], func=AF.Sqrt)
                dve.op(lambda e: e.reciprocal(out=ss[:, 3:4], in_=ss[:, 2:3]), reads=[b_ss], writes=[b_ss])
                stt(hf[:], x_t[:], ss[:, 3:4], modA[j][0][:], ALU.mult, ALU.mult, [bx, b_ss, modA[j][1]], [b_hf])
                tt(pool, hb[:], hf[:], mt[j][0][:, 0:D], ALU.add, [b_hf, mt[j][1]], [b_hb])
                ht, bht = hTt[ti % 2]
                for k4 in range(4):
                    ps, bps = bank()
                    for kk in range(4):
                        k = k4 * 4 + kk
                        mm(ps[:, kk * 128:(kk + 1) * 128], hb[:, k * 128:(k + 1) * 128], identb[:], [b_hb, b_identb], bps)
                    ecopy(ht[:, k4 * 4:(k4 + 1) * 4, :], ps[:, :].rearrange("p (a b) -> p a b", b=128), [bps], [bps, bht])
                act.dma(hTd[:, :, ti * 128:(ti + 1) * 128], ht[:], reads=[bht], writes=[bD], sem_buf=bht)
            fw.barrier()
            fw.emit()
        if phases <= 0:
            return nc, fw

        with ExitStack() as s1:
            cur_st = [s1]

            def T(name, shape, dt=F32):
                return fw.sbuf(cur_st[0], name, shape, dt)
            maskG, b_maskG = T("maskG", [128, 2, 128])
            maskL, b_maskL = T("maskL", [128, 2, 64])
            maskA, b_maskA = T("maskA", [64, 2, 64])
            bonesf, b_bonesf = T("bonesf", [128, 128])
            bones, b_bones = T("bones", [128, 128], BF16)
            ind2f, b_ind2f = T("ind2f", [128, 2])
            ind2, b_ind2 = T("ind2", [128, 2], BF16)
            rmask, b_rmask = T("rmask", [128, 256])
            aw2, b_aw2 = T("aw2", [128, 1024])
            aa2, b_aa2 = T("aa2", [128, 1024])
            gkw, b_gkw = T("gkw", [32, 2, 512])
            dcol, b_dcol = T("dcol", [128, 16])
            for t_, b_, d_ in ((maskG, b_maskG, maskG_d), (maskL, b_maskL, maskL_d), (maskA, b_maskA, maskA_d),
                               (bonesf, b_bonesf, bones_d), (ind2f, b_ind2f, ind2_d), (rmask, b_rmask, rmask_d[:, 0:256]),
                               (aw2, b_aw2, aw2_d), (aa2, b_aa2, aa2_d), (gkw, b_gkw, gkw_d)):
                sp.dma(t_[:], d_, writes=[b_])
            vcopy(bones[:], bonesf[:], [b_bonesf], [b_bones])
            vcopy(ind2[:], ind2f[:], [b_ind2f], [b_ind2])
            ts(dve, dcol[:, 0:8], colv[:, CV_KA:CV_KA + 8], -1.0, 1.0, ALU.mult, ALU.add, [b_colv], [b_dcol])
            ts(dve, dcol[:, 8:16], colv[:, CV_GB:CV_GB + 8], -1.0, None, ALU.mult, None, [b_colv], [b_dcol])

            hT, b_hT = T("hT", [128, 16, 1152], BF16)
            lw, b_lw = T("lw", [128, 1024])
            la, b_la = T("la", [128, 1024])
            lg, b_lg = T("lg", [32, 1024])
            wf = [T(f"wf{j}", [128, 16, 128]) for j in range(2)]
            wb = [T(f"wb{j}", [128, 16, 128], BF16) for j in range(4)]
            wcnt = [0]

            def load_w(col0, ncols, slot):
                wt, bw = wf[wcnt[0] % 2]
                wcnt[0] += 1
                sp.dma(wt[:, :, 0:ncols], w_in_v[:, :, col0:col0 + ncols], writes=[bw])
                dst, bd = wb[slot]
                acopy(dst[:, 0:8, 0:ncols], wt[:, 0:8, 0:ncols], [bw], [bd])
                pool.op(lambda e: e.tensor_copy(out=dst[:, 8:16, 0:ncols], in_=wt[:, 8:16, 0:ncols]), reads=[bw], writes=[bd])
                return dst, bd

            def proj_fm(wt, bw, ncols, col0, ntok, dst, bdst, func=AF.Copy):
                for tb in range((ntok + 511) // 512):
                    n = min(512, ntok - tb * 512)
                    ps, bps = bank()
                    for k in range(16):
                        mm(ps[0:ncols, 0:n], wt[:, k, 0:ncols], hT[:, k, col0 + tb * 512:col0 + tb * 512 + n], [bw, b_hT], bps,
                           start=(k == 0), stop=(k == 15))
                    if func == AF.Copy:
                        ecopy(dst[0:ncols, tb * 512:tb * 512 + n], ps[0:ncols, 0:n], [bps], [bps, bdst])
                    else:
                        acopy(dst[0:ncols, tb * 512:tb * 512 + n], ps[0:ncols, 0:n], [bps], [bps, bdst], func=func)

            cin = [T(f"cin{j}", [128, 384]) for j in range(3)]
            cout = [T(f"cout{j}", [128, 256]) for j in range(3)]
            A = {}
            for nm in ("kk", "t1", "sg", "icl", "cs", "g", "gx", "eng", "ec", "bb", "kd", "rk"):
                A[nm] = T("a_" + nm, [128, 256])
            A["kka"] = A["kk"]
            A["t2"] = A["t1"]
            A["egx"] = A["gx"]
            sqb, b_sqb = T("sqb", [128, 256], BF16)
            vbS = [T(f"vb{i}", [128, 256], BF16) for i in range(2)]
            rkbS = [T(f"rkb{i}", [128, 256], BF16) for i in range(2)]
            egdS = [[T(f"eg{i}_{d}", [128, 256]) for d in range(2)] for i in range(2)]
            rtdS = [[T(f"rt{i}_{d}", [128, 256]) for d in range(2)] for i in range(2)]
            ardS = [[T(f"ar{i}_{d}", [128, 2, 256], BF16) for d in range(2)] for i in range(2)]
            btdS = [[T(f"bt{i}_{d}", [128, 256], BF16) for d in range(2)] for i in range(2)]
            ktdS = [[T(f"kt{i}_{d}", [128, 256], BF16) for d in range(2)] for i in range(2)]
            bhdS = [[T(f"bh{i}_{d}", [128, 256], BF16) for d in range(2)] for i in range(2)]
            khdS = [[T(f"kh{i}_{d}", [128, 256], BF16) for d in range(2)] for i in range(2)]
            vb, b_vb = vbS[0]
            rkb, b_rkb = rkbS[0]
            egd, rtd, ard, btd, ktd, bhd, khd = egdS[0], rtdS[0], ardS[0], btdS[0], ktdS[0], bhdS[0], khdS[0]
            QS = 128.0 ** -0.5

            def reset_gla():
                pool.op(lambda e: e.memset(GN[:], 0.0), writes=[b_GN] + b_GNs)
                pool.op(lambda e: e.memset(GD[:], 1.0), writes=[b_GD])

            def gla_batch(job, S):
                g, col0, tok0, Tn, cg0, spill = job["args"]
                egd, btd, ktd, khd = egdS[S], btdS[S], ktdS[S], khdS[S]
                stG, b_stG = stGS[S]
                if job.get("seg") is not None:
                    hcol, ntok_own, halo_ = job["seg"]
                    own0 = load_seg(hcol, ntok_own, halo_)
                    wt, bw = load_w(C_LG, 32, 3)
                    proj_fm(wt, bw, 32, own0, ntok_own, lg, b_lg)
                    yield
                if job.get("newhead"):
                    load_w(C_QB + g * 128, 128, 0)
                    load_w(C_KB + g * 128, 128, 1)
                    load_w(C_VB + g * 256, 128, 2)
                    load_w(C_VB + g * 256 + 128, 128, 3)
                nch = Tn // 64
                sl = slice(0, Tn)
                qf, bqf = A["kk"]
                kf, bkf = A["t1"]
                vbg, bvbg = ardS[S][0]
                proj_fm(wb[0][0], wb[0][1], 128, col0, Tn, qf, bqf)
                yield
                proj_fm(wb[1][0], wb[1][1], 128, col0, Tn, kf, bkf)
                yield
                for j in range(2):
                    proj_fm(wb[2 + j][0], wb[2 + j][1], 128, col0, Tn, vbg[:, j, :], bvbg)
                    yield
                c3 = lambda ap: ap.rearrange("p (c t) -> p c t", t=64)
                for d in range(2):
                    e1, be1 = A["cs"]
                    spl, bspl = A["g"]
                    csg, bcsg = A["gx"]
                    Gs, bGs = A["eng"]
                    ek, bek = A["bb"]
                    tk, btk = A["kd"]
                    eq, beq = egd[d]
                    ps, bps = bank()
                    mm(ps[:, 0:Tn], gkw[0:32, d, g * 128:(g + 1) * 128], lg[0:32, tok0:tok0 + Tn], [b_gkw, b_lg], bps)
                    acopy(e1[:, sl], ps[:, 0:Tn], [bps, b_dcol], [bps, be1], func=AF.Exp, scale=-1.0,
                          bias=dcol[:, 8 + d * 4 + g:8 + d * 4 + g + 1])
                    acopy(spl[:, sl], e1[:, sl], [be1], [bspl], func=AF.Ln, bias=1.0)
                    yield
                    dve.op(lambda e: e.tensor_tensor_scan(out=csg[:, sl], data0=rmask[:, sl], data1=spl[:, sl], initial=0.0,
                                                          op0=ALU.mult, op1=ALU.add), reads=[b_rmask, bspl], writes=[bcsg])
                    if d == 0:
                        Gsrc, bGsrc = csg, bcsg
                    else:
                        tt(dve, c3(Gs[:, sl]), c3(csg[:, sl])[:, :, 63:64].to_broadcast([128, nch, 64]), c3(csg[:, sl]), ALU.subtract,
                           [bcsg], [bGs])
                        tt(pool, Gs[:, sl], Gs[:, sl], spl[:, sl], ALU.add, [bGs, bspl], [bGs])
                        Gsrc, bGsrc = Gs, bGs
                    acopy(eq[:, sl], Gsrc[:, sl], [bGsrc], [beq], func=AF.Exp, scale=-1.0 / 16)
                    acopy(ek[:, sl], Gsrc[:, sl], [bGsrc], [bek], func=AF.Exp, scale=1.0 / 16)
                    yield
                    eq3 = c3(eq[:, sl])
                    tcol = 63 if d == 0 else 0
                    stt(stG[:, 0:nch, d, 0:64], c3(qf[:, sl]), QS, eq3, ALU.mult, ALU.mult, [bqf, beq], [b_stG])
                    stt(btd[d][0][:, sl], qf[:, sl], QS, eq[:, sl], ALU.mult, ALU.mult, [bqf, beq], [btd[d][1]])
                    tt(pool, ktd[d][0][:, sl], kf[:, sl], ek[:, sl], ALU.mult, [bkf, bek], [ktd[d][1]])
                    tt(dve, c3(tk[:, sl]), c3(ek[:, sl]), eq3[:, :, tcol:tcol + 1].to_broadcast([128, nch, 64]), ALU.mult, [bek, beq], [btk])
                    tt(pool, khd[d][0][:, sl], kf[:, sl], tk[:, sl], ALU.mult, [bkf, btk], [khd[d][1]])
                    vcopy(stG[:, 0:nch, d, 320:321], eq3[:, :, tcol:tcol + 1], [beq], [b_stG])
                    yield
                yield "PREP_DONE"
                if job.get("reset"):
                    reset_gla()
                adv = DBG.get('adv', 3)

                def gcommon(ci):
                    c = slice(ci * 64, (ci + 1) * 64)
                    vtg, bvtg = VTg[ci]
                    psV, bpsV = bank()
                    for j in range(2):
                        mm(psV[0:64, j * 128:(j + 1) * 128], vbg[:, j, c], identb[:], [bvbg, b_identb], bpsV)
                    acopy(vtg[:, :], psV[0:64, 0:256], [bpsV], [bpsV, bvtg])

                def gchain(ci, d):
                    c = slice(ci * 64, (ci + 1) * 64)
                    vtg, bvtg = VTg[ci]
                    att, batt = AttT[ci * 2 + d]
                    kht, bkht = KhT[ci * 2 + d]
                    ps, bps = bank()
                    mm(ps[0:64, 0:64], ktd[d][0][:, c], btd[d][0][:, c], [ktd[d][1], btd[d][1]], bps)
                    mm(ps[0:64, 64:192], khd[d][0][:, c], identb[:], [khd[d][1], b_identb], bps)
                    tt(dve, att[:, :], ps[0:64, 0:64], maskA[:, d, :], ALU.mult, [bps, b_maskA], [bps, batt])
                    acopy(kht[:, :], ps[0:64, 64:192], [bps], [bps, bkht])
                    yield
                    psN, bpsN = bank()
                    mm(psN[:, 0:256], kht[:, :], vtg[:, :], [bkht, bvtg], bpsN)
                    acopy(stG[:, ci, d, 64:320], psN[:, 0:256], [bpsN], [bpsN, b_stG])
                    dc = stG[:, ci, d, 320:321]
                    gi = g * 2 + d
                    if d == 0:
                        stt(GN[:, gi, :], GN[:, gi, :], dc, psN[:, 0:256], ALU.mult, ALU.add, [b_GNs[gi], b_stG, bpsN], [bpsN, b_GNs[gi]])
                    else:
                        stt(GN[:, gi, :], psN[:, 0:256], GD[:, gi:gi + 1], GN[:, gi, :], ALU.mult, ALU.add, [b_GNs[gi], b_GD, bpsN],
                            [bpsN, b_GNs[gi]])
                    ts(dve, GD[:, gi:gi + 1], GD[:, gi:gi + 1], dc, None, ALU.mult, None, [b_GD, b_stG], [b_GD])
                    yield

                for ci in range(nch):
                    gcommon(ci)
                gens = [gchain(ci, d) for ci in range(nch) for d in range(2)]
                while gens:
                    alive = []
                    for g_ in gens:
                        try:
                            next(g_)
                            alive.append(g_)
                        except StopIteration:
                            pass
                    gens = alive
                    adv_extra(adv)
                if spill:
                    for ci in range(nch):
                        par = (cg0 + ci) % 2
                        pp = slice(par * 64, (par + 1) * 64)
                        vtg, bvtg = VTg[ci]
                        psO, bpsO = bank()
                        mm(psO[pp, 0:256], AttT[ci * 2][0][:, :], vtg[:, :], [AttT[ci * 2][1], bvtg], bpsO, start=True, stop=False)
                        mm(psO[pp, 0:256], AttT[ci * 2 + 1][0][:, :], vtg[:, :], [AttT[ci * 2 + 1][1], bvtg], bpsO, start=False, stop=True)
                        acopy(o0st[pp, ci // 2, :], psO[pp, 0:256], [bpsO], [bpsO, b_o0st])
                if spill:
                    for d in range(2):
                        act.dma(spG[d, cg0:cg0 + nch, :, g * 321:(g + 1) * 321].rearrange("c p f -> p c f"), stG[:, 0:nch, d, :],
                               reads=[b_stG], writes=[bD], sem_buf=b_stG)
                    tk0 = cg0 * 64
                    act.dma(o0d[tk0:tk0 + Tn, g * 256:(g + 1) * 256].rearrange("(a p) f -> p a f", p=128), o0st[:, 0:nch // 2, :],
                           reads=[b_o0st], writes=[bD], sem_buf=b_o0st)
                adv_extra(adv)
                if job.get("ctx_last"):
                    vcopy(Hgctx[:], GN[:], [b_GN] + b_GNs, [b_Hgctx])

            def reset_acc():
                for p in range(8):
                    for d, ACC in ((0, ACCf), (1, ACCb)):
                        accsel[p][d] = 0
                        t_, b_ = ACC[p][0]
                        pool.op(lambda e, t_=t_: e.memset(t_[:, 64:128], 0.0), writes=[b_])
                        pool.op(lambda e, t_=t_: e.tensor_copy(out=t_[:, 0:64], in_=identst[:]), reads=[b_identst], writes=[b_])

            PIPE = {"extra": None, "extra_done": True}

            def run_prep(g_):
                while next(g_) != "PREP_DONE":
                    pass

            def adv_extra(n):
                g_ = PIPE["extra"]
                if g_ is None or PIPE["extra_done"]:
                    return
                for _ in range(n):
                    if next(g_) == "PREP_DONE":
                        PIPE["extra_done"] = True
                        return

            def rwkv_batch(job, S):
                p, col0, tok0, Tn, rows, Wd, halo, cg0, spill, halo_mask = job["args"]
                vb, b_vb = vbS[S]
                rkb, b_rkb = rkbS[S]
                egd, rtd, ard, btd, ktd, bhd, khd = egdS[S], rtdS[S], ardS[S], btdS[S], ktdS[S], bhdS[S], khdS[S]
                if job.get("seg") is not None:
                    hcol, ntok_own, halo_ = job["seg"]
                    own0 = load_seg(hcol, ntok_own, halo_)
                    wt, bw = load_w(C_LW, 128, 3)
                    proj_fm(wt, bw, 128, own0, ntok_own, lw, b_lw, func=AF.Tanh)
                    yield
                    wt, bw = load_w(C_LA, 128, 3)
                    proj_fm(wt, bw, 128, own0, ntok_own, la, b_la)
                    yield
                if job.get("newpair"):
                    for j in range(3):
                        load_w(j * 1024 + p * 128, 128, j)
                nch = Tn // 64
                Tin = Tn + (2 * Wd if halo else 0)
                for j in range(3):
                    proj_fm(wb[j][0], wb[j][1], 128, col0, Tin, cin[j][0], cin[j][1])
                    yield
                    if halo_mask is not None:
                        hm_lo, hm_hi = halo_mask
                        if hm_lo:
                            ts(pool, cin[j][0][:, 0:Wd], cin[j][0][:, 0:Wd], colv[:, CV_HALO:CV_HALO + 1], None, ALU.mult, None,
                               [cin[j][1], b_colv], [cin[j][1]])
                        if hm_hi:
                            ts(pool, cin[j][0][:, Tin - Wd:Tin], cin[j][0][:, Tin - Wd:Tin], colv[:, CV_HALO + 1:CV_HALO + 2], None,
                               ALU.mult, None, [cin[j][1], b_colv], [cin[j][1]])
                    ti = j * 8 + p
                    i3 = cin[j][0][:, 0:Tin].rearrange("p (r w) -> p r w", w=Wd)
                    o3 = cout[j][0][:, 0:Tn].rearrange("p (r w) -> p r w", w=Wd)
                    r0 = 1 if halo else 0
                    ts(dve, o3, i3[:, r0:r0 + rows, :], colv[:, ti * 9 + 4:ti * 9 + 5], None, ALU.mult, None,
                       [cin[j][1], b_colv], [cout[j][1]])
                    for dy in ((-1, 0, 1) if halo else (0,)):
                        for dx in (-1, 0, 1):
                            if dy == 0 and dx == 0:
                                continue
                            tap = (dy + 1) * 3 + (dx + 1)
                            xo = slice(1, Wd) if dx == -1 else (slice(0, Wd - 1) if dx == 1 else slice(0, Wd))
                            xi = slice(0, Wd - 1) if dx == -1 else (slice(1, Wd) if dx == 1 else slice(0, Wd))
                            stt(o3[:, :, xo], i3[:, r0 + dy:r0 + dy + rows, xi], colv[:, ti * 9 + tap:ti * 9 + tap + 1], o3[:, :, xo],
                                ALU.mult, ALU.add, [cin[j][1], b_colv, cout[j][1]], [cout[j][1]])
                        yield
                r_, br = cout[0]
                k_, bk = cout[1]
                v_, bv_ = cout[2]
                sl = slice(0, Tn)
                acopy(vb[:, sl], v_[:, sl], [bv_], [b_vb])
                kka, bkka = A["kka"]
                kk, bkk = A["kk"]
                t1, bt1 = A["t1"]
                t2, bt2 = A["t2"]
                acopy(kka[:, sl], k_[:, sl], [bk, b_colv], [bkka], scale=colv[:, CV_KK + p:CV_KK + p + 1])
                acopy(sqb[:, sl], kka[:, sl], [bkka], [b_sqb], func=AF.Square)
                ps, bps = bank()
                mm(ps[:, 0:Tn], bones[:], sqb[:, sl], [b_bones, b_sqb], bps)
                acopy(t1[:, sl], ps[:, 0:Tn], [bps], [bps, bt1], func=AF.Sqrt, bias=1e-12)
                dve.op(lambda e: e.reciprocal(out=t2[:, sl], in_=t1[:, sl]), reads=[bt1], writes=[bt2])
                tt(dve, kk[:, sl], kka[:, sl], t2[:, sl], ALU.mult, [bkka, bt2], [bkk])
                yield
                rk, brk = A["rk"]
                for d in range(2):
                    sg, bsg = A["sg"]
                    icl, bicl = A["icl"]
                    cs, bcs = A["cs"]
                    g, bg = A["g"]
                    gx, bgx = A["gx"]
                    eng, beng = A["eng"]
                    egx, begx = A["egx"]
                    ec, bec = A["ec"]
                    bb, bbb = A["bb"]
                    kd, bkd = A["kd"]
                    eg, beg = egd[d]
                    ps, bps = bank()
                    mm(ps[:, 0:Tn], aw2[d * 64:(d + 1) * 64, p * 128:(p + 1) * 128], lw[d * 64:(d + 1) * 64, tok0:tok0 + Tn],
                       [b_aw2, b_lw], bps)
                    acopy(sg[:, sl], ps[:, 0:Tn], [bps, b_colv], [bps, bsg], func=AF.Sigmoid,
                          bias=colv[:, CV_W0 + d * 8 + p:CV_W0 + d * 8 + p + 1])
                    ps, bps = bank()
                    mm(ps[:, 0:Tn], aa2[d * 64:(d + 1) * 64, p * 128:(p + 1) * 128], la[d * 64:(d + 1) * 64, tok0:tok0 + Tn],
                       [b_aa2, b_la], bps)
                    acopy(icl[:, sl], ps[:, 0:Tn], [bps, b_colv], [bps, bicl], func=AF.Sigmoid,
                          bias=colv[:, CV_A0 + d * 8 + p:CV_A0 + d * 8 + p + 1])
                    yield
                    dve.op(lambda e: e.tensor_tensor_scan(out=cs[:, sl], data0=rmask[:, sl], data1=sg[:, sl], initial=0.0,
                                                          op0=ALU.mult, op1=ALU.add), reads=[b_rmask, bsg], writes=[bcs])
                    cs3 = cs[:, sl].rearrange("p (c t) -> p c t", t=64)
                    if d == 0:
                        gsrc, bgs = cs, bcs
                        tt(dve, gx[:, sl], cs[:, sl], sg[:, sl], ALU.subtract, [bcs, bsg], [bgx])
                    else:
                        tt(dve, gx[:, sl].rearrange("p (c t) -> p c t", t=64), cs3[:, :, 63:64].to_broadcast([128, nch, 64]), cs3,
                           ALU.subtract, [bcs], [bgx])
                        tt(dve, g[:, sl], gx[:, sl], sg[:, sl], ALU.add, [bgx, bsg], [bg])
                        gsrc, bgs = g, bg
                    acopy(eg[:, sl], gsrc[:, sl], [bgs], [beg], func=AF.Exp, scale=-KDEC)
                    acopy(eng[:, sl], gsrc[:, sl], [bgs], [beng], func=AF.Exp, scale=KDEC)
                    acopy(egx[:, sl], gx[:, sl], [bgx], [begx], func=AF.Exp, scale=-KDEC)
                    yield
                    eg3 = eg[:, sl].rearrange("p (c t) -> p c t", t=64)
                    tcol = 63 if d == 0 else 0
                    tt(dve, ec[:, sl].rearrange("p (c t) -> p c t", t=64), eng[:, sl].rearrange("p (c t) -> p c t", t=64),
                       eg3[:, :, tcol:tcol + 1].to_broadcast([128, nch, 64]), ALU.mult, [beng, beg], [bec])
                    ar, bar = ard[d]
                    stt(ar[:, 0, sl], kk[:, sl], -1.0, egx[:, sl], ALU.mult, ALU.mult, [bkk, begx], [bar])
                    tt(dve, bb[:, sl], kk[:, sl], icl[:, sl], ALU.mult, [bkk, bicl], [bbb])
                    tt(dve, btd[d][0][:, sl], bb[:, sl], eng[:, sl], ALU.mult, [bbb, beng], [btd[d][1]])
                    tt(dve, bhd[d][0][:, sl], bb[:, sl], ec[:, sl], ALU.mult, [bbb, bec], [bhd[d][1]])
                    yield
                    ts(dve, t1[:, sl], icl[:, sl], colv[:, CV_KA + p:CV_KA + p + 1], dcol[:, p:p + 1], ALU.mult, ALU.add,
                       [bicl, b_colv, b_dcol], [bt1])
                    tt(dve, kd[:, sl], t1[:, sl], k_[:, sl], ALU.mult, [bt1, bk], [bkd])
                    tt(dve, ktd[d][0][:, sl], kd[:, sl], eng[:, sl], ALU.mult, [bkd, beng], [ktd[d][1]])
                    tt(dve, khd[d][0][:, sl], kd[:, sl], ec[:, sl], ALU.mult, [bkd, bec], [khd[d][1]])
                    yield
                    tt(dve, rtd[d][0][:, sl], r_[:, sl], eg[:, sl], ALU.mult, [br, beg], [rtd[d][1]])
                    acopy(ar[:, 1, sl], rtd[d][0][:, sl], [rtd[d][1]], [bar])
                    if d == 0:
                        acopy(rk[:, sl], kd[:, sl], [bkd], [brk])
                    else:
                        tt(dve, rk[:, sl], rk[:, sl], kd[:, sl], ALU.add, [brk, bkd], [brk])
                stt(rkb[:, sl], rk[:, sl], colv[:, CV_RK + p:CV_RK + p + 1], r_[:, sl], ALU.mult, ALU.mult, [brk, b_colv, br], [b_rkb])

                yield "PREP_DONE"
                if job.get("reset"):
                    reset_acc()
                nci = min(nch, DBG.get('maxc', 9))

                def common(ci):
                    c = slice(ci * 64, (ci + 1) * 64)
                    vts, bvts = VTs[ci]
                    bo, bbo = bon[ci]
                    psV, bpsV = bank()
                    for h in range(2):
                        hs = slice(h * 64, (h + 1) * 64)
                        mm(psV[hs, 0:64], vb[hs, c], identb[hs, hs], [b_vb, b_identb], bpsV)
                    for h in range(2):
                        hs = slice(h * 64, (h + 1) * 64)
                        mm(psV[hs, 64:66], rkb[hs, c], ind2[hs, :], [b_rkb, b_ind2], bpsV)
                    acopy(vts[:], psV[:, 0:64], [bpsV], [bpsV, bvts])
                    vcopy(bo[:, :], psV[:, 64:66], [bpsV], [bpsV, bbo])
                    if spill and DBG.get('sp_bv', True):
                        for h in range(2):
                            hs = slice(h * 64, (h + 1) * 64)
                            ts(dve, bvst[hs, ci, :], psV[hs, 0:64], bo[hs, h:h + 1], None, ALU.mult, None, [bpsV, bbo],
                               [bpsV, b_bvst])

                RS = {}

                def chain(ci, d):
                    c = slice(ci * 64, (ci + 1) * 64)
                    sidx = ci * 2 + d
                    vts, bvts = VTs[ci]
                    ar, bar = ard[d]
                    tm3 = TM3all[:, sidx]
                    gxt, bgxt = GX[sidx]
                    l0, bl0 = L0[sidx]
                    bsg_ = b_stgs[sidx]
                    H2 = [slice(0, 64), slice(64, 128)]
                    ps, bps = bank()
                    for j, (X, bX) in enumerate(((ar, bar), bhd[d], khd[d])):
                        for hs in H2:
                            src = X[hs, 0, c] if j == 0 else X[hs, c]
                            mm(ps[hs, j * 64:(j + 1) * 64], src, identb[hs, hs], [bX, b_identb], bps)
                    ps2, bps2 = bank()
                    for hs in H2:
                        mm(ps2[hs, 0:128], btd[d][0][hs, c], ar[hs, :, c], [btd[d][1], bar], bps2)
                        mm(ps2[hs, 128:256], ktd[d][0][hs, c], ar[hs, :, c], [ktd[d][1], bar], bps2)
                        mm(ps2[hs, 256:320], ar[hs, 0, c], btd[d][0][hs, c], [btd[d][1], bar], bps2)
                    acopy(tm3[:, :, 0:64], ps[:, 0:192].rearrange("p (a b) -> p a b", b=64), [bps], [bps, b_TM3])
                    tt(dve, gxt[:], ps2[:, 0:256].rearrange("p (a b) -> p a b", b=128),
                       maskG[:, d:d + 1, :].to_broadcast([128, 2, 128]), ALU.mult, [bps2, b_maskG], [bps2, bgxt])
                    tt(dve, l0[:], ps2[:, 256:320], maskL[:, d, :], ALU.mult, [bps2, b_maskL], [bps2, bl0])
                    tt(dve, TTall[0][0][:, sidx, :], gxt[:, 0, 0:64], identst[:], ALU.add, [bgxt, b_identst], [TTall[0][1]])
                    yield
                    Xc, bXc = gxt[:, 0, 0:64], bgxt
                    Lc, bLc = l0[:], bl0
                    for j in range(6):
                        Tc, bTc = TTall[j % 2][0][:, sidx, :], TTall[j % 2][1]
                        if j < 5:
                            psq, bpsq = RS["bq"][sidx // 4]
                            o0_ = (sidx % 4) * 128
                            for hs in H2:
                                if j < 4:
                                    mm(psq[hs, o0_:o0_ + 64], Lc[hs], Xc[hs], [bLc, bXc], bpsq)
                                mm(psq[hs, o0_ + 64:o0_ + 128], Xc[hs], Lc[hs], [bLc, bXc], bpsq)
                        if j >= 1:
                            Tp, bTp = TTall[(j - 1) % 2][0][:, sidx, :], TTall[(j - 1) % 2][1]
                            pst, bpst = RS["bt"]
                            for hs in H2:
                                mm(pst[hs, sidx * 64:(sidx + 1) * 64], Lc[hs], Tp[hs], [bLc, bTp], bpst)
                        if j == 0:
                            psx, bpsx = RS["bx"]
                            for hs in H2:
                                mm(psx[hs, sidx * 64:(sidx + 1) * 64], gxt[hs, 1, 0:64], vts[hs], [bgxt, bvts], bpsx)
                        yield
                        if j < 5:
                            xl, bxl = XLall[j % 2]
                            Xc, bXc = xl[:, sidx, 0, :], bxl
                            Lc, bLc = xl[:, sidx, 1, :], bxl
                    Tc, bTc = TTall[5 % 2][0][:, sidx, :], TTall[5 % 2][1]
                    psa, bpsa = RS["ba"][sidx // 4]
                    o0_ = (sidx % 4) * 128
                    for hs in H2:
                        mm(psa[hs, o0_:o0_ + 128], Tc[hs], tm3[hs, 0, :], [bTc, b_TM3], bpsa)
                    yield
                    au, bau = AUall[:, sidx, :], b_AU
                    ps, bps = bank()
                    for hs in H2:
                        mm(ps[hs, 0:64], au[hs, 0:64], tm3[hs, 1, 0:64], [bau, b_TM3], bps)
                        mm(ps[hs, 64:128], au[hs, 0:64], gxt[hs, 0, 64:128], [bau, bgxt], bps)
                        if d == 1:
                            mm(ps[hs, 128:192], tm3[hs, 1, 0:64], au[hs, 0:64], [bau, b_TM3], bps)
                    ps2, bps2 = bank()
                    for hs in H2:
                        mm(ps2[hs, 0:64], tm3[hs, 1, 0:64], au[hs, 64:128], [b_TM3, bau], bps2, start=True, stop=False)
                        mm(ps2[hs, 0:64], tm3[hs, 2, 0:64], vts[hs], [b_TM3, bvts], bps2, start=False, stop=True)
                    eg3 = egd[d][0][:, sl].rearrange("p (c t) -> p c t", t=64)
                    tcol = 63 if d == 0 else 0
                    gam = eg3[:, ci, tcol:tcol + 1]
                    stt(stg[:, ci, d, 0:64], identst[:], gam, ps[:, 0:64], ALU.mult, ALU.add, [b_identst, egd[d][1], bps],
                        [bps, bsg_])
                    tt(dve, stg[:, ci, d, 64:128], ps[:, 64:128], rtd[d][0][:, c], ALU.add, [bps, rtd[d][1]], [bps, bsg_])
                    if d == 1:
                        stt(Mn[ci][0][:], identst[:], gam, ps[:, 128:192], ALU.mult, ALU.add, [b_identst, egd[d][1], bps],
                            [bps, Mn[ci][1]])
                    acopy(stg[:, ci, d, 128:192], ps2[:, 0:64], [bps2], [bps2, bsg_])
                    yield

                for ci in range(nci):
                    common(ci)
                gens = [chain(ci, d) for ci in range(nci) for d in range(2)]
                ns_ = len(gens)
                nh_ = (ns_ + 3) // 4
                adv = DBG.get('adv', 3)
                for g_ in gens:
                    next(g_)
                adv_extra(adv)
                for j in range(6):
                    if j < 5:
                        RS["bq"] = [bank() for _ in range(nh_)]
                    if j >= 1:
                        RS["bt"] = bank()
                    if j == 0:
                        RS["bx"] = bank()
                    for g_ in gens:
                        next(g_)
                    if j < 5:
                        xl, bxl = XLall[j % 2]
                        for hf in range(nh_):
                            n4 = min(4, ns_ - hf * 4)
                            psq, bpsq = RS["bq"][hf]
                            if j < 4:
                                acopy(xl[:, hf * 4:hf * 4 + n4, :, :].rearrange("p s a b -> p s (a b)"),
                                      psq[:, 0:n4 * 128].rearrange("p (s f) -> p s f", f=128), [bpsq], [bpsq, bxl])
                            else:
                                acopy(xl[:, hf * 4:hf * 4 + n4, 1, :], psq[:, 0:n4 * 128].rearrange("p (s f) -> p s f", f=128)[:, :, 64:128],
                                      [bpsq], [bpsq, bxl])
                    if j >= 1:
                        pst, bpst = RS["bt"]
                        tt(dve, TTall[j % 2][0][:, 0:ns_, :], pst[:, 0:ns_ * 64].rearrange("p (s f) -> p s f", f=64),
                           TTall[(j - 1) % 2][0][:, 0:ns_, :], ALU.add, [bpst, TTall[(j - 1) % 2][1]], [bpst, TTall[j % 2][1]])
                    if j == 0:
                        psx, bpsx = RS["bx"]
                        acopy(TM3all[:, 0:ns_, 0, 64:128], psx[:, 0:ns_ * 64].rearrange("p (s f) -> p s f", f=64), [bpsx], [bpsx, b_TM3])
                    adv_extra(adv)
                RS["ba"] = [bank() for _ in range(nh_)]
                for g_ in gens:
                    next(g_)
                for hf in range(nh_):
                    n4 = min(4, ns_ - hf * 4)
                    psa, bpsa = RS["ba"][hf]
                    acopy(AUall[:, hf * 4:hf * 4 + n4, :], psa[:, 0:n4 * 128].rearrange("p (s f) -> p s f", f=128), [bpsa], [bpsa, b_AU])
                adv_extra(adv)
                for g_ in gens:
                    next(g_)
                for g_ in gens:
                    for _ in g_:
                        pass
                adv_extra(adv)
                for ci in range(nci):
                    for d in range(2):
                        bsg_ = b_stgs[ci * 2 + d]
                        ACC = ACCf if d == 0 else ACCb
                        ao, bao = ACC[p][accsel[p][d]]
                        an, ban = ACC[p][1 - accsel[p][d]]
                        accsel[p][d] = 1 - accsel[p][d]
                        ps, bps = bank()
                        if d == 0:
                            for h in range(2):
                                hs = slice(h * 64, (h + 1) * 64)
                                mm(ps[hs, 0:128], stg[hs, ci, 0, 0:64], ao[hs, :], [bsg_, bao], bps)
                            acopy(an[:, 0:64], ps[:, 0:64], [bps], [bps, ban])
                            tt(dve, an[:, 64:128], ps[:, 64:128], stg[:, ci, 0, 128:192], ALU.add, [bps, bsg_], [bps, ban])
                        else:
                            mn_, bmn_ = Mn[ci]
                            for h in range(2):
                                hs = slice(h * 64, (h + 1) * 64)
                                mm(ps[hs, 0:64], mn_[hs], ao[hs, 0:64], [bmn_, bao], bps)
                                mm(ps[hs, 64:128], ao[hs, 0:64], stg[hs, ci, 1, 128:192], [bsg_, bao], bps)
                            acopy(an[:, 0:64], ps[:, 0:64], [bps], [bps, ban])
                            tt(dve, an[:, 64:128], ps[:, 64:128], ao[:, 64:128], ALU.add, [bps, bao], [bps, ban])
                    if spill and DBG.get('sp_y0', True):
                        vts, bvts = VTs[ci]
                        g0, bg0 = GX[ci * 2]
                        g1, bg1 = GX[ci * 2 + 1]
                        a0, ba0 = AUall[:, ci * 2, :], b_AU
                        a1, ba1 = AUall[:, ci * 2 + 1, :], b_AU
                        ps, bps = bank()
                        for h in range(2):
                            hs = slice(h * 64, (h + 1) * 64)
                            o_ = ps[hs, 0:64]
                            rds = [bg0, bg1, ba0, ba1, bvts]
                            mm(o_, g0[hs, 0, 64:128], a0[hs, 64:128], rds, bps, start=True, stop=False)
                            mm(o_, g0[hs, 1, 64:128], vts[hs], rds, bps, start=False, stop=False)
                            mm(o_, g1[hs, 0, 64:128], a1[hs, 64:128], rds, bps, start=False, stop=False)
                            mm(o_, g1[hs, 1, 64:128], vts[hs], rds, bps, start=False, stop=True)
                        acopy(y0st[:, ci, :], ps[:, 0:64], [bps], [bps, b_y0st])
                if spill and DBG.get('sp_dma', True):
                    for d in range(2):
                        act.dma(spA[d, cg0:cg0 + nch, :, p * 192:(p + 1) * 192].rearrange("c p f -> p c f"), stg[:, 0:nch, d, :],
                               reads=[b_stg] + b_stgs, writes=[bD], sem_buf=b_stg)
                    tk0 = cg0 * 64
                    act.dma(y0d[cg0:cg0 + nch, :, p, :].rearrange("c q v -> q c v"), y0st[:, 0:nch, :],
                           reads=[b_y0st], writes=[bD], sem_buf=b_y0st)
                    act.dma(bvd[cg0:cg0 + nch, :, p, :].rearrange("c q v -> q c v"), bvst[:, 0:nch, :],
                           reads=[b_bvst], writes=[bD], sem_buf=b_bvst)
                if job.get("ctx_last"):
                    for d, ACC in ((0, ACCf), (1, ACCb)):
                        a_, ba_ = ACC[p][accsel[p][d]]
                        vcopy(Hctx[p][0][:, d, :], a_[:, 64:128], [ba_], [Hctx[p][1]])

            segs = [("ctx", 4224, 256, 1, 256, False, 1, 256)] + [(f"q{i}", i * 1024, 1024, 4, 64, True, 4, 256) for i in range(4)]

            def load_seg(hcol, ntok_own, halo):
                ncols_h = ntok_own + (128 if halo else 0)
                sp.dma(hT[:, :, 0:ncols_h], hTd[:, :, hcol:hcol + ncols_h], reads=[bD], writes=[b_hT])
                return 64 if halo else 0

            with ExitStack() as s1a:
                cur_st[0] = s1a
                VTs = [T(f"VTs{i}", [128, 64], BF16) for i in range(4)]
                bon = [T(f"bon{i}", [128, 2]) for i in range(4)]
                TM3all, b_TM3 = T("TM3all", [128, 8, 3, 128], BF16)
                GX = [T(f"GX{i}", [128, 2, 128], BF16) for i in range(8)]
                AUall, b_AU = T("AUall", [128, 8, 128], BF16)
                L0 = [T(f"L0{i}", [128, 64], BF16) for i in range(8)]
                XLall = [T(f"XLall{j}", [128, 8, 2, 64], BF16) for j in range(2)]
                TTall = [T(f"TTall{j}", [128, 8, 64], BF16) for j in range(2)]
                Mn = [T(f"Mn{i}", [128, 64]) for i in range(4)]
                b_stgs = [Buf(f"stg{i}") for i in range(8)]
                stg, b_stg = T("stg", [128, 4, 2, 192])
                y0st, b_y0st = T("y0st", [128, 4, 64])
                bvst, b_bvst = T("bvst", [128, 4, 64])
                ACCf = [[T(f"ACCf{p}_{j}", [128, 128]) for j in range(2)] for p in range(8)]
                ACCb = [[T(f"ACCb{p}_{j}", [128, 128]) for j in range(2)] for p in range(8)]
                accsel = [[0, 0] for _ in range(8)]
                jobs = []
                for sname, hcol, ntok_own, rows, Wd, halo, nb, Tn in segs[:DBG.get('nsegs', 5)]:
                    for p in range(DBG.get('maxp', 8)):
                        for b in range(min(nb, DBG.get('maxb', 9))):
                            if sname == "ctx":
                                job = dict(args=(p, 0, 0, 256, 1, 256, False, 0, False, None), ctx_last=True)
                            else:
                                qi = int(sname[1])
                                hm = (qi == 0 and b == 0, qi == 3 and b == 3)
                                job = dict(args=(p, b * 256, b * 256, 256, 4, 64, True, qi * 16 + b * 4, DBG.get("spill", True), hm))
                            if p == 0 and b == 0:
                                job["seg"] = (hcol, ntok_own, halo)
                                if sname in ("ctx", "q0"):
                                    job["reset"] = True
                            if b == 0:
                                job["newpair"] = True
                            jobs.append(job)
                gens_j = [rwkv_batch(job, i % 2) for i, job in enumerate(jobs)]

                def run_prep(g_):
                    while next(g_) != "PREP_DONE":
                        pass

                run_prep(gens_j[0])
                for i in range(len(jobs)):
                    if i + 1 < len(jobs):
                        PIPE["extra"] = gens_j[i + 1]
                        PIPE["extra_done"] = False
                    else:
                        PIPE["extra"] = None
                        PIPE["extra_done"] = True
                    if not DBG.get("pipe", True) and PIPE["extra"] is not None:
                        pass
                    for _ in gens_j[i]:
                        pass
                    if PIPE["extra"] is not None and not PIPE["extra_done"]:
                        run_prep(PIPE["extra"])
                        PIPE["extra_done"] = True
                ci_ap = comp_in[0].ap()
                for p in range(8):
                    for d, ACC in ((0, ACCf), (1, ACCb)):
                        a_, ba_ = ACC[p][accsel[p][d]]
                        c0 = (p * 2 + d) * 128
                        if d == 0:
                            ps, bps = bank()
                            for h in range(2):
                                hs = slice(h * 64, (h + 1) * 64)
                                mm(ps[hs, 0:64], a_[hs, 0:64], identf[hs, hs], [ba_, b_identf], bps)
                            mn_, bmn_ = Mn[p % 4]
                            vcopy(mn_[:], ps[:, 0:64], [bps], [bps, bmn_])
                            act.dma(ci_ap[:, c0:c0 + 64], mn_[:], reads=[bmn_], writes=[bD], sem_buf=bmn_)
                            act.dma(ci_ap[:, c0 + 64:c0 + 128], a_[:, 64:128], reads=[ba_], writes=[bD], sem_buf=ba_)
                        else:
                            act.dma(ci_ap[:, c0:c0 + 128], a_[:, :], reads=[ba_], writes=[bD], sem_buf=ba_)
                fw.barrier()
                fw.emit()
            with ExitStack() as s1b:
                cur_st[0] = s1b
                stGS = [T(f"stG{i}", [128, 4, 2, 321]) for i in range(2)]
                o0st, b_o0st = T("o0st", [128, 2, 256])
                VTg = [T(f"VTg{i}", [64, 256], BF16) for i in range(4)]
                AttT = [T(f"AttT{i}", [64, 64], BF16) for i in range(8)]
                KhT = [T(f"KhT{i}", [64, 128], BF16) for i in range(8)]
                b_GNs = [Buf(f"GN{i}") for i in range(8)]
                GN, b_GN = T("GN", [128, 8, 256])
                GD, b_GD = T("GD", [128, 8])
                gjobs = []
                for sname, hcol, ntok_own, rows, Wd, halo, nb, Tn in segs[:DBG.get('nsegs', 5)]:
                    for g in range(DBG.get('maxg', 4)):
                        for b in range(nb):
                            if sname == "ctx":
                                job = dict(args=(g, 0, 0, 256, 0, False))
                                if g == 3:
                                    job["ctx_last"] = True
                            else:
                                qi = int(sname[1])
                                job = dict(args=(g, 64 + b * 256, b * 256, 256, qi * 16 + b * 4, True))
                            if g == 0 and b == 0:
                                job["seg"] = (hcol, ntok_own, halo)
                                if sname in ("ctx", "q0"):
                                    job["reset"] = True
                            if b == 0:
                                job["newhead"] = True
                            gjobs.append(job)
                gens_g = [gla_batch(job, i % 2) for i, job in enumerate(gjobs)]
                run_prep(gens_g[0])
                for i in range(len(gjobs)):
                    if i + 1 < len(gjobs):
                        PIPE["extra"] = gens_g[i + 1]
                        PIPE["extra_done"] = False
                    else:
                        PIPE["extra"] = None
                        PIPE["extra_done"] = True
                    for _ in gens_g[i]:
                        pass
                    if PIPE["extra"] is not None and not PIPE["extra_done"]:
                        run_prep(PIPE["extra"])
                        PIPE["extra_done"] = True
                act.dma(comp_in[1].ap()[:, 0:8], GD[:, :], reads=[b_GD], writes=[bD], sem_buf=b_GD)
                act.dma(comp_in[1].ap()[:, 8:1032], GN[:, 0:4, :].rearrange("p g f -> p (g f)"), reads=[b_GN] + b_GNs, writes=[bD], sem_buf=b_GN)
                act.dma(comp_in[2].ap()[:, :], GN[:, 4:8, :].rearrange("p g f -> p (g f)"), reads=[b_GN] + b_GNs, writes=[bD], sem_buf=b_GN)
                fw.barrier()
                fw.emit()
            pass
        with ExitStack() as s1c:
            def T(name, shape, dt=F32):
                return fw.sbuf(s1c, name, shape, dt)
            hT2, b_hT2 = T("hT2", [128, 16, 2048], BF16)
            w5f = [T(f"w5f{j}", [128, 16, 512]) for j in range(2)]
            w5b = [T(f"w5b{j}", [128, 16, 512], BF16) for j in range(2)]
            zst = [T(f"zst{j}", [128, 2048], BF16) for j in range(2)]
            gst = [T(f"gst{j}", [128, 512], BF16) for j in range(3)]
            zblocks = [(C_Z_A, 0), (C_Z_A + 512, 4), (C_ZB, 8), (C_ZB + 512, 12)]
            nw = 0
            ng_ = 0
            for half in range(2):
                sp.dma(hT2[:], hTd[:, :, 64 + half * 2048:64 + (half + 1) * 2048], reads=[bD], writes=[b_hT2])
                for colw, blk0 in zblocks:
                    wt, bw = w5f[nw % 2]
                    wbf, bwb = w5b[nw % 2]
                    nw += 1
                    sp.dma(wt[:], w_in_v[:, :, colw:colw + 512], writes=[bw])
                    for k4 in range(4):
                        eng_ = pool if k4 % 2 == 0 else act
                        if eng_ is pool:
                            pool.op(lambda e, wbf=wbf, wt=wt, k4=k4: e.tensor_copy(out=wbf[:, k4 * 4:(k4 + 1) * 4, :], in_=wt[:, k4 * 4:(k4 + 1) * 4, :]),
                                    reads=[bw], writes=[bwb])
                        else:
                            acopy(wbf[:, k4 * 4:(k4 + 1) * 4, :], wt[:, k4 * 4:(k4 + 1) * 4, :], [bw], [bwb])
                    for sb_ in range(4):
                        zt, bzt = zst[(blk0 + sb_) % 2]
                        for tb in range(4):
                            ps, bps = bank()
                            for k in range(16):
                                mm(ps[:, :], wbf[:, k, sb_ * 128:(sb_ + 1) * 128], hT2[:, k, tb * 512:(tb + 1) * 512], [bwb, b_hT2], bps,
                                   start=(k == 0), stop=(k == 15))
                            acopy(zt[:, tb * 512:(tb + 1) * 512], ps[:, :], [bps], [bps, bzt], func=AF.Silu)
                        act.dma(zaTd[blk0 + sb_, :, half * 2048:(half + 1) * 2048], zt[:, :], reads=[bzt], writes=[bD], sem_buf=bzt)
                for cb in range(8):
                    wt, bw = w5f[nw % 2]
                    wbf, bwb = w5b[nw % 2]
                    nw += 1
                    sp.dma(wt[:], w_in_v[:, :, C_GA + cb * 512:C_GA + (cb + 1) * 512], writes=[bw])
                    for k4 in range(4):
                        if k4 % 2 == 0:
                            pool.op(lambda e, wbf=wbf, wt=wt, k4=k4: e.tensor_copy(out=wbf[:, k4 * 4:(k4 + 1) * 4, :], in_=wt[:, k4 * 4:(k4 + 1) * 4, :]),
                                    reads=[bw], writes=[bwb])
                        else:
                            vcopy(wbf[:, k4 * 4:(k4 + 1) * 4, :], wt[:, k4 * 4:(k4 + 1) * 4, :], [bw], [bwb])
                    for tl in range(16):
                        ps, bps = bank()
                        for k in range(16):
                            mm(ps[:, :], hT2[:, k, tl * 128:(tl + 1) * 128], wbf[:, k, :], [b_hT2, bwb], bps, start=(k == 0), stop=(k == 15))
                        gt, bgt = gst[ng_ % 3]
                        ng_ += 1
                        acopy(gt[:], ps[:, :], [bps], [bps, bgt], func=AF.Sigmoid)
                        r0 = half * 2048 + tl * 128
                        act.dma(zgd[r0:r0 + 128, cb * 512:(cb + 1) * 512], gt[:], reads=[bgt], writes=[bD], sem_buf=bgt)
            fw.barrier()
            fw.emit()
        if phases <= 1:
            return nc, fw

        Hf0, b_Hf0 = fw.sbuf(top, "Hf0", [128, 8, 64])
        Hb0, b_Hb0 = fw.sbuf(top, "Hb0", [128, 8, 64])
        Hgf0, b_Hgf0 = fw.sbuf(top, "Hgf0", [128, 4, 256])
        Hgb0, b_Hgb0 = fw.sbuf(top, "Hgb0", [128, 4, 256])
        identstb, b_identstb = fw.sbuf(top, "identstb", [128, 64], BF16)
        vcopy(identstb[:], identst[:], [b_identst], [b_identstb])

        with ExitStack() as sE:
            gath, b_gath = fw.sbuf(sE, "gath", [128, 4, NCOMP])
            Hx = [fw.sbuf(sE, f"Hx{j}", [128, 8, 64]) for j in range(2)]
            t1x, b_t1x = fw.sbuf(sE, "t1x", [128, 8, 64])
            Gx = [fw.sbuf(sE, f"Gx{j}", [128, 4, 256]) for j in range(2)]
            t2x, b_t2x = fw.sbuf(sE, "t2x", [128, 4, 256])
            if DBG.get("nocc", False):
                for i in range(3):
                    for r in range(4):
                        sp.dma(comp_out[i].ap()[r * 128:(r + 1) * 128, :], comp_in[i].ap(), reads=[bD], writes=[bD], sem_buf=b_gath)
                bCO = bD
            else:
                cc_sem = fw.new_sem("cc")
                pool._wait(dict(bD.w))
                for i in range(3):
                    pool.prog.append(("o", (lambda e, i=i: e.collective_compute(
                        "AllGather", ALU.bypass, replica_groups=[[0, 1, 2, 3], [4, 5, 6, 7]],
                        ins=[comp_in[i].ap()], outs=[comp_out[i].ap()])), cc_sem, 1))
                bCO = Buf("comp_out", dram=True)
                bCO.w = {cc_sem: 3}
            for i, (ca, cb_) in enumerate(CSPL):
                sp.dma(gath[:, :, ca:cb_], comp_out[i].ap().rearrange("(r p) n -> p r n", p=128), reads=[bCO], writes=[b_gath])
            for d in range(2):
                cur, bcur = Hx[0]
                for p in range(8):
                    vcopy(cur[:, p, :], Hctx[p][0][:, d, :], [Hctx[p][1]], [bcur])
                order = (0, 1, 2) if d == 0 else (3, 2, 1)
                for i, sgi in enumerate(order):
                    nxt, bnxt = (Hx[(i + 1) % 2]) if i < 2 else ((Hf0, b_Hf0) if d == 0 else (Hb0, b_Hb0))
                    ps, bps = bank()
                    for p in range(8):
                        c0 = (p * 2 + d) * 128
                        for h in range(2):
                            hs = slice(h * 64, (h + 1) * 64)
                            mm(ps[hs, p * 64:(p + 1) * 64], gath[hs, sgi, c0:c0 + 64], cur[hs, p, :], [b_gath, bcur], bps)
                    nview = gath[:, sgi, 0:2048].rearrange("q (p d f) -> q p d f", d=2, f=128)[:, :, d, 64:128]
                    tt(dve, t1x[:], ps[:, :].rearrange("q (p f) -> q p f", f=64), nview, ALU.add, [bps, b_gath], [bps, b_t1x])
                    tt(dve, t1x[:], t1x[:], cur[:], ALU.subtract, [b_t1x, bcur], [b_t1x])
                    mcol = CV_FM + d * 4 + sgi
                    stt(nxt[:], t1x[:], colv[:, mcol:mcol + 1], cur[:], ALU.mult, ALU.add, [b_t1x, b_colv, bcur], [bnxt])
                    cur, bcur = nxt, bnxt
                cur, bcur = Gx[0]
                for g in range(4):
                    vcopy(cur[:, g, :], Hgctx[:, g * 2 + d, :], [b_Hgctx], [bcur])
                for i, sgi in enumerate(order):
                    nxt, bnxt = (Gx[(i + 1) % 2]) if i < 2 else ((Hgf0, b_Hgf0) if d == 0 else (Hgb0, b_Hgb0))
                    for g in range(4):
                        gi = g * 2 + d
                        stt(t2x[:, g, :], cur[:, g, :], gath[:, sgi, 2048 + gi:2048 + gi + 1],
                            gath[:, sgi, 2056 + gi * 256:2056 + (gi + 1) * 256], ALU.mult, ALU.add, [bcur, b_gath], [b_t2x])
                    tt(dve, t2x[:], t2x[:], cur[:], ALU.subtract, [b_t2x, bcur], [b_t2x])
                    mcol = CV_FM + d * 4 + sgi
                    stt(nxt[:], t2x[:], colv[:, mcol:mcol + 1], cur[:], ALU.mult, ALU.add, [b_t2x, b_colv, bcur], [bnxt])
                    cur, bcur = nxt, bnxt
            fw.barrier()
            fw.emit()
        if phases <= 2:
            return nc, fw

        def passB(stk, d, c, Hc, bHc, Hn, bHn, Gc, bGc, Gn_, bGn, stA_t, b_stA, stG_t, b_stG2):
            sp.dma(stA_t[:], spA[d, c], reads=[bD], writes=[b_stA])
            sp.dma(stG_t[:], spG[d, c], reads=[bD], writes=[b_stG2])
            par = c % 2
            pp = slice(par * 64, (par + 1) * 64)
            psH, bpsH = bank()
            psY, bpsY = bank()
            for p in range(8):
                for h in range(2):
                    hs = slice(h * 64, (h + 1) * 64)
                    mm(psH[hs, p * 64:(p + 1) * 64], stA_t[hs, p * 192:p * 192 + 64], Hc[hs, p, :], [b_stA, bHc], bpsH)
            nview = stA_t[:, :].rearrange("q (p f) -> q p f", f=192)[:, :, 128:192]
            tt(dve, Hn[:], psH[:, :].rearrange("q (p f) -> q p f", f=64), nview, ALU.add, [bpsH, b_stA], [bpsH, bHn])
            for g in range(4):
                stt(Gn_[:, g, :], Gc[:, g, :], stG_t[:, g * 321 + 320:g * 321 + 321], stG_t[:, g * 321 + 64:g * 321 + 320],
                    ALU.mult, ALU.add, [bGc, b_stG2], [bGn])
            for p in range(8):
                for h in range(2):
                    hs = slice(h * 64, (h + 1) * 64)
                    mm(psY[hs, p * 64:(p + 1) * 64], stA_t[hs, p * 192 + 64:p * 192 + 128], Hc[hs, p, :], [b_stA, bHc], bpsY)
            psO = [bank(), bank()]
            for g in range(4):
                po, bpo = psO[g // 2]
                mm(po[pp, (g % 2) * 256:(g % 2 + 1) * 256], stG_t[:, g * 321:g * 321 + 64], Gc[:, g, :], [b_stG2, bGc], bpo)
            return (psY, bpsY), psO, pp

        sW = ExitStack()
        wa_bf, b_wa = fw.sbuf(sW, "wa_bf", [128, 8, D], BF16)
        wb_bf, b_wbb = fw.sbuf(sW, "wb_bf", [128, 8, D], BF16)
        with ExitStack() as s2:
            wstW = [fw.sbuf(s2, f"wstW{j}", [128, D]) for j in range(2)]
            nwl = 0
            for (wsrc, wdst, bdst) in ((w_a, wa_bf, b_wa), (w_b, wb_bf, b_wbb)):
                for p in range(8):
                    wst_, b_wst_ = wstW[nwl % 2]
                    nwl += 1
                    sp.dma(wst_[:], wsrc[p * 128:(p + 1) * 128, :], writes=[b_wst_])
                    pool.op(lambda e, wdst=wdst, p=p, wst_=wst_: e.tensor_copy(out=wdst[:, p, :], in_=wst_[:]), reads=[b_wst_], writes=[bdst])
            def T2(name, shape, dt=F32):
                return fw.sbuf(s2, name, shape, dt)
            stA_b = [T2(f"stAb{j}", [128, 1536]) for j in range(2)]
            stG_b = [T2(f"stGb{j}", [128, 1284]) for j in range(2)]
            Hb = [T2(f"Hbk{j}", [128, 8, 64]) for j in range(2)]
            Gb = [T2(f"Gbk{j}", [128, 4, 256]) for j in range(2)]
            y0_t = [T2(f"y0t{j}", [128, 512]) for j in range(2)]
            yp_o = [T2(f"ypo{j}", [128, 512]) for j in range(2)]
            o0_t = [T2(f"o0t{j}", [128, 1024]) for j in range(2)]
            op_o = [T2(f"opo{j}", [128, 1024]) for j in range(2)]
            vcopy(Hb[0][0][:], Hb0[:], [b_Hb0], [Hb[0][1]])
            vcopy(Gb[0][0][:], Hgb0[:], [b_Hgb0], [Gb[0][1]])
            for i, c in enumerate(range(NCH - 1, -1, -1)):
                j = c // 2
                Hc, bHc = Hb[i % 2]
                Hn, bHn = Hb[(i + 1) % 2]
                Gc, bGc = Gb[i % 2]
                Gn_, bGn = Gb[(i + 1) % 2]
                (psY, bpsY), psO, pp = passB(s2, 1, c, Hc, bHc, Hn, bHn, Gc, bGc, Gn_, bGn, stA_b[i % 2][0], stA_b[i % 2][1],
                                             stG_b[i % 2][0], stG_b[i % 2][1])
                yt, byt = y0_t[i % 2]
                sp.dma(yt[:], y0d[c].rearrange("q p v -> q (p v)"), reads=[bD], writes=[byt])
                yo, byo = yp_o[i % 2]
                tt(dve, yo[:], psY[:, :], yt[:], ALU.add, [bpsY, byt], [bpsY, byo])
                act.dma(ypd[c].rearrange("q p v -> q (p v)"), yo[:], reads=[byo], writes=[bD], sem_buf=byo)
                ot, bot = o0_t[j % 2]
                oo, boo = op_o[j % 2]
                if c % 2 == 1:
                    sp.dma(ot[:], o0d[j * 128:(j + 1) * 128, :], reads=[bD], writes=[bot])
                for hb_ in range(2):
                    po, bpo = psO[hb_]
                    tt(dve, oo[pp, hb_ * 512:(hb_ + 1) * 512], po[pp, :], ot[pp, hb_ * 512:(hb_ + 1) * 512], ALU.add, [bpo, bot],
                       [bpo, boo])
                if c % 2 == 0:
                    act.dma(opd[j * 128:(j + 1) * 128, :], oo[:], reads=[boo], writes=[bD], sem_buf=boo)
            fw.barrier()
            fw.emit()
        if phases <= 3:
            return nc, fw

        with ExitStack() as s3:
            def T3(name, shape, dt=F32):
                return fw.sbuf(s3, name, shape, dt)
            lnxg_st, b_lnxg = T3("lnxg_st", [128, 8, 64])
            lnxb_st, b_lnxb = T3("lnxb_st", [128, 8, 64])
            bng_b, b_bng = T3("bng_b", [128, 4, 256])
            for h in range(2):
                hs = slice(h * 64, (h + 1) * 64)
                sp.dma(lnxg_st[hs, :, :], bass.AP(lnxg, h * 64, [[0, 64], [128, 8], [1, 64]]), writes=[b_lnxg])
                sp.dma(lnxb_st[hs, :, :], bass.AP(lnxb, h * 64, [[0, 64], [128, 8], [1, 64]]), writes=[b_lnxb])
            sp.dma(bng_b[:], bass.AP(bng, 0, [[0, 128], [0, 4], [1, 256]]), writes=[b_bng])
            stA_f = [T3(f"stAf{j}", [128, 1536]) for j in range(2)]
            stG_f = [T3(f"stGf{j}", [128, 1284]) for j in range(2)]
            Hf = [T3(f"Hfk{j}", [128, 8, 64]) for j in range(2)]
            Gf = [T3(f"Gfk{j}", [128, 4, 256]) for j in range(2)]
            yp_t = [T3(f"ypt{j}", [128, 512]) for j in range(2)]
            bv_t = [T3(f"bvt{j}", [128, 512]) for j in range(2)]
            w1S = [T3(f"w1_{j}", [128, 512]) for j in range(2)]
            w2S = [T3(f"w2_{j}", [128, 512]) for j in range(2)]
            stS = [T3(f"st_{j}", [128, 64]) for j in range(2)]
            ya_bdS = [T3(f"ya_bd{j}", [128, 8, 128], BF16) for j in range(2)]
            yaTS = [T3(f"yaT{j}", [128, 8, 128], BF16) for j in range(2)]
            ybTS = [T3(f"ybT{j}", [128, 8, 128], BF16) for j in range(1)] * 2
            zaT_t = [T3(f"zaTt{j}", [128, 16, 128], BF16) for j in range(2)]
            op_tS = [T3(f"op_t{j}", [128, 1024]) for j in range(1)] * 2
            obS = [T3(f"ob{j}", [128, 1024]) for j in range(2)]
            tmpb, b_tmpb = T3("tmpb", [128, 1024])
            yb_bfS = [T3(f"yb_bf{j}", [128, 1024], BF16) for j in range(1)] * 2
            zg_t, b_zgt = T3("zg_t", [128, 4096], BF16)
            m1, b_m1 = T3("m1", [128, 512])
            m2, b_m2 = T3("m2", [128, 512])
            mixedS = [T3(f"mixed{j}", [128, D], BF16) for j in range(1)] * 2
            mT_t = [T3(f"mTt{j}", [128, 16, 128], BF16) for j in range(1)] * 2
            for j_ in range(2):
                pool.op(lambda e, j_=j_: e.memset(ya_bdS[j_][0][:], 0.0), writes=[ya_bdS[j_][1]])
            vcopy(Hf[0][0][:], Hf0[:], [b_Hf0], [Hf[0][1]])
            vcopy(Gf[0][0][:], Hgf0[:], [b_Hgf0], [Gf[0][1]])
            b3 = lambda ap, n, m: ap.to_broadcast([128, n, m])
            for c in range(NCH):
                j = c // 2
                par = c % 2
                Hc, bHc = Hf[c % 2]
                Hn, bHn = Hf[(c + 1) % 2]
                Gc, bGc = Gf[c % 2]
                Gn_, bGn = Gf[(c + 1) % 2]
                zt_, bzt_ = zaT_t[j % 2]
                w1, b_w1 = w1S[c % 2]
                w2, b_w2 = w2S[c % 2]
                st, b_st = stS[c % 2]
                ya_bd, b_yabd = ya_bdS[c % 2]
                yaT, b_yaT = yaTS[j % 2]
                ybT, b_ybT = ybTS[j % 2]
                op_t, b_opt = op_tS[j % 2]
                ob, b_ob = obS[j % 2]
                yb_bf, b_ybbf = yb_bfS[j % 2]
                mixed, b_mixed = mixedS[j % 2]
                if par == 0:
                    sp.dma(zt_[:], zaTd[:, :, j * 128:(j + 1) * 128].rearrange("b p t -> p b t"), reads=[bD], writes=[bzt_])
                    sp.dma(op_t[:], opd[j * 128:(j + 1) * 128, :], reads=[bD], writes=[b_opt])
                    sp.dma(zg_t[:], zgd[j * 128:(j + 1) * 128, :], reads=[bD], writes=[b_zgt])
                (psY, bpsY), psO, pp = passB(s3, 0, c, Hc, bHc, Hn, bHn, Gc, bGc, Gn_, bGn, stA_f[c % 2][0], stA_f[c % 2][1],
                                             stG_f[c % 2][0], stG_f[c % 2][1])
                ypt, bypt = yp_t[c % 2]
                bvt, bbvt = bv_t[c % 2]
                sp.dma(ypt[:], ypd[c].rearrange("q p v -> q (p v)"), reads=[bD], writes=[bypt])
                sp.dma(bvt[:], bvd[c].rearrange("q p v -> q (p v)"), reads=[bD], writes=[bbvt])
                tt(dve, w1[:], psY[:, :], ypt[:], ALU.add, [bpsY, bypt], [bpsY, b_w1])
                for hb_ in range(2):
                    po, bpo = psO[hb_]
                    tt(dve, ob[pp, hb_ * 512:(hb_ + 1) * 512], po[pp, :], op_t[pp, hb_ * 512:(hb_ + 1) * 512], ALU.add, [bpo, b_opt],
                       [bpo, b_ob])
                w13 = w1[:].rearrange("q (p v) -> q p v", v=64)
                treduce(st[:, 0:8], w13, [b_w1], [b_st])
                acopy(w2[:], w1[:], [b_w1], [b_w2], func=AF.Square)
                treduce(st[:, 8:16], w2[:].rearrange("q (p v) -> q p v", v=64), [b_w2], [b_st])
                ts(dve, st[:, 16:24], st[:, 0:8], 1.0 / 64, None, ALU.mult, None, [b_st], [b_st])
                tt(dve, st[:, 24:32], st[:, 16:24], st[:, 16:24], ALU.mult, [b_st], [b_st])
                stt(st[:, 32:40], st[:, 8:16], 1.0 / 64, st[:, 24:32], ALU.mult, ALU.subtract, [b_st], [b_st])
                acopy(st[:, 40:48], st[:, 32:40], [b_st], [b_st], func=AF.Sqrt, bias=64e-5)
                recip(st[:, 48:56], st[:, 40:48], [b_st], [b_st])
                tt(dve, w13, w13, b3(st[:, 16:24].rearrange("q (p o) -> q p o", o=1), 8, 64), ALU.subtract, [b_w1, b_st], [b_w1])
                tt(dve, w13, w13, b3(st[:, 48:56].rearrange("q (p o) -> q p o", o=1), 8, 64), ALU.mult, [b_w1, b_st], [b_w1])
                tt(pool, w13, w13, lnxg_st[:], ALU.mult, [b_w1, b_lnxg], [b_w1])
                tt(pool, w13, w13, lnxb_st[:], ALU.add, [b_w1, b_lnxb], [b_w1])
                for h in range(2):
                    hs = slice(h * 64, (h + 1) * 64)
                    tt(dve if h == 0 else pool, ya_bd[hs, :, h * 64:(h + 1) * 64], w1[hs, :].rearrange("q (p v) -> q p v", v=64),
                       bvt[hs, :].rearrange("q (p v) -> q p v", v=64), ALU.add, [b_w1, bbvt], [b_yabd])
                psT, bpsT = bank()
                for p in range(8):
                    mm(psT[:, p * 64:(p + 1) * 64], ya_bd[:, p, :], identstb[:], [b_yabd, b_identstb], bpsT)
                tt(dve, yaT[:, :, par * 64:(par + 1) * 64], psT[:, :].rearrange("q (p t) -> q p t", t=64),
                   zt_[:, 0:8, par * 64:(par + 1) * 64], ALU.mult, [bpsT, bzt_], [bpsT, b_yaT])
                if par == 0:
                    continue
                acopy(tmpb[:], ob[:], [b_ob], [b_tmpb], func=AF.Square)
                treduce(st[:, 56:60], tmpb[:].rearrange("q (g v) -> q g v", v=256), [b_tmpb], [b_st])
                ts(dve, st[:, 60:64], st[:, 56:60], 1.0 / 256, None, ALU.mult, None, [b_st], [b_st])
                acopy(st[:, 56:60], st[:, 60:64], [b_st], [b_st], func=AF.Sqrt, bias=1e-6)
                recip(st[:, 60:64], st[:, 56:60], [b_st], [b_st])
                ob3 = ob[:].rearrange("q (g v) -> q g v", v=256)
                tt(dve, ob3, ob3, b3(st[:, 60:64].rearrange("q (g o) -> q g o", o=1), 4, 256), ALU.mult, [b_ob, b_st], [b_ob])
                tt(pool, yb_bf[:].rearrange("q (g v) -> q g v", v=256), ob3, bng_b[:], ALU.mult, [b_ob, b_bng], [b_ybbf])
                for k4 in range(2):
                    psT, bpsT = bank()
                    for kk in range(4):
                        k = k4 * 4 + kk
                        mm(psT[:, kk * 128:(kk + 1) * 128], yb_bf[:, k * 128:(k + 1) * 128], identb[:], [b_ybbf, b_identb], bpsT)
                    tt(dve, ybT[:, k4 * 4:(k4 + 1) * 4, :], psT[:, :].rearrange("q (a t) -> q a t", t=128),
                       zt_[:, 8 + k4 * 4:8 + (k4 + 1) * 4, :], ALU.mult, [bpsT, bzt_], [bpsT, b_ybT])
                for nbk in range(4):
                    ns = slice(nbk * 512, (nbk + 1) * 512)
                    psA, bpsA = bank()
                    for p in range(8):
                        mm(psA[:, :], yaT[:, p, :], wa_bf[:, p, ns], [b_yaT, b_wa], bpsA, start=(p == 0), stop=(p == 7))
                    psB, bpsB = bank()
                    for p in range(8):
                        mm(psB[:, :], ybT[:, p, :], wb_bf[:, p, ns], [b_ybT, b_wbb], bpsB, start=(p == 0), stop=(p == 7))
                    tt(dve, m1[:], psA[:, :], zg_t[:, ns], ALU.mult, [bpsA, b_zgt], [bpsA, b_m1])
                    tt(dve, m2[:], psB[:, :], zg_t[:, 2048 + nbk * 512:2048 + (nbk + 1) * 512], ALU.mult, [bpsB, b_zgt], [bpsB, b_m2])
                    tt(pool, mixed[:, ns], m1[:], m2[:], ALU.add, [b_m1, b_m2], [b_mixed])
                mt_, bmt_ = mT_t[j % 2]
                for k4 in range(4):
                    psT, bpsT = bank()
                    for kk in range(4):
                        k = k4 * 4 + kk
                        mm(psT[:, kk * 128:(kk + 1) * 128], mixed[:, k * 128:(k + 1) * 128], identb[:], [b_mixed, b_identb], bpsT)
                    ecopy(mt_[:, k4 * 4:(k4 + 1) * 4, :], psT[:, :].rearrange("q (a t) -> q a t", t=128), [bpsT], [bpsT, bmt_])
                act.dma(mTd[:, :, j * 128:(j + 1) * 128], mt_[:], reads=[bmt_], writes=[bD], sem_buf=bmt_)
            fw.barrier()
            fw.emit()
        sW.close()
        if phases <= 4:
            return nc, fw

        with ExitStack() as s4:
            def T4(name, shape, dt=F32):
                return fw.sbuf(s4, name, shape, dt)
            wo_bf, b_wo = T4("wo_bf", [128, 16, D], BF16)
            wst4 = [T4(f"wst4_{j}", [128, D]) for j in range(3)]
            b_wos = [Buf(f"wo{k}") for k in range(16)]
            for k in range(16):
                wst, b_wst = wst4[k % 3]
                sp.dma(wst[:], w_out[k * 128:(k + 1) * 128, :], writes=[b_wst])
                if k % 3 == 0:
                    pool.op(lambda e, k=k, wst=wst: e.tensor_copy(out=wo_bf[:, k, :], in_=wst[:]), reads=[b_wst], writes=[b_wos[k]])
                elif k % 3 == 1:
                    vcopy(wo_bf[:, k, :], wst[:], [b_wst], [b_wos[k]])
                else:
                    acopy(wo_bf[:, k, :], wst[:], [b_wst], [b_wos[k]])
            gate_t, b_gt = T4("gate_t", [128, D])
            fg_t, b_fg = T4("fg_t", [128, D])
            sp.dma(gate_t[:], gate_d, reads=[bD], writes=[b_gt])
            sp.dma(fg_t[:], bc(final_g, D), writes=[b_fg])
            mTi = [T4(f"mTi{j}", [128, 16, 128], BF16) for j in range(2)]
            xin = [T4(f"xin{j}", [128, D]) for j in range(2)]
            o_t = [T4(f"o_t{j}", [128, D]) for j in range(2)]
            res_t = [T4(f"res{j}", [128, D]) for j in range(2)]
            sq4, b_sq4 = T4("sq4", [128, D])
            s4t, b_s4t = T4("s4t", [128, 4])
            for j in range(32):
                mi, bmi = mTi[j % 2]
                xi, bxi = xin[j % 2]
                ot, bot = o_t[j % 2]
                rt_, brt = res_t[j % 2]
                sp.dma(mi[:], mTd[:, :, j * 128:(j + 1) * 128], reads=[bD], writes=[bmi])
                sp.dma(xi[:], xs[64 + j * 128:64 + (j + 1) * 128, :], writes=[bxi])
                for nbk in range(4):
                    ns = slice(nbk * 512, (nbk + 1) * 512)
                    psW, bpsW = bank()
                    for k in range(16):
                        mm(psW[:, :], mi[:, k, :], wo_bf[:, k, ns], [bmi, b_wos[k]], bpsW, start=(k == 0), stop=(k == 15))
                    tt(dve, ot[:, ns], psW[:, :], gate_t[:, ns], ALU.mult, [bpsW, b_gt], [bpsW, bot])
                    tt(pool, ot[:, ns], ot[:, ns], xi[:, ns], ALU.add, [bot, bxi], [bot])
                acopy(sq4[:], ot[:], [bot], [b_sq4, b_s4t], func=AF.Square, accum_out=s4t[:, 0:1])
                ts(dve, s4t[:, 1:2], s4t[:, 0:1], 1.0 / D, 1e-6, ALU.mult, ALU.add, [b_s4t], [b_s4t])
                acopy(s4t[:, 2:3], s4t[:, 1:2], [b_s4t], [b_s4t], func=AF.Sqrt)
                dve.op(lambda e: e.reciprocal(out=s4t[:, 3:4], in_=s4t[:, 2:3]), reads=[b_s4t], writes=[b_s4t])
                stt(rt_[:], ot[:], s4t[:, 3:4], fg_t[:], ALU.mult, ALU.mult, [bot, b_s4t, b_fg], [brt])
                act.dma(out_d[j * 128:(j + 1) * 128, :], rt_[:], reads=[brt], writes=[bD], sem_buf=brt)
            fw.barrier()
            fw.emit()
    return nc, fw


_CACHE = {}


def _consts():
    idx = np.arange(64)
    identf = np.eye(128, dtype=np.float32)
    identst = np.concatenate([np.eye(64), np.eye(64)], 0).astype(np.float32)
    st_f = (idx[None, :] > idx[:, None]).astype(np.float32)
    in_f = (idx[None, :] >= idx[:, None]).astype(np.float32)
    st_b = (idx[None, :] < idx[:, None]).astype(np.float32)
    in_b = (idx[None, :] <= idx[:, None]).astype(np.float32)
    mG = np.zeros((128, 2, 128), np.float32)
    for h in range(2):
        mG[h * 64:(h + 1) * 64, 0, 0:64] = st_f
        mG[h * 64:(h + 1) * 64, 0, 64:128] = in_f
        mG[h * 64:(h + 1) * 64, 1, 0:64] = st_b
        mG[h * 64:(h + 1) * 64, 1, 64:128] = in_b
    mL = np.zeros((128, 2, 64), np.float32)
    for h in range(2):
        mL[h * 64:(h + 1) * 64, 0, :] = st_f.T
        mL[h * 64:(h + 1) * 64, 1, :] = st_b.T
    mA = np.stack([in_f, in_b], 1).astype(np.float32)
    bones = np.zeros((128, 128), np.float32)
    bones[0:64, 0:64] = 1
    bones[64:128, 64:128] = 1
    ind2 = np.zeros((128, 2), np.float32)
    ind2[0:64, 0] = 1
    ind2[64:128, 1] = 1
    rmask = np.ones((128, 512), np.float32)
    rmask[:, ::64] = 0
    return dict(identf=identf, identst=identst, maskG=mG, maskL=mL, maskA=mA, bones=bones, ind2=ind2, rmask=rmask)


def _prep_inputs(inp):
    f = lambda a: np.ascontiguousarray(np.asarray(a, dtype=np.float32))
    x = f(inp["x"]); c = f(inp["c"]); ctx = f(inp["ctx"]); c_ctx = f(inp["c_ctx"])
    conv_w = f(inp["conv_w"])[0].reshape(9, 3072)
    shared = dict(
        w_mod=f(inp["w_mod"])[0], b_mod=f(inp["b_mod"])[0], norm_g=f(inp["norm_g"])[0], w_in=f(inp["w_in"])[0],
        aw2p=f(inp["a_w2"])[0].reshape(128, 1024), aa2p=f(inp["a_a2"])[0].reshape(128, 1024),
        lnxg=f(inp["a_lnx_g"])[0], lnxb=f(inp["a_lnx_b"])[0], bng=f(inp["b_norm_g"])[0],
        w_a=f(inp["w_a"])[0], w_b=f(inp["w_b"])[0], w_out=f(inp["w_out"])[0], final_g=f(inp["final_g"]),
    )
    gk = f(inp["b_gk_w2"])[0]
    gkw2p = np.zeros((32, 2, 512), np.float32)
    gkw2p[0:16, 0] = gk[0]
    gkw2p[16:32, 1] = gk[1]
    shared["gkw2p"] = gkw2p
    shared.update(_consts())
    colv0 = np.zeros((128, NCOLV), np.float32)
    colv0[:, 0:216] = conv_w.reshape(9, 24, 128).transpose(2, 1, 0).reshape(128, 216)
    for d in range(2):
        colv0[:, CV_W0 + d * 8:CV_W0 + d * 8 + 8] = f(inp["a_w0"])[0, d].reshape(8, 128).T
        colv0[:, CV_A0 + d * 8:CV_A0 + d * 8 + 8] = f(inp["a_a0"])[0, d].reshape(8, 128).T
        colv0[:, CV_GB + d * 4:CV_GB + d * 4 + 4] = f(inp["b_gk_b"])[0, d].reshape(4, 128).T
    colv0[:, CV_KK:CV_KK + 8] = f(inp["a_k_k"])[0].reshape(8, 128).T
    colv0[:, CV_KA:CV_KA + 8] = f(inp["a_k_a"])[0].reshape(8, 128).T
    colv0[:, CV_RK:CV_RK + 8] = f(inp["a_r_k"])[0].reshape(8, 128).T
    maps = []
    for core in range(8):
        b, q = core // 4, core % 4
        xs = np.zeros((4224, D), np.float32)
        lo, hi = q * SEG - 64, (q + 1) * SEG + 64
        slo, shi = max(lo, 0), min(hi, 4 * SEG)
        xs[slo - lo:shi - lo] = x[b, slo:shi]
        cv = colv0.copy()
        cv[:, CV_HALO] = 0.0 if q == 0 else 1.0
        cv[:, CV_HALO + 1] = 0.0 if q == 3 else 1.0
        for s in range(4):
            cv[:, CV_FM + s] = 1.0 if s < q else 0.0
            cv[:, CV_FM + 4 + s] = 1.0 if s > q else 0.0
        cT = np.stack([c[b], c_ctx], 1).reshape(16, 128, 2).transpose(1, 0, 2)
        m = dict(shared)
        m.update(xs=xs, ctxb=np.ascontiguousarray(ctx[b]), cT=np.ascontiguousarray(cT), colv=cv)
        maps.append(m)
    return maps


def kernel(**inputs):
    maps = _prep_inputs(inputs)
    if "nc" not in _CACHE:
        _CACHE["nc"] = build_program()[0]
    res = run_bass_kernel_spmd(_CACHE["nc"], maps, core_ids=list(range(8)))
    out = np.zeros((2, 4 * SEG, D), np.float32)
    for core in range(8):
        b, q = core // 4, core % 4
        out[b, q * SEG:(q + 1) * SEG] = res.results[core]["out"]
    return out
```

```python
import numpy as np
from contextlib import ExitStack
import concourse.bass as bass
import concourse.mybir as mybir
from concourse.bass_utils import run_bass_kernel_spmd

F32 = mybir.dt.float32
BF16 = mybir.dt.bfloat16
AF = mybir.ActivationFunctionType
ALU = mybir.AluOpType
AX = mybir.AxisListType

D = 2048
PIN = 11552
SEG = 4096
NCH = 64
KDEC = 0.6065306597126334
C_Z_A, C_LW, C_LA, C_QB, C_KB, C_VB, C_ZB, C_LG, C_GA, C_GB = 3072, 4096, 4224, 4352, 4864, 5376, 6400, 7424, 7456, 9504
NCOLV = 290
CV_W0, CV_A0, CV_KK, CV_KA, CV_RK, CV_GB, CV_HALO, CV_FM = 216, 232, 248, 256, 264, 272, 280, 282
NCOMP = 16 * 128 + 8 * 257
DBG = {}


class Buf:
    __slots__ = ("name", "w", "r", "sem", "cnt", "dram")

    def __init__(self, name, dram=False):
        self.name = name
        self.w = {}
        self.r = {}
        self.sem = None
        self.cnt = 0
        self.dram = dram


def _mx(need, d):
    for s, v in d.items():
        if v > need.get(s, 0):
            need[s] = v


class Eng:
    def __init__(self, fw, name, sem):
        self.fw = fw
        self.name = name
        self.sem = sem
        self.count = 0
        self.seen = {}
        self.prog = []

    def replay(self, e):
        for it in self.prog:
            if it[0] == "w":
                e.wait_ge(it[1], it[2])
            else:
                it[1](e).then_inc(it[2], it[3])
        self.prog = []

    def _wait(self, need):
        for sem, val in need.items():
            if self.seen.get(sem, 0) >= val:
                continue
            self.prog.append(("w", sem, val))
            self.seen[sem] = val

    def op(self, fn, reads=(), writes=()):
        need = {}
        for b in reads:
            _mx(need, b.w)
        for b in writes:
            _mx(need, b.w)
            _mx(need, b.r)
        if self.name == "pe":
            need.pop(self.sem, None)
        self._wait(need)
        self.count += 1
        self.prog.append(("o", fn, self.sem, 1))
        for b in reads:
            b.r[self.sem] = self.count
        for b in writes:
            b.w = {self.sem: self.count}
            b.r = {}

    def dma(self, out, in_, reads=(), writes=(), sem_buf=None):
        need = {}
        for b in reads:
            _mx(need, b.w)
        for b in writes:
            if b.dram:
                continue
            _mx(need, b.w)
            _mx(need, b.r)
        self._wait(need)
        sb = sem_buf
        if sb is None:
            for b in list(writes) + list(reads):
                if not b.dram:
                    sb = b
                    break
        if sb.sem is None:
            sb.sem, sb.cnt = self.fw.get_dsem(sb.name)
        self.prog.append(("o", (lambda e, o=out, i=in_: e.dma_start(out=o, in_=i)), sb.sem, 16))
        sb.cnt += 16
        for b in reads:
            b.r[sb.sem] = sb.cnt
        for b in writes:
            if b.dram:
                b.w[sb.sem] = sb.cnt
            else:
                b.w = {sb.sem: sb.cnt}
                b.r = {}


class FW:
    def __init__(self, nc, stack):
        self.nc = nc
        self.stack = stack
        self.dsems = []
        self.pe = Eng(self, "pe", self._sem("pe"))
        self.act = Eng(self, "act", self._sem("act"))
        self.dve = Eng(self, "dve", self._sem("dve"))
        self.pool = Eng(self, "pool", self._sem("pool"))
        self.sp = Eng(self, "sp", self._sem("sp"))
        self.engs = [self.pe, self.act, self.dve, self.pool, self.sp]
        self.dbufs = []
        self.sem_pool = []
        self.rr = 0
        self.erot = 0

    def _sem(self, name):
        return self.stack.enter_context(self.nc.semaphore(name))

    def new_sem(self, name):
        return self._sem(name)

    def get_dsem(self, name):
        if self.sem_pool:
            return self.sem_pool.pop()
        return self._sem("d_" + name), 0

    def sbuf(self, st, name, shape, dt=F32):
        self.nid = getattr(self, "nid", 0) + 1
        t = st.enter_context(self.nc.sbuf_tensor(f"sb{self.nid}_{name}", list(shape), dt))
        b = Buf(name)
        self.dbufs.append(b)
        return t, b

    def barrier(self):
        need = {}
        for e in self.engs:
            need[e.sem] = e.count
        for b in self.dbufs:
            if b.sem is not None:
                need[b.sem] = b.cnt
        for e in self.engs:
            n2 = dict(need)
            n2.pop(e.sem, None) if e.name == "pe" else None
            e._wait(n2)
        for b in self.dbufs:
            if b.sem is not None:
                self.sem_pool.append((b.sem, b.cnt))
                b.sem = None

    def emit(self):
        with self.nc.Block() as block:
            @block.tensor
            def _(e):
                self.pe.replay(e)

            @block.scalar
            def _(e):
                self.act.replay(e)

            @block.vector
            def _(e):
                self.dve.replay(e)

            @block.gpsimd
            def _(e):
                self.pool.replay(e)

            @block.sync
            def _(e):
                self.sp.replay(e)


def build_program(phases=9, dbg=()):
    nc = bass.Bass("TRN2", target_bir_lowering=False)

    def din(name, shape, dt=F32):
        return nc.dram_tensor(name, list(shape), dt, kind="ExternalInput")

    def dint(name, shape, dt=F32):
        return nc.dram_tensor(name, list(shape), dt, kind=("ExternalOutput" if name in dbg else "Internal"))

    xs = din("xs", [4224, D]).ap()
    ctxb = din("ctxb", [256, D]).ap()
    cT_d = din("cT", [128, 16, 2]).ap()
    w_mod = din("w_mod", [D, 3 * D]).ap()
    b_mod = din("b_mod", [3 * D])
    norm_g = din("norm_g", [D])
    w_in = din("w_in", [D, PIN]).ap()
    colv_d = din("colv", [128, NCOLV]).ap()
    aw2_d = din("aw2p", [128, 1024]).ap()
    aa2_d = din("aa2p", [128, 1024]).ap()
    gkw_d = din("gkw2p", [32, 2, 512]).ap()
    lnxg = din("lnxg", [1024])
    lnxb = din("lnxb", [1024])
    bng = din("bng", [256])
    w_a = din("w_a", [1024, D]).ap()
    w_b = din("w_b", [1024, D]).ap()
    w_out = din("w_out", [D, D]).ap()
    final_g = din("final_g", [D])
    identf_d = din("identf", [128, 128]).ap()
    identst_d = din("identst", [128, 64]).ap()
    maskG_d = din("maskG", [128, 2, 128]).ap()
    maskL_d = din("maskL", [128, 2, 64]).ap()
    maskA_d = din("maskA", [64, 2, 64]).ap()
    bones_d = din("bones", [128, 128]).ap()
    ind2_d = din("ind2", [128, 2]).ap()
    rmask_d = din("rmask", [128, 512]).ap()
    out_d = nc.dram_tensor("out", [SEG, D], F32, kind="ExternalOutput").ap()

    hTd = dint("hTd", [128, 16, 4480], BF16).ap()
    spA = dint("spA", [2, NCH, 128, 8 * 192]).ap()
    y0d = dint("y0d", [NCH, 128, 8, 64]).ap()
    bvd = dint("bvd", [NCH, 128, 8, 64]).ap()
    spG = dint("spG", [2, NCH, 128, 4 * 321]).ap()
    o0d = dint("o0d", [SEG, 1024]).ap()
    zgd = dint("zgd", [SEG, 4096], BF16).ap()
    zaTd = dint("zaTd", [16, 128, SEG], BF16).ap()
    mTd = dint("mTd", [128, 16, SEG], BF16).ap()
    gate_d = dint("gate_d", [128, D]).ap()
    ypd = dint("ypd", [NCH, 128, 8, 64]).ap()
    opd = dint("opd", [SEG, 1024]).ap()
    CSPL = [(0, 2048), (2048, 3080), (3080, 4104)]
    comp_in = [dint(f"comp_in{i}", [128, b - a]) for i, (a, b) in enumerate(CSPL)]
    comp_out = [dint(f"comp_out{i}", [512, b - a]) for i, (a, b) in enumerate(CSPL)]
    bD = Buf("dram", dram=True)

    def bc(t, n, inner=None):
        if inner is None:
            return bass.AP(t, 0, [[0, 128], [1, n]])
        return bass.AP(t, 0, [[0, 128], [0, inner], [1, n]])

    with ExitStack() as top:
        fw = FW(nc, top)
        pe, act, dve, pool, sp = fw.pe, fw.act, fw.dve, fw.pool, fw.sp
        PS = []
        for i in range(8):
            t = top.enter_context(nc.psum_tensor(f"ps{i}", [128, 512], F32))
            PS.append((t, Buf(f"ps{i}")))

        def bank():
            fw.rr = (fw.rr + 1) % 8
            return PS[fw.rr]

        def mm(out, lhsT, rhs, rd, bps, start=True, stop=True):
            pe.op(lambda e: e.matmul(out, lhsT=lhsT, rhs=rhs, start=start, stop=stop), reads=rd, writes=[bps])

        def acopy(out, in_, rd, wr, func=AF.Copy, **kw):
            act.op(lambda e: e.activation(out=out, in_=in_, func=func, **kw), reads=rd, writes=wr)

        def vcopy(out, in_, rd, wr):
            dve.op(lambda e: e.tensor_copy(out=out, in_=in_), reads=rd, writes=wr)

        def ecopy(out, in_, rd, wr):
            fw.erot += 1
            if fw.erot % 2:
                acopy(out, in_, rd, wr)
            else:
                vcopy(out, in_, rd, wr)

        def tt(eng, out, in0, in1, op, rd, wr):
            eng.op(lambda e: e.tensor_tensor(out=out, in0=in0, in1=in1, op=op), reads=rd, writes=wr)

        def ts(eng, out, in0, s1, s2, op0, op1, rd, wr):
            if s2 is None:
                eng.op(lambda e: e.tensor_scalar(out=out, in0=in0, scalar1=s1, scalar2=None, op0=op0), reads=rd, writes=wr)
            else:
                eng.op(lambda e: e.tensor_scalar(out=out, in0=in0, scalar1=s1, scalar2=s2, op0=op0, op1=op1), reads=rd, writes=wr)

        def treduce(out, in_, rd, wr):
            dve.op(lambda e: e.tensor_reduce(out=out, in_=in_, axis=AX.X, op=ALU.add), reads=rd, writes=wr)

        def recip(out, in_, rd, wr):
            dve.op(lambda e: e.reciprocal(out=out, in_=in_), reads=rd, writes=wr)

        def stt(out, in0, sc, in1, op0, op1, rd, wr):
            dve.op(lambda e: e.scalar_tensor_tensor(out=out, in0=in0, scalar=sc, in1=in1, op0=op0, op1=op1), reads=rd, writes=wr)

        identf, b_identf = fw.sbuf(top, "identf", [128, 128])
        identb, b_identb = fw.sbuf(top, "identb", [128, 128], BF16)
        identst, b_identst = fw.sbuf(top, "identst", [128, 64])
        colv, b_colv = fw.sbuf(top, "colv", [128, NCOLV])
        sp.dma(identf[:], identf_d, writes=[b_identf])
        sp.dma(identst[:], identst_d, writes=[b_identst])
        sp.dma(colv[:], colv_d, writes=[b_colv])
        vcopy(identb[:], identf[:], [b_identf], [b_identb])
        w_in_v = w_in.rearrange("(k p) n -> p k n", p=128)
        Hctx = [fw.sbuf(top, f"Hctx{p}", [128, 2, 64]) for p in range(8)]
        Hgctx, b_Hgctx = fw.sbuf(top, "Hgctx", [128, 8, 256])

        with ExitStack() as s0:
            cTt, b_cT = fw.sbuf(s0, "cTt", [128, 16, 2])
            sT, b_sT = fw.sbuf(s0, "sT", [128, 16, 2])
            srep, b_srep = fw.sbuf(s0, "srep", [128, 2, 16, 128])
            bmods = [fw.sbuf(s0, f"bmod{j}", [128, 256]) for j in range(2)]
            ng_t, b_ng = fw.sbuf(s0, "ng_t", [128, D])
            mt = [fw.sbuf(s0, f"m{j}", [128, 3 * D]) for j in range(2)]
            modA = [fw.sbuf(s0, f"modA{j}", [128, D]) for j in range(2)]
            wm = [fw.sbuf(s0, f"wm{j}", [128, 16, 256]) for j in range(2)]
            sp.dma(cTt[:], cT_d, writes=[b_cT])
            sp.dma(ng_t[:], bc(norm_g, D), writes=[b_ng])
            acopy(sT[:], cTt[:], [b_cT], [b_sT], func=AF.Silu)
            for j in range(2):
                vcopy(srep[:, j], sT[:, :, j:j + 1].to_broadcast([128, 16, 128]), [b_sT], [b_srep])
            wmv = w_mod.rearrange("(k p) n -> p k n", p=128)
            for nb in range(24):
                wt, bw = wm[nb % 2]
                sp.dma(wt[:], wmv[:, :, nb * 256:(nb + 1) * 256], writes=[bw])
                bmod_t, b_bmod = bmods[nb % 2]
                sp.dma(bmod_t[:], bass.AP(b_mod, nb * 256, [[0, 128], [1, 256]]), writes=[b_bmod])
                for j in range(2):
                    ps, bps = bank()
                    for k in range(16):
                        mm(ps[:, 0:256], srep[:, j, k, :], wt[:, k, :], [b_srep, bw], bps, start=(k == 0), stop=(k == 15))
                    tt(dve, mt[j][0][:, nb * 256:(nb + 1) * 256], ps[:, 0:256], bmod_t[:, :], ALU.add,
                       [bps, b_bmod], [bps, mt[j][1]])
            for j in range(2):
                stt(modA[j][0][:], mt[j][0][:, D:2 * D], 1.0, ng_t[:], ALU.add, ALU.mult, [mt[j][1], b_ng], [modA[j][1]])
            act.dma(gate_d, mt[0][0][:, 2 * D:3 * D], reads=[mt[0][1]], writes=[bD], sem_buf=mt[0][1])
            xt = [fw.sbuf(s0, f"xt{j}", [128, D]) for j in range(2)]
            hf, b_hf = fw.sbuf(s0, "hf", [128, D])
            hb, b_hb = fw.sbuf(s0, "hb", [128, D], BF16)
            ss, b_ss = fw.sbuf(s0, "ss", [128, 4])
            hTt = [fw.sbuf(s0, f"hTt{j}", [128, 16, 128], BF16) for j in range(2)]
            for ti in range(35):
                j = 0 if ti < 33 else 1
                src = xs[ti * 128:(ti + 1) * 128, :] if ti < 33 else ctxb[(ti - 33) * 128:(ti - 32) * 128, :]
                x_t, bx = xt[ti % 2]
                sp.dma(x_t[:], src, writes=[bx])
                acopy(hf[:], x_t[:], [bx], [b_hf, b_ss], func=AF.Square, accum_out=ss[:, 0:1])
                ts(dve, ss[:, 1:2], ss[:, 0:1], 1.0 / D, 1e-6, ALU.mult, ALU.add, [b_ss], [b_ss])
                acopy(ss[:, 2:3], ss[:, 1:2], [b_ss], [b_ss], func=AF.Sqrt)
                dve.op(lambda e: e.reciprocal(out=ss[:, 3:4], in_=ss[:, 2:3]), reads=[b_ss], writes=[b_ss])
                stt(hf[:], x_t[:], ss[:, 3:4], modA[j][0][:], ALU.mult, ALU.mult, [bx, b_ss, modA[j][1]], [b_hf])
                tt(pool, hb[:], hf[:], mt[j][0][:, 0:D], ALU.add, [b_hf, mt[j][1]], [b_hb])
                ht, bht = hTt[ti % 2]
                for k4 in range(4):
                    ps, bps = bank()
                    for kk in range(4):
                        k = k4 * 4 + kk
                        mm(ps[:, kk * 128:(kk + 1) * 128], hb[:, k * 128:(k + 1) * 128], identb[:], [b_hb, b_identb], bps)
                    ecopy(ht[:, k4 * 4:(k4 + 1) * 4, :], ps[:, :].rearrange("p (a b) -> p a b", b=128), [bps], [bps, bht])
                act.dma(hTd[:, :, ti * 128:(ti + 1) * 128], ht[:], reads=[bht], writes=[bD], sem_buf=bht)
            fw.barrier()
            fw.emit()
        if phases <= 0:
            return nc, fw

        with ExitStack() as s1:
            cur_st = [s1]

            def T(name, shape, dt=F32):
                return fw.sbuf(cur_st[0], name, shape, dt)
            maskG, b_maskG = T("maskG", [128, 2, 128])
            maskL, b_maskL = T("maskL", [128, 2, 64])
            maskA, b_maskA = T("maskA", [64, 2, 64])
            bonesf, b_bonesf = T("bonesf", [128, 128])
            bones, b_bones = T("bones", [128, 128], BF16)
            ind2f, b_ind2f = T("ind2f", [128, 2])
            ind2, b_ind2 = T("ind2", [128, 2], BF16)
            rmask, b_rmask = T("rmask", [128, 256])
            aw2, b_aw2 = T("aw2", [128, 1024])
            aa2, b_aa2 = T("aa2", [128, 1024])
            gkw, b_gkw = T("gkw", [32, 2, 512])
            dcol, b_dcol = T("dcol", [128, 16])
            for t_, b_, d_ in ((maskG, b_maskG, maskG_d), (maskL, b_maskL, maskL_d), (maskA, b_maskA, maskA_d),
                               (bonesf, b_bonesf, bones_d), (ind2f, b_ind2f, ind2_d), (rmask, b_rmask, rmask_d[:, 0:256]),
                               (aw2, b_aw2, aw2_d), (aa2, b_aa2, aa2_d), (gkw, b_gkw, gkw_d)):
                sp.dma(t_[:], d_, writes=[b_])
            vcopy(bones[:], bonesf[:], [b_bonesf], [b_bones])
            vcopy(ind2[:], ind2f[:], [b_ind2f], [b_ind2])
            ts(dve, dcol[:, 0:8], colv[:, CV_KA:CV_KA + 8], -1.0, 1.0, ALU.mult, ALU.add, [b_colv], [b_dcol])
            ts(dve, dcol[:, 8:16], colv[:, CV_GB:CV_GB + 8], -1.0, None, ALU.mult, None, [b_colv], [b_dcol])

            hT, b_hT = T("hT", [128, 16, 1152], BF16)
            lw, b_lw = T("lw", [128, 1024])
            la, b_la = T("la", [128, 1024])
            lg, b_lg = T("lg", [32, 1024])
            wf = [T(f"wf{j}", [128, 16, 128]) for j in range(2)]
            wb = [T(f"wb{j}", [128, 16, 128], BF16) for j in range(4)]
            wcnt = [0]

            def load_w(col0, ncols, slot):
                wt, bw = wf[wcnt[0] % 2]
                wcnt[0] += 1
                sp.dma(wt[:, :, 0:ncols], w_in_v[:, :, col0:col0 + ncols], writes=[bw])
                dst, bd = wb[slot]
                acopy(dst[:, 0:8, 0:ncols], wt[:, 0:8, 0:ncols], [bw], [bd])
                pool.op(lambda e: e.tensor_copy(out=dst[:, 8:16, 0:ncols], in_=wt[:, 8:16, 0:ncols]), reads=[bw], writes=[bd])
                return dst, bd

            def proj_fm(wt, bw, ncols, col0, ntok, dst, bdst, func=AF.Copy):
                for tb in range((ntok + 511) // 512):
                    n = min(512, ntok - tb * 512)
                    ps, bps = bank()
                    for k in range(16):
                        mm(ps[0:ncols, 0:n], wt[:, k, 0:ncols], hT[:, k, col0 + tb * 512:col0 + tb * 512 + n], [bw, b_hT], bps,
                           start=(k == 0), stop=(k == 15))
                    if func == AF.Copy:
                        ecopy(dst[0:ncols, tb * 512:tb * 512 + n], ps[0:ncols, 0:n], [bps], [bps, bdst])
                    else:
                        acopy(dst[0:ncols, tb * 512:tb * 512 + n], ps[0:ncols, 0:n], [bps], [bps, bdst], func=func)

            cin = [T(f"cin{j}", [128, 384]) for j in range(3)]
            cout = [T(f"cout{j}", [128, 256]) for j in range(3)]
            A = {}
            for nm in ("kk", "t1", "sg", "icl", "cs", "g", "gx", "eng", "ec", "bb", "kd", "rk"):
                A[nm] = T("a_" + nm, [128, 256])
            A["kka"] = A["kk"]
            A["t2"] = A["t1"]
            A["egx"] = A["gx"]
            sqb, b_sqb = T("sqb", [128, 256], BF16)
            vbS = [T(f"vb{i}", [128, 256], BF16) for i in range(2)]
            rkbS = [T(f"rkb{i}", [128, 256], BF16) for i in range(2)]
            egdS = [[T(f"eg{i}_{d}", [128, 256]) for d in range(2)] for i in range(2)]
            rtdS = [[T(f"rt{i}_{d}", [128, 256]) for d in range(2)] for i in range(2)]
            ardS = [[T(f"ar{i}_{d}", [128, 2, 256], BF16) for d in range(2)] for i in range(2)]
            btdS = [[T(f"bt{i}_{d}", [128, 256], BF16) for d in range(2)] for i in range(2)]
            ktdS = [[T(f"kt{i}_{d}", [128, 256], BF16) for d in range(2)] for i in range(2)]
            bhdS = [[T(f"bh{i}_{d}", [128, 256], BF16) for d in range(2)] for i in range(2)]
            khdS = [[T(f"kh{i}_{d}", [128, 256], BF16) for d in range(2)] for i in range(2)]
            vb, b_vb = vbS[0]
            rkb, b_rkb = rkbS[0]
            egd, rtd, ard, btd, ktd, bhd, khd = egdS[0], rtdS[0], ardS[0], btdS[0], ktdS[0], bhdS[0], khdS[0]
            QS = 128.0 ** -0.5

            def reset_gla():
                pool.op(lambda e: e.memset(GN[:], 0.0), writes=[b_GN] + b_GNs)
                pool.op(lambda e: e.memset(GD[:], 1.0), writes=[b_GD])

            def gla_batch(job, S):
                g, col0, tok0, Tn, cg0, spill = job["args"]
                egd, btd, ktd, khd = egdS[S], btdS[S], ktdS[S], khdS[S]
                stG, b_stG = stGS[S]
                if job.get("seg") is not None:
                    hcol, ntok_own, halo_ = job["seg"]
                    own0 = load_seg(hcol, ntok_own, halo_)
                    wt, bw = load_w(C_LG, 32, 3)
                    proj_fm(wt, bw, 32, own0, ntok_own, lg, b_lg)
                    yield
                if job.get("newhead"):
                    load_w(C_QB + g * 128, 128, 0)
                    load_w(C_KB + g * 128, 128, 1)
                    load_w(C_VB + g * 256, 128, 2)
                    load_w(C_VB + g * 256 + 128, 128, 3)
                nch = Tn // 64
                sl = slice(0, Tn)
                qf, bqf = A["kk"]
                kf, bkf = A["t1"]
                vbg, bvbg = ardS[S][0]
                proj_fm(wb[0][0], wb[0][1], 128, col0, Tn, qf, bqf)
                yield
                proj_fm(wb[1][0], wb[1][1], 128, col0, Tn, kf, bkf)
                yield
                for j in range(2):
                    proj_fm(wb[2 + j][0], wb[2 + j][1], 128, col0, Tn, vbg[:, j, :], bvbg)
                    yield
                c3 = lambda ap: ap.rearrange("p (c t) -> p c t", t=64)
                for d in range(2):
                    e1, be1 = A["cs"]
                    spl, bspl = A["g"]
                    csg, bcsg = A["gx"]
                    Gs, bGs = A["eng"]
                    ek, bek = A["bb"]
                    tk, btk = A["kd"]
                    eq, beq = egd[d]
                    ps, bps = bank()
                    mm(ps[:, 0:Tn], gkw[0:32, d, g * 128:(g + 1) * 128], lg[0:32, tok0:tok0 + Tn], [b_gkw, b_lg], bps)
                    acopy(e1[:, sl], ps[:, 0:Tn], [bps, b_dcol], [bps, be1], func=AF.Exp, scale=-1.0,
                          bias=dcol[:, 8 + d * 4 + g:8 + d * 4 + g + 1])
                    acopy(spl[:, sl], e1[:, sl], [be1], [bspl], func=AF.Ln, bias=1.0)
                    yield
                    dve.op(lambda e: e.tensor_tensor_scan(out=csg[:, sl], data0=rmask[:, sl], data1=spl[:, sl], initial=0.0,
                                                          op0=ALU.mult, op1=ALU.add), reads=[b_rmask, bspl], writes=[bcsg])
                    if d == 0:
                        Gsrc, bGsrc = csg, bcsg
                    else:
                        tt(dve, c3(Gs[:, sl]), c3(csg[:, sl])[:, :, 63:64].to_broadcast([128, nch, 64]), c3(csg[:, sl]), ALU.subtract,
                           [bcsg], [bGs])
                        tt(pool, Gs[:, sl], Gs[:, sl], spl[:, sl], ALU.add, [bGs, bspl], [bGs])
                        Gsrc, bGsrc = Gs, bGs
                    acopy(eq[:, sl], Gsrc[:, sl], [bGsrc], [beq], func=AF.Exp, scale=-1.0 / 16)
                    acopy(ek[:, sl], Gsrc[:, sl], [bGsrc], [bek], func=AF.Exp, scale=1.0 / 16)
                    yield
                    eq3 = c3(eq[:, sl])
                    tcol = 63 if d == 0 else 0
                    stt(stG[:, 0:nch, d, 0:64], c3(qf[:, sl]), QS, eq3, ALU.mult, ALU.mult, [bqf, beq], [b_stG])
                    stt(btd[d][0][:, sl], qf[:, sl], QS, eq[:, sl], ALU.mult, ALU.mult, [bqf, beq], [btd[d][1]])
                    tt(pool, ktd[d][0][:, sl], kf[:, sl], ek[:, sl], ALU.mult, [bkf, bek], [ktd[d][1]])
                    tt(dve, c3(tk[:, sl]), c3(ek[:, sl]), eq3[:, :, tcol:tcol + 1].to_broadcast([128, nch, 64]), ALU.mult, [bek, beq], [btk])
                    tt(pool, khd[d][0][:, sl], kf[:, sl], tk[:, sl], ALU.mult, [bkf, btk], [khd[d][1]])
                    vcopy(stG[:, 0:nch, d, 320:321], eq3[:, :, tcol:tcol + 1], [beq], [b_stG])
                    yield
                yield "PREP_DONE"
                if job.get("reset"):
                    reset_gla()
                adv = DBG.get('adv', 3)

                def gcommon(ci):
                    c = slice(ci * 64, (ci + 1) * 64)
                    vtg, bvtg = VTg[ci]
                    psV, bpsV = bank()
                    for j in range(2):
                        mm(psV[0:64, j * 128:(j + 1) * 128], vbg[:, j, c], identb[:], [bvbg, b_identb], bpsV)
                    acopy(vtg[:, :], psV[0:64, 0:256], [bpsV], [bpsV, bvtg])

                def gchain(ci, d):
                    c = slice(ci * 64, (ci + 1) * 64)
                    vtg, bvtg = VTg[ci]
                    att, batt = AttT[ci * 2 + d]
                    kht, bkht = KhT[ci * 2 + d]
                    ps, bps = bank()
                    mm(ps[0:64, 0:64], ktd[d][0][:, c], btd[d][0][:, c], [ktd[d][1], btd[d][1]], bps)
                    mm(ps[0:64, 64:192], khd[d][0][:, c], identb[:], [khd[d][1], b_identb], bps)
                    tt(dve, att[:, :], ps[0:64, 0:64], maskA[:, d, :], ALU.mult, [bps, b_maskA], [bps, batt])
                    acopy(kht[:, :], ps[0:64, 64:192], [bps], [bps, bkht])
                    yield
                    psN, bpsN = bank()
                    mm(psN[:, 0:256], kht[:, :], vtg[:, :], [bkht, bvtg], bpsN)
                    acopy(stG[:, ci, d, 64:320], psN[:, 0:256], [bpsN], [bpsN, b_stG])
                    dc = stG[:, ci, d, 320:321]
                    gi = g * 2 + d
                    if d == 0:
                        stt(GN[:, gi, :], GN[:, gi, :], dc, psN[:, 0:256], ALU.mult, ALU.add, [b_GNs[gi], b_stG, bpsN], [bpsN, b_GNs[gi]])
                    else:
                        stt(GN[:, gi, :], psN[:, 0:256], GD[:, gi:gi + 1], GN[:, gi, :], ALU.mult, ALU.add, [b_GNs[gi], b_GD, bpsN],
                            [bpsN, b_GNs[gi]])
                    ts(dve, GD[:, gi:gi + 1], GD[:, gi:gi + 1], dc, None, ALU.mult, None, [b_GD, b_stG], [b_GD])
                    yield

                for ci in range(nch):
                    gcommon(ci)
                gens = [gchain(ci, d) for ci in range(nch) for d in range(2)]
                while gens:
                    alive = []
                    for g_ in gens:
                        try:
                            next(g_)
                            alive.append(g_)
                        except StopIteration:
                            pass
                    gens = alive
                    adv_extra(adv)
                if spill:
                    for ci in range(nch):
                        par = (cg0 + ci) % 2
                        pp = slice(par * 64, (par + 1) * 64)
                        vtg, bvtg = VTg[ci]
                        psO, bpsO = bank()
                        mm(psO[pp, 0:256], AttT[ci * 2][0][:, :], vtg[:, :], [AttT[ci * 2][1], bvtg], bpsO, start=True, stop=False)
                        mm(psO[pp, 0:256], AttT[ci * 2 + 1][0][:, :], vtg[:, :], [AttT[ci * 2 + 1][1], bvtg], bpsO, start=False, stop=True)
                        acopy(o0st[pp, ci // 2, :], psO[pp, 0:256], [bpsO], [bpsO, b_o0st])
                if spill:
                    for d in range(2):
                        act.dma(spG[d, cg0:cg0 + nch, :, g * 321:(g + 1) * 321].rearrange("c p f -> p c f"), stG[:, 0:nch, d, :],
                               reads=[b_stG], writes=[bD], sem_buf=b_stG)
                    tk0 = cg0 * 64
                    act.dma(o0d[tk0:tk0 + Tn, g * 256:(g + 1) * 256].rearrange("(a p) f -> p a f", p=128), o0st[:, 0:nch // 2, :],
                           reads=[b_o0st], writes=[bD], sem_buf=b_o0st)
                adv_extra(adv)
                if job.get("ctx_last"):
                    vcopy(Hgctx[:], GN[:], [b_GN] + b_GNs, [b_Hgctx])

            def reset_acc():
                for p in range(8):
                    for d, ACC in ((0, ACCf), (1, ACCb)):
                        accsel[p][d] = 0
                        t_, b_ = ACC[p][0]
                        pool.op(lambda e, t_=t_: e.memset(t_[:, 64:128], 0.0), writes=[b_])
                        pool.op(lambda e, t_=t_: e.tensor_copy(out=t_[:, 0:64], in_=identst[:]), reads=[b_identst], writes=[b_])

            PIPE = {"extra": None, "extra_done": True}

            def run_prep(g_):
                while next(g_) != "PREP_DONE":
                    pass

            def adv_extra(n):
                g_ = PIPE["extra"]
                if g_ is None or PIPE["extra_done"]:
                    return
                for _ in range(n):
                    if next(g_) == "PREP_DONE":
                        PIPE["extra_done"] = True
                        return

            def rwkv_batch(job, S):
                p, col0, tok0, Tn, rows, Wd, halo, cg0, spill, halo_mask = job["args"]
                vb, b_vb = vbS[S]
                rkb, b_rkb = rkbS[S]
                egd, rtd, ard, btd, ktd, bhd, khd = egdS[S], rtdS[S], ardS[S], btdS[S], ktdS[S], bhdS[S], khdS[S]
                if job.get("seg") is not None:
                    hcol, ntok_own, halo_ = job["seg"]
                    own0 = load_seg(hcol, ntok_own, halo_)
                    wt, bw = load_w(C_LW, 128, 3)
                    proj_fm(wt, bw, 128, own0, ntok_own, lw, b_lw, func=AF.Tanh)
                    yield
                    wt, bw = load_w(C_LA, 128, 3)
                    proj_fm(wt, bw, 128, own0, ntok_own, la, b_la)
                    yield
                if job.get("newpair"):
                    for j in range(3):
                        load_w(j * 1024 + p * 128, 128, j)
                nch = Tn // 64
                Tin = Tn + (2 * Wd if halo else 0)
                for j in range(3):
                    proj_fm(wb[j][0], wb[j][1], 128, col0, Tin, cin[j][0], cin[j][1])
                    yield
                    if halo_mask is not None:
                        hm_lo, hm_hi = halo_mask
                        if hm_lo:
                            ts(pool, cin[j][0][:, 0:Wd], cin[j][0][:, 0:Wd], colv[:, CV_HALO:CV_HALO + 1], None, ALU.mult, None,
                               [cin[j][1], b_colv], [cin[j][1]])
                        if hm_hi:
                            ts(pool, cin[j][0][:, Tin - Wd:Tin], cin[j][0][:, Tin - Wd:Tin], colv[:, CV_HALO + 1:CV_HALO + 2], None,
                               ALU.mult, None, [cin[j][1], b_colv], [cin[j][1]])
                    ti = j * 8 + p
                    i3 = cin[j][0][:, 0:Tin].rearrange("p (r w) -> p r w", w=Wd)
                    o3 = cout[j][0][:, 0:Tn].rearrange("p (r w) -> p r w", w=Wd)
                    r0 = 1 if halo else 0
                    ts(dve, o3, i3[:, r0:r0 + rows, :], colv[:, ti * 9 + 4:ti * 9 + 5], None, ALU.mult, None,
                       [cin[j][1], b_colv], [cout[j][1]])
                    for dy in ((-1, 0, 1) if halo else (0,)):
                        for dx in (-1, 0, 1):
                            if dy == 0 and dx == 0:
                                continue
                            tap = (dy + 1) * 3 + (dx + 1)
                            xo = slice(1, Wd) if dx == -1 else (slice(0, Wd - 1) if dx == 1 else slice(0, Wd))
                            xi = slice(0, Wd - 1) if dx == -1 else (slice(1, Wd) if dx == 1 else slice(0, Wd))
                            stt(o3[:, :, xo], i3[:, r0 + dy:r0 + dy + rows, xi], colv[:, ti * 9 + tap:ti * 9 + tap + 1], o3[:, :, xo],
                                ALU.mult, ALU.add, [cin[j][1], b_colv, cout[j][1]], [cout[j][1]])
                        yield
                r_, br = cout[0]
                k_, bk = cout[1]
                v_, bv_ = cout[2]
                sl = slice(0, Tn)
                acopy(vb[:, sl], v_[:, sl], [bv_], [b_vb])
                kka, bkka = A["kka"]
                kk, bkk = A["kk"]
                t1, bt1 = A["t1"]
                t2, bt2 = A["t2"]
                acopy(kka[:, sl], k_[:, sl], [bk, b_colv], [bkka], scale=colv[:, CV_KK + p:CV_KK + p + 1])
                acopy(sqb[:, sl], kka[:, sl], [bkka], [b_sqb], func=AF.Square)
                ps, bps = bank()
                mm(ps[:, 0:Tn], bones[:], sqb[:, sl], [b_bones, b_sqb], bps)
                acopy(t1[:, sl], ps[:, 0:Tn], [bps], [bps, bt1], func=AF.Sqrt, bias=1e-12)
                dve.op(lambda e: e.reciprocal(out=t2[:, sl], in_=t1[:, sl]), reads=[bt1], writes=[bt2])
                tt(dve, kk[:, sl], kka[:, sl], t2[:, sl], ALU.mult, [bkka, bt2], [bkk])
                yield
                rk, brk = A["rk"]
                for d in range(2):
                    sg, bsg = A["sg"]
                    icl, bicl = A["icl"]
                    cs, bcs = A["cs"]
                    g, bg = A["g"]
                    gx, bgx = A["gx"]
                    eng, beng = A["eng"]
                    egx, begx = A["egx"]
                    ec, bec = A["ec"]
                    bb, bbb = A["bb"]
                    kd, bkd = A["kd"]
                    eg, beg = egd[d]
                    ps, bps = bank()
                    mm(ps[:, 0:Tn], aw2[d * 64:(d + 1) * 64, p * 128:(p + 1) * 128], lw[d * 64:(d + 1) * 64, tok0:tok0 + Tn],
                       [b_aw2, b_lw], bps)
                    acopy(sg[:, sl], ps[:, 0:Tn], [bps, b_colv], [bps, bsg], func=AF.Sigmoid,
                          bias=colv[:, CV_W0 + d * 8 + p:CV_W0 + d * 8 + p + 1])
                    ps, bps = bank()
                    mm(ps[:, 0:Tn], aa2[d * 64:(d + 1) * 64, p * 128:(p + 1) * 128], la[d * 64:(d + 1) * 64, tok0:tok0 + Tn],
                       [b_aa2, b_la], bps)
                    acopy(icl[:, sl], ps[:, 0:Tn], [bps, b_colv], [bps, bicl], func=AF.Sigmoid,
                          bias=colv[:, CV_A0 + d * 8 + p:CV_A0 + d * 8 + p + 1])
                    yield
                    dve.op(lambda e: e.tensor_tensor_scan(out=cs[:, sl], data0=rmask[:, sl], data1=sg[:, sl], initial=0.0,
                                                          op0=ALU.mult, op1=ALU.add), reads=[b_rmask, bsg], writes=[bcs])
                    cs3 = cs[:, sl].rearrange("p (c t) -> p c t", t=64)
                    if d == 0:
                        gsrc, bgs = cs, bcs
                        tt(dve, gx[:, sl], cs[:, sl], sg[:, sl], ALU.subtract, [bcs, bsg], [bgx])
                    else:
                        tt(dve, gx[:, sl].rearrange("p (c t) -> p c t", t=64), cs3[:, :, 63:64].to_broadcast([128, nch, 64]), cs3,
                           ALU.subtract, [bcs], [bgx])
                        tt(dve, g[:, sl], gx[:, sl], sg[:, sl], ALU.add, [bgx, bsg], [bg])
                        gsrc, bgs = g, bg
                    acopy(eg[:, sl], gsrc[:, sl], [bgs], [beg], func=AF.Exp, scale=-KDEC)
                    acopy(eng[:, sl], gsrc[:, sl], [bgs], [beng], func=AF.Exp, scale=KDEC)
                    acopy(egx[:, sl], gx[:, sl], [bgx], [begx], func=AF.Exp, scale=-KDEC)
                    yield
                    eg3 = eg[:, sl].rearrange("p (c t) -> p c t", t=64)
                    tcol = 63 if d == 0 else 0
                    tt(dve, ec[:, sl].rearrange("p (c t) -> p c t", t=64), eng[:, sl].rearrange("p (c t) -> p c t", t=64),
                       eg3[:, :, tcol:tcol + 1].to_broadcast([128, nch, 64]), ALU.mult, [beng, beg], [bec])
                    ar, bar = ard[d]
                    stt(ar[:, 0, sl], kk[:, sl], -1.0, egx[:, sl], ALU.mult, ALU.mult, [bkk, begx], [bar])
                    tt(dve, bb[:, sl], kk[:, sl], icl[:, sl], ALU.mult, [bkk, bicl], [bbb])
                    tt(dve, btd[d][0][:, sl], bb[:, sl], eng[:, sl], ALU.mult, [bbb, beng], [btd[d][1]])
                    tt(dve, bhd[d][0][:, sl], bb[:, sl], ec[:, sl], ALU.mult, [bbb, bec], [bhd[d][1]])
                    yield
                    ts(dve, t1[:, sl], icl[:, sl], colv[:, CV_KA + p:CV_KA + p + 1], dcol[:, p:p + 1], ALU.mult, ALU.add,
                       [bicl, b_colv, b_dcol], [bt1])
                    tt(dve, kd[:, sl], t1[:, sl], k_[:, sl], ALU.mult, [bt1, bk], [bkd])
                    tt(dve, ktd[d][0][:, sl], kd[:, sl], eng[:, sl], ALU.mult, [bkd, beng], [ktd[d][1]])
                    tt(dve, khd[d][0][:, sl], kd[:, sl], ec[:, sl], ALU.mult, [bkd, bec], [khd[d][1]])
                    yield
                    tt(dve, rtd[d][0][:, sl], r_[:, sl], eg[:, sl], ALU.mult, [br, beg], [rtd[d][1]])
                    acopy(ar[:, 1, sl], rtd[d][0][:, sl], [rtd[d][1]], [bar])
                    if d == 0:
                        acopy(rk[:, sl], kd[:, sl], [bkd], [brk])
                    else:
                        tt(dve, rk[:, sl], rk[:, sl], kd[:, sl], ALU.add, [brk, bkd], [brk])
                stt(rkb[:, sl], rk[:, sl], colv[:, CV_RK + p:CV_RK + p + 1], r_[:, sl], ALU.mult, ALU.mult, [brk, b_colv, br], [b_rkb])

                yield "PREP_DONE"
                if job.get("reset"):
                    reset_acc()
                nci = min(nch, DBG.get('maxc', 9))

                def common(ci):
                    c = slice(ci * 64, (ci + 1) * 64)
                    vts, bvts = VTs[ci]
                    bo, bbo = bon[ci]
                    psV, bpsV = bank()
                    for h in range(2):
                        hs = slice(h * 64, (h + 1) * 64)
                        mm(psV[hs, 0:64], vb[hs, c], identb[hs, hs], [b_vb, b_identb], bpsV)
                    for h in range(2):
                        hs = slice(h * 64, (h + 1) * 64)
                        mm(psV[hs, 64:66], rkb[hs, c], ind2[hs, :], [b_rkb, b_ind2], bpsV)
                    acopy(vts[:], psV[:, 0:64], [bpsV], [bpsV, bvts])
                    vcopy(bo[:, :], psV[:, 64:66], [bpsV], [bpsV, bbo])
                    if spill and DBG.get('sp_bv', True):
                        for h in range(2):
                            hs = slice(h * 64, (h + 1) * 64)
                            ts(dve, bvst[hs, ci, :], psV[hs, 0:64], bo[hs, h:h + 1], None, ALU.mult, None, [bpsV, bbo],
                               [bpsV, b_bvst])

                RS = {}

                def chain(ci, d):
                    c = slice(ci * 64, (ci + 1) * 64)
                    sidx = ci * 2 + d
                    vts, bvts = VTs[ci]
                    ar, bar = ard[d]
                    tm3 = TM3all[:, sidx]
                    gxt, bgxt = GX[sidx]
                    l0, bl0 = L0[sidx]
                    bsg_ = b_stgs[sidx]
                    H2 = [slice(0, 64), slice(64, 128)]
                    ps, bps = bank()
                    for j, (X, bX) in enumerate(((ar, bar), bhd[d], khd[d])):
                        for hs in H2:
                            src = X[hs, 0, c] if j == 0 else X[hs, c]
                            mm(ps[hs, j * 64:(j + 1) * 64], src, identb[hs, hs], [bX, b_identb], bps)
                    ps2, bps2 = bank()
                    for hs in H2:
                        mm(ps2[hs, 0:128], btd[d][0][hs, c], ar[hs, :, c], [btd[d][1], bar], bps2)
                        mm(ps2[hs, 128:256], ktd[d][0][hs, c], ar[hs, :, c], [ktd[d][1], bar], bps2)
                        mm(ps2[hs, 256:320], ar[hs, 0, c], btd[d][0][hs, c], [btd[d][1], bar], bps2)
                    acopy(tm3[:, :, 0:64], ps[:, 0:192].rearrange("p (a b) -> p a b", b=64), [bps], [bps, b_TM3])
                    tt(dve, gxt[:], ps2[:, 0:256].rearrange("p (a b) -> p a b", b=128),
                       maskG[:, d:d + 1, :].to_broadcast([128, 2, 128]), ALU.mult, [bps2, b_maskG], [bps2, bgxt])
                    tt(dve, l0[:], ps2[:, 256:320], maskL[:, d, :], ALU.mult, [bps2, b_maskL], [bps2, bl0])
                    tt(dve, TTall[0][0][:, sidx, :], gxt[:, 0, 0:64], identst[:], ALU.add, [bgxt, b_identst], [TTall[0][1]])
                    yield
                    Xc, bXc = gxt[:, 0, 0:64], bgxt
                    Lc, bLc = l0[:], bl0
                    for j in range(6):
                        Tc, bTc = TTall[j % 2][0][:, sidx, :], TTall[j % 2][1]
                        if j < 5:
                            psq, bpsq = RS["bq"][sidx // 4]
                            o0_ = (sidx % 4) * 128
                            for hs in H2:
                                if j < 4:
                                    mm(psq[hs, o0_:o0_ + 64], Lc[hs], Xc[hs], [bLc, bXc], bpsq)
                                mm(psq[hs, o0_ + 64:o0_ + 128], Xc[hs], Lc[hs], [bLc, bXc], bpsq)
                        if j >= 1:
                            Tp, bTp = TTall[(j - 1) % 2][0][:, sidx, :], TTall[(j - 1) % 2][1]
                            pst, bpst = RS["bt"]
                            for hs in H2:
                                mm(pst[hs, sidx * 64:(sidx + 1) * 64], Lc[hs], Tp[hs], [bLc, bTp], bpst)
                        if j == 0:
                            psx, bpsx = RS["bx"]
                            for hs in H2:
                                mm(psx[hs, sidx * 64:(sidx + 1) * 64], gxt[hs, 1, 0:64], vts[hs], [bgxt, bvts], bpsx)
                        yield
                        if j < 5:
                            xl, bxl = XLall[j % 2]
                            Xc, bXc = xl[:, sidx, 0, :], bxl
                            Lc, bLc = xl[:, sidx, 1, :], bxl
                    Tc, bTc = TTall[5 % 2][0][:, sidx, :], TTall[5 % 2][1]
                    psa, bpsa = RS["ba"][sidx // 4]
                    o0_ = (sidx % 4) * 128
                    for hs in H2:
                        mm(psa[hs, o0_:o0_ + 128], Tc[hs], tm3[hs, 0, :], [bTc, b_TM3], bpsa)
                    yield
                    au, bau = AUall[:, sidx, :], b_AU
                    ps, bps = bank()
                    for hs in H2:
                        mm(ps[hs, 0:64], au[hs, 0:64], tm3[hs, 1, 0:64], [bau, b_TM3], bps)
                        mm(ps[hs, 64:128], au[hs, 0:64], gxt[hs, 0, 64:128], [bau, bgxt], bps)
                        if d == 1:
                            mm(ps[hs, 128:192], tm3[hs, 1, 0:64], au[hs, 0:64], [bau, b_TM3], bps)
                    ps2, bps2 = bank()
                    for hs in H2:
                        mm(ps2[hs, 0:64], tm3[hs, 1, 0:64], au[hs, 64:128], [b_TM3, bau], bps2, start=True, stop=False)
                        mm(ps2[hs, 0:64], tm3[hs, 2, 0:64], vts[hs], [b_TM3, bvts], bps2, start=False, stop=True)
                    eg3 = egd[d][0][:, sl].rearrange("p (c t) -> p c t", t=64)
                    tcol = 63 if d == 0 else 0
                    gam = eg3[:, ci, tcol:tcol + 1]
                    stt(stg[:, ci, d, 0:64], identst[:], gam, ps[:, 0:64], ALU.mult, ALU.add, [b_identst, egd[d][1], bps],
                        [bps, bsg_])
                    tt(dve, stg[:, ci, d, 64:128], ps[:, 64:128], rtd[d][0][:, c], ALU.add, [bps, rtd[d][1]], [bps, bsg_])
                    if d == 1:
                        stt(Mn[ci][0][:], identst[:], gam, ps[:, 128:192], ALU.mult, ALU.add, [b_identst, egd[d][1], bps],
                            [bps, Mn[ci][1]])
                    acopy(stg[:, ci, d, 128:192], ps2[:, 0:64], [bps2], [bps2, bsg_])
                    yield

                for ci in range(nci):
                    common(ci)
                gens = [chain(ci, d) for ci in range(nci) for d in range(2)]
                ns_ = len(gens)
                nh_ = (ns_ + 3) // 4
                adv = DBG.get('adv', 3)
                for g_ in gens:
                    next(g_)
                adv_extra(adv)
                for j in range(6):
                    if j < 5:
                        RS["bq"] = [bank() for _ in range(nh_)]
                    if j >= 1:
                        RS["bt"] = bank()
                    if j == 0:
                        RS["bx"] = bank()
                    for g_ in gens:
                        next(g_)
                    if j < 5:
                        xl, bxl = XLall[j % 2]
                        for hf in range(nh_):
                            n4 = min(4, ns_ - hf * 4)
                            psq, bpsq = RS["bq"][hf]
                            if j < 4:
                                acopy(xl[:, hf * 4:hf * 4 + n4, :, :].rearrange("p s a b -> p s (a b)"),
                                      psq[:, 0:n4 * 128].rearrange("p (s f) -> p s f", f=128), [bpsq], [bpsq, bxl])
                            else:
                                acopy(xl[:, hf * 4:hf * 4 + n4, 1, :], psq[:, 0:n4 * 128].rearrange("p (s f) -> p s f", f=128)[:, :, 64:128],
                                      [bpsq], [bpsq, bxl])
                    if j >= 1:
                        pst, bpst = RS["bt"]
                        tt(dve, TTall[j % 2][0][:, 0:ns_, :], pst[:, 0:ns_ * 64].rearrange("p (s f) -> p s f", f=64),
                           TTall[(j - 1) % 2][0][:, 0:ns_, :], ALU.add, [bpst, TTall[(j - 1) % 2][1]], [bpst, TTall[j % 2][1]])
                    if j == 0:
                        psx, bpsx = RS["bx"]
                        acopy(TM3all[:, 0:ns_, 0, 64:128], psx[:, 0:ns_ * 64].rearrange("p (s f) -> p s f", f=64), [bpsx], [bpsx, b_TM3])
                    adv_extra(adv)
                RS["ba"] = [bank() for _ in range(nh_)]
                for g_ in gens:
                    next(g_)
                for hf in range(nh_):
                    n4 = min(4, ns_ - hf * 4)
                    psa, bpsa = RS["ba"][hf]
                    acopy(AUall[:, hf * 4:hf * 4 + n4, :], psa[:, 0:n4 * 128].rearrange("p (s f) -> p s f", f=128), [bpsa], [bpsa, b_AU])
                adv_extra(adv)
                for g_ in gens:
                    next(g_)
                for g_ in gens:
                    for _ in g_:
                        pass
                adv_extra(adv)
                for ci in range(nci):
                    for d in range(2):
                        bsg_ = b_stgs[ci * 2 + d]
                        ACC = ACCf if d == 0 else ACCb
                        ao, bao = ACC[p][accsel[p][d]]
                        an, ban = ACC[p][1 - accsel[p][d]]
                        accsel[p][d] = 1 - accsel[p][d]
                        ps, bps = bank()
                        if d == 0:
                            for h in range(2):
                                hs = slice(h * 64, (h + 1) * 64)
                                mm(ps[hs, 0:128], stg[hs, ci, 0, 0:64], ao[hs, :], [bsg_, bao], bps)
                            acopy(an[:, 0:64], ps[:, 0:64], [bps], [bps, ban])
                            tt(dve, an[:, 64:128], ps[:, 64:128], stg[:, ci, 0, 128:192], ALU.add, [bps, bsg_], [bps, ban])
                        else:
                            mn_, bmn_ = Mn[ci]
                            for h in range(2):
                                hs = slice(h * 64, (h + 1) * 64)
                                mm(ps[hs, 0:64], mn_[hs], ao[hs, 0:64], [bmn_, bao], bps)
                                mm(ps[hs, 64:128], ao[hs, 0:64], stg[hs, ci, 1, 128:192], [bsg_, bao], bps)
                            acopy(an[:, 0:64], ps[:, 0:64], [bps], [bps, ban])
                            tt(dve, an[:, 64:128], ps[:, 64:128], ao[:, 64:128], ALU.add, [bps, bao], [bps, ban])
                    if spill and DBG.get('sp_y0', True):
                        vts, bvts = VTs[ci]
                        g0, bg0 = GX[ci * 2]
                        g1, bg1 = GX[ci * 2 + 1]
                        a0, ba0 = AUall[:, ci * 2, :], b_AU
                        a1, ba1 = AUall[:, ci * 2 + 1, :], b_AU
                        ps, bps = bank()
                        for h in range(2):
                            hs = slice(h * 64, (h + 1) * 64)
                            o_ = ps[hs, 0:64]
                            rds = [bg0, bg1, ba0, ba1, bvts]
                            mm(o_, g0[hs, 0, 64:128], a0[hs, 64:128], rds, bps, start=True, stop=False)
                            mm(o_, g0[hs, 1, 64:128], vts[hs], rds, bps, start=False, stop=False)
                            mm(o_, g1[hs, 0, 64:128], a1[hs, 64:128], rds, bps, start=False, stop=False)
                            mm(o_, g1[hs, 1, 64:128], vts[hs], rds, bps, start=False, stop=True)
                        acopy(y0st[:, ci, :], ps[:, 0:64], [bps], [bps, b_y0st])
                if spill and DBG.get('sp_dma', True):
                    for d in range(2):
                        act.dma(spA[d, cg0:cg0 + nch, :, p * 192:(p + 1) * 192].rearrange("c p f -> p c f"), stg[:, 0:nch, d, :],
                               reads=[b_stg] + b_stgs, writes=[bD], sem_buf=b_stg)
                    tk0 = cg0 * 64
                    act.dma(y0d[cg0:cg0 + nch, :, p, :].rearrange("c q v -> q c v"), y0st[:, 0:nch, :],
                           reads=[b_y0st], writes=[bD], sem_buf=b_y0st)
                    act.dma(bvd[cg0:cg0 + nch, :, p, :].rearrange("c q v -> q c v"), bvst[:, 0:nch, :],
                           reads=[b_bvst], writes=[bD], sem_buf=b_bvst)
                if job.get("ctx_last"):
                    for d, ACC in ((0, ACCf), (1, ACCb)):
                        a_, ba_ = ACC[p][accsel[p][d]]
                        vcopy(Hctx[p][0][:, d, :], a_[:, 64:128], [ba_], [Hctx[p][1]])

            segs = [("ctx", 4224, 256, 1, 256, False, 1, 256)] + [(f"q{i}", i * 1024, 1024, 4, 64, True, 4, 256) for i in range(4)]

            def load_seg(hcol, ntok_own, halo):
                ncols_h = ntok_own + (128 if halo else 0)
                sp.dma(hT[:, :, 0:ncols_h], hTd[:, :, hcol:hcol + ncols_h], reads=[bD], writes=[b_hT])
                return 64 if halo else 0

            with ExitStack() as s1a:
                cur_st[0] = s1a
                VTs = [T(f"VTs{i}", [128, 64], BF16) for i in range(4)]
                bon = [T(f"bon{i}", [128, 2]) for i in range(4)]
                TM3all, b_TM3 = T("TM3all", [128, 8, 3, 128], BF16)
                GX = [T(f"GX{i}", [128, 2, 128], BF16) for i in range(8)]
                AUall, b_AU = T("AUall", [128, 8, 128], BF16)
                L0 = [T(f"L0{i}", [128, 64], BF16) for i in range(8)]
                XLall = [T(f"XLall{j}", [128, 8, 2, 64], BF16) for j in range(2)]
                TTall = [T(f"TTall{j}", [128, 8, 64], BF16) for j in range(2)]
                Mn = [T(f"Mn{i}", [128, 64]) for i in range(4)]
                b_stgs = [Buf(f"stg{i}") for i in range(8)]
                stg, b_stg = T("stg", [128, 4, 2, 192])
                y0st, b_y0st = T("y0st", [128, 4, 64])
                bvst, b_bvst = T("bvst", [128, 4, 64])
                ACCf = [[T(f"ACCf{p}_{j}", [128, 128]) for j in range(2)] for p in range(8)]
                ACCb = [[T(f"ACCb{p}_{j}", [128, 128]) for j in range(2)] for p in range(8)]
                accsel = [[0, 0] for _ in range(8)]
                jobs = []
                for sname, hcol, ntok_own, rows, Wd, halo, nb, Tn in segs[:DBG.get('nsegs', 5)]:
                    for p in range(DBG.get('maxp', 8)):
                        for b in range(min(nb, DBG.get('maxb', 9))):
                            if sname == "ctx":
                                job = dict(args=(p, 0, 0, 256, 1, 256, False, 0, False, None), ctx_last=True)
                            else:
                                qi = int(sname[1])
                                hm = (qi == 0 and b == 0, qi == 3 and b == 3)
                                job = dict(args=(p, b * 256, b * 256, 256, 4, 64, True, qi * 16 + b * 4, DBG.get("spill", True), hm))
                            if p == 0 and b == 0:
                                job["seg"] = (hcol, ntok_own, halo)
                                if sname in ("ctx", "q0"):
                                    job["reset"] = True
                            if b == 0:
                                job["newpair"] = True
                            jobs.append(job)
                gens_j = [rwkv_batch(job, i % 2) for i, job in enumerate(jobs)]

                def run_prep(g_):
                    while next(g_) != "PREP_DONE":
                        pass

                run_prep(gens_j[0])
                for i in range(len(jobs)):
                    if i + 1 < len(jobs):
                        PIPE["extra"] = gens_j[i + 1]
                        PIPE["extra_done"] = False
                    else:
                        PIPE["extra"] = None
                        PIPE["extra_done"] = True
                    if not DBG.get("pipe", True) and PIPE["extra"] is not None:
                        pass
                    for _ in gens_j[i]:
                        pass
                    if PIPE["extra"] is not None and not PIPE["extra_done"]:
                        run_prep(PIPE["extra"])
                        PIPE["extra_done"] = True
                ci_ap = comp_in[0].ap()
                for p in range(8):
                    for d, ACC in ((0, ACCf), (1, ACCb)):
                        a_, ba_ = ACC[p][accsel[p][d]]
                        c0 = (p * 2 + d) * 128
                        if d == 0:
                            ps, bps = bank()
                            for h in range(2):
                                hs = slice(h * 64, (h + 1) * 64)
                                mm(ps[hs, 0:64], a_[hs, 0:64], identf[hs, hs], [ba_, b_identf], bps)
                            mn_, bmn_ = Mn[p % 4]
                            vcopy(mn_[:], ps[:, 0:64], [bps], [bps, bmn_])
                            act.dma(ci_ap[:, c0:c0 + 64], mn_[:], reads=[bmn_], writes=[bD], sem_buf=bmn_)
                            act.dma(ci_ap[:, c0 + 64:c0 + 128], a_[:, 64:128], reads=[ba_], writes=[bD], sem_buf=ba_)
                        else:
                            act.dma(ci_ap[:, c0:c0 + 128], a_[:, :], reads=[ba_], writes=[bD], sem_buf=ba_)
                fw.barrier()
                fw.emit()
            with ExitStack() as s1b:
                cur_st[0] = s1b
                stGS = [T(f"stG{i}", [128, 4, 2, 321]) for i in range(2)]
                o0st, b_o0st = T("o0st", [128, 2, 256])
                VTg = [T(f"VTg{i}", [64, 256], BF16) for i in range(4)]
                AttT = [T(f"AttT{i}", [64, 64], BF16) for i in range(8)]
                KhT = [T(f"KhT{i}", [64, 128], BF16) for i in range(8)]
                b_GNs = [Buf(f"GN{i}") for i in range(8)]
                GN, b_GN = T("GN", [128, 8, 256])
                GD, b_GD = T("GD", [128, 8])
                gjobs = []
                for sname, hcol, ntok_own, rows, Wd, halo, nb, Tn in segs[:DBG.get('nsegs', 5)]:
                    for g in range(DBG.get('maxg', 4)):
                        for b in range(nb):
                            if sname == "ctx":
                                job = dict(args=(g, 0, 0, 256, 0, False))
                                if g == 3:
                                    job["ctx_last"] = True
                            else:
                                qi = int(sname[1])
                                job = dict(args=(g, 64 + b * 256, b * 256, 256, qi * 16 + b * 4, True))
                            if g == 0 and b == 0:
                                job["seg"] = (hcol, ntok_own, halo)
                                if sname in ("ctx", "q0"):
                                    job["reset"] = True
                            if b == 0:
                                job["newhead"] = True
                            gjobs.append(job)
                gens_g = [gla_batch(job, i % 2) for i, job in enumerate(gjobs)]
                run_prep(gens_g[0])
                for i in range(len(gjobs)):
                    if i + 1 < len(gjobs):
                        PIPE["extra"] = gens_g[i + 1]
                        PIPE["extra_done"] = False
                    else:
                        PIPE["extra"] = None
                        PIPE["extra_done"] = True
                    for _ in gens_g[i]:
                        pass
                    if PIPE["extra"] is not None and not PIPE["extra_done"]:
                        run_prep(PIPE["extra"])
                        PIPE["extra_done"] = True
                act.dma(comp_in[1].ap()[:, 0:8], GD[:, :], reads=[b_GD], writes=[bD], sem_buf=b_GD)
                act.dma(comp_in[1].ap()[:, 8:1032], GN[:, 0:4, :].rearrange("p g f -> p (g f)"), reads=[b_GN] + b_GNs, writes=[bD], sem_buf=b_GN)
                act.dma(comp_in[2].ap()[:, :], GN[:, 4:8, :].rearrange("p g f -> p (g f)"), reads=[b_GN] + b_GNs, writes=[bD], sem_buf=b_GN)
                fw.barrier()
                fw.emit()
            pass
        with ExitStack() as s1c:
            def T(name, shape, dt=F32):
                return fw.sbuf(s1c, name, shape, dt)
            hT2, b_hT2 = T("hT2", [128, 16, 2048], BF16)
            w5f = [T(f"w5f{j}", [128, 16, 512]) for j in range(2)]
            w5b = [T(f"w5b{j}", [128, 16, 512], BF16) for j in range(2)]
            zst = [T(f"zst{j}", [128, 2048], BF16) for j in range(2)]
            gst = [T(f"gst{j}", [128, 512], BF16) for j in range(3)]
            zblocks = [(C_Z_A, 0), (C_Z_A + 512, 4), (C_ZB, 8), (C_ZB + 512, 12)]
            nw = 0
            ng_ = 0
            for half in range(2):
                sp.dma(hT2[:], hTd[:, :, 64 + half * 2048:64 + (half + 1) * 2048], reads=[bD], writes=[b_hT2])
                for colw, blk0 in zblocks:
                    wt, bw = w5f[nw % 2]
                    wbf, bwb = w5b[nw % 2]
                    nw += 1
                    sp.dma(wt[:], w_in_v[:, :, colw:colw + 512], writes=[bw])
                    for k4 in range(4):
                        eng_ = pool if k4 % 2 == 0 else act
                        if eng_ is pool:
                            pool.op(lambda e, wbf=wbf, wt=wt, k4=k4: e.tensor_copy(out=wbf[:, k4 * 4:(k4 + 1) * 4, :], in_=wt[:, k4 * 4:(k4 + 1) * 4, :]),
                                    reads=[bw], writes=[bwb])
                        else:
                            acopy(wbf[:, k4 * 4:(k4 + 1) * 4, :], wt[:, k4 * 4:(k4 + 1) * 4, :], [bw], [bwb])
                    for sb_ in range(4):
                        zt, bzt = zst[(blk0 + sb_) % 2]
                        for tb in range(4):
                            ps, bps = bank()
                            for k in range(16):
                                mm(ps[:, :], wbf[:, k, sb_ * 128:(sb_ + 1) * 128], hT2[:, k, tb * 512:(tb + 1) * 512], [bwb, b_hT2], bps,
                                   start=(k == 0), stop=(k == 15))
                            acopy(zt[:, tb * 512:(tb + 1) * 512], ps[:, :], [bps], [bps, bzt], func=AF.Silu)
                        act.dma(zaTd[blk0 + sb_, :, half * 2048:(half + 1) * 2048], zt[:, :], reads=[bzt], writes=[bD], sem_buf=bzt)
                for cb in range(8):
                    wt, bw = w5f[nw % 2]
                    wbf, bwb = w5b[nw % 2]
                    nw += 1
                    sp.dma(wt[:], w_in_v[:, :, C_GA + cb * 512:C_GA + (cb + 1) * 512], writes=[bw])
                    for k4 in range(4):
                        if k4 % 2 == 0:
                            pool.op(lambda e, wbf=wbf, wt=wt, k4=k4: e.tensor_copy(out=wbf[:, k4 * 4:(k4 + 1) * 4, :], in_=wt[:, k4 * 4:(k4 + 1) * 4, :]),
                                    reads=[bw], writes=[bwb])
                        else:
                            vcopy(wbf[:, k4 * 4:(k4 + 1) * 4, :], wt[:, k4 * 4:(k4 + 1) * 4, :], [bw], [bwb])
                    for tl in range(16):
                        ps, bps = bank()
                        for k in range(16):
                            mm(ps[:, :], hT2[:, k, tl * 128:(tl + 1) * 128], wbf[:, k, :], [b_hT2, bwb], bps, start=(k == 0), stop=(k == 15))
                        gt, bgt = gst[ng_ % 3]
                        ng_ += 1
                        acopy(gt[:], ps[:, :], [bps], [bps, bgt], func=AF.Sigmoid)
                        r0 = half * 2048 + tl * 128
                        act.dma(zgd[r0:r0 + 128, cb * 512:(cb + 1) * 512], gt[:], reads=[bgt], writes=[bD], sem_buf=bgt)
            fw.barrier()
            fw.emit()
        if phases <= 1:
            return nc, fw

        Hf0, b_Hf0 = fw.sbuf(top, "Hf0", [128, 8, 64])
        Hb0, b_Hb0 = fw.sbuf(top, "Hb0", [128, 8, 64])
        Hgf0, b_Hgf0 = fw.sbuf(top, "Hgf0", [128, 4, 256])
        Hgb0, b_Hgb0 = fw.sbuf(top, "Hgb0", [128, 4, 256])
        identstb, b_identstb = fw.sbuf(top, "identstb", [128, 64], BF16)
        vcopy(identstb[:], identst[:], [b_identst], [b_identstb])

        with ExitStack() as sE:
            gath, b_gath = fw.sbuf(sE, "gath", [128, 4, NCOMP])
            Hx = [fw.sbuf(sE, f"Hx{j}", [128, 8, 64]) for j in range(2)]
            t1x, b_t1x = fw.sbuf(sE, "t1x", [128, 8, 64])
            Gx = [fw.sbuf(sE, f"Gx{j}", [128, 4, 256]) for j in range(2)]
            t2x, b_t2x = fw.sbuf(sE, "t2x", [128, 4, 256])
            if DBG.get("nocc", False):
                for i in range(3):
                    for r in range(4):
                        sp.dma(comp_out[i].ap()[r * 128:(r + 1) * 128, :], comp_in[i].ap(), reads=[bD], writes=[bD], sem_buf=b_gath)
                bCO = bD
            else:
                cc_sem = fw.new_sem("cc")
                pool._wait(dict(bD.w))
                for i in range(3):
                    pool.prog.append(("o", (lambda e, i=i: e.collective_compute(
                        "AllGather", ALU.bypass, replica_groups=[[0, 1, 2, 3], [4, 5, 6, 7]],
                        ins=[comp_in[i].ap()], outs=[comp_out[i].ap()])), cc_sem, 1))
                bCO = Buf("comp_out", dram=True)
                bCO.w = {cc_sem: 3}
            for i, (ca, cb_) in enumerate(CSPL):
                sp.dma(gath[:, :, ca:cb_], comp_out[i].ap().rearrange("(r p) n -> p r n", p=128), reads=[bCO], writes=[b_gath])
            for d in range(2):
                cur, bcur = Hx[0]
                for p in range(8):
                    vcopy(cur[:, p, :], Hctx[p][0][:, d, :], [Hctx[p][1]], [bcur])
                order = (0, 1, 2) if d == 0 else (3, 2, 1)
                for i, sgi in enumerate(order):
                    nxt, bnxt = (Hx[(i + 1) % 2]) if i < 2 else ((Hf0, b_Hf0) if d == 0 else (Hb0, b_Hb0))
                    ps, bps = bank()
                    for p in range(8):
                        c0 = (p * 2 + d) * 128
                        for h in range(2):
                            hs = slice(h * 64, (h + 1) * 64)
                            mm(ps[hs, p * 64:(p + 1) * 64], gath[hs, sgi, c0:c0 + 64], cur[hs, p, :], [b_gath, bcur], bps)
                    nview = gath[:, sgi, 0:2048].rearrange("q (p d f) -> q p d f", d=2, f=128)[:, :, d, 64:128]
                    tt(dve, t1x[:], ps[:, :].rearrange("q (p f) -> q p f", f=64), nview, ALU.add, [bps, b_gath], [bps, b_t1x])
                    tt(dve, t1x[:], t1x[:], cur[:], ALU.subtract, [b_t1x, bcur], [b_t1x])
                    mcol = CV_FM + d * 4 + sgi
                    stt(nxt[:], t1x[:], colv[:, mcol:mcol + 1], cur[:], ALU.mult, ALU.add, [b_t1x, b_colv, bcur], [bnxt])
                    cur, bcur = nxt, bnxt
                cur, bcur = Gx[0]
                for g in range(4):
                    vcopy(cur[:, g, :], Hgctx[:, g * 2 + d, :], [b_Hgctx], [bcur])
                for i, sgi in enumerate(order):
                    nxt, bnxt = (Gx[(i + 1) % 2]) if i < 2 else ((Hgf0, b_Hgf0) if d == 0 else (Hgb0, b_Hgb0))
                    for g in range(4):
                        gi = g * 2 + d
                        stt(t2x[:, g, :], cur[:, g, :], gath[:, sgi, 2048 + gi:2048 + gi + 1],
                            gath[:, sgi, 2056 + gi * 256:2056 + (gi + 1) * 256], ALU.mult, ALU.add, [bcur, b_gath], [b_t2x])
                    tt(dve, t2x[:], t2x[:], cur[:], ALU.subtract, [b_t2x, bcur], [b_t2x])
                    mcol = CV_FM + d * 4 + sgi
                    stt(nxt[:], t2x[:], colv[:, mcol:mcol + 1], cur[:], ALU.mult, ALU.add, [b_t2x, b_colv, bcur], [bnxt])
                    cur, bcur = nxt, bnxt
            fw.barrier()
            fw.emit()
        if phases <= 2:
            return nc, fw

        def passB(stk, d, c, Hc, bHc, Hn, bHn, Gc, bGc, Gn_, bGn, stA_t, b_stA, stG_t, b_stG2):
            sp.dma(stA_t[:], spA[d, c], reads=[bD], writes=[b_stA])
            sp.dma(stG_t[:], spG[d, c], reads=[bD], writes=[b_stG2])
            par = c % 2
            pp = slice(par * 64, (par + 1) * 64)
            psH, bpsH = bank()
            psY, bpsY = bank()
            for p in range(8):
                for h in range(2):
                    hs = slice(h * 64, (h + 1) * 64)
                    mm(psH[hs, p * 64:(p + 1) * 64], stA_t[hs, p * 192:p * 192 + 64], Hc[hs, p, :], [b_stA, bHc], bpsH)
            nview = stA_t[:, :].rearrange("q (p f) -> q p f", f=192)[:, :, 128:192]
            tt(dve, Hn[:], psH[:, :].rearrange("q (p f) -> q p f", f=64), nview, ALU.add, [bpsH, b_stA], [bpsH, bHn])
            for g in range(4):
                stt(Gn_[:, g, :], Gc[:, g, :], stG_t[:, g * 321 + 320:g * 321 + 321], stG_t[:, g * 321 + 64:g * 321 + 320],
                    ALU.mult, ALU.add, [bGc, b_stG2], [bGn])
            for p in range(8):
                for h in range(2):
                    hs = slice(h * 64, (h + 1) * 64)
                    mm(psY[hs, p * 64:(p + 1) * 64], stA_t[hs, p * 192 + 64:p * 192 + 128], Hc[hs, p, :], [b_stA, bHc], bpsY)
            psO = [bank(), bank()]
            for g in range(4):
                po, bpo = psO[g // 2]
                mm(po[pp, (g % 2) * 256:(g % 2 + 1) * 256], stG_t[:, g * 321:g * 321 + 64], Gc[:, g, :], [b_stG2, bGc], bpo)
            return (psY, bpsY), psO, pp

        sW = ExitStack()
        wa_bf, b_wa = fw.sbuf(sW, "wa_bf", [128, 8, D], BF16)
        wb_bf, b_wbb = fw.sbuf(sW, "wb_bf", [128, 8, D], BF16)
        with ExitStack() as s2:
            wstW = [fw.sbuf(s2, f"wstW{j}", [128, D]) for j in range(2)]
            nwl = 0
            for (wsrc, wdst, bdst) in ((w_a, wa_bf, b_wa), (w_b, wb_bf, b_wbb)):
                for p in range(8):
                    wst_, b_wst_ = wstW[nwl % 2]
                    nwl += 1
                    sp.dma(wst_[:], wsrc[p * 128:(p + 1) * 128, :], writes=[b_wst_])
                    pool.op(lambda e, wdst=wdst, p=p, wst_=wst_: e.tensor_copy(out=wdst[:, p, :], in_=wst_[:]), reads=[b_wst_], writes=[bdst])
            def T2(name, shape, dt=F32):
                return fw.sbuf(s2, name, shape, dt)
            stA_b = [T2(f"stAb{j}", [128, 1536]) for j in range(2)]
            stG_b = [T2(f"stGb{j}", [128, 1284]) for j in range(2)]
            Hb = [T2(f"Hbk{j}", [128, 8, 64]) for j in range(2)]
            Gb = [T2(f"Gbk{j}", [128, 4, 256]) for j in range(2)]
            y0_t = [T2(f"y0t{j}", [128, 512]) for j in range(2)]
            yp_o = [T2(f"ypo{j}", [128, 512]) for j in range(2)]
            o0_t = [T2(f"o0t{j}", [128, 1024]) for j in range(2)]
            op_o = [T2(f"opo{j}", [128, 1024]) for j in range(2)]
            vcopy(Hb[0][0][:], Hb0[:], [b_Hb0], [Hb[0][1]])
            vcopy(Gb[0][0][:], Hgb0[:], [b_Hgb0], [Gb[0][1]])
            for i, c in enumerate(range(NCH - 1, -1, -1)):
                j = c // 2
                Hc, bHc = Hb[i % 2]
                Hn, bHn = Hb[(i + 1) % 2]
                Gc, bGc = Gb[i % 2]
                Gn_, bGn = Gb[(i + 1) % 2]
                (psY, bpsY), psO, pp = passB(s2, 1, c, Hc, bHc, Hn, bHn, Gc, bGc, Gn_, bGn, stA_b[i % 2][0], stA_b[i % 2][1],
                                             stG_b[i % 2][0], stG_b[i % 2][1])
                yt, byt = y0_t[i % 2]
                sp.dma(yt[:], y0d[c].rearrange("q p v -> q (p v)"), reads=[bD], writes=[byt])
                yo, byo = yp_o[i % 2]
                tt(dve, yo[:], psY[:, :], yt[:], ALU.add, [bpsY, byt], [bpsY, byo])
                act.dma(ypd[c].rearrange("q p v -> q (p v)"), yo[:], reads=[byo], writes=[bD], sem_buf=byo)
                ot, bot = o0_t[j % 2]
                oo, boo = op_o[j % 2]
                if c % 2 == 1:
                    sp.dma(ot[:], o0d[j * 128:(j + 1) * 128, :], reads=[bD], writes=[bot])
                for hb_ in range(2):
                    po, bpo = psO[hb_]
                    tt(dve, oo[pp, hb_ * 512:(hb_ + 1) * 512], po[pp, :], ot[pp, hb_ * 512:(hb_ + 1) * 512], ALU.add, [bpo, bot],
                       [bpo, boo])
                if c % 2 == 0:
                    act.dma(opd[j * 128:(j + 1) * 128, :], oo[:], reads=[boo], writes=[bD], sem_buf=boo)
            fw.barrier()
            fw.emit()
        if phases <= 3:
            return nc, fw

        with ExitStack() as s3:
            def T3(name, shape, dt=F32):
                return fw.sbuf(s3, name, shape, dt)
            lnxg_st, b_lnxg = T3("lnxg_st", [128, 8, 64])
            lnxb_st, b_lnxb = T3("lnxb_st", [128, 8, 64])
            bng_b, b_bng = T3("bng_b", [128, 4, 256])
            for h in range(2):
                hs = slice(h * 64, (h + 1) * 64)
                sp.dma(lnxg_st[hs, :, :], bass.AP(lnxg, h * 64, [[0, 64], [128, 8], [1, 64]]), writes=[b_lnxg])
                sp.dma(lnxb_st[hs, :, :], bass.AP(lnxb, h * 64, [[0, 64], [128, 8], [1, 64]]), writes=[b_lnxb])
            sp.dma(bng_b[:], bass.AP(bng, 0, [[0, 128], [0, 4], [1, 256]]), writes=[b_bng])
            stA_f = [T3(f"stAf{j}", [128, 1536]) for j in range(2)]
            stG_f = [T3(f"stGf{j}", [128, 1284]) for j in range(2)]
            Hf = [T3(f"Hfk{j}", [128, 8, 64]) for j in range(2)]
            Gf = [T3(f"Gfk{j}", [128, 4, 256]) for j in range(2)]
            yp_t = [T3(f"ypt{j}", [128, 512]) for j in range(2)]
            bv_t = [T3(f"bvt{j}", [128, 512]) for j in range(2)]
            w1S = [T3(f"w1_{j}", [128, 512]) for j in range(2)]
            w2S = [T3(f"w2_{j}", [128, 512]) for j in range(2)]
            stS = [T3(f"st_{j}", [128, 64]) for j in range(2)]
            ya_bdS = [T3(f"ya_bd{j}", [128, 8, 128], BF16) for j in range(2)]
            yaTS = [T3(f"yaT{j}", [128, 8, 128], BF16) for j in range(2)]
            ybTS = [T3(f"ybT{j}", [128, 8, 128], BF16) for j in range(1)] * 2
            zaT_t = [T3(f"zaTt{j}", [128, 16, 128], BF16) for j in range(2)]
            op_tS = [T3(f"op_t{j}", [128, 1024]) for j in range(1)] * 2
            obS = [T3(f"ob{j}", [128, 1024]) for j in range(2)]
            tmpb, b_tmpb = T3("tmpb", [128, 1024])
            yb_bfS = [T3(f"yb_bf{j}", [128, 1024], BF16) for j in range(1)] * 2
            zg_t, b_zgt = T3("zg_t", [128, 4096], BF16)
            m1, b_m1 = T3("m1", [128, 512])
            m2, b_m2 = T3("m2", [128, 512])
            mixedS = [T3(f"mixed{j}", [128, D], BF16) for j in range(1)] * 2
            mT_t = [T3(f"mTt{j}", [128, 16, 128], BF16) for j in range(1)] * 2
            for j_ in range(2):
                pool.op(lambda e, j_=j_: e.memset(ya_bdS[j_][0][:], 0.0), writes=[ya_bdS[j_][1]])
            vcopy(Hf[0][0][:], Hf0[:], [b_Hf0], [Hf[0][1]])
            vcopy(Gf[0][0][:], Hgf0[:], [b_Hgf0], [Gf[0][1]])
            b3 = lambda ap, n, m: ap.to_broadcast([128, n, m])
            def tile_work(j, st, b_st, ob, b_ob, zt_, bzt_, yaT, b_yaT):
                sp.dma(zg_t[:], zgd[j * 128:(j + 1) * 128, :], reads=[bD], writes=[b_zgt])
                ybT, b_ybT = ybTS[j % 2]
                yb_bf, b_ybbf = yb_bfS[j % 2]
                mixed, b_mixed = mixedS[j % 2]
                acopy(tmpb[:], ob[:], [b_ob], [b_tmpb], func=AF.Square)
                treduce(st[:, 56:60], tmpb[:].rearrange("q (g v) -> q g v", v=256), [b_tmpb], [b_st])
                ts(dve, st[:, 60:64], st[:, 56:60], 1.0 / 256, None, ALU.mult, None, [b_st], [b_st])
                acopy(st[:, 56:60], st[:, 60:64], [b_st], [b_st], func=AF.Sqrt, bias=1e-6)
                recip(st[:, 60:64], st[:, 56:60], [b_st], [b_st])
                ob3 = ob[:].rearrange("q (g v) -> q g v", v=256)
                tt(dve, ob3, ob3, b3(st[:, 60:64].rearrange("q (g o) -> q g o", o=1), 4, 256), ALU.mult, [b_ob, b_st], [b_ob])
                tt(pool, yb_bf[:].rearrange("q (g v) -> q g v", v=256), ob3, bng_b[:], ALU.mult, [b_ob, b_bng], [b_ybbf])
                for k4 in range(2):
                    psT, bpsT = bank()
                    for kk in range(4):
                        k = k4 * 4 + kk
                        mm(psT[:, kk * 128:(kk + 1) * 128], yb_bf[:, k * 128:(k + 1) * 128], identb[:], [b_ybbf, b_identb], bpsT)
                    tt(dve, ybT[:, k4 * 4:(k4 + 1) * 4, :], psT[:, :].rearrange("q (a t) -> q a t", t=128),
                       zt_[:, 8 + k4 * 4:8 + (k4 + 1) * 4, :], ALU.mult, [bpsT, bzt_], [bpsT, b_ybT])
                for nbk in range(4):
                    ns = slice(nbk * 512, (nbk + 1) * 512)
                    psA, bpsA = bank()
                    for p in range(8):
                        mm(psA[:, :], yaT[:, p, :], wa_bf[:, p, ns], [b_yaT, b_wa], bpsA, start=(p == 0), stop=(p == 7))
                    psB, bpsB = bank()
                    for p in range(8):
                        mm(psB[:, :], ybT[:, p, :], wb_bf[:, p, ns], [b_ybT, b_wbb], bpsB, start=(p == 0), stop=(p == 7))
                    tt(dve, m1[:], psA[:, :], zg_t[:, ns], ALU.mult, [bpsA, b_zgt], [bpsA, b_m1])
                    tt(dve, m2[:], psB[:, :], zg_t[:, 2048 + nbk * 512:2048 + (nbk + 1) * 512], ALU.mult, [bpsB, b_zgt], [bpsB, b_m2])
                    tt(pool, mixed[:, ns], m1[:], m2[:], ALU.add, [b_m1, b_m2], [b_mixed])
                mt_, bmt_ = mT_t[j % 2]
                for k4 in range(4):
                    psT, bpsT = bank()
                    for kk in range(4):
                        k = k4 * 4 + kk
                        mm(psT[:, kk * 128:(kk + 1) * 128], mixed[:, k * 128:(k + 1) * 128], identb[:], [b_mixed, b_identb], bpsT)
                    ecopy(mt_[:, k4 * 4:(k4 + 1) * 4, :], psT[:, :].rearrange("q (a t) -> q a t", t=128), [bpsT], [bpsT, bmt_])
                act.dma(mTd[:, :, j * 128:(j + 1) * 128], mt_[:], reads=[bmt_], writes=[bD], sem_buf=bmt_)
            pending = [None]
            for c in range(NCH):
                j = c // 2
                par = c % 2
                Hc, bHc = Hf[c % 2]
                Hn, bHn = Hf[(c + 1) % 2]
                Gc, bGc = Gf[c % 2]
                Gn_, bGn = Gf[(c + 1) % 2]
                zt_, bzt_ = zaT_t[j % 2]
                w1, b_w1 = w1S[c % 2]
                w2, b_w2 = w2S[c % 2]
                st, b_st = stS[c % 2]
                ya_bd, b_yabd = ya_bdS[c % 2]
                yaT, b_yaT = yaTS[j % 2]
                ybT, b_ybT = ybTS[j % 2]
                op_t, b_opt = op_tS[j % 2]
                ob, b_ob = obS[j % 2]
                yb_bf, b_ybbf = yb_bfS[j % 2]
                mixed, b_mixed = mixedS[j % 2]
                if par == 0:
                    sp.dma(zt_[:], zaTd[:, :, j * 128:(j + 1) * 128].rearrange("b p t -> p b t"), reads=[bD], writes=[bzt_])
                    sp.dma(op_t[:], opd[j * 128:(j + 1) * 128, :], reads=[bD], writes=[b_opt])
                (psY, bpsY), psO, pp = passB(s3, 0, c, Hc, bHc, Hn, bHn, Gc, bGc, Gn_, bGn, stA_f[c % 2][0], stA_f[c % 2][1],
                                             stG_f[c % 2][0], stG_f[c % 2][1])
                ypt, bypt = yp_t[c % 2]
                bvt, bbvt = bv_t[c % 2]
                sp.dma(ypt[:], ypd[c].rearrange("q p v -> q (p v)"), reads=[bD], writes=[bypt])
                sp.dma(bvt[:], bvd[c].rearrange("q p v -> q (p v)"), reads=[bD], writes=[bbvt])
                tt(dve, w1[:], psY[:, :], ypt[:], ALU.add, [bpsY, bypt], [bpsY, b_w1])
                for hb_ in range(2):
                    po, bpo = psO[hb_]
                    tt(dve, ob[pp, hb_ * 512:(hb_ + 1) * 512], po[pp, :], op_t[pp, hb_ * 512:(hb_ + 1) * 512], ALU.add, [bpo, b_opt],
                       [bpo, b_ob])
                w13 = w1[:].rearrange("q (p v) -> q p v", v=64)
                treduce(st[:, 0:8], w13, [b_w1], [b_st])
                acopy(w2[:], w1[:], [b_w1], [b_w2], func=AF.Square)
                treduce(st[:, 8:16], w2[:].rearrange("q (p v) -> q p v", v=64), [b_w2], [b_st])
                ts(dve, st[:, 16:24], st[:, 0:8], 1.0 / 64, None, ALU.mult, None, [b_st], [b_st])
                tt(dve, st[:, 24:32], st[:, 16:24], st[:, 16:24], ALU.mult, [b_st], [b_st])
                stt(st[:, 32:40], st[:, 8:16], 1.0 / 64, st[:, 24:32], ALU.mult, ALU.subtract, [b_st], [b_st])
                acopy(st[:, 40:48], st[:, 32:40], [b_st], [b_st], func=AF.Sqrt, bias=64e-5)
                recip(st[:, 48:56], st[:, 40:48], [b_st], [b_st])
                tt(dve, w13, w13, b3(st[:, 16:24].rearrange("q (p o) -> q p o", o=1), 8, 64), ALU.subtract, [b_w1, b_st], [b_w1])
                tt(dve, w13, w13, b3(st[:, 48:56].rearrange("q (p o) -> q p o", o=1), 8, 64), ALU.mult, [b_w1, b_st], [b_w1])
                tt(pool, w13, w13, lnxg_st[:], ALU.mult, [b_w1, b_lnxg], [b_w1])
                tt(pool, w13, w13, lnxb_st[:], ALU.add, [b_w1, b_lnxb], [b_w1])
                for h in range(2):
                    hs = slice(h * 64, (h + 1) * 64)
                    tt(dve if h == 0 else pool, ya_bd[hs, :, h * 64:(h + 1) * 64], w1[hs, :].rearrange("q (p v) -> q p v", v=64),
                       bvt[hs, :].rearrange("q (p v) -> q p v", v=64), ALU.add, [b_w1, bbvt], [b_yabd])
                psT, bpsT = bank()
                for p in range(8):
                    mm(psT[:, p * 64:(p + 1) * 64], ya_bd[:, p, :], identstb[:], [b_yabd, b_identstb], bpsT)
                tt(dve, yaT[:, :, par * 64:(par + 1) * 64], psT[:, :].rearrange("q (p t) -> q p t", t=64),
                   zt_[:, 0:8, par * 64:(par + 1) * 64], ALU.mult, [bpsT, bzt_], [bpsT, b_yaT])
                if pending[0] is not None:
                    tile_work(*pending[0])
                    pending[0] = None
                if par == 1:
                    pending[0] = (j, st, b_st, ob, b_ob, zt_, bzt_, yaT, b_yaT)
            if pending[0] is not None:
                tile_work(*pending[0])
            fw.barrier()
            fw.emit()
        sW.close()
        if phases <= 4:
            return nc, fw

        with ExitStack() as s4:
            def T4(name, shape, dt=F32):
                return fw.sbuf(s4, name, shape, dt)
            wo_bf, b_wo = T4("wo_bf", [128, 16, D], BF16)
            wst4 = [T4(f"wst4_{j}", [128, D]) for j in range(3)]
            b_wos = [Buf(f"wo{k}") for k in range(16)]
            for k in range(16):
                wst, b_wst = wst4[k % 3]
                sp.dma(wst[:], w_out[k * 128:(k + 1) * 128, :], writes=[b_wst])
                if k % 3 == 0:
                    pool.op(lambda e, k=k, wst=wst: e.tensor_copy(out=wo_bf[:, k, :], in_=wst[:]), reads=[b_wst], writes=[b_wos[k]])
                elif k % 3 == 1:
                    vcopy(wo_bf[:, k, :], wst[:], [b_wst], [b_wos[k]])
                else:
                    acopy(wo_bf[:, k, :], wst[:], [b_wst], [b_wos[k]])
            gate_t, b_gt = T4("gate_t", [128, D])
            fg_t, b_fg = T4("fg_t", [128, D])
            sp.dma(gate_t[:], gate_d, reads=[bD], writes=[b_gt])
            sp.dma(fg_t[:], bc(final_g, D), writes=[b_fg])
            mTi = [T4(f"mTi{j}", [128, 16, 128], BF16) for j in range(2)]
            xin = [T4(f"xin{j}", [128, D]) for j in range(2)]
            o_t = [T4(f"o_t{j}", [128, D]) for j in range(2)]
            res_t = [T4(f"res{j}", [128, D]) for j in range(2)]
            sq4, b_sq4 = T4("sq4", [128, D])
            s4t, b_s4t = T4("s4t", [128, 4])
            for j in range(32):
                mi, bmi = mTi[j % 2]
                xi, bxi = xin[j % 2]
                ot, bot = o_t[j % 2]
                rt_, brt = res_t[j % 2]
                sp.dma(mi[:], mTd[:, :, j * 128:(j + 1) * 128], reads=[bD], writes=[bmi])
                sp.dma(xi[:], xs[64 + j * 128:64 + (j + 1) * 128, :], writes=[bxi])
                for nbk in range(4):
                    ns = slice(nbk * 512, (nbk + 1) * 512)
                    psW, bpsW = bank()
                    for k in range(16):
                        mm(psW[:, :], mi[:, k, :], wo_bf[:, k, ns], [bmi, b_wos[k]], bpsW, start=(k == 0), stop=(k == 15))
                    tt(dve, ot[:, ns], psW[:, :], gate_t[:, ns], ALU.mult, [bpsW, b_gt], [bpsW, bot])
                    tt(pool, ot[:, ns], ot[:, ns], xi[:, ns], ALU.add, [bot, bxi], [bot])
                acopy(sq4[:], ot[:], [bot], [b_sq4, b_s4t], func=AF.Square, accum_out=s4t[:, 0:1])
                ts(dve, s4t[:, 1:2], s4t[:, 0:1], 1.0 / D, 1e-6, ALU.mult, ALU.add, [b_s4t], [b_s4t])
                acopy(s4t[:, 2:3], s4t[:, 1:2], [b_s4t], [b_s4t], func=AF.Sqrt)
                dve.op(lambda e: e.reciprocal(out=s4t[:, 3:4], in_=s4t[:, 2:3]), reads=[b_s4t], writes=[b_s4t])
                stt(rt_[:], ot[:], s4t[:, 3:4], fg_t[:], ALU.mult, ALU.mult, [bot, b_s4t, b_fg], [brt])
                act.dma(out_d[j * 128:(j + 1) * 128, :], rt_[:], reads=[brt], writes=[bD], sem_buf=brt)
            fw.barrier()
            fw.emit()
    return nc, fw


_CACHE = {}


def _consts():
    idx = np.arange(64)
    identf = np.eye(128, dtype=np.float32)
    identst = np.concatenate([np.eye(64), np.eye(64)], 0).astype(np.float32)
    st_f = (idx[None, :] > idx[:, None]).astype(np.float32)
    in_f = (idx[None, :] >= idx[:, None]).astype(np.float32)
    st_b = (idx[None, :] < idx[:, None]).astype(np.float32)
    in_b = (idx[None, :] <= idx[:, None]).astype(np.float32)
    mG = np.zeros((128, 2, 128), np.float32)
    for h in range(2):
        mG[h * 64:(h + 1) * 64, 0, 0:64] = st_f
        mG[h * 64:(h + 1) * 64, 0, 64:128] = in_f
        mG[h * 64:(h + 1) * 64, 1, 0:64] = st_b
        mG[h * 64:(h + 1) * 64, 1, 64:128] = in_b
    mL = np.zeros((128, 2, 64), np.float32)
    for h in range(2):
        mL[h * 64:(h + 1) * 64, 0, :] = st_f.T
        mL[h * 64:(h + 1) * 64, 1, :] = st_b.T
    mA = np.stack([in_f, in_b], 1).astype(np.float32)
    bones = np.zeros((128, 128), np.float32)
    bones[0:64, 0:64] = 1
    bones[64:128, 64:128] = 1
    ind2 = np.zeros((128, 2), np.float32)
    ind2[0:64, 0] = 1
    ind2[64:128, 1] = 1
    rmask = np.ones((128, 512), np.float32)
    rmask[:, ::64] = 0
    return dict(identf=identf, identst=identst, maskG=mG, maskL=mL, maskA=mA, bones=bones, ind2=ind2, rmask=rmask)


def _prep_inputs(inp):
    f = lambda a: np.ascontiguousarray(np.asarray(a, dtype=np.float32))
    x = f(inp["x"]); c = f(inp["c"]); ctx = f(inp["ctx"]); c_ctx = f(inp["c_ctx"])
    conv_w = f(inp["conv_w"])[0].reshape(9, 3072)
    shared = dict(
        w_mod=f(inp["w_mod"])[0], b_mod=f(inp["b_mod"])[0], norm_g=f(inp["norm_g"])[0], w_in=f(inp["w_in"])[0],
        aw2p=f(inp["a_w2"])[0].reshape(128, 1024), aa2p=f(inp["a_a2"])[0].reshape(128, 1024),
        lnxg=f(inp["a_lnx_g"])[0], lnxb=f(inp["a_lnx_b"])[0], bng=f(inp["b_norm_g"])[0],
        w_a=f(inp["w_a"])[0], w_b=f(inp["w_b"])[0], w_out=f(inp["w_out"])[0], final_g=f(inp["final_g"]),
    )
    gk = f(inp["b_gk_w2"])[0]
    gkw2p = np.zeros((32, 2, 512), np.float32)
    gkw2p[0:16, 0] = gk[0]
    gkw2p[16:32, 1] = gk[1]
    shared["gkw2p"] = gkw2p
    shared.update(_consts())
    colv0 = np.zeros((128, NCOLV), np.float32)
    colv0[:, 0:216] = conv_w.reshape(9, 24, 128).transpose(2, 1, 0).reshape(128, 216)
    for d in range(2):
        colv0[:, CV_W0 + d * 8:CV_W0 + d * 8 + 8] = f(inp["a_w0"])[0, d].reshape(8, 128).T
        colv0[:, CV_A0 + d * 8:CV_A0 + d * 8 + 8] = f(inp["a_a0"])[0, d].reshape(8, 128).T
        colv0[:, CV_GB + d * 4:CV_GB + d * 4 + 4] = f(inp["b_gk_b"])[0, d].reshape(4, 128).T
    colv0[:, CV_KK:CV_KK + 8] = f(inp["a_k_k"])[0].reshape(8, 128).T
    colv0[:, CV_KA:CV_KA + 8] = f(inp["a_k_a"])[0].reshape(8, 128).T
    colv0[:, CV_RK:CV_RK + 8] = f(inp["a_r_k"])[0].reshape(8, 128).T
    maps = []
    for core in range(8):
        b, q = core // 4, core % 4
        xs = np.zeros((4224, D), np.float32)
        lo, hi = q * SEG - 64, (q + 1) * SEG + 64
        slo, shi = max(lo, 0), min(hi, 4 * SEG)
        xs[slo - lo:shi - lo] = x[b, slo:shi]
        cv = colv0.copy()
        cv[:, CV_HALO] = 0.0 if q == 0 else 1.0
        cv[:, CV_HALO + 1] = 0.0 if q == 3 else 1.0
        for s in range(4):
            cv[:, CV_FM + s] = 1.0 if s < q else 0.0
            cv[:, CV_FM + 4 + s] = 1.0 if s > q else 0.0
        cT = np.stack([c[b], c_ctx], 1).reshape(16, 128, 2).transpose(1, 0, 2)
        m = dict(shared)
        m.update(xs=xs, ctxb=np.ascontiguousarray(ctx[b]), cT=np.ascontiguousarray(cT), colv=cv)
        maps.append(m)
    return maps


def kernel(**inputs):
    maps = _prep_inputs(inputs)
    if "nc" not in _CACHE:
        _CACHE["nc"] = build_program()[0]
    res = run_bass_kernel_spmd(_CACHE["nc"], maps, core_ids=list(range(8)))
    out = np.zeros((2, 4 * SEG, D), np.float32)
    for core in range(8):
        b, q = core // 4, core % 4
        out[b, q * SEG:(q + 1) * SEG] = res.results[core]["out"]
    return out
```

```python
import numpy as np
from contextlib import ExitStack
import concourse.bass as bass
import concourse.mybir as mybir
from concourse.bass_utils import run_bass_kernel_spmd

F32 = mybir.dt.float32
BF16 = mybir.dt.bfloat16
AF = mybir.ActivationFunctionType
ALU = mybir.AluOpType
AX = mybir.AxisListType

D = 2048
PIN = 11552
SEG = 4096
NCH = 64
KDEC = 0.6065306597126334
C_Z_A, C_LW, C_LA, C_QB, C_KB, C_VB, C_ZB, C_LG, C_GA, C_GB = 3072, 4096, 4224, 4352, 4864, 5376, 6400, 7424, 7456, 9504
NCOLV = 290
CV_W0, CV_A0, CV_KK, CV_KA, CV_RK, CV_GB, CV_HALO, CV_FM = 216, 232, 248, 256, 264, 272, 280, 282
NCOMP = 16 * 128 + 8 * 257
DBG = {}


class Buf:
    __slots__ = ("name", "w", "r", "sem", "cnt", "dram")

    def __init__(self, name, dram=False):
        self.name = name
        self.w = {}
        self.r = {}
        self.sem = None
        self.cnt = 0
        self.dram = dram


def _mx(need, d):
    for s, v in d.items():
        if v > need.get(s, 0):
            need[s] = v


class Eng:
    def __init__(self, fw, name, sem):
        self.fw = fw
        self.name = name
        self.sem = sem
        self.count = 0
        self.seen = {}
        self.prog = []

    def replay(self, e):
        for it in self.prog:
            if it[0] == "w":
                e.wait_ge(it[1], it[2])
            else:
                it[1](e).then_inc(it[2], it[3])
        self.prog = []

    def _wait(self, need):
        for sem, val in need.items():
            if self.seen.get(sem, 0) >= val:
                continue
            self.prog.append(("w", sem, val))
            self.seen[sem] = val

    def op(self, fn, reads=(), writes=()):
        need = {}
        for b in reads:
            _mx(need, b.w)
        for b in writes:
            _mx(need, b.w)
            _mx(need, b.r)
        if self.name == "pe":
            need.pop(self.sem, None)
        self._wait(need)
        self.count += 1
        self.prog.append(("o", fn, self.sem, 1))
        for b in reads:
            b.r[self.sem] = self.count
        for b in writes:
            b.w = {self.sem: self.count}
            b.r = {}

    def dma(self, out, in_, reads=(), writes=(), sem_buf=None):
        need = {}
        for b in reads:
            _mx(need, b.w)
        for b in writes:
            if b.dram:
                continue
            _mx(need, b.w)
            _mx(need, b.r)
        self._wait(need)
        sb = sem_buf
        if sb is None:
            for b in list(writes) + list(reads):
                if not b.dram:
                    sb = b
                    break
        if sb.sem is None:
            sb.sem, sb.cnt = self.fw.get_dsem(sb.name)
        self.prog.append(("o", (lambda e, o=out, i=in_: e.dma_start(out=o, in_=i)), sb.sem, 16))
        sb.cnt += 16
        for b in reads:
            b.r[sb.sem] = sb.cnt
        for b in writes:
            if b.dram:
                b.w[sb.sem] = sb.cnt
            else:
                b.w = {sb.sem: sb.cnt}
                b.r = {}


class FW:
    def __init__(self, nc, stack):
        self.nc = nc
        self.stack = stack
        self.dsems = []
        self.pe = Eng(self, "pe", self._sem("pe"))
        self.act = Eng(self, "act", self._sem("act"))
        self.dve = Eng(self, "dve", self._sem("dve"))
        self.pool = Eng(self, "pool", self._sem("pool"))
        self.sp = Eng(self, "sp", self._sem("sp"))
        self.engs = [self.pe, self.act, self.dve, self.pool, self.sp]
        self.dbufs = []
        self.sem_pool = []
        self.rr = 0
        self.erot = 0

    def _sem(self, name):
        return self.stack.enter_context(self.nc.semaphore(name))

    def new_sem(self, name):
        return self._sem(name)

    def get_dsem(self, name):
        if self.sem_pool:
            return self.sem_pool.pop()
        return self._sem("d_" + name), 0

    def sbuf(self, st, name, shape, dt=F32):
        self.nid = getattr(self, "nid", 0) + 1
        t = st.enter_context(self.nc.sbuf_tensor(f"sb{self.nid}_{name}", list(shape), dt))
        b = Buf(name)
        self.dbufs.append(b)
        return t, b

    def barrier(self):
        need = {}
        for e in self.engs:
            need[e.sem] = e.count
        for b in self.dbufs:
            if b.sem is not None:
                need[b.sem] = b.cnt
        for e in self.engs:
            n2 = dict(need)
            n2.pop(e.sem, None) if e.name == "pe" else None
            e._wait(n2)
        for b in self.dbufs:
            if b.sem is not None:
                self.sem_pool.append((b.sem, b.cnt))
                b.sem = None

    def emit(self):
        with self.nc.Block() as block:
            @block.tensor
            def _(e):
                self.pe.replay(e)

            @block.scalar
            def _(e):
                self.act.replay(e)

            @block.vector
            def _(e):
                self.dve.replay(e)

            @block.gpsimd
            def _(e):
                self.pool.replay(e)

            @block.sync
            def _(e):
                self.sp.replay(e)


def build_program(phases=9, dbg=()):
    nc = bass.Bass("TRN2", target_bir_lowering=False)

    def din(name, shape, dt=F32):
        return nc.dram_tensor(name, list(shape), dt, kind="ExternalInput")

    def dint(name, shape, dt=F32):
        return nc.dram_tensor(name, list(shape), dt, kind=("ExternalOutput" if name in dbg else "Internal"))

    xs = din("xs", [4224, D]).ap()
    ctxb = din("ctxb", [256, D]).ap()
    cT_d = din("cT", [128, 16, 2]).ap()
    w_mod = din("w_mod", [D, 3 * D]).ap()
    b_mod = din("b_mod", [3 * D])
    norm_g = din("norm_g", [D])
    w_in = din("w_in", [D, PIN]).ap()
    colv_d = din("colv", [128, NCOLV]).ap()
    aw2_d = din("aw2p", [128, 1024]).ap()
    aa2_d = din("aa2p", [128, 1024]).ap()
    gkw_d = din("gkw2p", [32, 2, 512]).ap()
    lnxg = din("lnxg", [1024])
    lnxb = din("lnxb", [1024])
    bng = din("bng", [256])
    w_a = din("w_a", [1024, D]).ap()
    w_b = din("w_b", [1024, D]).ap()
    w_out = din("w_out", [D, D]).ap()
    final_g = din("final_g", [D])
    identf_d = din("identf", [128, 128]).ap()
    identst_d = din("identst", [128, 64]).ap()
    maskG_d = din("maskG", [128, 2, 128]).ap()
    maskL_d = din("maskL", [128, 2, 64]).ap()
    maskA_d = din("maskA", [64, 2, 64]).ap()
    bones_d = din("bones", [128, 128]).ap()
    ind2_d = din("ind2", [128, 2]).ap()
    rmask_d = din("rmask", [128, 512]).ap()
    out_d = nc.dram_tensor("out", [SEG, D], F32, kind="ExternalOutput").ap()

    hTd = dint("hTd", [128, 16, 4480], BF16).ap()
    spA = dint("spA", [2, NCH, 128, 8 * 192]).ap()
    y0d = dint("y0d", [NCH, 128, 8, 64]).ap()
    bvd = dint("bvd", [NCH, 128, 8, 64]).ap()
    spG = dint("spG", [2, NCH, 128, 4 * 321]).ap()
    o0d = dint("o0d", [SEG, 1024]).ap()
    zgd = dint("zgd", [SEG, 4096], BF16).ap()
    zaTd = dint("zaTd", [16, 128, SEG], BF16).ap()
    mTd = dint("mTd", [128, 16, SEG], BF16).ap()
    gate_d = dint("gate_d", [128, D]).ap()
    ypd = dint("ypd", [NCH, 128, 8, 64]).ap()
    opd = dint("opd", [SEG, 1024]).ap()
    CSPL = [(0, 2048), (2048, 3080), (3080, 4104)]
    comp_in = [dint(f"comp_in{i}", [128, b - a]) for i, (a, b) in enumerate(CSPL)]
    comp_out = [dint(f"comp_out{i}", [512, b - a]) for i, (a, b) in enumerate(CSPL)]
    bD = Buf("dram", dram=True)

    def bc(t, n, inner=None):
        if inner is None:
            return bass.AP(t, 0, [[0, 128], [1, n]])
        return bass.AP(t, 0, [[0, 128], [0, inner], [1, n]])

    with ExitStack() as top:
        fw = FW(nc, top)
        pe, act, dve, pool, sp = fw.pe, fw.act, fw.dve, fw.pool, fw.sp
        PS = []
        for i in range(8):
            t = top.enter_context(nc.psum_tensor(f"ps{i}", [128, 512], F32))
            PS.append((t, Buf(f"ps{i}")))

        def bank():
            fw.rr = (fw.rr + 1) % 8
            return PS[fw.rr]

        def mm(out, lhsT, rhs, rd, bps, start=True, stop=True):
            pe.op(lambda e: e.matmul(out, lhsT=lhsT, rhs=rhs, start=start, stop=stop), reads=rd, writes=[bps])

        def acopy(out, in_, rd, wr, func=AF.Copy, **kw):
            act.op(lambda e: e.activation(out=out, in_=in_, func=func, **kw), reads=rd, writes=wr)

        def vcopy(out, in_, rd, wr):
            dve.op(lambda e: e.tensor_copy(out=out, in_=in_), reads=rd, writes=wr)

        def ecopy(out, in_, rd, wr):
            fw.erot += 1
            if fw.erot % 2:
                acopy(out, in_, rd, wr)
            else:
                vcopy(out, in_, rd, wr)

        def tt(eng, out, in0, in1, op, rd, wr):
            eng.op(lambda e: e.tensor_tensor(out=out, in0=in0, in1=in1, op=op), reads=rd, writes=wr)

        def ts(eng, out, in0, s1, s2, op0, op1, rd, wr):
            if s2 is None:
                eng.op(lambda e: e.tensor_scalar(out=out, in0=in0, scalar1=s1, scalar2=None, op0=op0), reads=rd, writes=wr)
            else:
                eng.op(lambda e: e.tensor_scalar(out=out, in0=in0, scalar1=s1, scalar2=s2, op0=op0, op1=op1), reads=rd, writes=wr)

        def treduce(out, in_, rd, wr):
            dve.op(lambda e: e.tensor_reduce(out=out, in_=in_, axis=AX.X, op=ALU.add), reads=rd, writes=wr)

        def recip(out, in_, rd, wr):
            dve.op(lambda e: e.reciprocal(out=out, in_=in_), reads=rd, writes=wr)

        def stt(out, in0, sc, in1, op0, op1, rd, wr):
            dve.op(lambda e: e.scalar_tensor_tensor(out=out, in0=in0, scalar=sc, in1=in1, op0=op0, op1=op1), reads=rd, writes=wr)

        identf, b_identf = fw.sbuf(top, "identf", [128, 128])
        identb, b_identb = fw.sbuf(top, "identb", [128, 128], BF16)
        identst, b_identst = fw.sbuf(top, "identst", [128, 64])
        colv, b_colv = fw.sbuf(top, "colv", [128, NCOLV])
        sp.dma(identf[:], identf_d, writes=[b_identf])
        sp.dma(identst[:], identst_d, writes=[b_identst])
        sp.dma(colv[:], colv_d, writes=[b_colv])
        vcopy(identb[:], identf[:], [b_identf], [b_identb])
        w_in_v = w_in.rearrange("(k p) n -> p k n", p=128)
        Hctx = [fw.sbuf(top, f"Hctx{p}", [128, 2, 64]) for p in range(8)]
        Hgctx, b_Hgctx = fw.sbuf(top, "Hgctx", [128, 8, 256])

        with ExitStack() as s0:
            cTt, b_cT = fw.sbuf(s0, "cTt", [128, 16, 2])
            sT, b_sT = fw.sbuf(s0, "sT", [128, 16, 2])
            srep, b_srep = fw.sbuf(s0, "srep", [128, 2, 16, 128], BF16)
            bmods = [fw.sbuf(s0, f"bmod{j}", [128, 256]) for j in range(2)]
            ng_t, b_ng = fw.sbuf(s0, "ng_t", [128, D])
            mt = [fw.sbuf(s0, f"m{j}", [128, 3 * D]) for j in range(2)]
            modA = [fw.sbuf(s0, f"modA{j}", [128, D]) for j in range(2)]
            wm = [fw.sbuf(s0, f"wm{j}", [128, 16, 256]) for j in range(2)]
            wmb = [fw.sbuf(s0, f"wmb{j}", [128, 16, 256], BF16) for j in range(2)]
            sp.dma(cTt[:], cT_d, writes=[b_cT])
            sp.dma(ng_t[:], bc(norm_g, D), writes=[b_ng])
            acopy(sT[:], cTt[:], [b_cT], [b_sT], func=AF.Silu)
            for j in range(2):
                vcopy(srep[:, j], sT[:, :, j:j + 1].to_broadcast([128, 16, 128]), [b_sT], [b_srep])
            wmv = w_mod.rearrange("(k p) n -> p k n", p=128)
            for nb in range(24):
                wt, bw = wm[nb % 2]
                sp.dma(wt[:], wmv[:, :, nb * 256:(nb + 1) * 256], writes=[bw])
                bmod_t, b_bmod = bmods[nb % 2]
                sp.dma(bmod_t[:], bass.AP(b_mod, nb * 256, [[0, 128], [1, 256]]), writes=[b_bmod])
                wtb, bwtb = wmb[nb % 2]
                if nb % 2 == 0:
                    pool.op(lambda e, wtb=wtb, wt=wt: e.tensor_copy(out=wtb[:, 0:8, :], in_=wt[:, 0:8, :]), reads=[bw], writes=[bwtb])
                    acopy(wtb[:, 8:16, :], wt[:, 8:16, :], [bw], [bwtb])
                else:
                    vcopy(wtb[:, 0:8, :], wt[:, 0:8, :], [bw], [bwtb])
                    acopy(wtb[:, 8:16, :], wt[:, 8:16, :], [bw], [bwtb])
                for j in range(2):
                    ps, bps = bank()
                    for k in range(16):
                        mm(ps[:, 0:256], srep[:, j, k, :], wtb[:, k, :], [b_srep, bwtb], bps, start=(k == 0), stop=(k == 15))
                    tt(dve, mt[j][0][:, nb * 256:(nb + 1) * 256], ps[:, 0:256], bmod_t[:, :], ALU.add,
                       [bps, b_bmod], [bps, mt[j][1]])
            for j in range(2):
                stt(modA[j][0][:], mt[j][0][:, D:2 * D], 1.0, ng_t[:], ALU.add, ALU.mult, [mt[j][1], b_ng], [modA[j][1]])
            act.dma(gate_d, mt[0][0][:, 2 * D:3 * D], reads=[mt[0][1]], writes=[bD], sem_buf=mt[0][1])
            xt = [fw.sbuf(s0, f"xt{j}", [128, D]) for j in range(2)]
            hf, b_hf = fw.sbuf(s0, "hf", [128, D])
            hb, b_hb = fw.sbuf(s0, "hb", [128, D], BF16)
            ss, b_ss = fw.sbuf(s0, "ss", [128, 4])
            hTt = [fw.sbuf(s0, f"hTt{j}", [128, 16, 128], BF16) for j in range(2)]
            for ti in range(35):
                j = 0 if ti < 33 else 1
                src = xs[ti * 128:(ti + 1) * 128, :] if ti < 33 else ctxb[(ti - 33) * 128:(ti - 32) * 128, :]
                x_t, bx = xt[ti % 2]
                sp.dma(x_t[:], src, writes=[bx])
                acopy(hf[:], x_t[:], [bx], [b_hf, b_ss], func=AF.Square, accum_out=ss[:, 0:1])
                ts(dve, ss[:, 1:2], ss[:, 0:1], 1.0 / D, 1e-6, ALU.mult, ALU.add, [b_ss], [b_ss])
                acopy(ss[:, 2:3], ss[:, 1:2], [b_ss], [b_ss], func=AF.Sqrt)
                dve.op(lambda e: e.reciprocal(out=ss[:, 3:4], in_=ss[:, 2:3]), reads=[b_ss], writes=[b_ss])
                stt(hf[:], x_t[:], ss[:, 3:4], modA[j][0][:], ALU.mult, ALU.mult, [bx, b_ss, modA[j][1]], [b_hf])
                tt(pool, hb[:], hf[:], mt[j][0][:, 0:D], ALU.add, [b_hf, mt[j][1]], [b_hb])
                ht, bht = hTt[ti % 2]
                for k4 in range(4):
                    ps, bps = bank()
                    for kk in range(4):
                        k = k4 * 4 + kk
                        mm(ps[:, kk * 128:(kk + 1) * 128], hb[:, k * 128:(k + 1) * 128], identb[:], [b_hb, b_identb], bps)
                    ecopy(ht[:, k4 * 4:(k4 + 1) * 4, :], ps[:, :].rearrange("p (a b) -> p a b", b=128), [bps], [bps, bht])
                act.dma(hTd[:, :, ti * 128:(ti + 1) * 128], ht[:], reads=[bht], writes=[bD], sem_buf=bht)
            fw.barrier()
            fw.emit()
        if phases <= 0:
            return nc, fw

        with ExitStack() as s1:
            cur_st = [s1]

            def T(name, shape, dt=F32):
                return fw.sbuf(cur_st[0], name, shape, dt)
            maskG, b_maskG = T("maskG", [128, 2, 128])
            maskL, b_maskL = T("maskL", [128, 2, 64])
            maskA, b_maskA = T("maskA", [64, 2, 64])
            bonesf, b_bonesf = T("bonesf", [128, 128])
            bones, b_bones = T("bones", [128, 128], BF16)
            ind2f, b_ind2f = T("ind2f", [128, 2])
            ind2, b_ind2 = T("ind2", [128, 2], BF16)
            rmask, b_rmask = T("rmask", [128, 256])
            aw2, b_aw2 = T("aw2", [128, 1024])
            aa2, b_aa2 = T("aa2", [128, 1024])
            gkw, b_gkw = T("gkw", [32, 2, 512])
            dcol, b_dcol = T("dcol", [128, 16])
            for t_, b_, d_ in ((maskG, b_maskG, maskG_d), (maskL, b_maskL, maskL_d), (maskA, b_maskA, maskA_d),
                               (bonesf, b_bonesf, bones_d), (ind2f, b_ind2f, ind2_d), (rmask, b_rmask, rmask_d[:, 0:256]),
                               (aw2, b_aw2, aw2_d), (aa2, b_aa2, aa2_d), (gkw, b_gkw, gkw_d)):
                sp.dma(t_[:], d_, writes=[b_])
            vcopy(bones[:], bonesf[:], [b_bonesf], [b_bones])
            vcopy(ind2[:], ind2f[:], [b_ind2f], [b_ind2])
            ts(dve, dcol[:, 0:8], colv[:, CV_KA:CV_KA + 8], -1.0, 1.0, ALU.mult, ALU.add, [b_colv], [b_dcol])
            ts(dve, dcol[:, 8:16], colv[:, CV_GB:CV_GB + 8], -1.0, None, ALU.mult, None, [b_colv], [b_dcol])

            hT, b_hT = T("hT", [128, 16, 1152], BF16)
            lw, b_lw = T("lw", [128, 1024])
            la, b_la = T("la", [128, 1024])
            lg, b_lg = T("lg", [32, 1024])
            wf = [T(f"wf{j}", [128, 16, 128]) for j in range(2)]
            wb = [T(f"wb{j}", [128, 16, 128], BF16) for j in range(4)]
            wcnt = [0]

            def load_w(col0, ncols, slot):
                wt, bw = wf[wcnt[0] % 2]
                wcnt[0] += 1
                sp.dma(wt[:, :, 0:ncols], w_in_v[:, :, col0:col0 + ncols], writes=[bw])
                dst, bd = wb[slot]
                acopy(dst[:, 0:8, 0:ncols], wt[:, 0:8, 0:ncols], [bw], [bd])
                pool.op(lambda e: e.tensor_copy(out=dst[:, 8:16, 0:ncols], in_=wt[:, 8:16, 0:ncols]), reads=[bw], writes=[bd])
                return dst, bd

            def proj_fm(wt, bw, ncols, col0, ntok, dst, bdst, func=AF.Copy):
                for tb in range((ntok + 511) // 512):
                    n = min(512, ntok - tb * 512)
                    ps, bps = bank()
                    for k in range(16):
                        mm(ps[0:ncols, 0:n], wt[:, k, 0:ncols], hT[:, k, col0 + tb * 512:col0 + tb * 512 + n], [bw, b_hT], bps,
                           start=(k == 0), stop=(k == 15))
                    if func == AF.Copy:
                        ecopy(dst[0:ncols, tb * 512:tb * 512 + n], ps[0:ncols, 0:n], [bps], [bps, bdst])
                    else:
                        acopy(dst[0:ncols, tb * 512:tb * 512 + n], ps[0:ncols, 0:n], [bps], [bps, bdst], func=func)

            cin = [T(f"cin{j}", [128, 384]) for j in range(3)]
            cout = [T(f"cout{j}", [128, 256]) for j in range(3)]
            A = {}
            for nm in ("kk", "t1", "sg", "icl", "cs", "g", "gx", "eng", "ec", "bb", "kd", "rk"):
                A[nm] = T("a_" + nm, [128, 256])
            A["kka"] = A["kk"]
            A["t2"] = A["t1"]
            A["egx"] = A["gx"]
            sqb, b_sqb = T("sqb", [128, 256], BF16)
            vbS = [T(f"vb{i}", [128, 256], BF16) for i in range(2)]
            rkbS = [T(f"rkb{i}", [128, 256], BF16) for i in range(2)]
            egdS = [[T(f"eg{i}_{d}", [128, 256]) for d in range(2)] for i in range(2)]
            rtdS = [[T(f"rt{i}_{d}", [128, 256]) for d in range(2)] for i in range(2)]
            ardS = [[T(f"ar{i}_{d}", [128, 2, 256], BF16) for d in range(2)] for i in range(2)]
            btdS = [[T(f"bt{i}_{d}", [128, 256], BF16) for d in range(2)] for i in range(2)]
            ktdS = [[T(f"kt{i}_{d}", [128, 256], BF16) for d in range(2)] for i in range(2)]
            bhdS = [[T(f"bh{i}_{d}", [128, 256], BF16) for d in range(2)] for i in range(2)]
            khdS = [[T(f"kh{i}_{d}", [128, 256], BF16) for d in range(2)] for i in range(2)]
            vb, b_vb = vbS[0]
            rkb, b_rkb = rkbS[0]
            egd, rtd, ard, btd, ktd, bhd, khd = egdS[0], rtdS[0], ardS[0], btdS[0], ktdS[0], bhdS[0], khdS[0]
            QS = 128.0 ** -0.5

            def reset_gla():
                pool.op(lambda e: e.memset(GN[:], 0.0), writes=[b_GN] + b_GNs)
                pool.op(lambda e: e.memset(GD[:], 1.0), writes=[b_GD])

            def gla_batch(job, S):
                g, col0, tok0, Tn, cg0, spill = job["args"]
                egd, btd, ktd, khd = egdS[S], btdS[S], ktdS[S], khdS[S]
                stG, b_stG = stGS[S]
                if job.get("seg") is not None:
                    hcol, ntok_own, halo_ = job["seg"]
                    own0 = load_seg(hcol, ntok_own, halo_)
                    wt, bw = load_w(C_LG, 32, 3)
                    proj_fm(wt, bw, 32, own0, ntok_own, lg, b_lg)
                    yield
                if job.get("newhead"):
                    load_w(C_QB + g * 128, 128, 0)
                    load_w(C_KB + g * 128, 128, 1)
                    load_w(C_VB + g * 256, 128, 2)
                    load_w(C_VB + g * 256 + 128, 128, 3)
                nch = Tn // 64
                sl = slice(0, Tn)
                qf, bqf = A["kk"]
                kf, bkf = A["t1"]
                vbg, bvbg = ardS[S][0]
                proj_fm(wb[0][0], wb[0][1], 128, col0, Tn, qf, bqf)
                yield
                proj_fm(wb[1][0], wb[1][1], 128, col0, Tn, kf, bkf)
                yield
                for j in range(2):
                    proj_fm(wb[2 + j][0], wb[2 + j][1], 128, col0, Tn, vbg[:, j, :], bvbg)
                    yield
                c3 = lambda ap: ap.rearrange("p (c t) -> p c t", t=64)
                for d in range(2):
                    e1, be1 = A["cs"]
                    spl, bspl = A["g"]
                    csg, bcsg = A["gx"]
                    Gs, bGs = A["eng"]
                    ek, bek = A["bb"]
                    tk, btk = A["kd"]
                    eq, beq = egd[d]
                    ps, bps = bank()
                    mm(ps[:, 0:Tn], gkw[0:32, d, g * 128:(g + 1) * 128], lg[0:32, tok0:tok0 + Tn], [b_gkw, b_lg], bps)
                    acopy(e1[:, sl], ps[:, 0:Tn], [bps, b_dcol], [bps, be1], func=AF.Exp, scale=-1.0,
                          bias=dcol[:, 8 + d * 4 + g:8 + d * 4 + g + 1])
                    acopy(spl[:, sl], e1[:, sl], [be1], [bspl], func=AF.Ln, bias=1.0)
                    yield
                    dve.op(lambda e: e.tensor_tensor_scan(out=csg[:, sl], data0=rmask[:, sl], data1=spl[:, sl], initial=0.0,
                                                          op0=ALU.mult, op1=ALU.add), reads=[b_rmask, bspl], writes=[bcsg])
                    if d == 0:
                        Gsrc, bGsrc = csg, bcsg
                    else:
                        tt(dve, c3(Gs[:, sl]), c3(csg[:, sl])[:, :, 63:64].to_broadcast([128, nch, 64]), c3(csg[:, sl]), ALU.subtract,
                           [bcsg], [bGs])
                        tt(pool, Gs[:, sl], Gs[:, sl], spl[:, sl], ALU.add, [bGs, bspl], [bGs])
                        Gsrc, bGsrc = Gs, bGs
                    acopy(eq[:, sl], Gsrc[:, sl], [bGsrc], [beq], func=AF.Exp, scale=-1.0 / 16)
                    acopy(ek[:, sl], Gsrc[:, sl], [bGsrc], [bek], func=AF.Exp, scale=1.0 / 16)
                    yield
                    eq3 = c3(eq[:, sl])
                    tcol = 63 if d == 0 else 0
                    stt(stG[:, 0:nch, d, 0:64], c3(qf[:, sl]), QS, eq3, ALU.mult, ALU.mult, [bqf, beq], [b_stG])
                    stt(btd[d][0][:, sl], qf[:, sl], QS, eq[:, sl], ALU.mult, ALU.mult, [bqf, beq], [btd[d][1]])
                    tt(pool, ktd[d][0][:, sl], kf[:, sl], ek[:, sl], ALU.mult, [bkf, bek], [ktd[d][1]])
                    tt(dve, c3(tk[:, sl]), c3(ek[:, sl]), eq3[:, :, tcol:tcol + 1].to_broadcast([128, nch, 64]), ALU.mult, [bek, beq], [btk])
                    tt(pool, khd[d][0][:, sl], kf[:, sl], tk[:, sl], ALU.mult, [bkf, btk], [khd[d][1]])
                    vcopy(stG[:, 0:nch, d, 320:321], eq3[:, :, tcol:tcol + 1], [beq], [b_stG])
                    yield
                yield "PREP_DONE"
                if job.get("reset"):
                    reset_gla()
                adv = DBG.get('adv', 3)

                def gcommon(ci):
                    c = slice(ci * 64, (ci + 1) * 64)
                    vtg, bvtg = VTg[ci]
                    psV, bpsV = bank()
                    for j in range(2):
                        mm(psV[0:64, j * 128:(j + 1) * 128], vbg[:, j, c], identb[:], [bvbg, b_identb], bpsV)
                    acopy(vtg[:, :], psV[0:64, 0:256], [bpsV], [bpsV, bvtg])

                def gchain(ci, d):
                    c = slice(ci * 64, (ci + 1) * 64)
                    vtg, bvtg = VTg[ci]
                    att, batt = AttT[ci * 2 + d]
                    kht, bkht = KhT[ci * 2 + d]
                    ps, bps = bank()
                    mm(ps[0:64, 0:64], ktd[d][0][:, c], btd[d][0][:, c], [ktd[d][1], btd[d][1]], bps)
                    mm(ps[0:64, 64:192], khd[d][0][:, c], identb[:], [khd[d][1], b_identb], bps)
                    tt(dve, att[:, :], ps[0:64, 0:64], maskA[:, d, :], ALU.mult, [bps, b_maskA], [bps, batt])
                    acopy(kht[:, :], ps[0:64, 64:192], [bps], [bps, bkht])
                    yield
                    psN, bpsN = bank()
                    mm(psN[:, 0:256], kht[:, :], vtg[:, :], [bkht, bvtg], bpsN)
                    acopy(stG[:, ci, d, 64:320], psN[:, 0:256], [bpsN], [bpsN, b_stG])
                    dc = stG[:, ci, d, 320:321]
                    gi = g * 2 + d
                    if d == 0:
                        stt(GN[:, gi, :], GN[:, gi, :], dc, psN[:, 0:256], ALU.mult, ALU.add, [b_GNs[gi], b_stG, bpsN], [bpsN, b_GNs[gi]])
                    else:
                        stt(GN[:, gi, :], psN[:, 0:256], GD[:, gi:gi + 1], GN[:, gi, :], ALU.mult, ALU.add, [b_GNs[gi], b_GD, bpsN],
                            [bpsN, b_GNs[gi]])
                    ts(dve, GD[:, gi:gi + 1], GD[:, gi:gi + 1], dc, None, ALU.mult, None, [b_GD, b_stG], [b_GD])
                    yield

                for ci in range(nch):
                    gcommon(ci)
                gens = [gchain(ci, d) for ci in range(nch) for d in range(2)]
                while gens:
                    alive = []
                    for g_ in gens:
                        try:
                            next(g_)
                            alive.append(g_)
                        except StopIteration:
                            pass
                    gens = alive
                    adv_extra(adv)
                if spill:
                    for ci in range(nch):
                        par = (cg0 + ci) % 2
                        pp = slice(par * 64, (par + 1) * 64)
                        vtg, bvtg = VTg[ci]
                        psO, bpsO = bank()
                        mm(psO[pp, 0:256], AttT[ci * 2][0][:, :], vtg[:, :], [AttT[ci * 2][1], bvtg], bpsO, start=True, stop=False)
                        mm(psO[pp, 0:256], AttT[ci * 2 + 1][0][:, :], vtg[:, :], [AttT[ci * 2 + 1][1], bvtg], bpsO, start=False, stop=True)
                        acopy(o0st[pp, ci // 2, :], psO[pp, 0:256], [bpsO], [bpsO, b_o0st])
                if spill:
                    for d in range(2):
                        act.dma(spG[d, cg0:cg0 + nch, :, g * 321:(g + 1) * 321].rearrange("c p f -> p c f"), stG[:, 0:nch, d, :],
                               reads=[b_stG], writes=[bD], sem_buf=b_stG)
                    tk0 = cg0 * 64
                    act.dma(o0d[tk0:tk0 + Tn, g * 256:(g + 1) * 256].rearrange("(a p) f -> p a f", p=128), o0st[:, 0:nch // 2, :],
                           reads=[b_o0st], writes=[bD], sem_buf=b_o0st)
                adv_extra(adv)
                if job.get("ctx_last"):
                    vcopy(Hgctx[:], GN[:], [b_GN] + b_GNs, [b_Hgctx])

            def reset_acc():
                for p in range(8):
                    for d, ACC in ((0, ACCf), (1, ACCb)):
                        accsel[p][d] = 0
                        t_, b_ = ACC[p][0]
                        pool.op(lambda e, t_=t_: e.memset(t_[:, 64:128], 0.0), writes=[b_])
                        pool.op(lambda e, t_=t_: e.tensor_copy(out=t_[:, 0:64], in_=identst[:]), reads=[b_identst], writes=[b_])

            PIPE = {"extra": None, "extra_done": True}

            def run_prep(g_):
                while next(g_) != "PREP_DONE":
                    pass

            def adv_extra(n):
                g_ = PIPE["extra"]
                if g_ is None or PIPE["extra_done"]:
                    return
                for _ in range(n):
                    if next(g_) == "PREP_DONE":
                        PIPE["extra_done"] = True
                        return

            def rwkv_batch(job, S):
                p, col0, tok0, Tn, rows, Wd, halo, cg0, spill, halo_mask = job["args"]
                vb, b_vb = vbS[S]
                rkb, b_rkb = rkbS[S]
                egd, rtd, ard, btd, ktd, bhd, khd = egdS[S], rtdS[S], ardS[S], btdS[S], ktdS[S], bhdS[S], khdS[S]
                if job.get("seg") is not None:
                    hcol, ntok_own, halo_ = job["seg"]
                    own0 = load_seg(hcol, ntok_own, halo_)
                    wt, bw = load_w(C_LW, 128, 3)
                    proj_fm(wt, bw, 128, own0, ntok_own, lw, b_lw, func=AF.Tanh)
                    yield
                    wt, bw = load_w(C_LA, 128, 3)
                    proj_fm(wt, bw, 128, own0, ntok_own, la, b_la)
                    yield
                if job.get("newpair"):
                    for j in range(3):
                        load_w(j * 1024 + p * 128, 128, j)
                nch = Tn // 64
                Tin = Tn + (2 * Wd if halo else 0)
                for j in range(3):
                    proj_fm(wb[j][0], wb[j][1], 128, col0, Tin, cin[j][0], cin[j][1])
                    yield
                    if halo_mask is not None:
                        hm_lo, hm_hi = halo_mask
                        if hm_lo:
                            ts(pool, cin[j][0][:, 0:Wd], cin[j][0][:, 0:Wd], colv[:, CV_HALO:CV_HALO + 1], None, ALU.mult, None,
                               [cin[j][1], b_colv], [cin[j][1]])
                        if hm_hi:
                            ts(pool, cin[j][0][:, Tin - Wd:Tin], cin[j][0][:, Tin - Wd:Tin], colv[:, CV_HALO + 1:CV_HALO + 2], None,
                               ALU.mult, None, [cin[j][1], b_colv], [cin[j][1]])
                    ti = j * 8 + p
                    i3 = cin[j][0][:, 0:Tin].rearrange("p (r w) -> p r w", w=Wd)
                    o3 = cout[j][0][:, 0:Tn].rearrange("p (r w) -> p r w", w=Wd)
                    r0 = 1 if halo else 0
                    ts(dve, o3, i3[:, r0:r0 + rows, :], colv[:, ti * 9 + 4:ti * 9 + 5], None, ALU.mult, None,
                       [cin[j][1], b_colv], [cout[j][1]])
                    for dy in ((-1, 0, 1) if halo else (0,)):
                        for dx in (-1, 0, 1):
                            if dy == 0 and dx == 0:
                                continue
                            tap = (dy + 1) * 3 + (dx + 1)
                            xo = slice(1, Wd) if dx == -1 else (slice(0, Wd - 1) if dx == 1 else slice(0, Wd))
                            xi = slice(0, Wd - 1) if dx == -1 else (slice(1, Wd) if dx == 1 else slice(0, Wd))
                            stt(o3[:, :, xo], i3[:, r0 + dy:r0 + dy + rows, xi], colv[:, ti * 9 + tap:ti * 9 + tap + 1], o3[:, :, xo],
                                ALU.mult, ALU.add, [cin[j][1], b_colv, cout[j][1]], [cout[j][1]])
                        yield
                r_, br = cout[0]
                k_, bk = cout[1]
                v_, bv_ = cout[2]
                sl = slice(0, Tn)
                acopy(vb[:, sl], v_[:, sl], [bv_], [b_vb])
                kka, bkka = A["kka"]
                kk, bkk = A["kk"]
                t1, bt1 = A["t1"]
                t2, bt2 = A["t2"]
                acopy(kka[:, sl], k_[:, sl], [bk, b_colv], [bkka], scale=colv[:, CV_KK + p:CV_KK + p + 1])
                acopy(sqb[:, sl], kka[:, sl], [bkka], [b_sqb], func=AF.Square)
                ps, bps = bank()
                mm(ps[:, 0:Tn], bones[:], sqb[:, sl], [b_bones, b_sqb], bps)
                acopy(t1[:, sl], ps[:, 0:Tn], [bps], [bps, bt1], func=AF.Sqrt, bias=1e-12)
                dve.op(lambda e: e.reciprocal(out=t2[:, sl], in_=t1[:, sl]), reads=[bt1], writes=[bt2])
                tt(dve, kk[:, sl], kka[:, sl], t2[:, sl], ALU.mult, [bkka, bt2], [bkk])
                yield
                rk, brk = A["rk"]
                for d in range(2):
                    sg, bsg = A["sg"]
                    icl, bicl = A["icl"]
                    cs, bcs = A["cs"]
                    g, bg = A["g"]
                    gx, bgx = A["gx"]
                    eng, beng = A["eng"]
                    egx, begx = A["egx"]
                    ec, bec = A["ec"]
                    bb, bbb = A["bb"]
                    kd, bkd = A["kd"]
                    eg, beg = egd[d]
                    ps, bps = bank()
                    mm(ps[:, 0:Tn], aw2[d * 64:(d + 1) * 64, p * 128:(p + 1) * 128], lw[d * 64:(d + 1) * 64, tok0:tok0 + Tn],
                       [b_aw2, b_lw], bps)
                    acopy(sg[:, sl], ps[:, 0:Tn], [bps, b_colv], [bps, bsg], func=AF.Sigmoid,
                          bias=colv[:, CV_W0 + d * 8 + p:CV_W0 + d * 8 + p + 1])
                    ps, bps = bank()
                    mm(ps[:, 0:Tn], aa2[d * 64:(d + 1) * 64, p * 128:(p + 1) * 128], la[d * 64:(d + 1) * 64, tok0:tok0 + Tn],
                       [b_aa2, b_la], bps)
                    acopy(icl[:, sl], ps[:, 0:Tn], [bps, b_colv], [bps, bicl], func=AF.Sigmoid,
                          bias=colv[:, CV_A0 + d * 8 + p:CV_A0 + d * 8 + p + 1])
                    yield
                    dve.op(lambda e: e.tensor_tensor_scan(out=cs[:, sl], data0=rmask[:, sl], data1=sg[:, sl], initial=0.0,
                                                          op0=ALU.mult, op1=ALU.add), reads=[b_rmask, bsg], writes=[bcs])
                    cs3 = cs[:, sl].rearrange("p (c t) -> p c t", t=64)
                    if d == 0:
                        gsrc, bgs = cs, bcs
                        tt(dve, gx[:, sl], cs[:, sl], sg[:, sl], ALU.subtract, [bcs, bsg], [bgx])
                    else:
                        tt(dve, gx[:, sl].rearrange("p (c t) -> p c t", t=64), cs3[:, :, 63:64].to_broadcast([128, nch, 64]), cs3,
                           ALU.subtract, [bcs], [bgx])
                        tt(dve, g[:, sl], gx[:, sl], sg[:, sl], ALU.add, [bgx, bsg], [bg])
                        gsrc, bgs = g, bg
                    acopy(eg[:, sl], gsrc[:, sl], [bgs], [beg], func=AF.Exp, scale=-KDEC)
                    acopy(eng[:, sl], gsrc[:, sl], [bgs], [beng], func=AF.Exp, scale=KDEC)
                    acopy(egx[:, sl], gx[:, sl], [bgx], [begx], func=AF.Exp, scale=-KDEC)
                    yield
                    eg3 = eg[:, sl].rearrange("p (c t) -> p c t", t=64)
                    tcol = 63 if d == 0 else 0
                    tt(dve, ec[:, sl].rearrange("p (c t) -> p c t", t=64), eng[:, sl].rearrange("p (c t) -> p c t", t=64),
                       eg3[:, :, tcol:tcol + 1].to_broadcast([128, nch, 64]), ALU.mult, [beng, beg], [bec])
                    ar, bar = ard[d]
                    stt(ar[:, 0, sl], kk[:, sl], -1.0, egx[:, sl], ALU.mult, ALU.mult, [bkk, begx], [bar])
                    tt(dve, bb[:, sl], kk[:, sl], icl[:, sl], ALU.mult, [bkk, bicl], [bbb])
                    tt(dve, btd[d][0][:, sl], bb[:, sl], eng[:, sl], ALU.mult, [bbb, beng], [btd[d][1]])
                    tt(dve, bhd[d][0][:, sl], bb[:, sl], ec[:, sl], ALU.mult, [bbb, bec], [bhd[d][1]])
                    yield
                    ts(dve, t1[:, sl], icl[:, sl], colv[:, CV_KA + p:CV_KA + p + 1], dcol[:, p:p + 1], ALU.mult, ALU.add,
                       [bicl, b_colv, b_dcol], [bt1])
                    tt(dve, kd[:, sl], t1[:, sl], k_[:, sl], ALU.mult, [bt1, bk], [bkd])
                    tt(dve, ktd[d][0][:, sl], kd[:, sl], eng[:, sl], ALU.mult, [bkd, beng], [ktd[d][1]])
                    tt(dve, khd[d][0][:, sl], kd[:, sl], ec[:, sl], ALU.mult, [bkd, bec], [khd[d][1]])
                    yield
                    tt(dve, rtd[d][0][:, sl], r_[:, sl], eg[:, sl], ALU.mult, [br, beg], [rtd[d][1]])
                    acopy(ar[:, 1, sl], rtd[d][0][:, sl], [rtd[d][1]], [bar])
                    if d == 0:
                        acopy(rk[:, sl], kd[:, sl], [bkd], [brk])
                    else:
                        tt(dve, rk[:, sl], rk[:, sl], kd[:, sl], ALU.add, [brk, bkd], [brk])
                stt(rkb[:, sl], rk[:, sl], colv[:, CV_RK + p:CV_RK + p + 1], r_[:, sl], ALU.mult, ALU.mult, [brk, b_colv, br], [b_rkb])

                yield "PREP_DONE"
                if job.get("reset"):
                    reset_acc()
                nci = min(nch, DBG.get('maxc', 9))

                def common(ci):
                    c = slice(ci * 64, (ci + 1) * 64)
                    vts, bvts = VTs[ci]
                    bo, bbo = bon[ci]
                    psV, bpsV = bank()
                    for h in range(2):
                        hs = slice(h * 64, (h + 1) * 64)
                        mm(psV[hs, 0:64], vb[hs, c], identb[hs, hs], [b_vb, b_identb], bpsV)
                    for h in range(2):
                        hs = slice(h * 64, (h + 1) * 64)
                        mm(psV[hs, 64:66], rkb[hs, c], ind2[hs, :], [b_rkb, b_ind2], bpsV)
                    acopy(vts[:], psV[:, 0:64], [bpsV], [bpsV, bvts])
                    vcopy(bo[:, :], psV[:, 64:66], [bpsV], [bpsV, bbo])
                    if spill and DBG.get('sp_bv', True):
                        for h in range(2):
                            hs = slice(h * 64, (h + 1) * 64)
                            ts(dve, bvst[hs, ci, :], psV[hs, 0:64], bo[hs, h:h + 1], None, ALU.mult, None, [bpsV, bbo],
                               [bpsV, b_bvst])

                RS = {}

                def chain(ci, d):
                    c = slice(ci * 64, (ci + 1) * 64)
                    sidx = ci * 2 + d
                    vts, bvts = VTs[ci]
                    ar, bar = ard[d]
                    tm3 = TM3all[:, sidx]
                    gxt, bgxt = GX[sidx]
                    l0, bl0 = L0[sidx]
                    bsg_ = b_stgs[sidx]
                    H2 = [slice(0, 64), slice(64, 128)]
                    ps, bps = bank()
                    for j, (X, bX) in enumerate(((ar, bar), bhd[d], khd[d])):
                        for hs in H2:
                            src = X[hs, 0, c] if j == 0 else X[hs, c]
                            mm(ps[hs, j * 64:(j + 1) * 64], src, identb[hs, hs], [bX, b_identb], bps)
                    ps2, bps2 = bank()
                    for hs in H2:
                        mm(ps2[hs, 0:128], btd[d][0][hs, c], ar[hs, :, c], [btd[d][1], bar], bps2)
                        mm(ps2[hs, 128:256], ktd[d][0][hs, c], ar[hs, :, c], [ktd[d][1], bar], bps2)
                        mm(ps2[hs, 256:320], ar[hs, 0, c], btd[d][0][hs, c], [btd[d][1], bar], bps2)
                    acopy(tm3[:, :, 0:64], ps[:, 0:192].rearrange("p (a b) -> p a b", b=64), [bps], [bps, b_TM3])
                    tt(dve, gxt[:], ps2[:, 0:256].rearrange("p (a b) -> p a b", b=128),
                       maskG[:, d:d + 1, :].to_broadcast([128, 2, 128]), ALU.mult, [bps2, b_maskG], [bps2, bgxt])
                    tt(dve, l0[:], ps2[:, 256:320], maskL[:, d, :], ALU.mult, [bps2, b_maskL], [bps2, bl0])
                    tt(dve, TTall[0][0][:, sidx, :], gxt[:, 0, 0:64], identst[:], ALU.add, [bgxt, b_identst], [TTall[0][1]])
                    yield
                    Xc, bXc = gxt[:, 0, 0:64], bgxt
                    Lc, bLc = l0[:], bl0
                    for j in range(6):
                        Tc, bTc = TTall[j % 2][0][:, sidx, :], TTall[j % 2][1]
                        if j < 5:
                            psq, bpsq = RS["bq"][sidx // 4]
                            o0_ = (sidx % 4) * 128
                            for hs in H2:
                                if j < 4:
                                    mm(psq[hs, o0_:o0_ + 64], Lc[hs], Xc[hs], [bLc, bXc], bpsq)
                                mm(psq[hs, o0_ + 64:o0_ + 128], Xc[hs], Lc[hs], [bLc, bXc], bpsq)
                        if j >= 1:
                            Tp, bTp = TTall[(j - 1) % 2][0][:, sidx, :], TTall[(j - 1) % 2][1]
                            pst, bpst = RS["bt"]
                            for hs in H2:
                                mm(pst[hs, sidx * 64:(sidx + 1) * 64], Lc[hs], Tp[hs], [bLc, bTp], bpst)
                        if j == 0:
                            psx, bpsx = RS["bx"]
                            for hs in H2:
                                mm(psx[hs, sidx * 64:(sidx + 1) * 64], gxt[hs, 1, 0:64], vts[hs], [bgxt, bvts], bpsx)
                        yield
                        if j < 5:
                            xl, bxl = XLall[j % 2]
                            Xc, bXc = xl[:, sidx, 0, :], bxl
                            Lc, bLc = xl[:, sidx, 1, :], bxl
                    Tc, bTc = TTall[5 % 2][0][:, sidx, :], TTall[5 % 2][1]
                    psa, bpsa = RS["ba"][sidx // 4]
                    o0_ = (sidx % 4) * 128
                    for hs in H2:
                        mm(psa[hs, o0_:o0_ + 128], Tc[hs], tm3[hs, 0, :], [bTc, b_TM3], bpsa)
                    yield
                    au, bau = AUall[:, sidx, :], b_AU
                    ps, bps = bank()
                    for hs in H2:
                        mm(ps[hs, 0:64], au[hs, 0:64], tm3[hs, 1, 0:64], [bau, b_TM3], bps)
                        mm(ps[hs, 64:128], au[hs, 0:64], gxt[hs, 0, 64:128], [bau, bgxt], bps)
                        if d == 1:
                            mm(ps[hs, 128:192], tm3[hs, 1, 0:64], au[hs, 0:64], [bau, b_TM3], bps)
                    ps2, bps2 = bank()
                    for hs in H2:
                        mm(ps2[hs, 0:64], tm3[hs, 1, 0:64], au[hs, 64:128], [b_TM3, bau], bps2, start=True, stop=False)
                        mm(ps2[hs, 0:64], tm3[hs, 2, 0:64], vts[hs], [b_TM3, bvts], bps2, start=False, stop=True)
                    eg3 = egd[d][0][:, sl].rearrange("p (c t) -> p c t", t=64)
                    tcol = 63 if d == 0 else 0
                    gam = eg3[:, ci, tcol:tcol + 1]
                    stt(stg[:, ci, d, 0:64], identst[:], gam, ps[:, 0:64], ALU.mult, ALU.add, [b_identst, egd[d][1], bps],
                        [bps, bsg_])
                    tt(dve, stg[:, ci, d, 64:128], ps[:, 64:128], rtd[d][0][:, c], ALU.add, [bps, rtd[d][1]], [bps, bsg_])
                    if d == 1:
                        stt(Mn[ci][0][:], identst[:], gam, ps[:, 128:192], ALU.mult, ALU.add, [b_identst, egd[d][1], bps],
                            [bps, Mn[ci][1]])
                    acopy(stg[:, ci, d, 128:192], ps2[:, 0:64], [bps2], [bps2, bsg_])
                    yield

                for ci in range(nci):
                    common(ci)
                gens = [chain(ci, d) for ci in range(nci) for d in range(2)]
                ns_ = len(gens)
                nh_ = (ns_ + 3) // 4
                adv = DBG.get('adv', 3)
                for g_ in gens:
                    next(g_)
                adv_extra(adv)
                for j in range(6):
                    if j < 5:
                        RS["bq"] = [bank() for _ in range(nh_)]
                    if j >= 1:
                        RS["bt"] = bank()
                    if j == 0:
                        RS["bx"] = bank()
                    for g_ in gens:
                        next(g_)
                    if j < 5:
                        xl, bxl = XLall[j % 2]
                        for hf in range(nh_):
                            n4 = min(4, ns_ - hf * 4)
                            psq, bpsq = RS["bq"][hf]
                            if j < 4:
                                acopy(xl[:, hf * 4:hf * 4 + n4, :, :].rearrange("p s a b -> p s (a b)"),
                                      psq[:, 0:n4 * 128].rearrange("p (s f) -> p s f", f=128), [bpsq], [bpsq, bxl])
                            else:
                                acopy(xl[:, hf * 4:hf * 4 + n4, 1, :], psq[:, 0:n4 * 128].rearrange("p (s f) -> p s f", f=128)[:, :, 64:128],
                                      [bpsq], [bpsq, bxl])
                    if j >= 1:
                        pst, bpst = RS["bt"]
                        tt(dve, TTall[j % 2][0][:, 0:ns_, :], pst[:, 0:ns_ * 64].rearrange("p (s f) -> p s f", f=64),
                           TTall[(j - 1) % 2][0][:, 0:ns_, :], ALU.add, [bpst, TTall[(j - 1) % 2][1]], [bpst, TTall[j % 2][1]])
                    if j == 0:
                        psx, bpsx = RS["bx"]
                        acopy(TM3all[:, 0:ns_, 0, 64:128], psx[:, 0:ns_ * 64].rearrange("p (s f) -> p s f", f=64), [bpsx], [bpsx, b_TM3])
                    adv_extra(adv)
                RS["ba"] = [bank() for _ in range(nh_)]
                for g_ in gens:
                    next(g_)
                for hf in range(nh_):
                    n4 = min(4, ns_ - hf * 4)
                    psa, bpsa = RS["ba"][hf]
                    acopy(AUall[:, hf * 4:hf * 4 + n4, :], psa[:, 0:n4 * 128].rearrange("p (s f) -> p s f", f=128), [bpsa], [bpsa, b_AU])
                adv_extra(adv)
                for g_ in gens:
                    next(g_)
                for g_ in gens:
                    for _ in g_:
                        pass
                adv_extra(adv)
                for ci in range(nci):
                    for d in range(2):
                        bsg_ = b_stgs[ci * 2 + d]
                        ACC = ACCf if d == 0 else ACCb
                        ao, bao = ACC[p][accsel[p][d]]
                        an, ban = ACC[p][1 - accsel[p][d]]
                        accsel[p][d] = 1 - accsel[p][d]
                        ps, bps = bank()
                        if d == 0:
                            for h in range(2):
                                hs = slice(h * 64, (h + 1) * 64)
                                mm(ps[hs, 0:128], stg[hs, ci, 0, 0:64], ao[hs, :], [bsg_, bao], bps)
                            acopy(an[:, 0:64], ps[:, 0:64], [bps], [bps, ban])
                            tt(dve, an[:, 64:128], ps[:, 64:128], stg[:, ci, 0, 128:192], ALU.add, [bps, bsg_], [bps, ban])
                        else:
                            mn_, bmn_ = Mn[ci]
                            for h in range(2):
                                hs = slice(h * 64, (h + 1) * 64)
                                mm(ps[hs, 0:64], mn_[hs], ao[hs, 0:64], [bmn_, bao], bps)
                                mm(ps[hs, 64:128], ao[hs, 0:64], stg[hs, ci, 1, 128:192], [bsg_, bao], bps)
                            acopy(an[:, 0:64], ps[:, 0:64], [bps], [bps, ban])
                            tt(dve, an[:, 64:128], ps[:, 64:128], ao[:, 64:128], ALU.add, [bps, bao], [bps, ban])
                    if spill and DBG.get('sp_y0', True):
                        vts, bvts = VTs[ci]
                        g0, bg0 = GX[ci * 2]
                        g1, bg1 = GX[ci * 2 + 1]
                        a0, ba0 = AUall[:, ci * 2, :], b_AU
                        a1, ba1 = AUall[:, ci * 2 + 1, :], b_AU
                        ps, bps = bank()
                        for h in range(2):
                            hs = slice(h * 64, (h + 1) * 64)
                            o_ = ps[hs, 0:64]
                            rds = [bg0, bg1, ba0, ba1, bvts]
                            mm(o_, g0[hs, 0, 64:128], a0[hs, 64:128], rds, bps, start=True, stop=False)
                            mm(o_, g0[hs, 1, 64:128], vts[hs], rds, bps, start=False, stop=False)
                            mm(o_, g1[hs, 0, 64:128], a1[hs, 64:128], rds, bps, start=False, stop=False)
                            mm(o_, g1[hs, 1, 64:128], vts[hs], rds, bps, start=False, stop=True)
                        acopy(y0st[:, ci, :], ps[:, 0:64], [bps], [bps, b_y0st])
                if spill and DBG.get('sp_dma', True):
                    for d in range(2):
                        act.dma(spA[d, cg0:cg0 + nch, :, p * 192:(p + 1) * 192].rearrange("c p f -> p c f"), stg[:, 0:nch, d, :],
                               reads=[b_stg] + b_stgs, writes=[bD], sem_buf=b_stg)
                    tk0 = cg0 * 64
                    act.dma(y0d[cg0:cg0 + nch, :, p, :].rearrange("c q v -> q c v"), y0st[:, 0:nch, :],
                           reads=[b_y0st], writes=[bD], sem_buf=b_y0st)
                    act.dma(bvd[cg0:cg0 + nch, :, p, :].rearrange("c q v -> q c v"), bvst[:, 0:nch, :],
                           reads=[b_bvst], writes=[bD], sem_buf=b_bvst)
                if job.get("ctx_last"):
                    for d, ACC in ((0, ACCf), (1, ACCb)):
                        a_, ba_ = ACC[p][accsel[p][d]]
                        vcopy(Hctx[p][0][:, d, :], a_[:, 64:128], [ba_], [Hctx[p][1]])

            segs = [("ctx", 4224, 256, 1, 256, False, 1, 256)] + [(f"q{i}", i * 1024, 1024, 4, 64, True, 4, 256) for i in range(4)]

            def load_seg(hcol, ntok_own, halo):
                ncols_h = ntok_own + (128 if halo else 0)
                sp.dma(hT[:, :, 0:ncols_h], hTd[:, :, hcol:hcol + ncols_h], reads=[bD], writes=[b_hT])
                return 64 if halo else 0

            with ExitStack() as s1a:
                cur_st[0] = s1a
                VTs = [T(f"VTs{i}", [128, 64], BF16) for i in range(4)]
                bon = [T(f"bon{i}", [128, 2]) for i in range(4)]
                TM3all, b_TM3 = T("TM3all", [128, 8, 3, 128], BF16)
                GX = [T(f"GX{i}", [128, 2, 128], BF16) for i in range(8)]
                AUall, b_AU = T("AUall", [128, 8, 128], BF16)
                L0 = [T(f"L0{i}", [128, 64], BF16) for i in range(8)]
                XLall = [T(f"XLall{j}", [128, 8, 2, 64], BF16) for j in range(2)]
                TTall = [T(f"TTall{j}", [128, 8, 64], BF16) for j in range(2)]
                Mn = [T(f"Mn{i}", [128, 64]) for i in range(4)]
                b_stgs = [Buf(f"stg{i}") for i in range(8)]
                stg, b_stg = T("stg", [128, 4, 2, 192])
                y0st, b_y0st = T("y0st", [128, 4, 64])
                bvst, b_bvst = T("bvst", [128, 4, 64])
                ACCf = [[T(f"ACCf{p}_{j}", [128, 128]) for j in range(2)] for p in range(8)]
                ACCb = [[T(f"ACCb{p}_{j}", [128, 128]) for j in range(2)] for p in range(8)]
                accsel = [[0, 0] for _ in range(8)]
                jobs = []
                for sname, hcol, ntok_own, rows, Wd, halo, nb, Tn in segs[:DBG.get('nsegs', 5)]:
                    for p in range(DBG.get('maxp', 8)):
                        for b in range(min(nb, DBG.get('maxb', 9))):
                            if sname == "ctx":
                                job = dict(args=(p, 0, 0, 256, 1, 256, False, 0, False, None), ctx_last=True)
                            else:
                                qi = int(sname[1])
                                hm = (qi == 0 and b == 0, qi == 3 and b == 3)
                                job = dict(args=(p, b * 256, b * 256, 256, 4, 64, True, qi * 16 + b * 4, DBG.get("spill", True), hm))
                            if p == 0 and b == 0:
                                job["seg"] = (hcol, ntok_own, halo)
                                if sname in ("ctx", "q0"):
                                    job["reset"] = True
                            if b == 0:
                                job["newpair"] = True
                            jobs.append(job)
                gens_j = [rwkv_batch(job, i % 2) for i, job in enumerate(jobs)]

                def run_prep(g_):
                    while next(g_) != "PREP_DONE":
                        pass

                run_prep(gens_j[0])
                for i in range(len(jobs)):
                    if i + 1 < len(jobs):
                        PIPE["extra"] = gens_j[i + 1]
                        PIPE["extra_done"] = False
                    else:
                        PIPE["extra"] = None
                        PIPE["extra_done"] = True
                    if not DBG.get("pipe", True) and PIPE["extra"] is not None:
                        pass
                    for _ in gens_j[i]:
                        pass
                    if PIPE["extra"] is not None and not PIPE["extra_done"]:
                        run_prep(PIPE["extra"])
                        PIPE["extra_done"] = True
                ci_ap = comp_in[0].ap()
                for p in range(8):
                    for d, ACC in ((0, ACCf), (1, ACCb)):
                        a_, ba_ = ACC[p][accsel[p][d]]
                        c0 = (p * 2 + d) * 128
                        if d == 0:
                            ps, bps = bank()
                            for h in range(2):
                                hs = slice(h * 64, (h + 1) * 64)
                                mm(ps[hs, 0:64], a_[hs, 0:64], identf[hs, hs], [ba_, b_identf], bps)
                            mn_, bmn_ = Mn[p % 4]
                            vcopy(mn_[:], ps[:, 0:64], [bps], [bps, bmn_])
                            act.dma(ci_ap[:, c0:c0 + 64], mn_[:], reads=[bmn_], writes=[bD], sem_buf=bmn_)
                            act.dma(ci_ap[:, c0 + 64:c0 + 128], a_[:, 64:128], reads=[ba_], writes=[bD], sem_buf=ba_)
                        else:
                            act.dma(ci_ap[:, c0:c0 + 128], a_[:, :], reads=[ba_], writes=[bD], sem_buf=ba_)
                fw.barrier()
                fw.emit()
            with ExitStack() as s1b:
                cur_st[0] = s1b
                stGS = [T(f"stG{i}", [128, 4, 2, 321]) for i in range(2)]
                o0st, b_o0st = T("o0st", [128, 2, 256])
                VTg = [T(f"VTg{i}", [64, 256], BF16) for i in range(4)]
                AttT = [T(f"AttT{i}", [64, 64], BF16) for i in range(8)]
                KhT = [T(f"KhT{i}", [64, 128], BF16) for i in range(8)]
                b_GNs = [Buf(f"GN{i}") for i in range(8)]
                GN, b_GN = T("GN", [128, 8, 256])
                GD, b_GD = T("GD", [128, 8])
                gjobs = []
                for sname, hcol, ntok_own, rows, Wd, halo, nb, Tn in segs[:DBG.get('nsegs', 5)]:
                    for g in range(DBG.get('maxg', 4)):
                        for b in range(nb):
                            if sname == "ctx":
                                job = dict(args=(g, 0, 0, 256, 0, False))
                                if g == 3:
                                    job["ctx_last"] = True
                            else:
                                qi = int(sname[1])
                                job = dict(args=(g, 64 + b * 256, b * 256, 256, qi * 16 + b * 4, True))
                            if g == 0 and b == 0:
                                job["seg"] = (hcol, ntok_own, halo)
                                if sname in ("ctx", "q0"):
                                    job["reset"] = True
                            if b == 0:
                                job["newhead"] = True
                            gjobs.append(job)
                gens_g = [gla_batch(job, i % 2) for i, job in enumerate(gjobs)]
                run_prep(gens_g[0])
                for i in range(len(gjobs)):
                    if i + 1 < len(gjobs):
                        PIPE["extra"] = gens_g[i + 1]
                        PIPE["extra_done"] = False
                    else:
                        PIPE["extra"] = None
                        PIPE["extra_done"] = True
                    for _ in gens_g[i]:
                        pass
                    if PIPE["extra"] is not None and not PIPE["extra_done"]:
                        run_prep(PIPE["extra"])
                        PIPE["extra_done"] = True
                act.dma(comp_in[1].ap()[:, 0:8], GD[:, :], reads=[b_GD], writes=[bD], sem_buf=b_GD)
                act.dma(comp_in[1].ap()[:, 8:1032], GN[:, 0:4, :].rearrange("p g f -> p (g f)"), reads=[b_GN] + b_GNs, writes=[bD], sem_buf=b_GN)
                act.dma(comp_in[2].ap()[:, :], GN[:, 4:8, :].rearrange("p g f -> p (g f)"), reads=[b_GN] + b_GNs, writes=[bD], sem_buf=b_GN)
                fw.barrier()
                fw.emit()
            pass
        with ExitStack() as s1c:
            def T(name, shape, dt=F32):
                return fw.sbuf(s1c, name, shape, dt)
            hT2, b_hT2 = T("hT2", [128, 16, 2048], BF16)
            w5f = [T(f"w5f{j}", [128, 16, 512]) for j in range(2)]
            w5b = [T(f"w5b{j}", [128, 16, 512], BF16) for j in range(2)]
            zst = [T(f"zst{j}", [128, 2048], BF16) for j in range(2)]
            gst = [T(f"gst{j}", [128, 512], BF16) for j in range(3)]
            zblocks = [(C_Z_A, 0), (C_Z_A + 512, 4), (C_ZB, 8), (C_ZB + 512, 12)]
            nw = 0
            ng_ = 0
            for half in range(2):
                sp.dma(hT2[:], hTd[:, :, 64 + half * 2048:64 + (half + 1) * 2048], reads=[bD], writes=[b_hT2])
                for colw, blk0 in zblocks:
                    wt, bw = w5f[nw % 2]
                    wbf, bwb = w5b[nw % 2]
                    nw += 1
                    sp.dma(wt[:], w_in_v[:, :, colw:colw + 512], writes=[bw])
                    for k4 in range(4):
                        eng_ = pool if k4 % 2 == 0 else act
                        if eng_ is pool:
                            pool.op(lambda e, wbf=wbf, wt=wt, k4=k4: e.tensor_copy(out=wbf[:, k4 * 4:(k4 + 1) * 4, :], in_=wt[:, k4 * 4:(k4 + 1) * 4, :]),
                                    reads=[bw], writes=[bwb])
                        else:
                            acopy(wbf[:, k4 * 4:(k4 + 1) * 4, :], wt[:, k4 * 4:(k4 + 1) * 4, :], [bw], [bwb])
                    for sb_ in range(4):
                        zt, bzt = zst[(blk0 + sb_) % 2]
                        for tb in range(4):
                            ps, bps = bank()
                            for k in range(16):
                                mm(ps[:, :], wbf[:, k, sb_ * 128:(sb_ + 1) * 128], hT2[:, k, tb * 512:(tb + 1) * 512], [bwb, b_hT2], bps,
                                   start=(k == 0), stop=(k == 15))
                            acopy(zt[:, tb * 512:(tb + 1) * 512], ps[:, :], [bps], [bps, bzt], func=AF.Silu)
                        act.dma(zaTd[blk0 + sb_, :, half * 2048:(half + 1) * 2048], zt[:, :], reads=[bzt], writes=[bD], sem_buf=bzt)
                for cb in range(8):
                    wt, bw = w5f[nw % 2]
                    wbf, bwb = w5b[nw % 2]
                    nw += 1
                    sp.dma(wt[:], w_in_v[:, :, C_GA + cb * 512:C_GA + (cb + 1) * 512], writes=[bw])
                    for k4 in range(4):
                        if k4 % 2 == 0:
                            pool.op(lambda e, wbf=wbf, wt=wt, k4=k4: e.tensor_copy(out=wbf[:, k4 * 4:(k4 + 1) * 4, :], in_=wt[:, k4 * 4:(k4 + 1) * 4, :]),
                                    reads=[bw], writes=[bwb])
                        else:
                            vcopy(wbf[:, k4 * 4:(k4 + 1) * 4, :], wt[:, k4 * 4:(k4 + 1) * 4, :], [bw], [bwb])
                    for tl in range(16):
                        ps, bps = bank()
                        for k in range(16):
                            mm(ps[:, :], hT2[:, k, tl * 128:(tl + 1) * 128], wbf[:, k, :], [b_hT2, bwb], bps, start=(k == 0), stop=(k == 15))
                        gt, bgt = gst[ng_ % 3]
                        ng_ += 1
                        acopy(gt[:], ps[:, :], [bps], [bps, bgt], func=AF.Sigmoid)
                        r0 = half * 2048 + tl * 128
                        act.dma(zgd[r0:r0 + 128, cb * 512:(cb + 1) * 512], gt[:], reads=[bgt], writes=[bD], sem_buf=bgt)
            fw.barrier()
            fw.emit()
        if phases <= 1:
            return nc, fw

        Hf0, b_Hf0 = fw.sbuf(top, "Hf0", [128, 8, 64])
        Hb0, b_Hb0 = fw.sbuf(top, "Hb0", [128, 8, 64])
        Hgf0, b_Hgf0 = fw.sbuf(top, "Hgf0", [128, 4, 256])
        Hgb0, b_Hgb0 = fw.sbuf(top, "Hgb0", [128, 4, 256])
        identstb, b_identstb = fw.sbuf(top, "identstb", [128, 64], BF16)
        vcopy(identstb[:], identst[:], [b_identst], [b_identstb])

        with ExitStack() as sE:
            gath, b_gath = fw.sbuf(sE, "gath", [128, 4, NCOMP])
            Hx = [fw.sbuf(sE, f"Hx{j}", [128, 8, 64]) for j in range(2)]
            t1x, b_t1x = fw.sbuf(sE, "t1x", [128, 8, 64])
            Gx = [fw.sbuf(sE, f"Gx{j}", [128, 4, 256]) for j in range(2)]
            t2x, b_t2x = fw.sbuf(sE, "t2x", [128, 4, 256])
            if DBG.get("nocc", False):
                for i in range(3):
                    for r in range(4):
                        sp.dma(comp_out[i].ap()[r * 128:(r + 1) * 128, :], comp_in[i].ap(), reads=[bD], writes=[bD], sem_buf=b_gath)
                bCO = bD
            else:
                cc_sem = fw.new_sem("cc")
                pool._wait(dict(bD.w))
                for i in range(3):
                    pool.prog.append(("o", (lambda e, i=i: e.collective_compute(
                        "AllGather", ALU.bypass, replica_groups=[[0, 1, 2, 3], [4, 5, 6, 7]],
                        ins=[comp_in[i].ap()], outs=[comp_out[i].ap()])), cc_sem, 1))
                bCO = Buf("comp_out", dram=True)
                bCO.w = {cc_sem: 3}
            for i, (ca, cb_) in enumerate(CSPL):
                sp.dma(gath[:, :, ca:cb_], comp_out[i].ap().rearrange("(r p) n -> p r n", p=128), reads=[bCO], writes=[b_gath])
            for d in range(2):
                cur, bcur = Hx[0]
                for p in range(8):
                    vcopy(cur[:, p, :], Hctx[p][0][:, d, :], [Hctx[p][1]], [bcur])
                order = (0, 1, 2) if d == 0 else (3, 2, 1)
                for i, sgi in enumerate(order):
                    nxt, bnxt = (Hx[(i + 1) % 2]) if i < 2 else ((Hf0, b_Hf0) if d == 0 else (Hb0, b_Hb0))
                    ps, bps = bank()
                    for p in range(8):
                        c0 = (p * 2 + d) * 128
                        for h in range(2):
                            hs = slice(h * 64, (h + 1) * 64)
                            mm(ps[hs, p * 64:(p + 1) * 64], gath[hs, sgi, c0:c0 + 64], cur[hs, p, :], [b_gath, bcur], bps)
                    nview = gath[:, sgi, 0:2048].rearrange("q (p d f) -> q p d f", d=2, f=128)[:, :, d, 64:128]
                    tt(dve, t1x[:], ps[:, :].rearrange("q (p f) -> q p f", f=64), nview, ALU.add, [bps, b_gath], [bps, b_t1x])
                    tt(dve, t1x[:], t1x[:], cur[:], ALU.subtract, [b_t1x, bcur], [b_t1x])
                    mcol = CV_FM + d * 4 + sgi
                    stt(nxt[:], t1x[:], colv[:, mcol:mcol + 1], cur[:], ALU.mult, ALU.add, [b_t1x, b_colv, bcur], [bnxt])
                    cur, bcur = nxt, bnxt
                cur, bcur = Gx[0]
                for g in range(4):
                    vcopy(cur[:, g, :], Hgctx[:, g * 2 + d, :], [b_Hgctx], [bcur])
                for i, sgi in enumerate(order):
                    nxt, bnxt = (Gx[(i + 1) % 2]) if i < 2 else ((Hgf0, b_Hgf0) if d == 0 else (Hgb0, b_Hgb0))
                    for g in range(4):
                        gi = g * 2 + d
                        stt(t2x[:, g, :], cur[:, g, :], gath[:, sgi, 2048 + gi:2048 + gi + 1],
                            gath[:, sgi, 2056 + gi * 256:2056 + (gi + 1) * 256], ALU.mult, ALU.add, [bcur, b_gath], [b_t2x])
                    tt(dve, t2x[:], t2x[:], cur[:], ALU.subtract, [b_t2x, bcur], [b_t2x])
                    mcol = CV_FM + d * 4 + sgi
                    stt(nxt[:], t2x[:], colv[:, mcol:mcol + 1], cur[:], ALU.mult, ALU.add, [b_t2x, b_colv, bcur], [bnxt])
                    cur, bcur = nxt, bnxt
            fw.barrier()
            fw.emit()
        if phases <= 2:
            return nc, fw

        def passB(stk, d, c, Hc, bHc, Hn, bHn, Gc, bGc, Gn_, bGn, stA_t, b_stA, stG_t, b_stG2):
            sp.dma(stA_t[:], spA[d, c], reads=[bD], writes=[b_stA])
            sp.dma(stG_t[:], spG[d, c], reads=[bD], writes=[b_stG2])
            par = c % 2
            pp = slice(par * 64, (par + 1) * 64)
            psH, bpsH = bank()
            psY, bpsY = bank()
            for p in range(8):
                for h in range(2):
                    hs = slice(h * 64, (h + 1) * 64)
                    mm(psH[hs, p * 64:(p + 1) * 64], stA_t[hs, p * 192:p * 192 + 64], Hc[hs, p, :], [b_stA, bHc], bpsH)
            nview = stA_t[:, :].rearrange("q (p f) -> q p f", f=192)[:, :, 128:192]
            tt(dve, Hn[:], psH[:, :].rearrange("q (p f) -> q p f", f=64), nview, ALU.add, [bpsH, b_stA], [bpsH, bHn])
            for g in range(4):
                stt(Gn_[:, g, :], Gc[:, g, :], stG_t[:, g * 321 + 320:g * 321 + 321], stG_t[:, g * 321 + 64:g * 321 + 320],
                    ALU.mult, ALU.add, [bGc, b_stG2], [bGn])
            for p in range(8):
                for h in range(2):
                    hs = slice(h * 64, (h + 1) * 64)
                    mm(psY[hs, p * 64:(p + 1) * 64], stA_t[hs, p * 192 + 64:p * 192 + 128], Hc[hs, p, :], [b_stA, bHc], bpsY)
            psO = [bank(), bank()]
            for g in range(4):
                po, bpo = psO[g // 2]
                mm(po[pp, (g % 2) * 256:(g % 2 + 1) * 256], stG_t[:, g * 321:g * 321 + 64], Gc[:, g, :], [b_stG2, bGc], bpo)
            return (psY, bpsY), psO, pp

        sW = ExitStack()
        wa_bf, b_wa = fw.sbuf(sW, "wa_bf", [128, 8, D], BF16)
        wb_bf, b_wbb = fw.sbuf(sW, "wb_bf", [128, 8, D], BF16)
        with ExitStack() as s2:
            wstW = [fw.sbuf(s2, f"wstW{j}", [128, D]) for j in range(2)]
            nwl = 0
            for (wsrc, wdst, bdst) in ((w_a, wa_bf, b_wa), (w_b, wb_bf, b_wbb)):
                for p in range(8):
                    wst_, b_wst_ = wstW[nwl % 2]
                    nwl += 1
                    sp.dma(wst_[:], wsrc[p * 128:(p + 1) * 128, :], writes=[b_wst_])
                    pool.op(lambda e, wdst=wdst, p=p, wst_=wst_: e.tensor_copy(out=wdst[:, p, :], in_=wst_[:]), reads=[b_wst_], writes=[bdst])
            def T2(name, shape, dt=F32):
                return fw.sbuf(s2, name, shape, dt)
            stA_b = [T2(f"stAb{j}", [128, 1536]) for j in range(2)]
            stG_b = [T2(f"stGb{j}", [128, 1284]) for j in range(2)]
            Hb = [T2(f"Hbk{j}", [128, 8, 64]) for j in range(2)]
            Gb = [T2(f"Gbk{j}", [128, 4, 256]) for j in range(2)]
            y0_t = [T2(f"y0t{j}", [128, 512]) for j in range(2)]
            yp_o = [T2(f"ypo{j}", [128, 512]) for j in range(2)]
            o0_t = [T2(f"o0t{j}", [128, 1024]) for j in range(2)]
            op_o = [T2(f"opo{j}", [128, 1024]) for j in range(2)]
            vcopy(Hb[0][0][:], Hb0[:], [b_Hb0], [Hb[0][1]])
            vcopy(Gb[0][0][:], Hgb0[:], [b_Hgb0], [Gb[0][1]])
            for i, c in enumerate(range(NCH - 1, -1, -1)):
                j = c // 2
                Hc, bHc = Hb[i % 2]
                Hn, bHn = Hb[(i + 1) % 2]
                Gc, bGc = Gb[i % 2]
                Gn_, bGn = Gb[(i + 1) % 2]
                (psY, bpsY), psO, pp = passB(s2, 1, c, Hc, bHc, Hn, bHn, Gc, bGc, Gn_, bGn, stA_b[i % 2][0], stA_b[i % 2][1],
                                             stG_b[i % 2][0], stG_b[i % 2][1])
                yt, byt = y0_t[i % 2]
                sp.dma(yt[:], y0d[c].rearrange("q p v -> q (p v)"), reads=[bD], writes=[byt])
                yo, byo = yp_o[i % 2]
                tt(dve, yo[:], psY[:, :], yt[:], ALU.add, [bpsY, byt], [bpsY, byo])
                act.dma(ypd[c].rearrange("q p v -> q (p v)"), yo[:], reads=[byo], writes=[bD], sem_buf=byo)
                ot, bot = o0_t[j % 2]
                oo, boo = op_o[j % 2]
                if c % 2 == 1:
                    sp.dma(ot[:], o0d[j * 128:(j + 1) * 128, :], reads=[bD], writes=[bot])
                for hb_ in range(2):
                    po, bpo = psO[hb_]
                    tt(dve, oo[pp, hb_ * 512:(hb_ + 1) * 512], po[pp, :], ot[pp, hb_ * 512:(hb_ + 1) * 512], ALU.add, [bpo, bot],
                       [bpo, boo])
                if c % 2 == 0:
                    act.dma(opd[j * 128:(j + 1) * 128, :], oo[:], reads=[boo], writes=[bD], sem_buf=boo)
            fw.barrier()
            fw.emit()
        if phases <= 3:
            return nc, fw

        with ExitStack() as s3:
            def T3(name, shape, dt=F32):
                return fw.sbuf(s3, name, shape, dt)
            lnxg_st, b_lnxg = T3("lnxg_st", [128, 8, 64])
            lnxb_st, b_lnxb = T3("lnxb_st", [128, 8, 64])
            bng_b, b_bng = T3("bng_b", [128, 4, 256])
            for h in range(2):
                hs = slice(h * 64, (h + 1) * 64)
                sp.dma(lnxg_st[hs, :, :], bass.AP(lnxg, h * 64, [[0, 64], [128, 8], [1, 64]]), writes=[b_lnxg])
                sp.dma(lnxb_st[hs, :, :], bass.AP(lnxb, h * 64, [[0, 64], [128, 8], [1, 64]]), writes=[b_lnxb])
            sp.dma(bng_b[:], bass.AP(bng, 0, [[0, 128], [0, 4], [1, 256]]), writes=[b_bng])
            stA_f = [T3(f"stAf{j}", [128, 1536]) for j in range(2)]
            stG_f = [T3(f"stGf{j}", [128, 1284]) for j in range(2)]
            Hf = [T3(f"Hfk{j}", [128, 8, 64]) for j in range(2)]
            Gf = [T3(f"Gfk{j}", [128, 4, 256]) for j in range(2)]
            yp_t = [T3(f"ypt{j}", [128, 512]) for j in range(2)]
            bv_t = [T3(f"bvt{j}", [128, 512]) for j in range(2)]
            w1S = [T3(f"w1_{j}", [128, 512]) for j in range(2)]
            w2S = [T3(f"w2_{j}", [128, 512]) for j in range(2)]
            stS = [T3(f"st_{j}", [128, 64]) for j in range(2)]
            ya_bdS = [T3(f"ya_bd{j}", [128, 8, 128], BF16) for j in range(2)]
            yaTS = [T3(f"yaT{j}", [128, 8, 128], BF16) for j in range(2)]
            ybTS = [T3(f"ybT{j}", [128, 8, 128], BF16) for j in range(1)] * 2
            zaT_t = [T3(f"zaTt{j}", [128, 16, 128], BF16) for j in range(2)]
            op_tS = [T3(f"op_t{j}", [128, 1024]) for j in range(1)] * 2
            obS = [T3(f"ob{j}", [128, 1024]) for j in range(2)]
            tmpb, b_tmpb = T3("tmpb", [128, 1024])
            yb_bfS = [T3(f"yb_bf{j}", [128, 1024], BF16) for j in range(1)] * 2
            zg_t, b_zgt = T3("zg_t", [128, 4096], BF16)
            m1, b_m1 = T3("m1", [128, 512])
            m2, b_m2 = T3("m2", [128, 512])
            mixedS = [T3(f"mixed{j}", [128, D], BF16) for j in range(1)] * 2
            mT_t = [T3(f"mTt{j}", [128, 16, 128], BF16) for j in range(1)] * 2
            for j_ in range(2):
                pool.op(lambda e, j_=j_: e.memset(ya_bdS[j_][0][:], 0.0), writes=[ya_bdS[j_][1]])
            vcopy(Hf[0][0][:], Hf0[:], [b_Hf0], [Hf[0][1]])
            vcopy(Gf[0][0][:], Hgf0[:], [b_Hgf0], [Gf[0][1]])
            b3 = lambda ap, n, m: ap.to_broadcast([128, n, m])
            def tile_work(j, st, b_st, ob, b_ob, zt_, bzt_, yaT, b_yaT):
                sp.dma(zg_t[:], zgd[j * 128:(j + 1) * 128, :], reads=[bD], writes=[b_zgt])
                ybT, b_ybT = ybTS[j % 2]
                yb_bf, b_ybbf = yb_bfS[j % 2]
                mixed, b_mixed = mixedS[j % 2]
                acopy(tmpb[:], ob[:], [b_ob], [b_tmpb], func=AF.Square)
                treduce(st[:, 56:60], tmpb[:].rearrange("q (g v) -> q g v", v=256), [b_tmpb], [b_st])
                ts(dve, st[:, 60:64], st[:, 56:60], 1.0 / 256, None, ALU.mult, None, [b_st], [b_st])
                acopy(st[:, 56:60], st[:, 60:64], [b_st], [b_st], func=AF.Sqrt, bias=1e-6)
                recip(st[:, 60:64], st[:, 56:60], [b_st], [b_st])
                ob3 = ob[:].rearrange("q (g v) -> q g v", v=256)
                tt(dve, ob3, ob3, b3(st[:, 60:64].rearrange("q (g o) -> q g o", o=1), 4, 256), ALU.mult, [b_ob, b_st], [b_ob])
                tt(pool, yb_bf[:].rearrange("q (g v) -> q g v", v=256), ob3, bng_b[:], ALU.mult, [b_ob, b_bng], [b_ybbf])
                for k4 in range(2):
                    psT, bpsT = bank()
                    for kk in range(4):
                        k = k4 * 4 + kk
                        mm(psT[:, kk * 128:(kk + 1) * 128], yb_bf[:, k * 128:(k + 1) * 128], identb[:], [b_ybbf, b_identb], bpsT)
                    tt(dve, ybT[:, k4 * 4:(k4 + 1) * 4, :], psT[:, :].rearrange("q (a t) -> q a t", t=128),
                       zt_[:, 8 + k4 * 4:8 + (k4 + 1) * 4, :], ALU.mult, [bpsT, bzt_], [bpsT, b_ybT])
                for nbk in range(4):
                    ns = slice(nbk * 512, (nbk + 1) * 512)
                    psA, bpsA = bank()
                    for p in range(8):
                        mm(psA[:, :], yaT[:, p, :], wa_bf[:, p, ns], [b_yaT, b_wa], bpsA, start=(p == 0), stop=(p == 7))
                    psB, bpsB = bank()
                    for p in range(8):
                        mm(psB[:, :], ybT[:, p, :], wb_bf[:, p, ns], [b_ybT, b_wbb], bpsB, start=(p == 0), stop=(p == 7))
                    tt(dve, m1[:], psA[:, :], zg_t[:, ns], ALU.mult, [bpsA, b_zgt], [bpsA, b_m1])
                    tt(dve, m2[:], psB[:, :], zg_t[:, 2048 + nbk * 512:2048 + (nbk + 1) * 512], ALU.mult, [bpsB, b_zgt], [bpsB, b_m2])
                    tt(pool, mixed[:, ns], m1[:], m2[:], ALU.add, [b_m1, b_m2], [b_mixed])
                mt_, bmt_ = mT_t[j % 2]
                for k4 in range(4):
                    psT, bpsT = bank()
                    for kk in range(4):
                        k = k4 * 4 + kk
                        mm(psT[:, kk * 128:(kk + 1) * 128], mixed[:, k * 128:(k + 1) * 128], identb[:], [b_mixed, b_identb], bpsT)
                    ecopy(mt_[:, k4 * 4:(k4 + 1) * 4, :], psT[:, :].rearrange("q (a t) -> q a t", t=128), [bpsT], [bpsT, bmt_])
                act.dma(mTd[:, :, j * 128:(j + 1) * 128], mt_[:], reads=[bmt_], writes=[bD], sem_buf=bmt_)
            pending = [None]
            for c in range(NCH):
                j = c // 2
                par = c % 2
                Hc, bHc = Hf[c % 2]
                Hn, bHn = Hf[(c + 1) % 2]
                Gc, bGc = Gf[c % 2]
                Gn_, bGn = Gf[(c + 1) % 2]
                zt_, bzt_ = zaT_t[j % 2]
                w1, b_w1 = w1S[c % 2]
                w2, b_w2 = w2S[c % 2]
                st, b_st = stS[c % 2]
                ya_bd, b_yabd = ya_bdS[c % 2]
                yaT, b_yaT = yaTS[j % 2]
                ybT, b_ybT = ybTS[j % 2]
                op_t, b_opt = op_tS[j % 2]
                ob, b_ob = obS[j % 2]
                yb_bf, b_ybbf = yb_bfS[j % 2]
                mixed, b_mixed = mixedS[j % 2]
                if par == 0:
                    sp.dma(zt_[:], zaTd[:, :, j * 128:(j + 1) * 128].rearrange("b p t -> p b t"), reads=[bD], writes=[bzt_])
                    sp.dma(op_t[:], opd[j * 128:(j + 1) * 128, :], reads=[bD], writes=[b_opt])
                (psY, bpsY), psO, pp = passB(s3, 0, c, Hc, bHc, Hn, bHn, Gc, bGc, Gn_, bGn, stA_f[c % 2][0], stA_f[c % 2][1],
                                             stG_f[c % 2][0], stG_f[c % 2][1])
                ypt, bypt = yp_t[c % 2]
                bvt, bbvt = bv_t[c % 2]
                sp.dma(ypt[:], ypd[c].rearrange("q p v -> q (p v)"), reads=[bD], writes=[bypt])
                sp.dma(bvt[:], bvd[c].rearrange("q p v -> q (p v)"), reads=[bD], writes=[bbvt])
                tt(dve, w1[:], psY[:, :], ypt[:], ALU.add, [bpsY, bypt], [bpsY, b_w1])
                for hb_ in range(2):
                    po, bpo = psO[hb_]
                    tt(dve, ob[pp, hb_ * 512:(hb_ + 1) * 512], po[pp, :], op_t[pp, hb_ * 512:(hb_ + 1) * 512], ALU.add, [bpo, b_opt],
                       [bpo, b_ob])
                w13 = w1[:].rearrange("q (p v) -> q p v", v=64)
                treduce(st[:, 0:8], w13, [b_w1], [b_st])
                acopy(w2[:], w1[:], [b_w1], [b_w2], func=AF.Square)
                treduce(st[:, 8:16], w2[:].rearrange("q (p v) -> q p v", v=64), [b_w2], [b_st])
                ts(dve, st[:, 16:24], st[:, 0:8], 1.0 / 64, None, ALU.mult, None, [b_st], [b_st])
                tt(dve, st[:, 24:32], st[:, 16:24], st[:, 16:24], ALU.mult, [b_st], [b_st])
                stt(st[:, 32:40], st[:, 8:16], 1.0 / 64, st[:, 24:32], ALU.mult, ALU.subtract, [b_st], [b_st])
                acopy(st[:, 40:48], st[:, 32:40], [b_st], [b_st], func=AF.Sqrt, bias=64e-5)
                recip(st[:, 48:56], st[:, 40:48], [b_st], [b_st])
                tt(dve, w13, w13, b3(st[:, 16:24].rearrange("q (p o) -> q p o", o=1), 8, 64), ALU.subtract, [b_w1, b_st], [b_w1])
                tt(dve, w13, w13, b3(st[:, 48:56].rearrange("q (p o) -> q p o", o=1), 8, 64), ALU.mult, [b_w1, b_st], [b_w1])
                tt(pool, w13, w13, lnxg_st[:], ALU.mult, [b_w1, b_lnxg], [b_w1])
                tt(pool, w13, w13, lnxb_st[:], ALU.add, [b_w1, b_lnxb], [b_w1])
                for h in range(2):
                    hs = slice(h * 64, (h + 1) * 64)
                    tt(dve if h == 0 else pool, ya_bd[hs, :, h * 64:(h + 1) * 64], w1[hs, :].rearrange("q (p v) -> q p v", v=64),
                       bvt[hs, :].rearrange("q (p v) -> q p v", v=64), ALU.add, [b_w1, bbvt], [b_yabd])
                psT, bpsT = bank()
                for p in range(8):
                    mm(psT[:, p * 64:(p + 1) * 64], ya_bd[:, p, :], identstb[:], [b_yabd, b_identstb], bpsT)
                tt(dve, yaT[:, :, par * 64:(par + 1) * 64], psT[:, :].rearrange("q (p t) -> q p t", t=64),
                   zt_[:, 0:8, par * 64:(par + 1) * 64], ALU.mult, [bpsT, bzt_], [bpsT, b_yaT])
                if pending[0] is not None:
                    tile_work(*pending[0])
                    pending[0] = None
                if par == 1:
                    pending[0] = (j, st, b_st, ob, b_ob, zt_, bzt_, yaT, b_yaT)
            if pending[0] is not None:
                tile_work(*pending[0])
            fw.barrier()
            fw.emit()
        sW.close()
        if phases <= 4:
            return nc, fw

        with ExitStack() as s4:
            def T4(name, shape, dt=F32):
                return fw.sbuf(s4, name, shape, dt)
            wo_bf, b_wo = T4("wo_bf", [128, 16, D], BF16)
            wst4 = [T4(f"wst4_{j}", [128, D]) for j in range(3)]
            b_wos = [Buf(f"wo{k}") for k in range(16)]
            for k in range(16):
                wst, b_wst = wst4[k % 3]
                sp.dma(wst[:], w_out[k * 128:(k + 1) * 128, :], writes=[b_wst])
                if k % 3 == 0:
                    pool.op(lambda e, k=k, wst=wst: e.tensor_copy(out=wo_bf[:, k, :], in_=wst[:]), reads=[b_wst], writes=[b_wos[k]])
                elif k % 3 == 1:
                    vcopy(wo_bf[:, k, :], wst[:], [b_wst], [b_wos[k]])
                else:
                    acopy(wo_bf[:, k, :], wst[:], [b_wst], [b_wos[k]])
            gate_t, b_gt = T4("gate_t", [128, D])
            fg_t, b_fg = T4("fg_t", [128, D])
            sp.dma(gate_t[:], gate_d, reads=[bD], writes=[b_gt])
            sp.dma(fg_t[:], bc(final_g, D), writes=[b_fg])
            mTi = [T4(f"mTi{j}", [128, 16, 128], BF16) for j in range(2)]
            xin = [T4(f"xin{j}", [128, D]) for j in range(2)]
            o_t = [T4(f"o_t{j}", [128, D]) for j in range(2)]
            res_t = [T4(f"res{j}", [128, D]) for j in range(2)]
            sq4, b_sq4 = T4("sq4", [128, D])
            s4t, b_s4t = T4("s4t", [128, 4])
            for j in range(32):
                mi, bmi = mTi[j % 2]
                xi, bxi = xin[j % 2]
                ot, bot = o_t[j % 2]
                rt_, brt = res_t[j % 2]
                sp.dma(mi[:], mTd[:, :, j * 128:(j + 1) * 128], reads=[bD], writes=[bmi])
                sp.dma(xi[:], xs[64 + j * 128:64 + (j + 1) * 128, :], writes=[bxi])
                for nbk in range(4):
                    ns = slice(nbk * 512, (nbk + 1) * 512)
                    psW, bpsW = bank()
                    for k in range(16):
                        mm(psW[:, :], mi[:, k, :], wo_bf[:, k, ns], [bmi, b_wos[k]], bpsW, start=(k == 0), stop=(k == 15))
                    tt(dve, ot[:, ns], psW[:, :], gate_t[:, ns], ALU.mult, [bpsW, b_gt], [bpsW, bot])
                    tt(pool, ot[:, ns], ot[:, ns], xi[:, ns], ALU.add, [bot, bxi], [bot])
                acopy(sq4[:], ot[:], [bot], [b_sq4, b_s4t], func=AF.Square, accum_out=s4t[:, 0:1])
                ts(dve, s4t[:, 1:2], s4t[:, 0:1], 1.0 / D, 1e-6, ALU.mult, ALU.add, [b_s4t], [b_s4t])
                acopy(s4t[:, 2:3], s4t[:, 1:2], [b_s4t], [b_s4t], func=AF.Sqrt)
                dve.op(lambda e: e.reciprocal(out=s4t[:, 3:4], in_=s4t[:, 2:3]), reads=[b_s4t], writes=[b_s4t])
                stt(rt_[:], ot[:], s4t[:, 3:4], fg_t[:], ALU.mult, ALU.mult, [bot, b_s4t, b_fg], [brt])
                act.dma(out_d[j * 128:(j + 1) * 128, :], rt_[:], reads=[brt], writes=[bD], sem_buf=brt)
            fw.barrier()
            fw.emit()
    return nc, fw


_CACHE = {}


def _consts():
    idx = np.arange(64)
    identf = np.eye(128, dtype=np.float32)
    identst = np.concatenate([np.eye(64), np.eye(64)], 0).astype(np.float32)
    st_f = (idx[None, :] > idx[:, None]).astype(np.float32)
    in_f = (idx[None, :] >= idx[:, None]).astype(np.float32)
    st_b = (idx[None, :] < idx[:, None]).astype(np.float32)
    in_b = (idx[None, :] <= idx[:, None]).astype(np.float32)
    mG = np.zeros((128, 2, 128), np.float32)
    for h in range(2):
        mG[h * 64:(h + 1) * 64, 0, 0:64] = st_f
        mG[h * 64:(h + 1) * 64, 0, 64:128] = in_f
        mG[h * 64:(h + 1) * 64, 1, 0:64] = st_b
        mG[h * 64:(h + 1) * 64, 1, 64:128] = in_b
    mL = np.zeros((128, 2, 64), np.float32)
    for h in range(2):
        mL[h * 64:(h + 1) * 64, 0, :] = st_f.T
        mL[h * 64:(h + 1) * 64, 1, :] = st_b.T
    mA = np.stack([in_f, in_b], 1).astype(np.float32)
    bones = np.zeros((128, 128), np.float32)
    bones[0:64, 0:64] = 1
    bones[64:128, 64:128] = 1
    ind2 = np.zeros((128, 2), np.float32)
    ind2[0:64, 0] = 1
    ind2[64:128, 1] = 1
    rmask = np.ones((128, 512), np.float32)
    rmask[:, ::64] = 0
    return dict(identf=identf, identst=identst, maskG=mG, maskL=mL, maskA=mA, bones=bones, ind2=ind2, rmask=rmask)


def _prep_inputs(inp):
    f = lambda a: np.ascontiguousarray(np.asarray(a, dtype=np.float32))
    x = f(inp["x"]); c = f(inp["c"]); ctx = f(inp["ctx"]); c_ctx = f(inp["c_ctx"])
    conv_w = f(inp["conv_w"])[0].reshape(9, 3072)
    shared = dict(
        w_mod=f(inp["w_mod"])[0], b_mod=f(inp["b_mod"])[0], norm_g=f(inp["norm_g"])[0], w_in=f(inp["w_in"])[0],
        aw2p=f(inp["a_w2"])[0].reshape(128, 1024), aa2p=f(inp["a_a2"])[0].reshape(128, 1024),
        lnxg=f(inp["a_lnx_g"])[0], lnxb=f(inp["a_lnx_b"])[0], bng=f(inp["b_norm_g"])[0],
        w_a=f(inp["w_a"])[0], w_b=f(inp["w_b"])[0], w_out=f(inp["w_out"])[0], final_g=f(inp["final_g"]),
    )
    gk = f(inp["b_gk_w2"])[0]
    gkw2p = np.zeros((32, 2, 512), np.float32)
    gkw2p[0:16, 0] = gk[0]
    gkw2p[16:32, 1] = gk[1]
    shared["gkw2p"] = gkw2p
    shared.update(_consts())
    colv0 = np.zeros((128, NCOLV), np.float32)
    colv0[:, 0:216] = conv_w.reshape(9, 24, 128).transpose(2, 1, 0).reshape(128, 216)
    for d in range(2):
        colv0[:, CV_W0 + d * 8:CV_W0 + d * 8 + 8] = f(inp["a_w0"])[0, d].reshape(8, 128).T
        colv0[:, CV_A0 + d * 8:CV_A0 + d * 8 + 8] = f(inp["a_a0"])[0, d].reshape(8, 128).T
        colv0[:, CV_GB + d * 4:CV_GB + d * 4 + 4] = f(inp["b_gk_b"])[0, d].reshape(4, 128).T
    colv0[:, CV_KK:CV_KK + 8] = f(inp["a_k_k"])[0].reshape(8, 128).T
    colv0[:, CV_KA:CV_KA + 8] = f(inp["a_k_a"])[0].reshape(8, 128).T
    colv0[:, CV_RK:CV_RK + 8] = f(inp["a_r_k"])[0].reshape(8, 128).T
    maps = []
    for core in range(8):
        b, q = core // 4, core % 4
        xs = np.zeros((4224, D), np.float32)
        lo, hi = q * SEG - 64, (q + 1) * SEG + 64
        slo, shi = max(lo, 0), min(hi, 4 * SEG)
        xs[slo - lo:shi - lo] = x[b, slo:shi]
        cv = colv0.copy()
        cv[:, CV_HALO] = 0.0 if q == 0 else 1.0
        cv[:, CV_HALO + 1] = 0.0 if q == 3 else 1.0
        for s in range(4):
            cv[:, CV_FM + s] = 1.0 if s < q else 0.0
            cv[:, CV_FM + 4 + s] = 1.0 if s > q else 0.0
        cT = np.stack([c[b], c_ctx], 1).reshape(16, 128, 2).transpose(1, 0, 2)
        m = dict(shared)
        m.update(xs=xs, ctxb=np.ascontiguousarray(ctx[b]), cT=np.ascontiguousarray(cT), colv=cv)
        maps.append(m)
    return maps


def kernel(**inputs):
    maps = _prep_inputs(inputs)
    if "nc" not in _CACHE:
        _CACHE["nc"] = build_program()[0]
    res = run_bass_kernel_spmd(_CACHE["nc"], maps, core_ids=list(range(8)))
    out = np.zeros((2, 4 * SEG, D), np.float32)
    for core in range(8):
        b, q = core // 4, core % 4
        out[b, q * SEG:(q + 1) * SEG] = res.results[core]["out"]
    return out
```
